# Optimizing a Trainium2 kernel written in Bass

```python
import math
import jax, jax.numpy as jnp
from jax import lax
import numpy as np

D_MODEL = 1024
BATCH = 2
SEQ = 8192
DEPTH = 4

GRID_W = 64
CTX_LEN = 256

ATTN_HEADS = 4
ATTN_HEAD_DIM = 64
ATTN_V_DIM = 2 * ATTN_HEAD_DIM
ATTN_WIDTH = ATTN_HEADS * ATTN_V_DIM
QK_WIDTH = ATTN_HEADS * 2 * ATTN_HEAD_DIM
POOL_WINDOWS = (2, 4, 8, 16)
POOL_WIDTH = D_MODEL // 4
POOL_GROUP = POOL_WIDTH // len(POOL_WINDOWS)
CONV_WIDTH = D_MODEL // 4
CONV_KERNEL = 31
CONV_PAD = CONV_KERNEL // 2

MIX_WIDTH = ATTN_WIDTH + POOL_WIDTH + CONV_WIDTH
IN_WIDTH = 2 * QK_WIDTH + ATTN_WIDTH + POOL_WIDTH + 2 * CONV_WIDTH
SPLITS = (QK_WIDTH, 2 * QK_WIDTH, 2 * QK_WIDTH + ATTN_WIDTH,
          2 * QK_WIDTH + ATTN_WIDTH + POOL_WIDTH,
          2 * QK_WIDTH + ATTN_WIDTH + POOL_WIDTH + CONV_WIDTH)

FFN_HIDDEN = ((8 * D_MODEL + 3 * 256 - 1) // (3 * 256)) * 256
ROPE_THETA = 10000.0
EPS = 1e-6
BLOCK_Q = 128

kernel_name = "hybrid_diffattn_pool_conformer_dit"


def rms_norm(x, g):
    xf = x.astype(jnp.float32)
    y = xf * lax.rsqrt(jnp.mean(xf * xf, axis=-1, keepdims=True) + EPS)
    return (y * g.astype(jnp.float32)).astype(x.dtype)


def layer_norm(x, g, b):
    xf = x.astype(jnp.float32)
    mu = jnp.mean(xf, axis=-1, keepdims=True)
    var = jnp.mean(jnp.square(xf - mu), axis=-1, keepdims=True)
    y = (xf - mu) * lax.rsqrt(var + EPS)
    return (y * g.astype(jnp.float32) + b.astype(jnp.float32)).astype(x.dtype)


def adaln(cond, w_mod, b_mod):
    m = jnp.dot(jax.nn.silu(cond), w_mod) + b_mod
    return jnp.split(m[..., None, :], 6, axis=-1)


def modulate(h, shift, scale):
    return h * (1.0 + scale) + shift


def axial_rope_tables(n_rows):
    row = jnp.repeat(jnp.arange(n_rows), GRID_W).astype(jnp.float32)
    col = jnp.tile(jnp.arange(GRID_W), n_rows).astype(jnp.float32)
    half = ATTN_HEAD_DIM // 2
    inv_freq = ROPE_THETA ** (-jnp.arange(0, half, 2, dtype=jnp.float32) / half)
    ang = jnp.concatenate([row[:, None] * inv_freq, col[:, None] * inv_freq], axis=-1)
    return jnp.cos(ang), jnp.sin(ang)


def apply_rope(x, cos, sin):
    x1 = x[..., 0::2].astype(jnp.float32)
    x2 = x[..., 1::2].astype(jnp.float32)
    y1 = x1 * cos - x2 * sin
    y2 = x1 * sin + x2 * cos
    return jnp.stack([y1, y2], axis=-1).reshape(x.shape).astype(x.dtype)


def project_inputs(h, w_in, q_g, k_g):
    B, L, _ = h.shape
    proj = jnp.einsum('bld,de->ble', h, w_in)
    q, k, v, u_pool, a_conv, b_conv = jnp.split(proj, list(SPLITS), axis=-1)
    q = rms_norm(q.reshape(B, L, ATTN_HEADS, 2, ATTN_HEAD_DIM), q_g).transpose(0, 2, 3, 1, 4)
    k = rms_norm(k.reshape(B, L, ATTN_HEADS, 2, ATTN_HEAD_DIM), k_g).transpose(0, 2, 3, 1, 4)
    v = v.reshape(B, L, ATTN_HEADS, ATTN_V_DIM).transpose(0, 2, 1, 3)
    return q, k, v, u_pool, a_conv, b_conv


def diff_attention(q, k_all, v_all, lam):
    B, H, _, Lq, dh = q.shape
    nb = Lq // BLOCK_Q
    qb = q.reshape(B, H, 2, nb, BLOCK_Q, dh).transpose(3, 0, 1, 2, 4, 5)
    scale = dh ** -0.5

    def one_block(q_blk):
        s = jnp.einsum('bhcqd,bhckd->bhcqk', q_blk, k_all).astype(jnp.float32) * scale
        p = jax.nn.softmax(s, axis=-1)
        a = p[:, :, 0] - lam * p[:, :, 1]
        return jnp.einsum('bhqk,bhkd->bhqd', a.astype(v_all.dtype), v_all)

    o = lax.map(one_block, qb)
    return o.transpose(1, 2, 0, 3, 4).reshape(B, H, Lq, v_all.shape[-1])


def diff_attn_post(o, subln_g, lambda_init):
    B, H, L, dv = o.shape
    o = rms_norm(o, subln_g) * (1.0 - lambda_init)
    return o.transpose(0, 2, 1, 3).reshape(B, L, H * dv)


def multiscale_pool(u):
    B, L, C = u.shape
    uf = u.astype(jnp.float32)
    csum = jnp.concatenate([jnp.zeros((B, 1, C), jnp.float32), jnp.cumsum(uf, axis=1)], axis=1)
    t = jnp.arange(L)
    outs = []
    for gi, w in enumerate(POOL_WINDOWS):
        lo = jnp.clip(t - w // 2, 0, L)
        hi = jnp.clip(t + w - w // 2, 0, L)
        sl = slice(gi * POOL_GROUP, (gi + 1) * POOL_GROUP)
        seg = csum[:, :, sl]
        cnt = (hi - lo).astype(jnp.float32)[None, :, None]
        outs.append((seg[:, hi] - seg[:, lo]) / cnt - uf[:, :, sl])
    return jnp.concatenate(outs, axis=-1).astype(u.dtype)


def pool_mixer(u, pool_w, pool_scale):
    B, L, _ = u.shape
    p = multiscale_pool(u).reshape(B, L, len(POOL_WINDOWS), POOL_GROUP)
    y = jnp.einsum('blgc,gce->blge', p, pool_w).reshape(B, L, POOL_WIDTH)
    return y * pool_scale


def conformer_conv(a, b, dw_w, dw_b, ln_g, ln_b):
    u = a * jax.nn.sigmoid(b)
    y = lax.conv_general_dilated(
        u, dw_w[:, None, :].astype(u.dtype), window_strides=(1,),
        padding=((CONV_PAD, CONV_PAD),), dimension_numbers=('NWC', 'WIO', 'NWC'),
        feature_group_count=CONV_WIDTH)
    y = layer_norm(y + dw_b, ln_g, ln_b)
    return jax.nn.silu(y)


def swiglu(h, w_in, w_out):
    gu = jnp.einsum('bld,de->ble', h, w_in)
    g, u = jnp.split(gu, 2, axis=-1)
    return jnp.einsum('blf,fd->bld', jax.nn.silu(g) * u, w_out)


def setup_inputs(seed: int = 0) -> dict:
    key = jax.random.key(seed)
    ks = jax.random.split(key, 26)
    f32 = jnp.float32
    n = lambda k, s: jax.random.normal(k, s, f32)
    return {
        "x": n(ks[0], (BATCH, SEQ, D_MODEL)),
        "c": n(ks[1], (BATCH, D_MODEL)),
        "ctx": n(ks[2], (BATCH, CTX_LEN, D_MODEL)),
        "c_ctx": n(ks[3], (D_MODEL,)),
        "w_mod": n(ks[4], (DEPTH, D_MODEL, 6 * D_MODEL)) * (0.5 * D_MODEL ** -0.5),
        "b_mod": n(ks[5], (DEPTH, 6 * D_MODEL)) * 0.01,
        "norm1_g": 1.0 + 0.02 * n(ks[6], (DEPTH, D_MODEL)),
        "w_in": n(ks[7], (DEPTH, D_MODEL, IN_WIDTH)) * D_MODEL ** -0.5,
        "q_norm_g": 1.0 + 0.02 * n(ks[8], (DEPTH, ATTN_HEAD_DIM)),
        "k_norm_g": 1.0 + 0.02 * n(ks[9], (DEPTH, ATTN_HEAD_DIM)),
        "lambda_q1": 0.1 * n(ks[10], (DEPTH, ATTN_HEAD_DIM)),
        "lambda_k1": 0.1 * n(ks[11], (DEPTH, ATTN_HEAD_DIM)),
        "lambda_q2": 0.1 * n(ks[12], (DEPTH, ATTN_HEAD_DIM)),
        "lambda_k2": 0.1 * n(ks[13], (DEPTH, ATTN_HEAD_DIM)),
        "subln_g": 1.0 + 0.02 * n(ks[14], (DEPTH, ATTN_V_DIM)),
        "pool_w": n(ks[15], (DEPTH, len(POOL_WINDOWS), POOL_GROUP, POOL_GROUP)) * POOL_GROUP ** -0.5,
        "pool_scale": 1.0 + 0.02 * n(ks[16], (DEPTH, POOL_WIDTH)),
        "conv_dw_w": n(ks[17], (DEPTH, CONV_KERNEL, CONV_WIDTH)) * CONV_KERNEL ** -0.5,
        "conv_dw_b": 0.01 * n(ks[18], (DEPTH, CONV_WIDTH)),
        "conv_ln_g": 1.0 + 0.02 * n(ks[19], (DEPTH, CONV_WIDTH)),
        "conv_ln_b": 0.01 * n(ks[20], (DEPTH, CONV_WIDTH)),
        "w_out": n(ks[21], (DEPTH, MIX_WIDTH, D_MODEL)) * MIX_WIDTH ** -0.5,
        "norm2_g": 1.0 + 0.02 * n(ks[22], (DEPTH, D_MODEL)),
        "w_ffn_in": n(ks[23], (DEPTH, D_MODEL, 2 * FFN_HIDDEN)) * D_MODEL ** -0.5,
        "w_ffn_out": n(ks[24], (DEPTH, FFN_HIDDEN, D_MODEL)) * FFN_HIDDEN ** -0.5,
    }


def reference(x, c, ctx, c_ctx, w_mod, b_mod, norm1_g, w_in, q_norm_g, k_norm_g,
              lambda_q1, lambda_k1, lambda_q2, lambda_k2, subln_g, pool_w, pool_scale,
              conv_dw_w, conv_dw_b, conv_ln_g, conv_ln_b, w_out, norm2_g, w_ffn_in,
              w_ffn_out):
    B, L, _ = x.shape
    rows = L // GRID_W
    cos, sin = axial_rope_tables(rows)
    xc = ctx
    for l in range(DEPTH):
        last = l == DEPTH - 1
        lambda_init = 0.8 - 0.6 * math.exp(-0.3 * l)
        lam = (jnp.exp(jnp.sum(lambda_q1[l] * lambda_k1[l]).astype(jnp.float32))
               - jnp.exp(jnp.sum(lambda_q2[l] * lambda_k2[l]).astype(jnp.float32))
               + lambda_init)
        sh1, sc1, g1, sh2, sc2, g2 = adaln(c, w_mod[l], b_mod[l])
        csh1, csc1, cg1, csh2, csc2, cg2 = adaln(c_ctx, w_mod[l], b_mod[l])

        h = modulate(rms_norm(x, norm1_g[l]), sh1, sc1)
        hc = modulate(rms_norm(xc, norm1_g[l]), csh1, csc1)
        q, k, v, up, ca, cb = project_inputs(h, w_in[l], q_norm_g[l], k_norm_g[l])
        qc, kc, vc, upc, cac, cbc = project_inputs(hc, w_in[l], q_norm_g[l], k_norm_g[l])
        q = apply_rope(q, cos, sin)
        k = apply_rope(k, cos, sin)
        k_all = jnp.concatenate([kc, k], axis=3)
        v_all = jnp.concatenate([vc, v], axis=2)
        attn = diff_attn_post(diff_attention(q, k_all, v_all, lam), subln_g[l], lambda_init)
        pool_o = pool_mixer(up, pool_w[l], pool_scale[l])
        conv_o = conformer_conv(ca, cb, conv_dw_w[l], conv_dw_b[l], conv_ln_g[l], conv_ln_b[l])
        mix = jnp.concatenate([attn, pool_o, conv_o], axis=-1)
        x = x + g1 * jnp.einsum('blm,md->bld', mix, w_out[l])

        x = x + g2 * swiglu(modulate(rms_norm(x, norm2_g[l]), sh2, sc2), w_ffn_in[l], w_ffn_out[l])

        if not last:
            attn_c = diff_attn_post(diff_attention(qc, kc, vc, lam), subln_g[l], lambda_init)
            pool_c = pool_mixer(upc, pool_w[l], pool_scale[l])
            conv_c = conformer_conv(cac, cbc, conv_dw_w[l], conv_dw_b[l], conv_ln_g[l], conv_ln_b[l])
            mix_c = jnp.concatenate([attn_c, pool_c, conv_c], axis=-1)
            xc = xc + cg1 * jnp.einsum('blm,md->bld', mix_c, w_out[l])
            xc = xc + cg2 * swiglu(modulate(rms_norm(xc, norm2_g[l]), csh2, csc2),
                                   w_ffn_in[l], w_ffn_out[l])
    return x
```

```python
import math
from contextlib import ExitStack

import numpy as np
import ml_dtypes

import concourse.bass as bass
import concourse.mybir as mybir
from concourse.bass_utils import run_bass_kernel_spmd

F32 = mybir.dt.float32
BF16 = mybir.dt.bfloat16
AF = mybir.ActivationFunctionType
ALU = mybir.AluOpType
AX = mybir.AxisListType
NPBF = ml_dtypes.bfloat16

D = 1024
DEPTH = 4
NCORE = 8
TL = 2048
TC = 256
T = TL + TC
SEQ = 8192
NKEY = SEQ + TC
NKT = NKEY // 128
FF = 2816
NJ = FF // 128
EPS = 1e-6
HALO = 16
GROUPS = [(0, 512), (512, 512), (1024, 512), (1536, 512), (2048, 256)]


class Sched:
    ENGS = ("pe", "act", "dve", "pool", "sp")

    def __init__(self, nc):
        self.nc = nc
        self.q = {e: [] for e in self.ENGS}
        self.cnt = {}
        self.res = {}
        self.waited = {e: {} for e in self.ENGS}
        self.bar = {}

    def barrier(self):
        self.bar = dict(self.cnt)

    def op(self, eng, fn, r=(), w=(), key=None, inc1=False):
        if key is None:
            sem, inc = "S_" + eng, 1
        elif inc1:
            sem, inc = "C_" + key, 1
        else:
            sem, inc = "D_" + key, 16
        deps = dict(self.bar)

        def need(sv):
            s, v = sv
            if deps.get(s, 0) < v:
                deps[s] = v

        for k in r:
            st = self.res.get(k)
            if st and st[0]:
                need(st[0])
            if st and k.startswith("ps"):
                for sv in st[1].items():
                    if sv[0] != sem:
                        need(sv)
        for k in w:
            st = self.res.get(k)
            if st:
                if st[0]:
                    need(st[0])
                for sv in st[1].items():
                    need(sv)
        waits = []
        for s, v in deps.items():
            if eng == "pe" and s == "S_pe":
                continue
            if self.waited[eng].get(s, 0) >= v:
                continue
            self.waited[eng][s] = v
            waits.append((s, v))
        val = self.cnt.get(sem, 0) + inc
        self.cnt[sem] = val
        self.q[eng].append((waits, fn, sem, inc))
        for k in r:
            st = self.res.setdefault(k, [None, {}])
            if st[1].get(sem, 0) < val:
                st[1][sem] = val
        for k in w:
            self.res[k] = [(sem, val), {}]

    def finish(self):
        waits = [(s, v) for s, v in self.cnt.items() if s.startswith("D_") or s.startswith("C_")]
        self.q["sp"].append((waits, None, None, 0))

    def emit(self, es):
        nc = self.nc
        sems = {n: es.enter_context(nc.semaphore(n)) for n in sorted(self.cnt)}
        block = es.enter_context(nc.Block())

        def run(name):
            def f(e):
                for waits, fn, sem, inc in self.q[name]:
                    attach = fn is not None and waits and not sem.startswith("C_")
                    for s, v in (waits[:-1] if attach else waits):
                        e.wait_ge(sems[s], v)
                    if fn is not None:
                        ins = fn(e)
                        if attach:
                            ins._wait_ge(sems[waits[-1][0]], waits[-1][1])
                        ins.then_inc(sems[sem], inc)
            return f

        block.tensor(run("pe"))
        block.scalar(run("act"))
        block.vector(run("dve"))
        block.gpsimd(run("pool"))
        block.sync(run("sp"))


class Ctx:
    def __init__(self):
        self.nc = bass.Bass("TRN2", target_bir_lowering=False)
        self.es = ExitStack()
        self.S = Sched(self.nc)
        self.n = 0
        self.stacks = [self.es]
        self.pfx = ""

    def push(self, pfx=None):
        if pfx is not None:
            self.pfx = pfx
        st = ExitStack()
        self.stacks.append(st)
        return st

    def pop(self):
        self.stacks.pop().close()
        self.S.barrier()

    def sb(self, name, shape, dt):
        return self.stacks[-1].enter_context(self.nc.sbuf_tensor(self.pfx + name, list(shape), dt))

    def din(self, name, shape, dt):
        return self.nc.dram_tensor(name, list(shape), dt, kind="ExternalInput").ap()

    def dout(self, name, shape, dt):
        return self.nc.dram_tensor(name, list(shape), dt, kind="ExternalOutput").ap()

    def dscratch(self, name, shape, dt):
        return self.nc.dram_tensor(name, list(shape), dt, kind="Internal").ap()

    def uid(self, p):
        self.n += 1
        return f"{p}{self.n}"

    def dma(self, q, out, in_, r, w, key, slow=False):
        if slow:
            self.S.op(q, lambda e: e.dma_start(out=out, in_=in_, allow_slow_non_contiguous=True), r=r, w=w, key=key)
        else:
            self.S.op(q, lambda e: e.dma_start(out=out, in_=in_), r=r, w=w, key=key)

    def mm(self, out, lhsT, rhs, start, stop, r, w):
        self.S.op("pe", lambda e: e.matmul(out, lhsT, rhs, start=start, stop=stop), r=r, w=w)

    def act(self, out, in_, func, r, w, bias=None, scale=None):
        kw = {}
        if bias is not None:
            kw["bias"] = bias
        if scale is not None:
            kw["scale"] = scale
        self.S.op("act", lambda e: e.activation(out=out, in_=in_, func=func, **kw), r=r, w=w)

    def tt(self, eng, out, in0, in1, op, r, w):
        self.S.op(eng, lambda e: e.tensor_tensor(out=out, in0=in0, in1=in1, op=op), r=r, w=w)

    def ts(self, eng, out, in0, s1, op0, r, w, s2=None, op1=None):
        if op1 is None:
            self.S.op(eng, lambda e: e.tensor_scalar(out=out, in0=in0, scalar1=s1, scalar2=None, op0=op0),
                      r=r, w=w)
        else:
            self.S.op(eng, lambda e: e.tensor_scalar(out=out, in0=in0, scalar1=s1, scalar2=s2, op0=op0, op1=op1),
                      r=r, w=w)

    def stt(self, out, in0, scalar, in1, op0, op1, r, w):
        self.S.op("dve", lambda e: e.scalar_tensor_tensor(out=out, in0=in0, scalar=scalar, in1=in1,
                                                          op0=op0, op1=op1), r=r, w=w)

    def copy(self, eng, out, in_, r, w):
        self.S.op(eng, lambda e: e.tensor_copy(out=out, in_=in_), r=r, w=w)

    def memset(self, eng, ap, val, w):
        self.S.op(eng, lambda e: e.memset(ap, val), w=w)

    def recip(self, out, in_, r, w):
        self.S.op("dve", lambda e: e.reciprocal(out=out, in_=in_), r=r, w=w)

    def recip_act(self, out, in_, r, w, one=None):
        if one is not None:
            self.act(out, in_, AF.Ln, r=list(r) + ["cmat_f"], w=w, bias=one)
        else:
            self.act(out, in_, AF.Ln, r=r, w=w)
        self.act(out, out, AF.Exp, r=w, w=w, scale=-1.0)

    def finish(self):
        self.S.finish()
        self.S.emit(self.es)
        self.es.close()
        return self.nc


def rstd_from_psum(cx, ps_ap, ps_key, tmp_ap, tmp_key, mhalf_ap, n):
    cx.act(tmp_ap, ps_ap, AF.Ln, r=[ps_key, "consts"], w=[tmp_key], bias=mhalf_ap[:, 0:1])
    cx.act(tmp_ap, tmp_ap, AF.Exp, r=[tmp_key], w=[tmp_key], scale=-0.5)


def emit_mod_pre(cx, cv, lam_in, lam_out, silb):
    cvs = cx.sb("m_cv", [128, 8, 2], F32)
    th = cx.sb("m_th", [128, 8, 2], F32)
    sil = cx.sb("m_sil", [128, 8, 2], F32)
    lamt = cx.sb("m_lamt", [128, 4, 4, 64], F32)
    prod = cx.sb("m_prod", [128, 4, 2, 64], F32)
    lsum = cx.sb("m_lsum", [128, 4, 2], F32)
    lexp = cx.sb("m_lexp", [128, 4, 2], F32)
    lams = cx.sb("m_lams", [128, 4], F32)
    cx.dma("sp", cvs[:], cv, r=[], w=["m_cv"], key="m_cv")
    cx.dma("sp", lamt[:], lam_in, r=[], w=["m_lamt"], key="m_lamt")
    cx.act(th[:], cvs[:], AF.Exp, r=["m_cv"], w=["m_th"], scale=-1.0)
    cx.ts("dve", th[:], th[:], 1.0, ALU.add, r=["m_th"], w=["m_th"])
    cx.recip(th[:], th[:], r=["m_th"], w=["m_th"])
    cx.tt("dve", sil[:], th[:], cvs[:], ALU.mult, r=["m_th", "m_cv"], w=["m_sil"])
    cx.copy("dve", silb[:], sil[:], r=["m_sil"], w=["silb"])
    cx.tt("dve", prod[:, :, 0, :], lamt[:, :, 0, :], lamt[:, :, 1, :], ALU.mult, r=["m_lamt"], w=["m_prod0"])
    cx.tt("dve", prod[:, :, 1, :], lamt[:, :, 2, :], lamt[:, :, 3, :], ALU.mult, r=["m_lamt"], w=["m_prod1"])
    cx.S.op("dve", lambda e: e.tensor_reduce(out=lsum[:], in_=prod[:], axis=AX.X, op=ALU.add),
            r=["m_prod0", "m_prod1"], w=["m_lsum"])
    cx.act(lexp[:], lsum[:], AF.Exp, r=["m_lsum"], w=["m_lexp"])
    cx.tt("dve", lams[:], lexp[:, :, 0], lexp[:, :, 1], ALU.subtract, r=["m_lexp"], w=["m_lams"])
    cx.dma("sp", lam_out, lams[:], r=["m_lams"], w=["lam_out"], key="m_lamo")


def mod_layer_gen(cx, ps, l, silb, w_mod, b_mod_fm, modT_out, bank_fn, alloc, dual=False):
    SW = 384
    slab = [alloc(f"ml_slab{i}", [128, 8, SW], BF16) for i in range(2)]
    if dual:
        stage = alloc("ml_stage", [128, 8, SW], F32)
        slab2 = alloc("ml_slab2", [128, 8, SW], BF16)
    modsb = alloc("ml_modsb", [128, 48, 2], F32)
    bfm = alloc("ml_bfm", [128, 48], F32)
    cx.dma("sp", bfm[:], b_mod_fm[l], r=[], w=["ml_bfm"], key="ml_bfm")
    NE = SW // 128
    for sidx in range(6 * D // SW):
        e0 = sidx * SW
        src = w_mod[l, :, e0:e0 + SW].rearrange("(c p) e -> p c e", p=128)
        if dual and sidx % 2 == 1:
            sl, sk = slab2, "ml_slab2"
            cx.dma("sp", stage[:], src, r=[], w=["ml_stage"], key="ml_stage")
            cx.copy("dve", sl[:], stage[:], r=["ml_stage"], w=[sk])
        else:
            i2 = (sidx // 2) % 2 if dual else sidx % 2
            sl = slab[i2]
            sk = f"ml_slab{i2}"
            cx.dma("pool", sl[:], src, r=[], w=[sk], key=sk)
        yield
        bank, pk, rel = bank_fn()
        for j in range(NE):
            out = ps[:, bank, 2 * j:2 * j + 2]
            for c in range(8):
                cx.mm(out, sl[:, c, j * 128:(j + 1) * 128], silb[:, c, :], start=(c == 0), stop=(c == 7),
                      r=[sk, "silb"], w=[pk])
        cx.tt("dve", modsb[:, sidx * NE:(sidx + 1) * NE, :],
              ps[:, bank, 0:2 * NE].rearrange("p (j t) -> p j t", t=2),
              bfm[:, sidx * NE:(sidx + 1) * NE, None].to_broadcast([128, NE, 2]), ALU.add,
              r=[pk, "ml_bfm"], w=["ml_modsb"])
        rel()
        yield
    cx.dma("sp", modT_out[:, l], modsb[:], r=["ml_modsb"], w=[f"modT{l}"], key="ml_modo")
    yield


class Rot:
    def __init__(self, cx, name, shape, dt, n, alloc=None):
        alloc = alloc or cx.sb
        self.bufs = [alloc(f"{name}{i}", shape, dt) for i in range(n)]
        self.keys = [f"{name}{i}" for i in range(n)]
        self.i = 0

    def next(self):
        i = self.i % len(self.bufs)
        self.i += 1
        return self.bufs[i], self.keys[i]


class BankRot:
    def __init__(self, banks, held=None):
        self.banks = banks
        self.held = set() if held is None else held
        self.i = 0

    def next(self):
        for _ in range(len(self.banks)):
            b = self.banks[self.i % len(self.banks)]
            self.i += 1
            if b not in self.held:
                self.held.add(b)
                return b, f"ps{b}"
        raise RuntimeError(f"no free PSUM bank among {self.banks}")

    def release(self, b):
        self.held.discard(b)


def load_consts(cx, cmat_d, nm=6):
    cf = cx.sb("cmat_f", [128, nm, 128], F32)
    cb = cx.sb("cmat_b", [128, nm, 128], BF16)
    mh = cx.sb("epsb", [128, 1024], F32)
    cx.dma("sp", cf[:], cmat_d, r=[], w=["cmat_f"], key="cmat_f")
    cx.copy("dve", cb[:], cf[:], r=["cmat_f"], w=["consts"])
    cx.memset("dve", mh[:], EPS, w=["consts"])
    return cf, cb, mh


def emit_norm_mod(*a, **k):
    for _ in norm_mod_gen(*a, **k):
        pass


def norm_mod_gen(cx, g, xg, xgk, hT, hTk, sq, u, rs_rot, onesD, mhalf, ps, auxb, Gs, shs):
    off, n = GROUPS[g]
    col = 1 if off >= TL else 0
    cx.act(sq[:, :, :n], xg[:, :, :n], AF.Square, r=[xgk], w=["sq"])
    yield
    b, bk = auxb.next()
    for c in range(8):
        cx.mm(ps[:, b, :n], onesD, sq[:, c, :n], start=(c == 0), stop=(c == 7), r=["sq", "consts"], w=[bk])
    rs, rsk = rs_rot.next()
    rstd_from_psum(cx, ps[:, b, :n], bk, rs[:, :n], rsk, mhalf[:, :n], n)
    auxb.release(b)
    yield
    uk = "u"
    if u is None:
        u, uk = xg, xgk
    cx.tt("dve", u[:, :, :n], xg[:, :, :n], rs[:, None, :n].to_broadcast([128, 8, n]), ALU.mult,
          r=[xgk, rsk], w=[uk])
    yield
    yield
    for c in range(8):
        if c % 2 == 0:
            cx.act(hT[:, c, :n], u[:, c, :n], AF.Identity, r=[uk, "modc"], w=[hTk],
                   bias=shs[:, c, col:col + 1], scale=Gs[:, c, col:col + 1])
        else:
            cx.ts("dve", hT[:, c, :n], u[:, c, :n], Gs[:, c, col:col + 1], ALU.mult, r=[uk, "modc"], w=[hTk],
                  s2=shs[:, c, col:col + 1], op1=ALU.add)


def emit_A(cx, consts, ps, xT, mod, n1g, w_in, qkg, cosT, sinT, sinks):
    nc = cx.nc
    S = cx.S
    cf, cb, mhalf = consts
    onesD, ones64, pswap = cb[:, 0, :], cb[:, 1, :], cb[:, 2, :]
    mainb = BankRot([0, 1, 2, 3, 4])
    auxb = BankRot([5, 6, 7], held=mainb.held)
    allb = BankRot([0, 1, 2, 3, 4, 5, 6, 7], held=mainb.held)

    w_sb = cx.sb("a_w", [128, 8, 2304], BF16)
    for wk, c0, c1 in (("a_wk", 512, 1024), ("a_wv", 1024, 1536), ("a_wp", 1536, 2304), ("a_wq", 0, 512)):
        cx.dma("pool", w_sb[:, :, c0:c1], w_in[:, :, c0:c1], r=[], w=[wk], key=wk)
    mods = cx.sb("a_mod", [128, 48, 2], F32)
    n1gs = cx.sb("a_n1g", [128, 8], F32)
    qkgs = cx.sb("a_qkg", [128, 2], F32)
    cos_s = cx.sb("a_cos", [128, TL], F32)
    sin_s = cx.sb("a_sin", [128, TL], F32)
    cx.dma("sp", mods[:], mod, r=[], w=["a_mod"], key="a_mod")
    cx.dma("sp", n1gs[:], n1g, r=[], w=["a_n1g"], key="a_n1g")
    cx.dma("sp", qkgs[:], qkg, r=[], w=["a_qkg"], key="a_qkg")
    Gs = cx.sb("a_G", [128, 8, 2], F32)
    cx.stt(Gs[:], mods[:, 8:16, :], 1.0, n1gs[:, :, None].to_broadcast([128, 8, 2]), ALU.add, ALU.mult,
           r=["a_mod", "a_n1g"], w=["modc"])
    shs = mods[:, 0:8, :]

    xg_rot = Rot(cx, "a_xg", [128, 8, 512], F32, 2)
    sq = cx.sb("a_sq", [128, 8, 512], BF16)
    u = None
    rs_rot = Rot(cx, "a_rs", [128, 512], F32, 2)
    sq2_rot = Rot(cx, "a_sq2", [128, 512], BF16, 2)
    r2_rot = Rot(cx, "a_r2", [128, 512], F32, 2)
    qn_rot = Rot(cx, "a_qn", [128, 512], F32, 3)
    hi_rot = Rot(cx, "a_hi", [128, 512], BF16, 2)
    lo_rot = Rot(cx, "a_lo", [128, 512], BF16, 2)
    t1_rot = Rot(cx, "a_t1", [128, 512], F32, 2)
    t2_rot = Rot(cx, "a_t2", [128, 512], F32, 2)
    qo_rot = Rot(cx, "a_qo", [128, 512], BF16, 8)
    po_rot = Rot(cx, "a_po", [128, 512], F32, 4)
    th_rot = Rot(cx, "a_th", [128, 512], F32, 4)
    gl_rot = Rot(cx, "a_gl", [128, 512], F32, 4)
    vo_rot = Rot(cx, "a_vo", [128, 512], BF16, 4)

    pending = []

    def advance():
        for gen in list(pending):
            try:
                next(gen)
            except StopIteration:
                pending.remove(gen)

    def load_x(g):
        off, n = GROUPS[g]
        xg, xgk = xg_rot.next()
        cx.dma("sp", xg[:, :, :n], xT[:, :, off:off + n], r=[], w=[xgk], key=xgk)
        return xg, xgk

    def qk_post(g, which, h, b, bk):
        off, n = GROUPS[g]
        latent = off < TL
        sq2, sq2k = sq2_rot.next()
        cx.act(sq2[:, :n], ps[:, b, :n], AF.Square, r=[bk], w=[sq2k])
        yield
        b2, b2k = auxb.next()
        cx.mm(ps[:, b2, :n], ones64, sq2[:, :n], start=True, stop=True, r=[sq2k, "consts"], w=[b2k])
        yield
        r2, r2k = r2_rot.next()
        rstd_from_psum(cx, ps[:, b2, :n], b2k, r2[:, :n], r2k, mhalf[:, :n], n)
        auxb.release(b2)
        qn, qnk = qn_rot.next()
        cx.stt(qn[:, :n], ps[:, b, :n], qkgs[:, which:which + 1], r2[:, :n], ALU.mult, ALU.mult,
               r=[bk, r2k, "a_qkg"], w=[qnk])
        mainb.release(b)
        dst, dres = sinks["q" if which == 0 else "k"](h, off, n)
        dkey = f"{'qT' if which == 0 else 'kT'}_o"
        qo, qok = qo_rot.next()
        if not latent:
            cx.act(qo[:, :n], qn[:, :n], AF.Copy, r=[qnk], w=[qok])
            yield
            cx.dma("sp", dst, qo[:, :n], r=[qok], w=[dres or cx.uid(dkey)], key=qok)
            return
        hi, hik = hi_rot.next()
        lo, lok = lo_rot.next()
        cx.act(hi[:, :n], qn[:, :n], AF.Copy, r=[qnk], w=[hik])
        cx.tt("dve", lo[:, :n], qn[:, :n], hi[:, :n], ALU.subtract, r=[qnk, hik], w=[lok])
        yield
        b3, b3k = auxb.next()
        cx.mm(ps[:, b3, :n], pswap, hi[:, :n], start=True, stop=False, r=[hik, "consts"], w=[b3k])
        cx.mm(ps[:, b3, :n], pswap, lo[:, :n], start=False, stop=True, r=[lok, "consts"], w=[b3k])
        t1, t1k = t1_rot.next()
        cx.tt("dve", t1[:, :n], qn[:, :n], cos_s[:, off:off + n], ALU.mult, r=[qnk, "a_cos"], w=[t1k])
        yield
        t2, t2k = t2_rot.next()
        cx.tt("dve", t2[:, :n], ps[:, b3, :n], sin_s[:, off:off + n], ALU.mult, r=[b3k, "a_sin"], w=[t2k])
        auxb.release(b3)
        cx.tt("dve", qo[:, :n], t1[:, :n], t2[:, :n], ALU.add, r=[t1k, t2k], w=[qok])
        yield
        cx.dma("sp", dst, qo[:, :n], r=[qok], w=[dres or cx.uid(dkey)], key=qok)

    def pool_post(g, ci, b, bk):
        off, n = GROUPS[g]
        po, pok = po_rot.next()
        cx.act(po[:, :n], ps[:, b, :n], AF.Copy, r=[bk], w=[pok])
        mainb.release(b)
        yield
        cx.dma("sp", sinks["up"](ci, off, n), po[:, :n], r=[pok], w=[cx.uid("up_o")], key=pok)
        for side, sl in ((0, slice(0, HALO)), (1, slice(n - HALO, n))):
            e_ap = sinks["edge"](0, ci, side, off)
            if e_ap is not None:
                cx.dma("sp", e_ap, po[:, sl], r=[pok], w=["sendE"], key=f"edge0{ci}{side}")

    def glu_post(g, ci, ba, bak, bb, bbk):
        off, n = GROUPS[g]
        th, thk = th_rot.next()
        cx.act(th[:, :n], ps[:, bb, :n], AF.Exp, r=[bbk], w=[thk], scale=-1.0)
        mainb.release(bb)
        yield
        gl, glk = gl_rot.next()
        cx.recip_act(th[:, :n], th[:, :n], r=[thk], w=[thk], one=cf[:, 4, 0:1])
        cx.tt("dve", gl[:, :n], th[:, :n], ps[:, ba, :n], ALU.mult, r=[thk, bak], w=[glk])
        mainb.release(ba)
        yield
        cx.dma("sp", sinks["glu"](ci, off, n), gl[:, :n], r=[glk], w=[cx.uid("glu_o")], key=glk)
        for side, sl in ((0, slice(0, HALO)), (1, slice(n - HALO, n))):
            e_ap = sinks["edge"](1, ci, side, off)
            if e_ap is not None:
                cx.dma("sp", e_ap, gl[:, sl], r=[glk], w=["sendE"], key=f"edge1{ci}{side}")

    def v_post(g, ti, b, bk):
        off, n = GROUPS[g]
        vo, vok = vo_rot.next()
        cx.copy("dve", vo[:], ps[:, b, :], r=[bk], w=[vok])
        mainb.release(b)
        yield
        vdst, vres = sinks["v"](off // 128 + ti)
        cx.dma("sp", vdst, vo[:].rearrange("p (h d) -> p h d", h=4), r=[vok],
               w=[vres or cx.uid("v_o")], key=vok)

    ng = len(GROUPS)
    hT_all = cx.sb("a_hTall", [128, 8, T], BF16)

    def hT_of(g):
        off, n = GROUPS[g]
        return hT_all[:, :, off:off + n], f"a_hT{g}"

    def run_chunks(g, chunks, hook=None, alloc=None, per_chunk=None):
        alloc = alloc or mainb
        off, n = GROUPS[g]
        hT, hTk = hT_of(g)
        ca_banks = {}
        for ci, (kind, idx, co, wk) in enumerate(chunks):
            b, bk = alloc.next()
            if kind == "v":
                for c in range(8):
                    cx.mm(ps[:, b, :], hT[:, c, idx * 128:(idx + 1) * 128], w_sb[:, c, 1024:1536],
                          start=(c == 0), stop=(c == 7), r=[hTk, wk], w=[bk])
            else:
                for c in range(8):
                    cx.mm(ps[:, b, :n], w_sb[:, c, co:co + 128], hT[:, c, :n],
                          start=(c == 0), stop=(c == 7), r=[hTk, wk], w=[bk])
            advance()
            if kind == "q":
                pending.append(qk_post(g, 0, idx, b, bk))
            elif kind == "k":
                pending.append(qk_post(g, 1, idx, b, bk))
            elif kind == "pool":
                pending.append(pool_post(g, idx, b, bk))
            elif kind == "ca":
                ca_banks[idx] = (b, bk)
            elif kind == "cb":
                ba, bak = ca_banks[idx]
                pending.append(glu_post(g, idx, ba, bak, b, bk))
            elif kind == "v":
                pending.append(v_post(g, idx, b, bk))
            if hook is not None and ci == 3:
                hook()
            if per_chunk is not None:
                per_chunk(ci)

    def drain():
        while pending:
            advance()

    def norm_group(g, xgx):
        xg, xgk = xgx
        hT, hTk = hT_of(g)
        emit_norm_mod(cx, g, xg, xgk, hT, hTk, sq, u, rs_rot, onesD, mhalf, ps, auxb, Gs, shs)

    cc = sinks.get("cc", lambda name: None)
    nxt_x = load_x(0)
    norm_group(0, nxt_x)
    nxt_x = load_x(1)
    cx.dma("sp", cos_s[:], cosT, r=[], w=["a_cos"], key="a_cos")
    cx.dma("sp", sin_s[:], sinT, r=[], w=["a_sin"], key="a_sin")
    for g in range(ng):
        off, n = GROUPS[g]
        chunks = [("k", h, 512 + h * 128, "a_wk") for h in range(4)]
        chunks += [("v", ti, 1024, "a_wv") for ti in range(n // 128)]

        ngen = [None]

        def per_chunk(ci, g=g, ngen=ngen):
            nonlocal nxt_x
            if ci == 0 and g + 1 < ng:
                xg1, xg1k = nxt_x
                hT1, hT1k = hT_of(g + 1)
                ngen[0] = norm_mod_gen(cx, g + 1, xg1, xg1k, hT1, hT1k, sq, u, rs_rot, onesD, mhalf, ps, auxb,
                                       Gs, shs)
                if g + 2 < ng:
                    nxt_x = load_x(g + 2)
            if ngen[0] is not None:
                try:
                    next(ngen[0])
                except StopIteration:
                    ngen[0] = None
            if ci == 3 and g in (2, 4):
                cc("K%d" % (g // 2 - 1))
                cc("V%d" % (g // 2 - 1))

        run_chunks(g, chunks, None, per_chunk=per_chunk)
        while ngen[0] is not None:
            per_chunk(-1)
    for g in range(ng):
        chunks = [("pool", i, 1536 + i * 128, "a_wp") for i in range(2)]
        chunks += [("ca", i, 1792 + i * 128, "a_wp") for i in range(2)]
        chunks += [("cb", i, 2048 + i * 128, "a_wp") for i in range(2)]
        run_chunks(g, chunks, (lambda: cc("E")) if g == 4 else None, alloc=allb)
    for g in range(ng):
        run_chunks(g, [("q", h, h * 128, "a_wq") for h in range(4)])
    drain()


def fm(v, nch):
    return np.ascontiguousarray(np.asarray(v, np.float32).reshape(nch, 128).T)


def fm_w(w):
    k, e = w.shape
    return np.ascontiguousarray(w.reshape(k // 128, 128, e).transpose(1, 0, 2))


def rope_tables(core):
    t = (core % 4) * TL + np.arange(TL)
    row = (t // 64).astype(np.float64)
    col = (t % 64).astype(np.float64)
    inv_freq = 10000.0 ** (-np.arange(0, 32, 2, dtype=np.float64) / 32)
    ang = np.concatenate([row[:, None] * inv_freq, col[:, None] * inv_freq], -1)
    p = np.arange(128)
    pair = (p % 64) // 2
    cosT = np.cos(ang)[:, pair].T
    sgn = np.where(p % 2 == 0, -1.0, 1.0)[:, None]
    sinT = np.sin(ang)[:, pair].T * sgn
    return np.ascontiguousarray(cosT, np.float32), np.ascontiguousarray(sinT, np.float32)


def const_mats():
    m = np.zeros((128, 6, 128), np.float32)
    m[:, 0, :] = 1.0 / 1024
    m[0:64, 1, 0:64] = 1.0 / 64
    m[64:128, 1, 64:128] = 1.0 / 64
    idx = np.arange(128)
    m[idx ^ 1, 2, idx] = 1.0
    m[:, 3, :] = 1.0 / 128
    m[:, 4, :] = 1.0
    m[:, 5, :] = 1.0 / 256
    return m


NVEC = 24
TPL = TL + 2 * HALO
TPC = TC + 2 * HALO
TP = TPL + TPC
KPIECES = 6
KTP = NKT // KPIECES


class SAlloc:
    def __init__(self):
        self.held = set()
        self.i = 0
        self.j = 0

    def pair(self):
        for _ in range(2):
            p = self.i % 2
            self.i += 1
            if (2 * p) not in self.held and (2 * p + 1) not in self.held:
                return 2 * p
        raise RuntimeError("no free score pair")

    def one(self):
        for _ in range(2):
            b = self.j % 2
            self.j += 1
            if b not in self.held:
                self.held.add(b)
                return b, f"ps{b}"
        raise RuntimeError("no free stats bank")

    def release(self, b):
        self.held.discard(b)


def emit_B(cx, consts, ps, xT, mod, lam_ap, qT, pieces, kv_src, up_pad, glu_pad, invcnt, poolw, vecs, convw,
           w_out, w_ffn_in, w_ffn_out, xT_o_fn, last=False, bg=None):
    nc = cx.nc
    S = cx.S
    cf, cb, mhalf = consts
    onesD, ones128n, ones1, = cb[:, 0, :], cb[:, 3, :], cb[:, 4, :]
    ones256f = cf[:, 5, :]
    sa = SAlloc()
    groups_run = [g for g in range(len(GROUPS)) if not (last and GROUPS[g][0] >= TL)]
    piece_of = {}
    for pi, (k0, kc) in enumerate(pieces):
        for kt in range(k0, k0 + kc):
            piece_of[kt] = pi

    mods = cx.sb("b_mod", [128, 48, 2], F32)
    vec = cx.sb("b_vec", [128, NVEC], F32)
    cw = cx.sb("b_cw", [128, 2, 31], F32)
    pwf = cx.sb("b_pwf", [128, 2, 128], F32)
    pwb = cx.sb("b_pwb", [128, 2, 128], BF16)
    cx.dma("sp", mods[:], mod, r=[], w=["b_mod"], key="b_mod")
    cx.dma("sp", vec[:], vecs, r=[], w=["b_vec"], key="b_vec")
    cx.dma("sp", cw[:], convw, r=[], w=["b_cw"], key="b_cw")
    cx.dma("sp", pwf[:], poolw, r=[], w=["b_pwf"], key="b_pwf")
    lamt = cx.sb("b_lamt", [128, 1], F32)
    cx.dma("sp", lamt[:], lam_ap, r=[], w=["b_lamt"], key="b_lamt", slow=True)
    cx.copy("dve", pwb[:], pwf[:], r=["b_pwf"], w=["b_pwb"])
    der = cx.sb("b_der", [128, 8], F32)
    cx.tt("dve", der[:, 0:1], lamt[:, 0:1], vec[:, 10:11], ALU.add, r=["b_vec", "b_lamt"], w=["b_der0"])
    cx.ts("dve", der[:, 0:1], der[:, 0:1], -1.0, ALU.mult, r=["b_der0"], w=["b_der0"])
    cx.tt("dve", der[:, 1:2], vec[:, 8:9], vec[:, 11:12], ALU.mult, r=["b_vec"], w=["b_der1"])
    cx.copy("dve", der[:, 2:6], vec[:, 4:8], r=["b_vec"], w=["b_der2"])
    derk = ["b_der0", "b_der1", "b_der2"]
    G2 = cx.sb("b_G2", [128, 8, 2], F32)
    g2h = cx.sb("b_g2h", [128, 8, 2], F32)
    cx.stt(G2[:], mods[:, 32:40, :], 1.0, vec[:, 12:20, None].to_broadcast([128, 8, 2]), ALU.add, ALU.mult,
           r=["b_mod", "b_vec"], w=["modc"])
    cx.copy("dve", g2h[:], mods[:, 40:48, :], r=["b_mod"], w=["modc2"])
    sh2 = mods[:, 24:32, :]
    g1 = mods[:, 16:24, :]

    mixT = cx.sb("b_mix", [128, 8, T], BF16)

    cx.push()
    sb1 = cx.sb

    NP = 512 + 2 * HALO
    U = sb1("p_U", [128, 2, NP], F32)
    IC = sb1("p_IC", [128, 2, 512], F32)
    Sa = sb1("p_Sa", [128, 2, NP], F32)
    Sb = sb1("p_Sb", [128, 2, NP], F32)
    Sc = sb1("p_Sc", [128, 2, NP], F32)
    Sd = sb1("p_Sd", [128, 2, NP], F32)
    ptmp = sb1("p_tmp", [128, 2, 512], F32)
    pin = sb1("p_in", [128, 2, 512], BF16)
    G = sb1("c_G", [128, 2, NP], F32)
    acc = sb1("c_acc", [128, 2, 512], F32)
    sqa = sb1("c_sqa", [128, 2, 512], F32)
    msb = sb1("c_msb", [128, 512], F32)
    m2 = sb1("c_m2", [128, 512], F32)
    vr = sb1("c_vr", [128, 512], F32)
    dd = sb1("c_dd", [128, 2, 512], F32)
    zh = sb1("c_zh", [128, 2, 512], F32)
    cth = sb1("c_th", [128, 2, 512], F32)

    def b1_gen():
        for g in groups_run:
            off, n = GROUPS[g]
            po = off if off < TL else TPL + (off - TL)
            m = n + 2 * HALO
            cx.dma("sp", U[:, :, :m], up_pad[:, :, po:po + m], r=["halo_up"], w=["p_U"], key="p_U")
            cx.dma("sp", IC[:, :, :n], invcnt[:, :, off:off + n], r=[], w=["p_IC"], key="p_IC")
            yield
            cx.tt("dve", Sa[:, :, 1:m], U[:, :, 0:m - 1], U[:, :, 1:m], ALU.add, r=["p_U"], w=["p_Sa"])
            yield
            cx.tt("dve", Sb[:, :, 2:m - 1], Sa[:, :, 1:m - 2], Sa[:, :, 3:m], ALU.add, r=["p_Sa"], w=["p_Sb"])
            yield
            cx.tt("dve", Sc[:, 1, 4:m - 3], Sb[:, 1, 2:m - 5], Sb[:, 1, 6:m - 1], ALU.add, r=["p_Sb"], w=["p_Sc"])
            yield
            cx.tt("dve", Sd[64:128, 1, 8:m - 7], Sc[64:128, 1, 4:m - 11], Sc[64:128, 1, 12:m - 3], ALU.add,
                  r=["p_Sc"], w=["p_Sd"])
            yield
            srcs = [(Sa, "p_Sa", 0, 0), (Sb, "p_Sb", 64, 0), (Sc, "p_Sc", 0, 1), (Sd, "p_Sd", 64, 1)]
            for sbuf, sk, p0, ci in srcs:
                cx.tt("dve", ptmp[p0:p0 + 64, ci, :n], sbuf[p0:p0 + 64, ci, HALO:HALO + n],
                      IC[p0:p0 + 64, ci, :n], ALU.mult, r=[sk, "p_IC"], w=[f"p_tmp{p0}{ci}"])
                cx.tt("dve", pin[p0:p0 + 64, ci, :n], ptmp[p0:p0 + 64, ci, :n],
                      U[p0:p0 + 64, ci, HALO:HALO + n], ALU.subtract, r=[f"p_tmp{p0}{ci}", "p_U"],
                      w=[f"p_in{p0}{ci}"])
                yield
            pk = [f"p_in{p0}{ci}" for _, _, p0, ci in srcs]
            for ci in range(2):
                b, bk = sa.one()
                cx.mm(ps[:, b, :n], pwb[:, ci, :], pin[:, ci, :n], start=True, stop=True,
                      r=pk + ["b_pwb"], w=[bk])
                cx.ts("dve", mixT[:, 4 + ci, off:off + n], ps[:, b, :n], vec[:, ci:ci + 1], ALU.mult,
                      r=[bk, "b_vec"], w=[f"mix{4 + ci}_{g}"])
                sa.release(b)
                yield
            cx.dma("sp", G[:, :, :m], glu_pad[:, :, po:po + m], r=["halo_glu"], w=["c_G"], key="c_G")
            yield
            for ci in range(2):
                cx.ts("dve", acc[:, ci, :n], G[:, ci, 1:1 + n], cw[:, ci, 0:1], ALU.mult,
                      r=["c_G", "b_cw", "b_vec"], w=[f"c_acc{ci}"], s2=vec[:, 2 + ci:3 + ci], op1=ALU.add)
                yield
                for k in range(1, 31):
                    cx.stt(acc[:, ci, :n], G[:, ci, 1 + k:1 + k + n], cw[:, ci, k:k + 1], acc[:, ci, :n],
                           ALU.mult, ALU.add, r=["c_G", "b_cw", f"c_acc{ci}"], w=[f"c_acc{ci}"])
                    yield
            cx.tt("dve", sqa[:, :, :n], acc[:, :, :n], acc[:, :, :n], ALU.mult, r=["c_acc0", "c_acc1"], w=["c_sqa"])
            yield
            b1_, b1k = sa.one()
            for ci in range(2):
                cx.mm(ps[:, b1_, :n], ones256f, acc[:, ci, :n], start=(ci == 0), stop=(ci == 1),
                      r=["c_acc0", "c_acc1", "cmat_f"], w=[b1k])
            cx.copy("dve", msb[:, :n], ps[:, b1_, :n], r=[b1k], w=["c_msb"])
            sa.release(b1_)
            cx.tt("dve", m2[:, :n], msb[:, :n], msb[:, :n], ALU.mult, r=["c_msb"], w=["c_m2"])
            yield
            b2_, b2k = sa.one()
            for ci in range(2):
                cx.mm(ps[:, b2_, :n], ones256f, sqa[:, ci, :n], start=(ci == 0), stop=(ci == 1),
                      r=["c_sqa", "cmat_f"], w=[b2k])
            cx.stt(vr[:, :n], ps[:, b2_, :n], EPS, m2[:, :n], ALU.add, ALU.subtract, r=[b2k, "c_m2"], w=["c_vr"])
            sa.release(b2_)
            cx.act(vr[:, :n], vr[:, :n], AF.Ln, r=["c_vr"], w=["c_vr"])
            cx.act(vr[:, :n], vr[:, :n], AF.Exp, r=["c_vr"], w=["c_vr"], scale=-0.5)
            yield
            cx.tt("dve", dd[:, :, :n], acc[:, :, :n], msb[:, None, :n].to_broadcast([128, 2, n]), ALU.subtract,
                  r=["c_acc0", "c_acc1", "c_msb"], w=["c_dd"])
            cx.tt("dve", dd[:, :, :n], dd[:, :, :n], vr[:, None, :n].to_broadcast([128, 2, n]), ALU.mult,
                  r=["c_dd", "c_vr"], w=["c_dd"])
            yield
            for ci in range(2):
                cx.ts("dve", zh[:, ci, :n], dd[:, ci, :n], der[:, 2 + ci:3 + ci], ALU.mult,
                      r=["c_dd", "b_der2"], w=[f"c_zh{ci}"], s2=der[:, 4 + ci:5 + ci], op1=ALU.add)
            yield
            cx.act(cth[:, :, :n], zh[:, :, :n], AF.Exp, r=["c_zh0", "c_zh1"], w=["c_th"], scale=-1.0)
            yield
            cx.recip_act(cth[:, :, :n], cth[:, :, :n], r=["c_th"], w=["c_th"], one=cf[:, 4, 0:1])
            yield
            cx.tt("dve", mixT[:, 6:8, off:off + n], cth[:, :, :n], zh[:, :, :n], ALU.mult,
                  r=["c_th", "c_zh0", "c_zh1"], w=[f"mix6_{g}", f"mix7_{g}"])
            yield

    b1 = b1_gen()
    b1_done = [False]

    def bg_bank():
        b, bk = sa.one()
        return b, bk, (lambda: sa.release(b))

    bg_gen = bg(sb1, bg_bank) if bg is not None else None
    bg_done = [bg_gen is None]

    def bg_step():
        if bg_done[0]:
            return
        try:
            next(bg_gen)
        except StopIteration:
            bg_done[0] = True

    def b1_step():
        if b1_done[0]:
            return
        try:
            next(b1)
        except StopIteration:
            b1_done[0] = True

    Kh = sb1("a_K", [128, NKEY], BF16)
    Vh = sb1("a_V", [128, NKT, 128], BF16)
    qh_rot = [sb1(f"a_q{i}", [128, T], BF16) for i in range(2)]
    NPT = 6
    Pt = [sb1(f"a_P{i}", [128, 2, 512], BF16) for i in range(NPT)]
    rD = sb1("a_rD", [128, 2, 512], F32)
    oo = sb1("a_oo", [128, 2, 512], F32)
    pacc_rot = Rot(cx, "a_Pacc", [128, 2, 512], BF16, 2, alloc=sb1)
    QD = 8
    o_rot = Rot(cx, "a_o", [128, 512], F32, 2, alloc=sb1)
    osq_rot = Rot(cx, "a_osq", [128, 512], BF16, 2, alloc=sb1)
    r3_rot = Rot(cx, "a_r3", [128, 512], F32, 2, alloc=sb1)

    def load_kv(h, piece):
        k0, kc = pieces[piece]
        ksrc, vsrc = kv_src(h, piece)
        cx.dma("sp", Kh[:, k0 * 128:(k0 + kc) * 128], ksrc, r=["recv"], w=[f"K{piece}"], key=f"K{piece}")
        cx.dma("sp", Vh[:, k0:k0 + kc, :], vsrc, r=["recv"], w=[f"V{piece}"], key=f"V{piece}")

    def load_q(h):
        cx.dma("sp", qh_rot[h % 2][:], qT[h], r=[], w=[f"a_q{h % 2}"], key=f"a_q{h % 2}")

    load_q(0)
    for piece in range(len(pieces)):
        load_kv(0, piece)

    gorder = [g for g in [4, 0, 1, 2, 3] if g in groups_run]
    post_pending = []
    it = 0
    for h in range(4):
        qh = qh_rot[h % 2]
        qk = f"a_q{h % 2}"
        if h + 1 < 4:
            load_q(h + 1)
        for gi, g in enumerate(gorder):
            off, n = GROUPS[g]
            kts = list(range(2)) if off >= TL else list(range(NKT))
            last_group = (gi == len(gorder) - 1)
            pvq = []
            quad = []
            dq = []
            dstate = {"first": True, "acc": None}

            def flush_d(final, n=n, dq=dq, dstate=dstate):
                while dq:
                    src, srck = dq.pop(0)
                    lastd = final and not dq
                    for c in range(2):
                        cx.mm(ps[:, 6 + c, :n], ones1, src[:, c, :n], start=dstate["first"], stop=lastd,
                              r=["consts", srck], w=[f"ps{6 + c}"])
                    dstate["first"] = False

            def emit_pv(prev, final, n=n, quad=quad, dq=dq, dstate=dstate):
                pkt, ppt, pptk, pidx = prev
                pp = piece_of[pkt]
                flush_d(False)
                for c in range(2):
                    cx.mm(ps[:, 4 + c, :n], Vh[:, pkt, :], ppt[:, c, :n], start=(pidx == 0), stop=final,
                          r=[f"V{pp}", pptk], w=[f"ps{4 + c}"])
                quad.append((ppt, pptk))
                if len(quad) == 2:
                    dstate["acc"] = pacc_rot.next()
                    acc, acck = dstate["acc"]
                    cx.tt("dve", acc[:, :, :n], quad[0][0][:, :, :n], ppt[:, :, :n], ALU.add,
                          r=[quad[0][1], pptk], w=[acck])
                elif len(quad) > 2:
                    acc, acck = dstate["acc"]
                    cx.tt("dve", acc[:, :, :n], acc[:, :, :n], ppt[:, :, :n], ALU.add, r=[acck, pptk], w=[acck])
                if len(quad) == QD or final:
                    dq.append(dstate["acc"] if len(quad) > 1 else (ppt, pptk))
                    quad.clear()
                if final:
                    flush_d(True)

            for idx, kt in enumerate(kts):
                piece = piece_of[kt]
                b0 = sa.pair()
                for c in range(2):
                    cx.mm(ps[:, b0 + c, :n], Kh[64 * c:64 * c + 64, kt * 128:(kt + 1) * 128],
                          qh[64 * c:64 * c + 64, off:off + n], start=True, stop=True,
                          r=[f"K{piece}", qk], w=[f"ps{b0 + c}"])
                pt = Pt[it % NPT]
                ptk = f"a_P{it % NPT}"
                cx.act(pt[:, :, :n], ps[:, b0:b0 + 2, :n], AF.Exp, r=[f"ps{b0}", f"ps{b0 + 1}"], w=[ptk],
                       scale=0.125)
                it += 1
                pvq.append((kt, pt, ptk, idx))
                if len(pvq) > 2:
                    prev = pvq.pop(0)
                    emit_pv(prev, False)
                    pkt = prev[0]
                    pp = piece_of[pkt]
                    if last_group and h + 1 < 4 and pkt == pieces[pp][0] + pieces[pp][1] - 1 \
                            and pp != len(pieces) - 1:
                        load_kv(h + 1, pp)
                if it % 2 == 0:
                    b1_step()
                if it % 5 == 2:
                    bg_step()
                if idx in (1, 5) and post_pending:
                    for gen in list(post_pending):
                        try:
                            next(gen)
                        except StopIteration:
                            post_pending.remove(gen)
            while pvq:
                prev = pvq.pop(0)
                emit_pv(prev, not pvq)
            if last_group and h + 1 < 4:
                load_kv(h + 1, len(pieces) - 1)
            for gen in list(post_pending):
                for _ in gen:
                    pass
                post_pending.remove(gen)
            cx.copy("dve", oo[:, :, :n], ps[:, 4:6, :n], r=["ps4", "ps5"], w=["a_oo"])
            o, ok = o_rot.next()
            osq, osqk = osq_rot.next()

            def post(h=h, off=off, n=n, o=o, ok=ok, osq=osq, osqk=osqk):
                cx.recip_act(rD[:, :, :n], ps[:, 6:8, :n], r=["ps6", "ps7"], w=["a_rD"])
                cx.tt("dve", oo[:, :, :n], oo[:, :, :n], rD[:, :, :n], ALU.mult, r=["a_oo", "a_rD"], w=["a_oo"])
                cx.stt(o[:, :n], oo[:, 1, :n], der[:, 0:1], oo[:, 0, :n], ALU.mult, ALU.add,
                       r=["a_oo", "b_der0"], w=[ok])
                cx.tt("dve", osq[:, :n], o[:, :n], o[:, :n], ALU.mult, r=[ok], w=[osqk])
                yield
                b, bk = sa.one()
                cx.mm(ps[:, b, :n], ones128n, osq[:, :n], start=True, stop=True, r=[osqk, "consts"], w=[bk])
                r3, r3k = r3_rot.next()
                rstd_from_psum(cx, ps[:, b, :n], bk, r3[:, :n], r3k, mhalf[:, :n], n)
                sa.release(b)
                cx.stt(mixT[:, h, off:off + n], o[:, :n], der[:, 1:2], r3[:, :n], ALU.mult, ALU.mult,
                       r=[ok, r3k, "b_der1"], w=[f"mix{h}_{off}"])
                yield

            post_pending.append(post())
    for gen in list(post_pending):
        for _ in gen:
            pass
    while not b1_done[0]:
        b1_step()
    while not bg_done[0]:
        bg_step()

    cx.pop()
    banks = BankRot([0, 1, 2, 3, 4, 5, 6, 7])
    wo_sb = cx.sb("f_wout", [128, 8, 1024], BF16)
    for hf_ in range(2):
        cx.dma("pool", wo_sb[:, :, hf_ * 512:(hf_ + 1) * 512], w_out[:, :, hf_ * 512:(hf_ + 1) * 512], r=[],
               w=[f"f_wout{hf_}"], key=f"f_wout{hf_}")
    xg_rot = Rot(cx, "f_xg", [128, 8, 512], F32, 2)
    sq = cx.sb("f_sq", [128, 8, 512], BF16)
    u = cx.sb("f_u", [128, 8, 512], F32)
    rs_rot = Rot(cx, "f_rs", [128, 512], F32, 2)
    aT = cx.sb("f_aT", [128, NJ, 512], BF16)
    win_rot = Rot(cx, "f_win", [128, 8, 256], BF16, 4)
    wo2_rot = Rot(cx, "f_wo2", [128, NJ, 128], BF16, 3)
    th_rot = Rot(cx, "f_th", [128, 512], F32, 2)
    s_rot = Rot(cx, "f_s", [128, 512], F32, 2)

    h2_rot = Rot(cx, "f_h2T", [128, 8, 512], BF16, 2)

    def outproj_norm(g):
        off, n = GROUPS[g]
        col = 1 if off >= TL else 0
        xg, xgk = xg_rot.next()
        cx.dma("sp", xg[:, :, :n], xT[:, :, off:off + n], r=[], w=[xgk], key=xgk)
        mixk = [f"mix{m}_{off}" for m in range(4)] + [f"mix{m}_{g}" for m in range(4, 8)]
        for dc in range(8):
            b, bk = banks.next()
            for m in range(8):
                cx.mm(ps[:, b, :n], wo_sb[:, m, dc * 128:(dc + 1) * 128], mixT[:, m, off:off + n],
                      start=(m == 0), stop=(m == 7), r=[f"f_wout{dc // 4}", mixk[m]], w=[bk])
            cx.stt(xg[:, dc, :n], ps[:, b, :n], g1[:, dc, col:col + 1], xg[:, dc, :n], ALU.mult, ALU.add,
                   r=[bk, "b_mod", xgk], w=[xgk])
            banks.release(b)
        h2T, h2k = h2_rot.next()
        gen = norm_mod_gen(cx, g, xg, xgk, h2T, h2k, sq, u, rs_rot, onesD, mhalf, ps, banks, G2, sh2)
        return xg, xgk, h2T, h2k, gen

    def run_gen(gen):
        for _ in gen:
            pass

    nxt = outproj_norm(groups_run[0])
    run_gen(nxt[4])
    for gi, g in enumerate(groups_run):
        off, n = GROUPS[g]
        col = 1 if off >= TL else 0
        xg, xgk, h2T, h2k, _ = nxt
        ngen = None
        if gi + 1 < len(groups_run):
            nxt = outproj_norm(groups_run[gi + 1])
            ngen = nxt[4]
        for j in range(NJ):
            wj, wjk = win_rot.next()
            cx.dma("pool", wj[:], w_ffn_in[j], r=[], w=[wjk], key=wjk)
            bg, bgk = banks.next()
            bu, buk = banks.next()
            for c in range(8):
                cx.mm(ps[:, bg, :n], wj[:, c, 0:128], h2T[:, c, :n], start=(c == 0), stop=(c == 7),
                      r=[wjk, h2k], w=[bgk])
            for c in range(8):
                cx.mm(ps[:, bu, :n], wj[:, c, 128:256], h2T[:, c, :n], start=(c == 0), stop=(c == 7),
                      r=[wjk, h2k], w=[buk])
            th, thk = th_rot.next()
            cx.act(th[:, :n], ps[:, bg, :n], AF.Exp, r=[bgk], w=[thk], scale=-1.0)
            sv, svk = s_rot.next()
            cx.recip_act(th[:, :n], th[:, :n], r=[thk], w=[thk], one=cf[:, 4, 0:1])
            cx.tt("dve", sv[:, :n], th[:, :n], ps[:, bg, :n], ALU.mult, r=[thk, bgk], w=[svk])
            banks.release(bg)
            cx.tt("dve", aT[:, j, :n], sv[:, :n], ps[:, bu, :n], ALU.mult, r=[svk, buk], w=[f"f_aT{j}"])
            banks.release(bu)
            if ngen is not None and j >= 6 and j % 2 == 0:
                try:
                    next(ngen)
                except StopIteration:
                    ngen = None
        if ngen is not None:
            run_gen(ngen)
        for dc in range(8):
            w2, w2k = wo2_rot.next()
            cx.dma("pool", w2[:], w_ffn_out[dc], r=[], w=[w2k], key=w2k)
            b, bk = banks.next()
            for j in range(NJ):
                cx.mm(ps[:, b, :n], w2[:, j, :], aT[:, j, :n], start=(j == 0), stop=(j == NJ - 1),
                      r=[w2k, f"f_aT{j}"], w=[bk])
            cx.stt(xg[:, dc, :n], ps[:, b, :n], g2h[:, dc, col:col + 1], xg[:, dc, :n], ALU.mult, ALU.add,
                   r=[bk, "modc2", xgk], w=[xgk])
            banks.release(b)
        cx.dma("sp", xT_o_fn(off, n), xg[:, :, :n], r=[xgk], w=[cx.uid("xT_o")], key=xgk)


def invcnt_table(core):
    out = np.zeros((128, 2, T), np.float32)
    wins = {(0, 0): 2, (64, 0): 4, (0, 1): 8, (64, 1): 16}
    for (p0, ci), w in wins.items():
        for off, L, base in ((0, SEQ, (core % 4) * TL), (TL, TC, 0)):
            nt = TL if off == 0 else TC
            t = base + np.arange(nt)
            lo = np.clip(t - w // 2, 0, L)
            hi = np.clip(t + w - w // 2, 0, L)
            out[p0:p0 + 64, ci, off:off + nt] = (1.0 / (hi - lo))[None, :]
    return out


CC_GROUPS = [[0, 1, 2, 3], [4, 5, 6, 7]]
PIECES = [(0, 2)] + [(2 + 8 * i, 8) for i in range(8)]


def build_fused(depth=DEPTH):
    cx = Ctx()
    nc = cx.nc
    x0T = cx.din("x0T", [128, 8, T], F32)
    cv = cx.din("cv", [128, 8, 2], F32)
    w_mod = cx.din("w_mod", [DEPTH, D, 6 * D], F32)
    b_mod_fm = cx.din("b_mod_fm", [DEPTH, 128, 48], F32)
    lam_in = cx.din("lam_in", [128, DEPTH, 4, 64], F32)
    cmat = cx.din("cmat", [128, 6, 128], F32)
    n1g = cx.din("n1g", [DEPTH, 128, 8], F32)
    w_in = cx.din("w_in", [DEPTH, 128, 8, 2304], F32)
    qkg = cx.din("qkg", [DEPTH, 128, 2], F32)
    poolw = cx.din("poolw", [DEPTH, 128, 2, 128], F32)
    vecs = cx.din("vecs", [DEPTH, 128, NVEC], F32)
    convw = cx.din("convw", [DEPTH, 128, 2, 31], F32)
    w_out = cx.din("w_out", [DEPTH, 128, 8, 1024], F32)
    w_ffn_in = cx.din("w_ffn_in", [DEPTH, NJ, 128, 8, 256], F32)
    w_ffn_out = cx.din("w_ffn_out", [DEPTH, 8, 128, NJ, 128], F32)
    cosT = cx.din("cosT", [128, TL], F32)
    sinT = cx.din("sinT", [128, TL], F32)
    invcnt = cx.din("invcnt", [128, 2, T], F32)
    selT = cx.din("selT", [128, 8], F32)
    outT = cx.dout("outT", [128, 8, TL], F32)
    modT = cx.dscratch("modT", [128, DEPTH, 48, 2], F32)
    lamd = cx.dscratch("lamd", [128, DEPTH], F32)
    xs = [cx.dscratch(f"xs{i}", [128, 8, T], F32) for i in range(2)]
    qTs = cx.dscratch("qTs", [4, 128, T], BF16)
    kcs = cx.dscratch("kcs", [4, 128, TC], BF16)
    vcs = cx.dscratch("vcs", [4, 128, 2, 128], BF16)
    up_pad = cx.dscratch("up_pad", [128, 2, TP], F32)
    glu_pad = cx.dscratch("glu_pad", [128, 2, TP], F32)
    sendK = [[cx.dscratch(f"sendK{p}{h}", [512, 1024], BF16) for h in range(2)] for p in range(2)]
    sendV = [[cx.dscratch(f"sendV{p}{h}", [512, 1024], BF16) for h in range(2)] for p in range(2)]
    recvK = [[cx.dscratch(f"recvK{p}{h}", [2048, 1024], BF16) for h in range(2)] for p in range(2)]
    recvV = [[cx.dscratch(f"recvV{p}{h}", [2048, 1024], BF16) for h in range(2)] for p in range(2)]
    sendE = [cx.dscratch(f"sendE{p}", [256, 64], F32) for p in range(2)]
    recvE = [cx.dscratch(f"recvE{p}", [1024, 64], F32) for p in range(2)]

    ps = cx.es.enter_context(nc.psum_tensor("ps", [128, 8, 512], F32))
    consts = load_consts(cx, cmat)
    sel = cx.sb("sel_sb", [128, 8], F32)
    cx.dma("sp", sel[:], selT, r=[], w=["selT"], key="selT")
    E = cx.sb("E", [128, 4, 2, 64], F32)
    H = cx.sb("H", [128, 2, 2, 2, HALO], F32)
    zt = cx.sb("zt", [128, 2, HALO], F32)
    cx.memset("dve", zt[:], 0.0, w=["zt"])
    zi = 0
    for pad in (up_pad, glu_pad):
        for o in (0, HALO + TL, TPL, TPL + HALO + TC):
            cx.dma("sp", pad[:, :, o:o + HALO], zt[:], r=["zt"], w=[f"zpad{zi}"], key=f"zpad{zi % 4}")
            zi += 1

    silb = cx.sb("silb", [128, 8, 2], BF16)
    cx.push("M_")
    emit_mod_pre(cx, cv, lam_in, lamd, silb)
    for _ in mod_layer_gen(cx, ps, 0, silb, w_mod, b_mod_fm, modT, lambda: (0, "ps0", lambda: None), cx.sb,
                           dual=True):
        pass
    cx.pop()

    for l in range(depth):
        par = l % 2
        last = (l == depth - 1)
        x_in = x0T if l == 0 else xs[(l - 1) % 2]

        def k_sink(h, off, n, par=par):
            if off >= TL:
                return kcs[h, :, :], None
            return (sendK[par][off // 1024][h * 128:(h + 1) * 128, off % 1024: off % 1024 + n],
                    f"sendK{off // 1024}")

        def v_sink(ti, par=par):
            if ti >= 16:
                return vcs.rearrange("h p t d -> p h t d")[:, :, ti - 16, :], None
            return (sendV[par][ti // 8].rearrange("(h p) c -> p h c", p=128)[:, :, (ti % 8) * 128:(ti % 8 + 1) * 128],
                    f"sendV{ti // 8}")

        def pad_sink(pad):
            def f(ci, off, n):
                o = HALO + off if off < TL else TPL + HALO + (off - TL)
                return pad[:, ci, o:o + n]
            return f

        def edge_sink(tz, ci, side, off, par=par):
            if (side == 0 and off == 0) or (side == 1 and off == TL - 512):
                return sendE[par][ci * 128:(ci + 1) * 128, tz * 32 + side * HALO: tz * 32 + (side + 1) * HALO]
            return None

        def issue_cc(name, par=par):
            tab = {"K0": (sendK[par][0], recvK[par][0], "sendK0", 0), "V0": (sendV[par][0], recvV[par][0], "sendV0", 1),
                   "K1": (sendK[par][1], recvK[par][1], "sendK1", 2), "V1": (sendV[par][1], recvV[par][1], "sendV1", 3),
                   "E": (sendE[par], recvE[par], "sendE", 4)}
            sbuf_, rbuf_, skey, ci_ = tab[name]
            cx.S.op("pool", lambda e, a=sbuf_, b=rbuf_: e.collective_compute(
                "AllGather", ALU.bypass, replica_groups=CC_GROUPS, ins=[a], outs=[b]),
                r=[skey], w=["recv", f"recv_{skey}"], key=f"cc{ci_}", inc1=True)

        sinks = {"q": lambda h, off, n: (qTs[h, :, off:off + n], None), "k": k_sink, "v": v_sink,
                 "up": pad_sink(up_pad), "glu": pad_sink(glu_pad), "edge": edge_sink, "cc": issue_cc}
        cx.push(f"A{l}_")
        emit_A(cx, consts, ps, x_in, modT[:, l], n1g[l], w_in[l], qkg[l], cosT, sinT, sinks)
        cx.pop()
        cx.dma("sp", E[:], recvE[par].rearrange("(j c p) x -> p j c x", j=4, c=2, p=128), r=["recv_sendE"],
               w=["E"], key="E")
        for tz in range(2):
            for side in range(2):
                c0 = tz * 32 + (HALO if side == 0 else 0)
                hv = H[:, tz, side, :, :]
                for j in range(4):
                    sc = sel[:, 4 * side + j: 4 * side + j + 1]
                    if j == 0:
                        cx.ts("dve", hv, E[:, j, :, c0:c0 + HALO], sc, ALU.mult, r=["E", "selT"], w=[f"H{tz}{side}"])
                    else:
                        cx.stt(hv, E[:, j, :, c0:c0 + HALO], sc, hv, ALU.mult, ALU.add,
                               r=["E", "selT", f"H{tz}{side}"], w=[f"H{tz}{side}"])
                pad = up_pad if tz == 0 else glu_pad
                o = 0 if side == 0 else HALO + TL
                cx.dma("sp", pad[:, :, o:o + HALO], hv, r=[f"H{tz}{side}"], w=["halo_up" if tz == 0 else "halo_glu"],
                       key=f"zpad{2 * tz + side}")
        def kv_src(h, piece, par=par):
            if piece == 0:
                return kcs[h], vcs[h]
            j, hf = (piece - 1) // 2, (piece - 1) % 2
            rows = slice(j * 512 + h * 128, j * 512 + (h + 1) * 128)
            return recvK[par][hf][rows, :], recvV[par][hf][rows, :].rearrange("p (t d) -> p t d", d=128)

        if last:
            xo_fn = lambda off, n: outT[:, :, off:off + n]
        else:
            xo_fn = lambda off, n, l=l: xs[l % 2][:, :, off:off + n]
        cx.push(f"B{l}_")
        bg = None
        if not last:
            bg = lambda alloc, bank_fn, l=l: mod_layer_gen(cx, ps, l + 1, silb, w_mod, b_mod_fm, modT, bank_fn, alloc)
        emit_B(cx, consts, ps, x_in, modT[:, l], lamd[:, l:l + 1], qTs, PIECES, kv_src, up_pad, glu_pad, invcnt,
               poolw[l], vecs[l], convw[l], w_out[l], w_ffn_in[l], w_ffn_out[l], xo_fn, last=last, bg=bg)
        cx.pop()
    return cx.finish()


_CACHE = {}


def layer_consts(l):
    return 0.8 - 0.6 * math.exp(-0.3 * l)


def prep_weights(inputs):
    f32 = np.float32
    w = {}
    w["n1g"] = np.stack([fm(inputs["norm1_g"][l], 8) for l in range(DEPTH)])
    w["w_in"] = np.stack([fm_w(np.asarray(inputs["w_in"][l], f32)) for l in range(DEPTH)])
    w["qkg"] = np.stack([np.stack([np.tile(inputs["q_norm_g"][l], 2), np.tile(inputs["k_norm_g"][l], 2)], -1)
                         for l in range(DEPTH)]).astype(f32)
    poolw = np.zeros((DEPTH, 128, 2, 128), f32)
    vecs = np.zeros((DEPTH, 128, NVEC), f32)
    for l in range(DEPTH):
        for ci in range(2):
            poolw[l, 0:64, ci, 0:64] = inputs["pool_w"][l][2 * ci]
            poolw[l, 64:128, ci, 64:128] = inputs["pool_w"][l][2 * ci + 1]
        li = layer_consts(l)
        vecs[l, :, 0:2] = fm(inputs["pool_scale"][l], 2)
        vecs[l, :, 2:4] = fm(inputs["conv_dw_b"][l], 2)
        vecs[l, :, 4:6] = fm(inputs["conv_ln_g"][l], 2)
        vecs[l, :, 6:8] = fm(inputs["conv_ln_b"][l], 2)
        vecs[l, :, 8] = inputs["subln_g"][l]
        vecs[l, :, 10] = li
        vecs[l, :, 11] = 1.0 - li
        vecs[l, :, 12:20] = fm(inputs["norm2_g"][l], 8)
    w["poolw"] = poolw
    w["vecs"] = vecs
    w["convw"] = np.stack([np.asarray(inputs["conv_dw_w"][l], f32).T.reshape(2, 128, 31).transpose(1, 0, 2)
                           for l in range(DEPTH)])
    w["w_out"] = np.stack([fm_w(np.asarray(inputs["w_out"][l], f32)) for l in range(DEPTH)])
    wfi = np.asarray(inputs["w_ffn_in"], f32)
    wg = wfi[:, :, :FF].reshape(DEPTH, 8, 128, NJ, 128)
    wu = wfi[:, :, FF:].reshape(DEPTH, 8, 128, NJ, 128)
    w["w_ffn_in"] = np.ascontiguousarray(np.concatenate([wg, wu], -1).transpose(0, 3, 2, 1, 4))
    w["w_ffn_out"] = np.ascontiguousarray(
        np.asarray(inputs["w_ffn_out"], f32).reshape(DEPTH, NJ, 128, 8, 128).transpose(0, 3, 2, 1, 4))
    return {k: np.ascontiguousarray(v, dtype=f32) for k, v in w.items()}


def make_in_maps(inputs):
    f32 = np.float32
    x = np.asarray(inputs["x"], f32)
    c = np.asarray(inputs["c"], f32)
    ctx = np.asarray(inputs["ctx"], f32)
    c_ctx = np.asarray(inputs["c_ctx"], f32)
    w = prep_weights(inputs)
    lam_in = np.stack([inputs["lambda_q1"], inputs["lambda_k1"], inputs["lambda_q2"], inputs["lambda_k2"]], 1)
    shared = dict(w)
    shared["lam_in"] = np.ascontiguousarray(np.broadcast_to(np.asarray(lam_in, f32)[None], (128, DEPTH, 4, 64)))
    shared["w_mod"] = np.ascontiguousarray(np.asarray(inputs["w_mod"], f32))
    shared["b_mod_fm"] = np.ascontiguousarray(np.asarray(inputs["b_mod"], f32).reshape(DEPTH, 48, 128).transpose(0, 2, 1))
    shared["cmat"] = const_mats()
    in_maps = []
    for i in range(NCORE):
        b, r = i // 4, i % 4
        t0 = r * TL
        m = dict(shared)
        xall = np.concatenate([x[b, t0:t0 + TL], ctx[b]], 0)
        m["x0T"] = np.ascontiguousarray(xall.T.reshape(8, 128, T).transpose(1, 0, 2))
        cvv = np.stack([c[b], c_ctx], -1)
        m["cv"] = np.ascontiguousarray(cvv.reshape(8, 128, 2).transpose(1, 0, 2))
        m["cosT"], m["sinT"] = rope_tables(i)
        m["invcnt"] = invcnt_table(i)
        sel = np.zeros((128, 8), f32)
        if r > 0:
            sel[:, r - 1] = 1.0
        if r < 3:
            sel[:, 4 + r + 1] = 1.0
        m["selT"] = sel
        in_maps.append(m)
    return in_maps


def kernel(**inputs):
    in_maps = make_in_maps(inputs)
    if "nc" not in _CACHE:
        _CACHE["nc"] = build_fused()
    res = run_bass_kernel_spmd(_CACHE["nc"], in_maps, core_ids=list(range(NCORE))).results
    out = np.zeros((2, SEQ, D), np.float32)
    for i in range(NCORE):
        b, t0 = i // 4, (i % 4) * TL
        xo = res[i]["outT"].transpose(1, 0, 2).reshape(D, TL)
        out[b, t0:t0 + TL] = xo.T
    return out
```

```python
import math
from contextlib import ExitStack

import numpy as np
import ml_dtypes

import concourse.bass as bass
import concourse.mybir as mybir
from concourse.bass_utils import run_bass_kernel_spmd

F32 = mybir.dt.float32
BF16 = mybir.dt.bfloat16
AF = mybir.ActivationFunctionType
ALU = mybir.AluOpType
AX = mybir.AxisListType
NPBF = ml_dtypes.bfloat16

D = 1024
DEPTH = 4
NCORE = 8
TL = 2048
TC = 256
T = TL + TC
SEQ = 8192
NKEY = SEQ + TC
NKT = NKEY // 128
FF = 2816
NJ = FF // 128
EPS = 1e-6
HALO = 16
GROUPS = [(0, 512), (512, 512), (1024, 512), (1536, 512), (2048, 256)]


class Sched:
    ENGS = ("pe", "act", "dve", "pool", "sp")

    def __init__(self, nc):
        self.nc = nc
        self.q = {e: [] for e in self.ENGS}
        self.cnt = {}
        self.res = {}
        self.waited = {e: {} for e in self.ENGS}
        self.bar = {}

    def barrier(self):
        self.bar = dict(self.cnt)

    def op(self, eng, fn, r=(), w=(), key=None, inc1=False):
        if key is None:
            sem, inc = "S_" + eng, 1
        elif inc1:
            sem, inc = "C_" + key, 1
        else:
            sem, inc = "D_" + key, 16
        deps = dict(self.bar)

        def need(sv):
            s, v = sv
            if deps.get(s, 0) < v:
                deps[s] = v

        for k in r:
            st = self.res.get(k)
            if st and st[0]:
                need(st[0])
            if st and k.startswith("ps"):
                for sv in st[1].items():
                    if sv[0] != sem:
                        need(sv)
        for k in w:
            st = self.res.get(k)
            if st:
                if st[0]:
                    need(st[0])
                for sv in st[1].items():
                    need(sv)
        waits = []
        for s, v in deps.items():
            if eng == "pe" and s == "S_pe":
                continue
            if self.waited[eng].get(s, 0) >= v:
                continue
            self.waited[eng][s] = v
            waits.append((s, v))
        val = self.cnt.get(sem, 0) + inc
        self.cnt[sem] = val
        self.q[eng].append((waits, fn, sem, inc))
        for k in r:
            st = self.res.setdefault(k, [None, {}])
            if st[1].get(sem, 0) < val:
                st[1][sem] = val
        for k in w:
            self.res[k] = [(sem, val), {}]

    def finish(self):
        waits = [(s, v) for s, v in self.cnt.items() if s.startswith("D_") or s.startswith("C_")]
        self.q["sp"].append((waits, None, None, 0))

    def emit(self, es):
        nc = self.nc
        sems = {n: es.enter_context(nc.semaphore(n)) for n in sorted(self.cnt)}
        block = es.enter_context(nc.Block())

        def run(name):
            def f(e):
                for waits, fn, sem, inc in self.q[name]:
                    attach = fn is not None and waits and not sem.startswith("C_")
                    for s, v in (waits[:-1] if attach else waits):
                        e.wait_ge(sems[s], v)
                    if fn is not None:
                        ins = fn(e)
                        if attach:
                            ins._wait_ge(sems[waits[-1][0]], waits[-1][1])
                        ins.then_inc(sems[sem], inc)
            return f

        block.tensor(run("pe"))
        block.scalar(run("act"))
        block.vector(run("dve"))
        block.gpsimd(run("pool"))
        block.sync(run("sp"))


class Ctx:
    def __init__(self):
        self.nc = bass.Bass("TRN2", target_bir_lowering=False)
        self.es = ExitStack()
        self.S = Sched(self.nc)
        self.n = 0
        self.stacks = [self.es]
        self.pfx = ""

    def push(self, pfx=None):
        if pfx is not None:
            self.pfx = pfx
        st = ExitStack()
        self.stacks.append(st)
        return st

    def pop(self):
        self.stacks.pop().close()
        self.S.barrier()

    def sb(self, name, shape, dt):
        return self.stacks[-1].enter_context(self.nc.sbuf_tensor(self.pfx + name, list(shape), dt))

    def din(self, name, shape, dt):
        return self.nc.dram_tensor(name, list(shape), dt, kind="ExternalInput").ap()

    def dout(self, name, shape, dt):
        return self.nc.dram_tensor(name, list(shape), dt, kind="ExternalOutput").ap()

    def dscratch(self, name, shape, dt):
        return self.nc.dram_tensor(name, list(shape), dt, kind="Internal").ap()

    def uid(self, p):
        self.n += 1
        return f"{p}{self.n}"

    def dma(self, q, out, in_, r, w, key, slow=False):
        if slow:
            self.S.op(q, lambda e: e.dma_start(out=out, in_=in_, allow_slow_non_contiguous=True), r=r, w=w, key=key)
        else:
            self.S.op(q, lambda e: e.dma_start(out=out, in_=in_), r=r, w=w, key=key)

    def mm(self, out, lhsT, rhs, start, stop, r, w):
        self.S.op("pe", lambda e: e.matmul(out, lhsT, rhs, start=start, stop=stop), r=r, w=w)

    def act(self, out, in_, func, r, w, bias=None, scale=None):
        kw = {}
        if bias is not None:
            kw["bias"] = bias
        if scale is not None:
            kw["scale"] = scale
        self.S.op("act", lambda e: e.activation(out=out, in_=in_, func=func, **kw), r=r, w=w)

    def tt(self, eng, out, in0, in1, op, r, w):
        self.S.op(eng, lambda e: e.tensor_tensor(out=out, in0=in0, in1=in1, op=op), r=r, w=w)

    def ts(self, eng, out, in0, s1, op0, r, w, s2=None, op1=None):
        if op1 is None:
            self.S.op(eng, lambda e: e.tensor_scalar(out=out, in0=in0, scalar1=s1, scalar2=None, op0=op0),
                      r=r, w=w)
        else:
            self.S.op(eng, lambda e: e.tensor_scalar(out=out, in0=in0, scalar1=s1, scalar2=s2, op0=op0, op1=op1),
                      r=r, w=w)

    def stt(self, out, in0, scalar, in1, op0, op1, r, w):
        self.S.op("dve", lambda e: e.scalar_tensor_tensor(out=out, in0=in0, scalar=scalar, in1=in1,
                                                          op0=op0, op1=op1), r=r, w=w)

    def copy(self, eng, out, in_, r, w):
        self.S.op(eng, lambda e: e.tensor_copy(out=out, in_=in_), r=r, w=w)

    def memset(self, eng, ap, val, w):
        self.S.op(eng, lambda e: e.memset(ap, val), w=w)

    def recip(self, out, in_, r, w):
        self.S.op("dve", lambda e: e.reciprocal(out=out, in_=in_), r=r, w=w)

    def recip_act(self, out, in_, r, w, one=None):
        if one is not None:
            self.act(out, in_, AF.Ln, r=list(r) + ["cmat_f"], w=w, bias=one)
        else:
            self.act(out, in_, AF.Ln, r=r, w=w)
        self.act(out, out, AF.Exp, r=w, w=w, scale=-1.0)

    def finish(self):
        self.S.finish()
        self.S.emit(self.es)
        self.es.close()
        return self.nc


def rstd_from_psum(cx, ps_ap, ps_key, tmp_ap, tmp_key, mhalf_ap, n):
    cx.act(tmp_ap, ps_ap, AF.Ln, r=[ps_key, "consts"], w=[tmp_key], bias=mhalf_ap[:, 0:1])
    cx.act(tmp_ap, tmp_ap, AF.Exp, r=[tmp_key], w=[tmp_key], scale=-0.5)


def emit_mod_pre(cx, cv, lam_in, lam_out, silb):
    cvs = cx.sb("m_cv", [128, 8, 2], F32)
    th = cx.sb("m_th", [128, 8, 2], F32)
    sil = cx.sb("m_sil", [128, 8, 2], F32)
    lamt = cx.sb("m_lamt", [128, 4, 4, 64], F32)
    prod = cx.sb("m_prod", [128, 4, 2, 64], F32)
    lsum = cx.sb("m_lsum", [128, 4, 2], F32)
    lexp = cx.sb("m_lexp", [128, 4, 2], F32)
    lams = cx.sb("m_lams", [128, 4], F32)
    cx.dma("sp", cvs[:], cv, r=[], w=["m_cv"], key="m_cv")
    cx.dma("sp", lamt[:], lam_in, r=[], w=["m_lamt"], key="m_lamt")
    cx.act(th[:], cvs[:], AF.Exp, r=["m_cv"], w=["m_th"], scale=-1.0)
    cx.ts("dve", th[:], th[:], 1.0, ALU.add, r=["m_th"], w=["m_th"])
    cx.recip(th[:], th[:], r=["m_th"], w=["m_th"])
    cx.tt("dve", sil[:], th[:], cvs[:], ALU.mult, r=["m_th", "m_cv"], w=["m_sil"])
    cx.copy("dve", silb[:], sil[:], r=["m_sil"], w=["silb"])
    cx.tt("dve", prod[:, :, 0, :], lamt[:, :, 0, :], lamt[:, :, 1, :], ALU.mult, r=["m_lamt"], w=["m_prod0"])
    cx.tt("dve", prod[:, :, 1, :], lamt[:, :, 2, :], lamt[:, :, 3, :], ALU.mult, r=["m_lamt"], w=["m_prod1"])
    cx.S.op("dve", lambda e: e.tensor_reduce(out=lsum[:], in_=prod[:], axis=AX.X, op=ALU.add),
            r=["m_prod0", "m_prod1"], w=["m_lsum"])
    cx.act(lexp[:], lsum[:], AF.Exp, r=["m_lsum"], w=["m_lexp"])
    cx.tt("dve", lams[:], lexp[:, :, 0], lexp[:, :, 1], ALU.subtract, r=["m_lexp"], w=["m_lams"])
    cx.dma("sp", lam_out, lams[:], r=["m_lams"], w=["lam_out"], key="m_lamo")


def mod_layer_gen(cx, ps, l, silb, w_mod, b_mod_fm, modT_out, bank_fn, alloc):
    SW = 384
    slab = [alloc(f"ml_slab{i}", [128, 8, SW], BF16) for i in range(2)]
    modsb = alloc("ml_modsb", [128, 48, 2], F32)
    bfm = alloc("ml_bfm", [128, 48], F32)
    cx.dma("sp", bfm[:], b_mod_fm[l], r=[], w=["ml_bfm"], key="ml_bfm")
    NE = SW // 128
    for sidx in range(6 * D // SW):
        sl = slab[sidx % 2]
        sk = f"ml_slab{sidx % 2}"
        e0 = sidx * SW
        cx.dma("pool", sl[:], w_mod[l, :, e0:e0 + SW].rearrange("(c p) e -> p c e", p=128), r=[], w=[sk], key=sk)
        yield
        bank, pk, rel = bank_fn()
        for j in range(NE):
            out = ps[:, bank, 2 * j:2 * j + 2]
            for c in range(8):
                cx.mm(out, sl[:, c, j * 128:(j + 1) * 128], silb[:, c, :], start=(c == 0), stop=(c == 7),
                      r=[sk, "silb"], w=[pk])
        cx.tt("dve", modsb[:, sidx * NE:(sidx + 1) * NE, :],
              ps[:, bank, 0:2 * NE].rearrange("p (j t) -> p j t", t=2),
              bfm[:, sidx * NE:(sidx + 1) * NE, None].to_broadcast([128, NE, 2]), ALU.add,
              r=[pk, "ml_bfm"], w=["ml_modsb"])
        rel()
        yield
    cx.dma("sp", modT_out[:, l], modsb[:], r=["ml_modsb"], w=[f"modT{l}"], key="ml_modo")
    yield


class Rot:
    def __init__(self, cx, name, shape, dt, n, alloc=None):
        alloc = alloc or cx.sb
        self.bufs = [alloc(f"{name}{i}", shape, dt) for i in range(n)]
        self.keys = [f"{name}{i}" for i in range(n)]
        self.i = 0

    def next(self):
        i = self.i % len(self.bufs)
        self.i += 1
        return self.bufs[i], self.keys[i]


class BankRot:
    def __init__(self, banks, held=None):
        self.banks = banks
        self.held = set() if held is None else held
        self.i = 0

    def next(self):
        for _ in range(len(self.banks)):
            b = self.banks[self.i % len(self.banks)]
            self.i += 1
            if b not in self.held:
                self.held.add(b)
                return b, f"ps{b}"
        raise RuntimeError(f"no free PSUM bank among {self.banks}")

    def release(self, b):
        self.held.discard(b)


def load_consts(cx, cmat_d, nm=6):
    cf = cx.sb("cmat_f", [128, nm, 128], F32)
    cb = cx.sb("cmat_b", [128, nm, 128], BF16)
    mh = cx.sb("epsb", [128, 1024], F32)
    cx.dma("sp", cf[:], cmat_d, r=[], w=["cmat_f"], key="cmat_f")
    cx.copy("dve", cb[:], cf[:], r=["cmat_f"], w=["consts"])
    cx.memset("dve", mh[:], EPS, w=["consts"])
    return cf, cb, mh


def emit_norm_mod(*a, **k):
    for _ in norm_mod_gen(*a, **k):
        pass


def norm_mod_gen(cx, g, xg, xgk, hT, hTk, sq, u, rs_rot, onesD, mhalf, ps, auxb, Gs, shs):
    off, n = GROUPS[g]
    col = 1 if off >= TL else 0
    cx.act(sq[:, :, :n], xg[:, :, :n], AF.Square, r=[xgk], w=["sq"])
    yield
    b, bk = auxb.next()
    for c in range(8):
        cx.mm(ps[:, b, :n], onesD, sq[:, c, :n], start=(c == 0), stop=(c == 7), r=["sq", "consts"], w=[bk])
    rs, rsk = rs_rot.next()
    rstd_from_psum(cx, ps[:, b, :n], bk, rs[:, :n], rsk, mhalf[:, :n], n)
    auxb.release(b)
    yield
    uk = "u"
    if u is None:
        u, uk = xg, xgk
    cx.tt("dve", u[:, :, :n], xg[:, :, :n], rs[:, None, :n].to_broadcast([128, 8, n]), ALU.mult,
          r=[xgk, rsk], w=[uk])
    yield
    yield
    for c in range(8):
        if c % 2 == 0:
            cx.act(hT[:, c, :n], u[:, c, :n], AF.Identity, r=[uk, "modc"], w=[hTk],
                   bias=shs[:, c, col:col + 1], scale=Gs[:, c, col:col + 1])
        else:
            cx.ts("dve", hT[:, c, :n], u[:, c, :n], Gs[:, c, col:col + 1], ALU.mult, r=[uk, "modc"], w=[hTk],
                  s2=shs[:, c, col:col + 1], op1=ALU.add)


def emit_A(cx, consts, ps, xT, mod, n1g, w_in, qkg, cosT, sinT, sinks):
    nc = cx.nc
    S = cx.S
    cf, cb, mhalf = consts
    onesD, ones64, pswap = cb[:, 0, :], cb[:, 1, :], cb[:, 2, :]
    mainb = BankRot([0, 1, 2, 3, 4])
    auxb = BankRot([5, 6, 7], held=mainb.held)
    allb = BankRot([0, 1, 2, 3, 4, 5, 6, 7], held=mainb.held)

    w_sb = cx.sb("a_w", [128, 8, 2304], BF16)
    for wk, c0, c1 in (("a_wk", 512, 1024), ("a_wv", 1024, 1536), ("a_wp", 1536, 2304), ("a_wq", 0, 512)):
        cx.dma("pool", w_sb[:, :, c0:c1], w_in[:, :, c0:c1], r=[], w=[wk], key=wk)
    mods = cx.sb("a_mod", [128, 48, 2], F32)
    n1gs = cx.sb("a_n1g", [128, 8], F32)
    qkgs = cx.sb("a_qkg", [128, 2], F32)
    cos_s = cx.sb("a_cos", [128, TL], F32)
    sin_s = cx.sb("a_sin", [128, TL], F32)
    cx.dma("sp", mods[:], mod, r=[], w=["a_mod"], key="a_mod")
    cx.dma("sp", n1gs[:], n1g, r=[], w=["a_n1g"], key="a_n1g")
    cx.dma("sp", qkgs[:], qkg, r=[], w=["a_qkg"], key="a_qkg")
    Gs = cx.sb("a_G", [128, 8, 2], F32)
    cx.stt(Gs[:], mods[:, 8:16, :], 1.0, n1gs[:, :, None].to_broadcast([128, 8, 2]), ALU.add, ALU.mult,
           r=["a_mod", "a_n1g"], w=["modc"])
    shs = mods[:, 0:8, :]

    xg_rot = Rot(cx, "a_xg", [128, 8, 512], F32, 2)
    sq = cx.sb("a_sq", [128, 8, 512], BF16)
    u = None
    rs_rot = Rot(cx, "a_rs", [128, 512], F32, 2)
    sq2_rot = Rot(cx, "a_sq2", [128, 512], BF16, 2)
    r2_rot = Rot(cx, "a_r2", [128, 512], F32, 2)
    qn_rot = Rot(cx, "a_qn", [128, 512], F32, 3)
    hi_rot = Rot(cx, "a_hi", [128, 512], BF16, 2)
    lo_rot = Rot(cx, "a_lo", [128, 512], BF16, 2)
    t1_rot = Rot(cx, "a_t1", [128, 512], F32, 2)
    t2_rot = Rot(cx, "a_t2", [128, 512], F32, 2)
    qo_rot = Rot(cx, "a_qo", [128, 512], BF16, 8)
    po_rot = Rot(cx, "a_po", [128, 512], F32, 4)
    th_rot = Rot(cx, "a_th", [128, 512], F32, 4)
    gl_rot = Rot(cx, "a_gl", [128, 512], F32, 4)
    vo_rot = Rot(cx, "a_vo", [128, 512], BF16, 4)

    pending = []

    def advance():
        for gen in list(pending):
            try:
                next(gen)
            except StopIteration:
                pending.remove(gen)

    def load_x(g):
        off, n = GROUPS[g]
        xg, xgk = xg_rot.next()
        cx.dma("sp", xg[:, :, :n], xT[:, :, off:off + n], r=[], w=[xgk], key=xgk)
        return xg, xgk

    def qk_post(g, which, h, b, bk):
        off, n = GROUPS[g]
        latent = off < TL
        sq2, sq2k = sq2_rot.next()
        cx.act(sq2[:, :n], ps[:, b, :n], AF.Square, r=[bk], w=[sq2k])
        yield
        b2, b2k = auxb.next()
        cx.mm(ps[:, b2, :n], ones64, sq2[:, :n], start=True, stop=True, r=[sq2k, "consts"], w=[b2k])
        yield
        r2, r2k = r2_rot.next()
        rstd_from_psum(cx, ps[:, b2, :n], b2k, r2[:, :n], r2k, mhalf[:, :n], n)
        auxb.release(b2)
        qn, qnk = qn_rot.next()
        cx.stt(qn[:, :n], ps[:, b, :n], qkgs[:, which:which + 1], r2[:, :n], ALU.mult, ALU.mult,
               r=[bk, r2k, "a_qkg"], w=[qnk])
        mainb.release(b)
        dst, dres = sinks["q" if which == 0 else "k"](h, off, n)
        dkey = f"{'qT' if which == 0 else 'kT'}_o"
        qo, qok = qo_rot.next()
        if not latent:
            cx.act(qo[:, :n], qn[:, :n], AF.Copy, r=[qnk], w=[qok])
            yield
            cx.dma("sp", dst, qo[:, :n], r=[qok], w=[dres or cx.uid(dkey)], key=qok)
            return
        hi, hik = hi_rot.next()
        lo, lok = lo_rot.next()
        cx.act(hi[:, :n], qn[:, :n], AF.Copy, r=[qnk], w=[hik])
        cx.tt("dve", lo[:, :n], qn[:, :n], hi[:, :n], ALU.subtract, r=[qnk, hik], w=[lok])
        yield
        b3, b3k = auxb.next()
        cx.mm(ps[:, b3, :n], pswap, hi[:, :n], start=True, stop=False, r=[hik, "consts"], w=[b3k])
        cx.mm(ps[:, b3, :n], pswap, lo[:, :n], start=False, stop=True, r=[lok, "consts"], w=[b3k])
        t1, t1k = t1_rot.next()
        cx.tt("dve", t1[:, :n], qn[:, :n], cos_s[:, off:off + n], ALU.mult, r=[qnk, "a_cos"], w=[t1k])
        yield
        t2, t2k = t2_rot.next()
        cx.tt("dve", t2[:, :n], ps[:, b3, :n], sin_s[:, off:off + n], ALU.mult, r=[b3k, "a_sin"], w=[t2k])
        auxb.release(b3)
        cx.tt("dve", qo[:, :n], t1[:, :n], t2[:, :n], ALU.add, r=[t1k, t2k], w=[qok])
        yield
        cx.dma("sp", dst, qo[:, :n], r=[qok], w=[dres or cx.uid(dkey)], key=qok)

    def pool_post(g, ci, b, bk):
        off, n = GROUPS[g]
        po, pok = po_rot.next()
        cx.act(po[:, :n], ps[:, b, :n], AF.Copy, r=[bk], w=[pok])
        mainb.release(b)
        yield
        cx.dma("sp", sinks["up"](ci, off, n), po[:, :n], r=[pok], w=[cx.uid("up_o")], key=pok)
        for side, sl in ((0, slice(0, HALO)), (1, slice(n - HALO, n))):
            e_ap = sinks["edge"](0, ci, side, off)
            if e_ap is not None:
                cx.dma("sp", e_ap, po[:, sl], r=[pok], w=["sendE"], key=f"edge0{ci}{side}")

    def glu_post(g, ci, ba, bak, bb, bbk):
        off, n = GROUPS[g]
        th, thk = th_rot.next()
        cx.act(th[:, :n], ps[:, bb, :n], AF.Exp, r=[bbk], w=[thk], scale=-1.0)
        mainb.release(bb)
        yield
        gl, glk = gl_rot.next()
        cx.recip_act(th[:, :n], th[:, :n], r=[thk], w=[thk], one=cf[:, 4, 0:1])
        cx.tt("dve", gl[:, :n], th[:, :n], ps[:, ba, :n], ALU.mult, r=[thk, bak], w=[glk])
        mainb.release(ba)
        yield
        cx.dma("sp", sinks["glu"](ci, off, n), gl[:, :n], r=[glk], w=[cx.uid("glu_o")], key=glk)
        for side, sl in ((0, slice(0, HALO)), (1, slice(n - HALO, n))):
            e_ap = sinks["edge"](1, ci, side, off)
            if e_ap is not None:
                cx.dma("sp", e_ap, gl[:, sl], r=[glk], w=["sendE"], key=f"edge1{ci}{side}")

    def v_post(g, ti, b, bk):
        off, n = GROUPS[g]
        vo, vok = vo_rot.next()
        cx.copy("dve", vo[:], ps[:, b, :], r=[bk], w=[vok])
        mainb.release(b)
        yield
        vdst, vres = sinks["v"](off // 128 + ti)
        cx.dma("sp", vdst, vo[:].rearrange("p (h d) -> p h d", h=4), r=[vok],
               w=[vres or cx.uid("v_o")], key=vok)

    ng = len(GROUPS)
    hT_all = cx.sb("a_hTall", [128, 8, T], BF16)

    def hT_of(g):
        off, n = GROUPS[g]
        return hT_all[:, :, off:off + n], f"a_hT{g}"

    def run_chunks(g, chunks, hook=None, alloc=None, per_chunk=None):
        alloc = alloc or mainb
        off, n = GROUPS[g]
        hT, hTk = hT_of(g)
        ca_banks = {}
        for ci, (kind, idx, co, wk) in enumerate(chunks):
            b, bk = alloc.next()
            if kind == "v":
                for c in range(8):
                    cx.mm(ps[:, b, :], hT[:, c, idx * 128:(idx + 1) * 128], w_sb[:, c, 1024:1536],
                          start=(c == 0), stop=(c == 7), r=[hTk, wk], w=[bk])
            else:
                for c in range(8):
                    cx.mm(ps[:, b, :n], w_sb[:, c, co:co + 128], hT[:, c, :n],
                          start=(c == 0), stop=(c == 7), r=[hTk, wk], w=[bk])
            advance()
            if kind == "q":
                pending.append(qk_post(g, 0, idx, b, bk))
            elif kind == "k":
                pending.append(qk_post(g, 1, idx, b, bk))
            elif kind == "pool":
                pending.append(pool_post(g, idx, b, bk))
            elif kind == "ca":
                ca_banks[idx] = (b, bk)
            elif kind == "cb":
                ba, bak = ca_banks[idx]
                pending.append(glu_post(g, idx, ba, bak, b, bk))
            elif kind == "v":
                pending.append(v_post(g, idx, b, bk))
            if hook is not None and ci == 3:
                hook()
            if per_chunk is not None:
                per_chunk(ci)

    def drain():
        while pending:
            advance()

    def norm_group(g, xgx):
        xg, xgk = xgx
        hT, hTk = hT_of(g)
        emit_norm_mod(cx, g, xg, xgk, hT, hTk, sq, u, rs_rot, onesD, mhalf, ps, auxb, Gs, shs)

    cc = sinks.get("cc", lambda name: None)
    nxt_x = load_x(0)
    norm_group(0, nxt_x)
    nxt_x = load_x(1)
    cx.dma("sp", cos_s[:], cosT, r=[], w=["a_cos"], key="a_cos")
    cx.dma("sp", sin_s[:], sinT, r=[], w=["a_sin"], key="a_sin")
    for g in range(ng):
        off, n = GROUPS[g]
        chunks = [("k", h, 512 + h * 128, "a_wk") for h in range(4)]
        chunks += [("v", ti, 1024, "a_wv") for ti in range(n // 128)]

        ngen = [None]

        def per_chunk(ci, g=g, ngen=ngen):
            nonlocal nxt_x
            if ci == 0 and g + 1 < ng:
                xg1, xg1k = nxt_x
                hT1, hT1k = hT_of(g + 1)
                ngen[0] = norm_mod_gen(cx, g + 1, xg1, xg1k, hT1, hT1k, sq, u, rs_rot, onesD, mhalf, ps, auxb,
                                       Gs, shs)
                if g + 2 < ng:
                    nxt_x = load_x(g + 2)
            if ngen[0] is not None:
                try:
                    next(ngen[0])
                except StopIteration:
                    ngen[0] = None
            if ci == 3 and g in (2, 4):
                cc("K%d" % (g // 2 - 1))
                cc("V%d" % (g // 2 - 1))

        run_chunks(g, chunks, None, per_chunk=per_chunk)
        while ngen[0] is not None:
            per_chunk(-1)
    for g in range(ng):
        chunks = [("pool", i, 1536 + i * 128, "a_wp") for i in range(2)]
        chunks += [("ca", i, 1792 + i * 128, "a_wp") for i in range(2)]
        chunks += [("cb", i, 2048 + i * 128, "a_wp") for i in range(2)]
        run_chunks(g, chunks, (lambda: cc("E")) if g == 4 else None, alloc=allb)
    for g in range(ng):
        run_chunks(g, [("q", h, h * 128, "a_wq") for h in range(4)])
    drain()


def fm(v, nch):
    return np.ascontiguousarray(np.asarray(v, np.float32).reshape(nch, 128).T)


def fm_w(w):
    k, e = w.shape
    return np.ascontiguousarray(w.reshape(k // 128, 128, e).transpose(1, 0, 2))


def rope_tables(core):
    t = (core % 4) * TL + np.arange(TL)
    row = (t // 64).astype(np.float64)
    col = (t % 64).astype(np.float64)
    inv_freq = 10000.0 ** (-np.arange(0, 32, 2, dtype=np.float64) / 32)
    ang = np.concatenate([row[:, None] * inv_freq, col[:, None] * inv_freq], -1)
    p = np.arange(128)
    pair = (p % 64) // 2
    cosT = np.cos(ang)[:, pair].T
    sgn = np.where(p % 2 == 0, -1.0, 1.0)[:, None]
    sinT = np.sin(ang)[:, pair].T * sgn
    return np.ascontiguousarray(cosT, np.float32), np.ascontiguousarray(sinT, np.float32)


def const_mats():
    m = np.zeros((128, 6, 128), np.float32)
    m[:, 0, :] = 1.0 / 1024
    m[0:64, 1, 0:64] = 1.0 / 64
    m[64:128, 1, 64:128] = 1.0 / 64
    idx = np.arange(128)
    m[idx ^ 1, 2, idx] = 1.0
    m[:, 3, :] = 1.0 / 128
    m[:, 4, :] = 1.0
    m[:, 5, :] = 1.0 / 256
    return m


NVEC = 24
TPL = TL + 2 * HALO
TPC = TC + 2 * HALO
TP = TPL + TPC
KPIECES = 6
KTP = NKT // KPIECES


class SAlloc:
    def __init__(self):
        self.held = set()
        self.i = 0
        self.j = 0

    def pair(self):
        for _ in range(2):
            p = self.i % 2
            self.i += 1
            if (2 * p) not in self.held and (2 * p + 1) not in self.held:
                return 2 * p
        raise RuntimeError("no free score pair")

    def one(self):
        for _ in range(2):
            b = self.j % 2
            self.j += 1
            if b not in self.held:
                self.held.add(b)
                return b, f"ps{b}"
        raise RuntimeError("no free stats bank")

    def release(self, b):
        self.held.discard(b)


def emit_B(cx, consts, ps, xT, mod, lam_ap, qT, pieces, kv_src, up_pad, glu_pad, invcnt, poolw, vecs, convw,
           w_out, w_ffn_in, w_ffn_out, xT_o_fn, last=False, bg=None):
    nc = cx.nc
    S = cx.S
    cf, cb, mhalf = consts
    onesD, ones128n, ones1, = cb[:, 0, :], cb[:, 3, :], cb[:, 4, :]
    ones256f = cf[:, 5, :]
    sa = SAlloc()
    groups_run = [g for g in range(len(GROUPS)) if not (last and GROUPS[g][0] >= TL)]
    piece_of = {}
    for pi, (k0, kc) in enumerate(pieces):
        for kt in range(k0, k0 + kc):
            piece_of[kt] = pi

    mods = cx.sb("b_mod", [128, 48, 2], F32)
    vec = cx.sb("b_vec", [128, NVEC], F32)
    cw = cx.sb("b_cw", [128, 2, 31], F32)
    pwf = cx.sb("b_pwf", [128, 2, 128], F32)
    pwb = cx.sb("b_pwb", [128, 2, 128], BF16)
    cx.dma("sp", mods[:], mod, r=[], w=["b_mod"], key="b_mod")
    cx.dma("sp", vec[:], vecs, r=[], w=["b_vec"], key="b_vec")
    cx.dma("sp", cw[:], convw, r=[], w=["b_cw"], key="b_cw")
    cx.dma("sp", pwf[:], poolw, r=[], w=["b_pwf"], key="b_pwf")
    lamt = cx.sb("b_lamt", [128, 1], F32)
    cx.dma("sp", lamt[:], lam_ap, r=[], w=["b_lamt"], key="b_lamt", slow=True)
    cx.copy("dve", pwb[:], pwf[:], r=["b_pwf"], w=["b_pwb"])
    der = cx.sb("b_der", [128, 8], F32)
    cx.tt("dve", der[:, 0:1], lamt[:, 0:1], vec[:, 10:11], ALU.add, r=["b_vec", "b_lamt"], w=["b_der0"])
    cx.ts("dve", der[:, 0:1], der[:, 0:1], -1.0, ALU.mult, r=["b_der0"], w=["b_der0"])
    cx.tt("dve", der[:, 1:2], vec[:, 8:9], vec[:, 11:12], ALU.mult, r=["b_vec"], w=["b_der1"])
    cx.copy("dve", der[:, 2:6], vec[:, 4:8], r=["b_vec"], w=["b_der2"])
    derk = ["b_der0", "b_der1", "b_der2"]
    G2 = cx.sb("b_G2", [128, 8, 2], F32)
    g2h = cx.sb("b_g2h", [128, 8, 2], F32)
    cx.stt(G2[:], mods[:, 32:40, :], 1.0, vec[:, 12:20, None].to_broadcast([128, 8, 2]), ALU.add, ALU.mult,
           r=["b_mod", "b_vec"], w=["modc"])
    cx.copy("dve", g2h[:], mods[:, 40:48, :], r=["b_mod"], w=["modc2"])
    sh2 = mods[:, 24:32, :]
    g1 = mods[:, 16:24, :]

    mixT = cx.sb("b_mix", [128, 8, T], BF16)

    cx.push()
    sb1 = cx.sb

    NP = 512 + 2 * HALO
    U = sb1("p_U", [128, 2, NP], F32)
    IC = sb1("p_IC", [128, 2, 512], F32)
    Sa = sb1("p_Sa", [128, 2, NP], F32)
    Sb = sb1("p_Sb", [128, 2, NP], F32)
    Sc = sb1("p_Sc", [128, 2, NP], F32)
    Sd = sb1("p_Sd", [128, 2, NP], F32)
    ptmp = sb1("p_tmp", [128, 2, 512], F32)
    pin = sb1("p_in", [128, 2, 512], BF16)
    G = sb1("c_G", [128, 2, NP], F32)
    acc = sb1("c_acc", [128, 2, 512], F32)
    sqa = sb1("c_sqa", [128, 2, 512], F32)
    msb = sb1("c_msb", [128, 512], F32)
    m2 = sb1("c_m2", [128, 512], F32)
    vr = sb1("c_vr", [128, 512], F32)
    dd = sb1("c_dd", [128, 2, 512], F32)
    zh = sb1("c_zh", [128, 2, 512], F32)
    cth = sb1("c_th", [128, 2, 512], F32)

    def b1_gen():
        for g in groups_run:
            off, n = GROUPS[g]
            po = off if off < TL else TPL + (off - TL)
            m = n + 2 * HALO
            cx.dma("sp", U[:, :, :m], up_pad[:, :, po:po + m], r=["halo_up"], w=["p_U"], key="p_U")
            cx.dma("sp", IC[:, :, :n], invcnt[:, :, off:off + n], r=[], w=["p_IC"], key="p_IC")
            yield
            cx.tt("dve", Sa[:, :, 1:m], U[:, :, 0:m - 1], U[:, :, 1:m], ALU.add, r=["p_U"], w=["p_Sa"])
            yield
            cx.tt("dve", Sb[:, :, 2:m - 1], Sa[:, :, 1:m - 2], Sa[:, :, 3:m], ALU.add, r=["p_Sa"], w=["p_Sb"])
            yield
            cx.tt("dve", Sc[:, 1, 4:m - 3], Sb[:, 1, 2:m - 5], Sb[:, 1, 6:m - 1], ALU.add, r=["p_Sb"], w=["p_Sc"])
            yield
            cx.tt("dve", Sd[64:128, 1, 8:m - 7], Sc[64:128, 1, 4:m - 11], Sc[64:128, 1, 12:m - 3], ALU.add,
                  r=["p_Sc"], w=["p_Sd"])
            yield
            srcs = [(Sa, "p_Sa", 0, 0), (Sb, "p_Sb", 64, 0), (Sc, "p_Sc", 0, 1), (Sd, "p_Sd", 64, 1)]
            for sbuf, sk, p0, ci in srcs:
                cx.tt("dve", ptmp[p0:p0 + 64, ci, :n], sbuf[p0:p0 + 64, ci, HALO:HALO + n],
                      IC[p0:p0 + 64, ci, :n], ALU.mult, r=[sk, "p_IC"], w=[f"p_tmp{p0}{ci}"])
                cx.tt("dve", pin[p0:p0 + 64, ci, :n], ptmp[p0:p0 + 64, ci, :n],
                      U[p0:p0 + 64, ci, HALO:HALO + n], ALU.subtract, r=[f"p_tmp{p0}{ci}", "p_U"],
                      w=[f"p_in{p0}{ci}"])
                yield
            pk = [f"p_in{p0}{ci}" for _, _, p0, ci in srcs]
            for ci in range(2):
                b, bk = sa.one()
                cx.mm(ps[:, b, :n], pwb[:, ci, :], pin[:, ci, :n], start=True, stop=True,
                      r=pk + ["b_pwb"], w=[bk])
                cx.ts("dve", mixT[:, 4 + ci, off:off + n], ps[:, b, :n], vec[:, ci:ci + 1], ALU.mult,
                      r=[bk, "b_vec"], w=[f"mix{4 + ci}_{g}"])
                sa.release(b)
                yield
            cx.dma("sp", G[:, :, :m], glu_pad[:, :, po:po + m], r=["halo_glu"], w=["c_G"], key="c_G")
            yield
            for ci in range(2):
                cx.ts("dve", acc[:, ci, :n], G[:, ci, 1:1 + n], cw[:, ci, 0:1], ALU.mult,
                      r=["c_G", "b_cw", "b_vec"], w=[f"c_acc{ci}"], s2=vec[:, 2 + ci:3 + ci], op1=ALU.add)
                yield
                for k in range(1, 31):
                    cx.stt(acc[:, ci, :n], G[:, ci, 1 + k:1 + k + n], cw[:, ci, k:k + 1], acc[:, ci, :n],
                           ALU.mult, ALU.add, r=["c_G", "b_cw", f"c_acc{ci}"], w=[f"c_acc{ci}"])
                    yield
            cx.tt("dve", sqa[:, :, :n], acc[:, :, :n], acc[:, :, :n], ALU.mult, r=["c_acc0", "c_acc1"], w=["c_sqa"])
            yield
            b1_, b1k = sa.one()
            for ci in range(2):
                cx.mm(ps[:, b1_, :n], ones256f, acc[:, ci, :n], start=(ci == 0), stop=(ci == 1),
                      r=["c_acc0", "c_acc1", "cmat_f"], w=[b1k])
            cx.copy("dve", msb[:, :n], ps[:, b1_, :n], r=[b1k], w=["c_msb"])
            sa.release(b1_)
            cx.tt("dve", m2[:, :n], msb[:, :n], msb[:, :n], ALU.mult, r=["c_msb"], w=["c_m2"])
            yield
            b2_, b2k = sa.one()
            for ci in range(2):
                cx.mm(ps[:, b2_, :n], ones256f, sqa[:, ci, :n], start=(ci == 0), stop=(ci == 1),
                      r=["c_sqa", "cmat_f"], w=[b2k])
            cx.stt(vr[:, :n], ps[:, b2_, :n], EPS, m2[:, :n], ALU.add, ALU.subtract, r=[b2k, "c_m2"], w=["c_vr"])
            sa.release(b2_)
            cx.act(vr[:, :n], vr[:, :n], AF.Ln, r=["c_vr"], w=["c_vr"])
            cx.act(vr[:, :n], vr[:, :n], AF.Exp, r=["c_vr"], w=["c_vr"], scale=-0.5)
            yield
            cx.tt("dve", dd[:, :, :n], acc[:, :, :n], msb[:, None, :n].to_broadcast([128, 2, n]), ALU.subtract,
                  r=["c_acc0", "c_acc1", "c_msb"], w=["c_dd"])
            cx.tt("dve", dd[:, :, :n], dd[:, :, :n], vr[:, None, :n].to_broadcast([128, 2, n]), ALU.mult,
                  r=["c_dd", "c_vr"], w=["c_dd"])
            yield
            for ci in range(2):
                cx.ts("dve", zh[:, ci, :n], dd[:, ci, :n], der[:, 2 + ci:3 + ci], ALU.mult,
                      r=["c_dd", "b_der2"], w=[f"c_zh{ci}"], s2=der[:, 4 + ci:5 + ci], op1=ALU.add)
            yield
            cx.act(cth[:, :, :n], zh[:, :, :n], AF.Exp, r=["c_zh0", "c_zh1"], w=["c_th"], scale=-1.0)
            yield
            cx.recip_act(cth[:, :, :n], cth[:, :, :n], r=["c_th"], w=["c_th"], one=cf[:, 4, 0:1])
            yield
            cx.tt("dve", mixT[:, 6:8, off:off + n], cth[:, :, :n], zh[:, :, :n], ALU.mult,
                  r=["c_th", "c_zh0", "c_zh1"], w=[f"mix6_{g}", f"mix7_{g}"])
            yield

    b1 = b1_gen()
    b1_done = [False]

    def bg_bank():
        b, bk = sa.one()
        return b, bk, (lambda: sa.release(b))

    bg_gen = bg(sb1, bg_bank) if bg is not None else None
    bg_done = [bg_gen is None]

    def bg_step():
        if bg_done[0]:
            return
        try:
            next(bg_gen)
        except StopIteration:
            bg_done[0] = True

    def b1_step():
        if b1_done[0]:
            return
        try:
            next(b1)
        except StopIteration:
            b1_done[0] = True

    Kh = sb1("a_K", [128, NKEY], BF16)
    Vh = sb1("a_V", [128, NKT, 128], BF16)
    qh_rot = [sb1(f"a_q{i}", [128, T], BF16) for i in range(2)]
    NPT = 8
    Pt = [sb1(f"a_P{i}", [128, 2, 512], BF16) for i in range(NPT)]
    rD = sb1("a_rD", [128, 2, 512], F32)
    oo = sb1("a_oo", [128, 2, 512], F32)
    pacc_rot = Rot(cx, "a_Pacc", [128, 2, 512], BF16, 2, alloc=sb1)
    QD = 8
    o_rot = Rot(cx, "a_o", [128, 512], F32, 2, alloc=sb1)
    osq_rot = Rot(cx, "a_osq", [128, 512], BF16, 2, alloc=sb1)
    r3_rot = Rot(cx, "a_r3", [128, 512], F32, 2, alloc=sb1)

    def load_kv(h, piece):
        k0, kc = pieces[piece]
        ksrc, vsrc = kv_src(h, piece)
        cx.dma("sp", Kh[:, k0 * 128:(k0 + kc) * 128], ksrc, r=["recv"], w=[f"K{piece}"], key=f"K{piece}")
        cx.dma("sp", Vh[:, k0:k0 + kc, :], vsrc, r=["recv"], w=[f"V{piece}"], key=f"V{piece}")

    def load_q(h):
        cx.dma("sp", qh_rot[h % 2][:], qT[h], r=[], w=[f"a_q{h % 2}"], key=f"a_q{h % 2}")

    load_q(0)
    for piece in range(len(pieces)):
        load_kv(0, piece)

    gorder = [g for g in [4, 0, 1, 2, 3] if g in groups_run]
    post_pending = []
    it = 0
    for h in range(4):
        qh = qh_rot[h % 2]
        qk = f"a_q{h % 2}"
        if h + 1 < 4:
            load_q(h + 1)
        for gi, g in enumerate(gorder):
            off, n = GROUPS[g]
            kts = list(range(2)) if off >= TL else list(range(NKT))
            last_group = (gi == len(gorder) - 1)
            pvq = []
            quad = []
            dq = []
            dstate = {"first": True, "acc": None}

            def flush_d(final, n=n, dq=dq, dstate=dstate):
                while dq:
                    src, srck = dq.pop(0)
                    lastd = final and not dq
                    for c in range(2):
                        cx.mm(ps[:, 6 + c, :n], ones1, src[:, c, :n], start=dstate["first"], stop=lastd,
                              r=["consts", srck], w=[f"ps{6 + c}"])
                    dstate["first"] = False

            def emit_pv(prev, final, n=n, quad=quad, dq=dq, dstate=dstate):
                pkt, ppt, pptk, pidx = prev
                pp = piece_of[pkt]
                flush_d(False)
                for c in range(2):
                    cx.mm(ps[:, 4 + c, :n], Vh[:, pkt, :], ppt[:, c, :n], start=(pidx == 0), stop=final,
                          r=[f"V{pp}", pptk], w=[f"ps{4 + c}"])
                quad.append((ppt, pptk))
                if len(quad) == 2:
                    dstate["acc"] = pacc_rot.next()
                    acc, acck = dstate["acc"]
                    cx.tt("dve", acc[:, :, :n], quad[0][0][:, :, :n], ppt[:, :, :n], ALU.add,
                          r=[quad[0][1], pptk], w=[acck])
                elif len(quad) > 2:
                    acc, acck = dstate["acc"]
                    cx.tt("dve", acc[:, :, :n], acc[:, :, :n], ppt[:, :, :n], ALU.add, r=[acck, pptk], w=[acck])
                if len(quad) == QD or final:
                    dq.append(dstate["acc"] if len(quad) > 1 else (ppt, pptk))
                    quad.clear()
                if final:
                    flush_d(True)

            for idx, kt in enumerate(kts):
                piece = piece_of[kt]
                b0 = sa.pair()
                for c in range(2):
                    cx.mm(ps[:, b0 + c, :n], Kh[64 * c:64 * c + 64, kt * 128:(kt + 1) * 128],
                          qh[64 * c:64 * c + 64, off:off + n], start=True, stop=True,
                          r=[f"K{piece}", qk], w=[f"ps{b0 + c}"])
                pt = Pt[it % NPT]
                ptk = f"a_P{it % NPT}"
                cx.act(pt[:, :, :n], ps[:, b0:b0 + 2, :n], AF.Exp, r=[f"ps{b0}", f"ps{b0 + 1}"], w=[ptk],
                       scale=0.125)
                it += 1
                pvq.append((kt, pt, ptk, idx))
                if len(pvq) > 2:
                    prev = pvq.pop(0)
                    emit_pv(prev, False)
                    pkt = prev[0]
                    pp = piece_of[pkt]
                    if last_group and h + 1 < 4 and pkt == pieces[pp][0] + pieces[pp][1] - 1 \
                            and pp != len(pieces) - 1:
                        load_kv(h + 1, pp)
                if it % 2 == 0:
                    b1_step()
                if it % 5 == 2:
                    bg_step()
                if idx in (1, 5) and post_pending:
                    for gen in list(post_pending):
                        try:
                            next(gen)
                        except StopIteration:
                            post_pending.remove(gen)
            while pvq:
                prev = pvq.pop(0)
                emit_pv(prev, not pvq)
            if last_group and h + 1 < 4:
                load_kv(h + 1, len(pieces) - 1)
            for gen in list(post_pending):
                for _ in gen:
                    pass
                post_pending.remove(gen)
            cx.copy("dve", oo[:, :, :n], ps[:, 4:6, :n], r=["ps4", "ps5"], w=["a_oo"])
            o, ok = o_rot.next()
            osq, osqk = osq_rot.next()

            def post(h=h, off=off, n=n, o=o, ok=ok, osq=osq, osqk=osqk):
                cx.recip_act(rD[:, :, :n], ps[:, 6:8, :n], r=["ps6", "ps7"], w=["a_rD"])
                cx.tt("dve", oo[:, :, :n], oo[:, :, :n], rD[:, :, :n], ALU.mult, r=["a_oo", "a_rD"], w=["a_oo"])
                cx.stt(o[:, :n], oo[:, 1, :n], der[:, 0:1], oo[:, 0, :n], ALU.mult, ALU.add,
                       r=["a_oo", "b_der0"], w=[ok])
                cx.tt("dve", osq[:, :n], o[:, :n], o[:, :n], ALU.mult, r=[ok], w=[osqk])
                yield
                b, bk = sa.one()
                cx.mm(ps[:, b, :n], ones128n, osq[:, :n], start=True, stop=True, r=[osqk, "consts"], w=[bk])
                r3, r3k = r3_rot.next()
                rstd_from_psum(cx, ps[:, b, :n], bk, r3[:, :n], r3k, mhalf[:, :n], n)
                sa.release(b)
                cx.stt(mixT[:, h, off:off + n], o[:, :n], der[:, 1:2], r3[:, :n], ALU.mult, ALU.mult,
                       r=[ok, r3k, "b_der1"], w=[f"mix{h}_{off}"])
                yield

            post_pending.append(post())
    for gen in list(post_pending):
        for _ in gen:
            pass
    while not b1_done[0]:
        b1_step()
    while not bg_done[0]:
        bg_step()

    cx.pop()
    banks = BankRot([0, 1, 2, 3, 4, 5, 6, 7])
    wo_sb = cx.sb("f_wout", [128, 8, 1024], BF16)
    for hf_ in range(2):
        cx.dma("pool", wo_sb[:, :, hf_ * 512:(hf_ + 1) * 512], w_out[:, :, hf_ * 512:(hf_ + 1) * 512], r=[],
               w=[f"f_wout{hf_}"], key=f"f_wout{hf_}")
    xg_rot = Rot(cx, "f_xg", [128, 8, 512], F32, 2)
    sq = cx.sb("f_sq", [128, 8, 512], BF16)
    u = cx.sb("f_u", [128, 8, 512], F32)
    rs_rot = Rot(cx, "f_rs", [128, 512], F32, 2)
    aT = cx.sb("f_aT", [128, NJ, 512], BF16)
    win_rot = Rot(cx, "f_win", [128, 8, 256], BF16, 4)
    wo2_rot = Rot(cx, "f_wo2", [128, NJ, 128], BF16, 3)
    th_rot = Rot(cx, "f_th", [128, 512], F32, 2)
    s_rot = Rot(cx, "f_s", [128, 512], F32, 2)

    h2_rot = Rot(cx, "f_h2T", [128, 8, 512], BF16, 2)

    def outproj_norm(g):
        off, n = GROUPS[g]
        col = 1 if off >= TL else 0
        xg, xgk = xg_rot.next()
        cx.dma("sp", xg[:, :, :n], xT[:, :, off:off + n], r=[], w=[xgk], key=xgk)
        mixk = [f"mix{m}_{off}" for m in range(4)] + [f"mix{m}_{g}" for m in range(4, 8)]
        for dc in range(8):
            b, bk = banks.next()
            for m in range(8):
                cx.mm(ps[:, b, :n], wo_sb[:, m, dc * 128:(dc + 1) * 128], mixT[:, m, off:off + n],
                      start=(m == 0), stop=(m == 7), r=[f"f_wout{dc // 4}", mixk[m]], w=[bk])
            cx.stt(xg[:, dc, :n], ps[:, b, :n], g1[:, dc, col:col + 1], xg[:, dc, :n], ALU.mult, ALU.add,
                   r=[bk, "b_mod", xgk], w=[xgk])
            banks.release(b)
        h2T, h2k = h2_rot.next()
        gen = norm_mod_gen(cx, g, xg, xgk, h2T, h2k, sq, u, rs_rot, onesD, mhalf, ps, banks, G2, sh2)
        return xg, xgk, h2T, h2k, gen

    def run_gen(gen):
        for _ in gen:
            pass

    nxt = outproj_norm(groups_run[0])
    run_gen(nxt[4])
    for gi, g in enumerate(groups_run):
        off, n = GROUPS[g]
        col = 1 if off >= TL else 0
        xg, xgk, h2T, h2k, _ = nxt
        ngen = None
        if gi + 1 < len(groups_run):
            nxt = outproj_norm(groups_run[gi + 1])
            ngen = nxt[4]
        for j in range(NJ):
            wj, wjk = win_rot.next()
            cx.dma("pool", wj[:], w_ffn_in[j], r=[], w=[wjk], key=wjk)
            bg, bgk = banks.next()
            bu, buk = banks.next()
            for c in range(8):
                cx.mm(ps[:, bg, :n], wj[:, c, 0:128], h2T[:, c, :n], start=(c == 0), stop=(c == 7),
                      r=[wjk, h2k], w=[bgk])
            for c in range(8):
                cx.mm(ps[:, bu, :n], wj[:, c, 128:256], h2T[:, c, :n], start=(c == 0), stop=(c == 7),
                      r=[wjk, h2k], w=[buk])
            th, thk = th_rot.next()
            cx.act(th[:, :n], ps[:, bg, :n], AF.Exp, r=[bgk], w=[thk], scale=-1.0)
            sv, svk = s_rot.next()
            cx.recip_act(th[:, :n], th[:, :n], r=[thk], w=[thk], one=cf[:, 4, 0:1])
            cx.tt("dve", sv[:, :n], th[:, :n], ps[:, bg, :n], ALU.mult, r=[thk, bgk], w=[svk])
            banks.release(bg)
            cx.tt("dve", aT[:, j, :n], sv[:, :n], ps[:, bu, :n], ALU.mult, r=[svk, buk], w=[f"f_aT{j}"])
            banks.release(bu)
            if ngen is not None and j >= 6 and j % 2 == 0:
                try:
                    next(ngen)
                except StopIteration:
                    ngen = None
        if ngen is not None:
            run_gen(ngen)
        for dc in range(8):
            w2, w2k = wo2_rot.next()
            cx.dma("pool", w2[:], w_ffn_out[dc], r=[], w=[w2k], key=w2k)
            b, bk = banks.next()
            for j in range(NJ):
                cx.mm(ps[:, b, :n], w2[:, j, :], aT[:, j, :n], start=(j == 0), stop=(j == NJ - 1),
                      r=[w2k, f"f_aT{j}"], w=[bk])
            cx.stt(xg[:, dc, :n], ps[:, b, :n], g2h[:, dc, col:col + 1], xg[:, dc, :n], ALU.mult, ALU.add,
                   r=[bk, "modc2", xgk], w=[xgk])
            banks.release(b)
        cx.dma("sp", xT_o_fn(off, n), xg[:, :, :n], r=[xgk], w=[cx.uid("xT_o")], key=xgk)


def invcnt_table(core):
    out = np.zeros((128, 2, T), np.float32)
    wins = {(0, 0): 2, (64, 0): 4, (0, 1): 8, (64, 1): 16}
    for (p0, ci), w in wins.items():
        for off, L, base in ((0, SEQ, (core % 4) * TL), (TL, TC, 0)):
            nt = TL if off == 0 else TC
            t = base + np.arange(nt)
            lo = np.clip(t - w // 2, 0, L)
            hi = np.clip(t + w - w // 2, 0, L)
            out[p0:p0 + 64, ci, off:off + nt] = (1.0 / (hi - lo))[None, :]
    return out


CC_GROUPS = [[0, 1, 2, 3], [4, 5, 6, 7]]
PIECES = [(0, 2)] + [(2 + 8 * i, 8) for i in range(8)]


def build_fused(depth=DEPTH):
    cx = Ctx()
    nc = cx.nc
    x0T = cx.din("x0T", [128, 8, T], F32)
    cv = cx.din("cv", [128, 8, 2], F32)
    w_mod = cx.din("w_mod", [DEPTH, D, 6 * D], F32)
    b_mod_fm = cx.din("b_mod_fm", [DEPTH, 128, 48], F32)
    lam_in = cx.din("lam_in", [128, DEPTH, 4, 64], F32)
    cmat = cx.din("cmat", [128, 6, 128], F32)
    n1g = cx.din("n1g", [DEPTH, 128, 8], F32)
    w_in = cx.din("w_in", [DEPTH, 128, 8, 2304], F32)
    qkg = cx.din("qkg", [DEPTH, 128, 2], F32)
    poolw = cx.din("poolw", [DEPTH, 128, 2, 128], F32)
    vecs = cx.din("vecs", [DEPTH, 128, NVEC], F32)
    convw = cx.din("convw", [DEPTH, 128, 2, 31], F32)
    w_out = cx.din("w_out", [DEPTH, 128, 8, 1024], F32)
    w_ffn_in = cx.din("w_ffn_in", [DEPTH, NJ, 128, 8, 256], F32)
    w_ffn_out = cx.din("w_ffn_out", [DEPTH, 8, 128, NJ, 128], F32)
    cosT = cx.din("cosT", [128, TL], F32)
    sinT = cx.din("sinT", [128, TL], F32)
    invcnt = cx.din("invcnt", [128, 2, T], F32)
    selT = cx.din("selT", [128, 8], F32)
    outT = cx.dout("outT", [128, 8, TL], F32)
    modT = cx.dscratch("modT", [128, DEPTH, 48, 2], F32)
    lamd = cx.dscratch("lamd", [128, DEPTH], F32)
    xs = [cx.dscratch(f"xs{i}", [128, 8, T], F32) for i in range(2)]
    qTs = cx.dscratch("qTs", [4, 128, T], BF16)
    kcs = cx.dscratch("kcs", [4, 128, TC], BF16)
    vcs = cx.dscratch("vcs", [4, 128, 2, 128], BF16)
    up_pad = cx.dscratch("up_pad", [128, 2, TP], F32)
    glu_pad = cx.dscratch("glu_pad", [128, 2, TP], F32)
    sendK = [[cx.dscratch(f"sendK{p}{h}", [512, 1024], BF16) for h in range(2)] for p in range(2)]
    sendV = [[cx.dscratch(f"sendV{p}{h}", [512, 1024], BF16) for h in range(2)] for p in range(2)]
    recvK = [[cx.dscratch(f"recvK{p}{h}", [2048, 1024], BF16) for h in range(2)] for p in range(2)]
    recvV = [[cx.dscratch(f"recvV{p}{h}", [2048, 1024], BF16) for h in range(2)] for p in range(2)]
    sendE = [cx.dscratch(f"sendE{p}", [256, 64], F32) for p in range(2)]
    recvE = [cx.dscratch(f"recvE{p}", [1024, 64], F32) for p in range(2)]

    ps = cx.es.enter_context(nc.psum_tensor("ps", [128, 8, 512], F32))
    consts = load_consts(cx, cmat)
    sel = cx.sb("sel_sb", [128, 8], F32)
    cx.dma("sp", sel[:], selT, r=[], w=["selT"], key="selT")
    E = cx.sb("E", [128, 4, 2, 64], F32)
    H = cx.sb("H", [128, 2, 2, 2, HALO], F32)
    zt = cx.sb("zt", [128, 2, HALO], F32)
    cx.memset("dve", zt[:], 0.0, w=["zt"])
    zi = 0
    for pad in (up_pad, glu_pad):
        for o in (0, HALO + TL, TPL, TPL + HALO + TC):
            cx.dma("sp", pad[:, :, o:o + HALO], zt[:], r=["zt"], w=[f"zpad{zi}"], key=f"zpad{zi % 4}")
            zi += 1

    silb = cx.sb("silb", [128, 8, 2], BF16)
    cx.push("M_")
    emit_mod_pre(cx, cv, lam_in, lamd, silb)
    for _ in mod_layer_gen(cx, ps, 0, silb, w_mod, b_mod_fm, modT, lambda: (0, "ps0", lambda: None), cx.sb):
        pass
    cx.pop()

    for l in range(depth):
        par = l % 2
        last = (l == depth - 1)
        x_in = x0T if l == 0 else xs[(l - 1) % 2]

        def k_sink(h, off, n, par=par):
            if off >= TL:
                return kcs[h, :, :], None
            return (sendK[par][off // 1024][h * 128:(h + 1) * 128, off % 1024: off % 1024 + n],
                    f"sendK{off // 1024}")

        def v_sink(ti, par=par):
            if ti >= 16:
                return vcs.rearrange("h p t d -> p h t d")[:, :, ti - 16, :], None
            return (sendV[par][ti // 8].rearrange("(h p) c -> p h c", p=128)[:, :, (ti % 8) * 128:(ti % 8 + 1) * 128],
                    f"sendV{ti // 8}")

        def pad_sink(pad):
            def f(ci, off, n):
                o = HALO + off if off < TL else TPL + HALO + (off - TL)
                return pad[:, ci, o:o + n]
            return f

        def edge_sink(tz, ci, side, off, par=par):
            if (side == 0 and off == 0) or (side == 1 and off == TL - 512):
                return sendE[par][ci * 128:(ci + 1) * 128, tz * 32 + side * HALO: tz * 32 + (side + 1) * HALO]
            return None

        def issue_cc(name, par=par):
            tab = {"K0": (sendK[par][0], recvK[par][0], "sendK0", 0), "V0": (sendV[par][0], recvV[par][0], "sendV0", 1),
                   "K1": (sendK[par][1], recvK[par][1], "sendK1", 2), "V1": (sendV[par][1], recvV[par][1], "sendV1", 3),
                   "E": (sendE[par], recvE[par], "sendE", 4)}
            sbuf_, rbuf_, skey, ci_ = tab[name]
            cx.S.op("pool", lambda e, a=sbuf_, b=rbuf_: e.collective_compute(
                "AllGather", ALU.bypass, replica_groups=CC_GROUPS, ins=[a], outs=[b]),
                r=[skey], w=["recv", f"recv_{skey}"], key=f"cc{ci_}", inc1=True)

        sinks = {"q": lambda h, off, n: (qTs[h, :, off:off + n], None), "k": k_sink, "v": v_sink,
                 "up": pad_sink(up_pad), "glu": pad_sink(glu_pad), "edge": edge_sink, "cc": issue_cc}
        cx.push(f"A{l}_")
        emit_A(cx, consts, ps, x_in, modT[:, l], n1g[l], w_in[l], qkg[l], cosT, sinT, sinks)
        cx.pop()
        cx.dma("sp", E[:], recvE[par].rearrange("(j c p) x -> p j c x", j=4, c=2, p=128), r=["recv_sendE"],
               w=["E"], key="E")
        for tz in range(2):
            for side in range(2):
                c0 = tz * 32 + (HALO if side == 0 else 0)
                hv = H[:, tz, side, :, :]
                for j in range(4):
                    sc = sel[:, 4 * side + j: 4 * side + j + 1]
                    if j == 0:
                        cx.ts("dve", hv, E[:, j, :, c0:c0 + HALO], sc, ALU.mult, r=["E", "selT"], w=[f"H{tz}{side}"])
                    else:
                        cx.stt(hv, E[:, j, :, c0:c0 + HALO], sc, hv, ALU.mult, ALU.add,
                               r=["E", "selT", f"H{tz}{side}"], w=[f"H{tz}{side}"])
                pad = up_pad if tz == 0 else glu_pad
                o = 0 if side == 0 else HALO + TL
                cx.dma("sp", pad[:, :, o:o + HALO], hv, r=[f"H{tz}{side}"], w=["halo_up" if tz == 0 else "halo_glu"],
                       key=f"zpad{2 * tz + side}")
        def kv_src(h, piece, par=par):
            if piece == 0:
                return kcs[h], vcs[h]
            j, hf = (piece - 1) // 2, (piece - 1) % 2
            rows = slice(j * 512 + h * 128, j * 512 + (h + 1) * 128)
            return recvK[par][hf][rows, :], recvV[par][hf][rows, :].rearrange("p (t d) -> p t d", d=128)

        if last:
            xo_fn = lambda off, n: outT[:, :, off:off + n]
        else:
            xo_fn = lambda off, n, l=l: xs[l % 2][:, :, off:off + n]
        cx.push(f"B{l}_")
        bg = None
        if not last:
            bg = lambda alloc, bank_fn, l=l: mod_layer_gen(cx, ps, l + 1, silb, w_mod, b_mod_fm, modT, bank_fn, alloc)
        emit_B(cx, consts, ps, x_in, modT[:, l], lamd[:, l:l + 1], qTs, PIECES, kv_src, up_pad, glu_pad, invcnt,
               poolw[l], vecs[l], convw[l], w_out[l], w_ffn_in[l], w_ffn_out[l], xo_fn, last=last, bg=bg)
        cx.pop()
    return cx.finish()


_CACHE = {}


def layer_consts(l):
    return 0.8 - 0.6 * math.exp(-0.3 * l)


def prep_weights(inputs):
    f32 = np.float32
    w = {}
    w["n1g"] = np.stack([fm(inputs["norm1_g"][l], 8) for l in range(DEPTH)])
    w["w_in"] = np.stack([fm_w(np.asarray(inputs["w_in"][l], f32)) for l in range(DEPTH)])
    w["qkg"] = np.stack([np.stack([np.tile(inputs["q_norm_g"][l], 2), np.tile(inputs["k_norm_g"][l], 2)], -1)
                         for l in range(DEPTH)]).astype(f32)
    poolw = np.zeros((DEPTH, 128, 2, 128), f32)
    vecs = np.zeros((DEPTH, 128, NVEC), f32)
    for l in range(DEPTH):
        for ci in range(2):
            poolw[l, 0:64, ci, 0:64] = inputs["pool_w"][l][2 * ci]
            poolw[l, 64:128, ci, 64:128] = inputs["pool_w"][l][2 * ci + 1]
        li = layer_consts(l)
        vecs[l, :, 0:2] = fm(inputs["pool_scale"][l], 2)
        vecs[l, :, 2:4] = fm(inputs["conv_dw_b"][l], 2)
        vecs[l, :, 4:6] = fm(inputs["conv_ln_g"][l], 2)
        vecs[l, :, 6:8] = fm(inputs["conv_ln_b"][l], 2)
        vecs[l, :, 8] = inputs["subln_g"][l]
        vecs[l, :, 10] = li
        vecs[l, :, 11] = 1.0 - li
        vecs[l, :, 12:20] = fm(inputs["norm2_g"][l], 8)
    w["poolw"] = poolw
    w["vecs"] = vecs
    w["convw"] = np.stack([np.asarray(inputs["conv_dw_w"][l], f32).T.reshape(2, 128, 31).transpose(1, 0, 2)
                           for l in range(DEPTH)])
    w["w_out"] = np.stack([fm_w(np.asarray(inputs["w_out"][l], f32)) for l in range(DEPTH)])
    wfi = np.asarray(inputs["w_ffn_in"], f32)
    wg = wfi[:, :, :FF].reshape(DEPTH, 8, 128, NJ, 128)
    wu = wfi[:, :, FF:].reshape(DEPTH, 8, 128, NJ, 128)
    w["w_ffn_in"] = np.ascontiguousarray(np.concatenate([wg, wu], -1).transpose(0, 3, 2, 1, 4))
    w["w_ffn_out"] = np.ascontiguousarray(
        np.asarray(inputs["w_ffn_out"], f32).reshape(DEPTH, NJ, 128, 8, 128).transpose(0, 3, 2, 1, 4))
    return {k: np.ascontiguousarray(v, dtype=f32) for k, v in w.items()}


def make_in_maps(inputs):
    f32 = np.float32
    x = np.asarray(inputs["x"], f32)
    c = np.asarray(inputs["c"], f32)
    ctx = np.asarray(inputs["ctx"], f32)
    c_ctx = np.asarray(inputs["c_ctx"], f32)
    w = prep_weights(inputs)
    lam_in = np.stack([inputs["lambda_q1"], inputs["lambda_k1"], inputs["lambda_q2"], inputs["lambda_k2"]], 1)
    shared = dict(w)
    shared["lam_in"] = np.ascontiguousarray(np.broadcast_to(np.asarray(lam_in, f32)[None], (128, DEPTH, 4, 64)))
    shared["w_mod"] = np.ascontiguousarray(np.asarray(inputs["w_mod"], f32))
    shared["b_mod_fm"] = np.ascontiguousarray(np.asarray(inputs["b_mod"], f32).reshape(DEPTH, 48, 128).transpose(0, 2, 1))
    shared["cmat"] = const_mats()
    in_maps = []
    for i in range(NCORE):
        b, r = i // 4, i % 4
        t0 = r * TL
        m = dict(shared)
        xall = np.concatenate([x[b, t0:t0 + TL], ctx[b]], 0)
        m["x0T"] = np.ascontiguousarray(xall.T.reshape(8, 128, T).transpose(1, 0, 2))
        cvv = np.stack([c[b], c_ctx], -1)
        m["cv"] = np.ascontiguousarray(cvv.reshape(8, 128, 2).transpose(1, 0, 2))
        m["cosT"], m["sinT"] = rope_tables(i)
        m["invcnt"] = invcnt_table(i)
        sel = np.zeros((128, 8), f32)
        if r > 0:
            sel[:, r - 1] = 1.0
        if r < 3:
            sel[:, 4 + r + 1] = 1.0
        m["selT"] = sel
        in_maps.append(m)
    return in_maps


def kernel(**inputs):
    in_maps = make_in_maps(inputs)
    if "nc" not in _CACHE:
        _CACHE["nc"] = build_fused()
    res = run_bass_kernel_spmd(_CACHE["nc"], in_maps, core_ids=list(range(NCORE))).results
    out = np.zeros((2, SEQ, D), np.float32)
    for i in range(NCORE):
        b, t0 = i // 4, (i % 4) * TL
        xo = res[i]["outT"].transpose(1, 0, 2).reshape(D, TL)
        out[b, t0:t0 + TL] = xo.T
    return out
```

```python
import math
from contextlib import ExitStack

import numpy as np
import ml_dtypes

import concourse.bass as bass
import concourse.mybir as mybir
from concourse.bass_utils import run_bass_kernel_spmd

F32 = mybir.dt.float32
BF16 = mybir.dt.bfloat16
AF = mybir.ActivationFunctionType
ALU = mybir.AluOpType
AX = mybir.AxisListType
NPBF = ml_dtypes.bfloat16

D = 1024
DEPTH = 4
NCORE = 8
TL = 2048
TC = 256
T = TL + TC
SEQ = 8192
NKEY = SEQ + TC
NKT = NKEY // 128
FF = 2816
NJ = FF // 128
EPS = 1e-6
HALO = 16
GROUPS = [(0, 512), (512, 512), (1024, 512), (1536, 512), (2048, 256)]


class Sched:
    ENGS = ("pe", "act", "dve", "pool", "sp")

    def __init__(self, nc):
        self.nc = nc
        self.q = {e: [] for e in self.ENGS}
        self.cnt = {}
        self.res = {}
        self.waited = {e: {} for e in self.ENGS}
        self.bar = {}

    def barrier(self):
        self.bar = {k: v for k, v in self.cnt.items() if not k.startswith("C_")}

    def op(self, eng, fn, r=(), w=(), key=None, inc1=False):
        if key is None:
            sem, inc = "S_" + eng, 1
        elif inc1:
            sem, inc = "C_" + key, 1
        else:
            sem, inc = "D_" + key, 16
        deps = dict(self.bar)

        def need(sv):
            s, v = sv
            if deps.get(s, 0) < v:
                deps[s] = v

        for k in r:
            st = self.res.get(k)
            if st and st[0]:
                need(st[0])
            if st and k.startswith("ps"):
                for sv in st[1].items():
                    if sv[0] != sem:
                        need(sv)
        for k in w:
            st = self.res.get(k)
            if st:
                if st[0]:
                    need(st[0])
                for sv in st[1].items():
                    need(sv)
        waits = []
        for s, v in deps.items():
            if eng == "pe" and s == "S_pe":
                continue
            if self.waited[eng].get(s, 0) >= v:
                continue
            self.waited[eng][s] = v
            waits.append((s, v))
        val = self.cnt.get(sem, 0) + inc
        self.cnt[sem] = val
        self.q[eng].append((waits, fn, sem, inc))
        for k in r:
            st = self.res.setdefault(k, [None, {}])
            if st[1].get(sem, 0) < val:
                st[1][sem] = val
        for k in w:
            self.res[k] = [(sem, val), {}]

    def finish(self):
        waits = [(s, v) for s, v in self.cnt.items() if s.startswith("D_") or s.startswith("C_")]
        self.q["sp"].append((waits, None, None, 0))

    def emit(self, es):
        nc = self.nc
        sems = {n: es.enter_context(nc.semaphore(n)) for n in sorted(self.cnt)}
        block = es.enter_context(nc.Block())

        def run(name):
            def f(e):
                for waits, fn, sem, inc in self.q[name]:
                    attach = fn is not None and waits and not sem.startswith("C_")
                    for s, v in (waits[:-1] if attach else waits):
                        e.wait_ge(sems[s], v)
                    if fn is not None:
                        ins = fn(e)
                        if attach:
                            ins._wait_ge(sems[waits[-1][0]], waits[-1][1])
                        ins.then_inc(sems[sem], inc)
            return f

        block.tensor(run("pe"))
        block.scalar(run("act"))
        block.vector(run("dve"))
        block.gpsimd(run("pool"))
        block.sync(run("sp"))


class Ctx:
    def __init__(self):
        self.nc = bass.Bass("TRN2", target_bir_lowering=False)
        self.es = ExitStack()
        self.S = Sched(self.nc)
        self.n = 0
        self.stacks = [self.es]
        self.pfx = ""

    def push(self, pfx=None):
        if pfx is not None:
            self.pfx = pfx
        st = ExitStack()
        self.stacks.append(st)
        return st

    def pop(self):
        self.stacks.pop().close()
        self.S.barrier()

    def sb(self, name, shape, dt):
        return self.stacks[-1].enter_context(self.nc.sbuf_tensor(self.pfx + name, list(shape), dt))

    def din(self, name, shape, dt):
        return self.nc.dram_tensor(name, list(shape), dt, kind="ExternalInput").ap()

    def dout(self, name, shape, dt):
        return self.nc.dram_tensor(name, list(shape), dt, kind="ExternalOutput").ap()

    def dscratch(self, name, shape, dt):
        return self.nc.dram_tensor(name, list(shape), dt, kind="Internal").ap()

    def uid(self, p):
        self.n += 1
        return f"{p}{self.n}"

    def dma(self, q, out, in_, r, w, key, slow=False):
        if slow:
            self.S.op(q, lambda e: e.dma_start(out=out, in_=in_, allow_slow_non_contiguous=True), r=r, w=w, key=key)
        else:
            self.S.op(q, lambda e: e.dma_start(out=out, in_=in_), r=r, w=w, key=key)

    def mm(self, out, lhsT, rhs, start, stop, r, w):
        self.S.op("pe", lambda e: e.matmul(out, lhsT, rhs, start=start, stop=stop), r=r, w=w)

    def act(self, out, in_, func, r, w, bias=None, scale=None):
        kw = {}
        if bias is not None:
            kw["bias"] = bias
        if scale is not None:
            kw["scale"] = scale
        self.S.op("act", lambda e: e.activation(out=out, in_=in_, func=func, **kw), r=r, w=w)

    def tt(self, eng, out, in0, in1, op, r, w):
        self.S.op(eng, lambda e: e.tensor_tensor(out=out, in0=in0, in1=in1, op=op), r=r, w=w)

    def ts(self, eng, out, in0, s1, op0, r, w, s2=None, op1=None):
        if op1 is None:
            self.S.op(eng, lambda e: e.tensor_scalar(out=out, in0=in0, scalar1=s1, scalar2=None, op0=op0),
                      r=r, w=w)
        else:
            self.S.op(eng, lambda e: e.tensor_scalar(out=out, in0=in0, scalar1=s1, scalar2=s2, op0=op0, op1=op1),
                      r=r, w=w)

    def stt(self, out, in0, scalar, in1, op0, op1, r, w):
        self.S.op("dve", lambda e: e.scalar_tensor_tensor(out=out, in0=in0, scalar=scalar, in1=in1,
                                                          op0=op0, op1=op1), r=r, w=w)

    def copy(self, eng, out, in_, r, w):
        self.S.op(eng, lambda e: e.tensor_copy(out=out, in_=in_), r=r, w=w)

    def memset(self, eng, ap, val, w):
        self.S.op(eng, lambda e: e.memset(ap, val), w=w)

    def recip(self, out, in_, r, w):
        self.S.op("dve", lambda e: e.reciprocal(out=out, in_=in_), r=r, w=w)

    def recip_act(self, out, in_, r, w, one=None):
        if one is not None:
            self.act(out, in_, AF.Ln, r=list(r) + ["cmat_f"], w=w, bias=one)
        else:
            self.act(out, in_, AF.Ln, r=r, w=w)
        self.act(out, out, AF.Exp, r=w, w=w, scale=-1.0)

    def finish(self):
        self.S.finish()
        self.S.emit(self.es)
        self.es.close()
        return self.nc


def rstd_from_psum(cx, ps_ap, ps_key, tmp_ap, tmp_key, mhalf_ap, n):
    cx.act(tmp_ap, ps_ap, AF.Ln, r=[ps_key, "consts"], w=[tmp_key], bias=mhalf_ap[:, 0:1])
    cx.act(tmp_ap, tmp_ap, AF.Exp, r=[tmp_key], w=[tmp_key], scale=-0.5)


def emit_mod_pre(cx, cv, lam_in, lam_out, silb):
    cvs = cx.sb("m_cv", [128, 8, 2], F32)
    th = cx.sb("m_th", [128, 8, 2], F32)
    sil = cx.sb("m_sil", [128, 8, 2], F32)
    lamt = cx.sb("m_lamt", [128, 4, 4, 64], F32)
    prod = cx.sb("m_prod", [128, 4, 2, 64], F32)
    lsum = cx.sb("m_lsum", [128, 4, 2], F32)
    lexp = cx.sb("m_lexp", [128, 4, 2], F32)
    lams = cx.sb("m_lams", [128, 4], F32)
    cx.dma("sp", cvs[:], cv, r=[], w=["m_cv"], key="m_cv")
    cx.dma("sp", lamt[:], lam_in, r=[], w=["m_lamt"], key="m_lamt")
    cx.act(th[:], cvs[:], AF.Exp, r=["m_cv"], w=["m_th"], scale=-1.0)
    cx.ts("dve", th[:], th[:], 1.0, ALU.add, r=["m_th"], w=["m_th"])
    cx.recip(th[:], th[:], r=["m_th"], w=["m_th"])
    cx.tt("dve", sil[:], th[:], cvs[:], ALU.mult, r=["m_th", "m_cv"], w=["m_sil"])
    cx.copy("dve", silb[:], sil[:], r=["m_sil"], w=["silb"])
    cx.tt("dve", prod[:, :, 0, :], lamt[:, :, 0, :], lamt[:, :, 1, :], ALU.mult, r=["m_lamt"], w=["m_prod0"])
    cx.tt("dve", prod[:, :, 1, :], lamt[:, :, 2, :], lamt[:, :, 3, :], ALU.mult, r=["m_lamt"], w=["m_prod1"])
    cx.S.op("dve", lambda e: e.tensor_reduce(out=lsum[:], in_=prod[:], axis=AX.X, op=ALU.add),
            r=["m_prod0", "m_prod1"], w=["m_lsum"])
    cx.act(lexp[:], lsum[:], AF.Exp, r=["m_lsum"], w=["m_lexp"])
    cx.tt("dve", lams[:], lexp[:, :, 0], lexp[:, :, 1], ALU.subtract, r=["m_lexp"], w=["m_lams"])
    cx.dma("sp", lam_out, lams[:], r=["m_lams"], w=["lam_out"], key="m_lamo")


def mod_layer_gen(cx, ps, jobs, silb, w_mod, b_mod_fm, modT_out, bank_fn, alloc):
    SW = 384
    slab = [alloc(f"ml_slab{i}", [128, 8, SW], BF16) for i in range(2)]
    modsb = alloc("ml_modsb", [128, 48, 2], F32)
    bfm = alloc("ml_bfm", [128, 48], F32)
    NE = SW // 128
    cnt = 0
    for l, s_lo, s_hi in jobs:
        cx.dma("sp", bfm[:], b_mod_fm[l], r=[], w=["ml_bfm"], key="ml_bfm")
        for sidx in range(s_lo, s_hi):
            sl = slab[cnt % 2]
            sk = f"ml_slab{cnt % 2}"
            cnt += 1
            e0 = sidx * SW
            cx.dma("pool", sl[:], w_mod[l, :, e0:e0 + SW].rearrange("(c p) e -> p c e", p=128), r=[], w=[sk],
                   key=sk)
            yield
            bank, pk, rel = bank_fn()
            for j in range(NE):
                out = ps[:, bank, 2 * j:2 * j + 2]
                for c in range(8):
                    cx.mm(out, sl[:, c, j * 128:(j + 1) * 128], silb[:, c, :], start=(c == 0), stop=(c == 7),
                          r=[sk, "silb"], w=[pk])
            cx.tt("dve", modsb[:, sidx * NE:(sidx + 1) * NE, :],
                  ps[:, bank, 0:2 * NE].rearrange("p (j t) -> p j t", t=2),
                  bfm[:, sidx * NE:(sidx + 1) * NE, None].to_broadcast([128, NE, 2]), ALU.add,
                  r=[pk, "ml_bfm"], w=["ml_modsb"])
            rel()
            yield
        cx.dma("sp", modT_out[:, l, s_lo * NE:s_hi * NE, :], modsb[:, s_lo * NE:s_hi * NE, :], r=["ml_modsb"],
               w=[cx.uid("modT")], key="ml_modo")
        yield


class Rot:
    def __init__(self, cx, name, shape, dt, n, alloc=None):
        alloc = alloc or cx.sb
        self.bufs = [alloc(f"{name}{i}", shape, dt) for i in range(n)]
        self.keys = [f"{name}{i}" for i in range(n)]
        self.i = 0

    def next(self):
        i = self.i % len(self.bufs)
        self.i += 1
        return self.bufs[i], self.keys[i]


class BankRot:
    def __init__(self, banks, held=None):
        self.banks = banks
        self.held = set() if held is None else held
        self.i = 0

    def next(self):
        for _ in range(len(self.banks)):
            b = self.banks[self.i % len(self.banks)]
            self.i += 1
            if b not in self.held:
                self.held.add(b)
                return b, f"ps{b}"
        raise RuntimeError(f"no free PSUM bank among {self.banks}")

    def release(self, b):
        self.held.discard(b)


def load_consts(cx, cmat_d, nm=6):
    cf = cx.sb("cmat_f", [128, nm, 128], F32)
    cb = cx.sb("cmat_b", [128, nm, 128], BF16)
    mh = cx.sb("epsb", [128, 1024], F32)
    cx.dma("sp", cf[:], cmat_d, r=[], w=["cmat_f"], key="cmat_f")
    cx.copy("dve", cb[:], cf[:], r=["cmat_f"], w=["consts"])
    cx.memset("dve", mh[:], EPS, w=["consts"])
    return cf, cb, mh


def emit_norm_mod(*a, **k):
    for _ in norm_mod_gen(*a, **k):
        pass


def norm_mod_gen(cx, g, xg, xgk, hT, hTk, sq, u, rs_rot, onesD, mhalf, ps, auxb, Gs, shs):
    off, n = GROUPS[g]
    col = 1 if off >= TL else 0
    cx.act(sq[:, :, :n], xg[:, :, :n], AF.Square, r=[xgk], w=["sq"])
    yield
    b, bk = auxb.next()
    for c in range(8):
        cx.mm(ps[:, b, :n], onesD, sq[:, c, :n], start=(c == 0), stop=(c == 7), r=["sq", "consts"], w=[bk])
    rs, rsk = rs_rot.next()
    rstd_from_psum(cx, ps[:, b, :n], bk, rs[:, :n], rsk, mhalf[:, :n], n)
    auxb.release(b)
    yield
    uk = "u"
    if u is None:
        u, uk = xg, xgk
    cx.tt("dve", u[:, :, :n], xg[:, :, :n], rs[:, None, :n].to_broadcast([128, 8, n]), ALU.mult,
          r=[xgk, rsk], w=[uk])
    yield
    yield
    for c in range(8):
        if c % 2 == 0:
            cx.act(hT[:, c, :n], u[:, c, :n], AF.Identity, r=[uk, "modc"], w=[hTk],
                   bias=shs[:, c, col:col + 1], scale=Gs[:, c, col:col + 1])
        else:
            cx.ts("dve", hT[:, c, :n], u[:, c, :n], Gs[:, c, col:col + 1], ALU.mult, r=[uk, "modc"], w=[hTk],
                  s2=shs[:, c, col:col + 1], op1=ALU.add)


def emit_A(cx, consts, ps, xT, mod, n1g, w_in, qkg, cosT, sinT, sinks):
    nc = cx.nc
    S = cx.S
    cf, cb, mhalf = consts
    onesD, ones64, pswap = cb[:, 0, :], cb[:, 1, :], cb[:, 2, :]
    mainb = BankRot([0, 1, 2, 3, 4])
    auxb = BankRot([5, 6, 7], held=mainb.held)
    allb = BankRot([0, 1, 2, 3, 4, 5, 6, 7], held=mainb.held)

    w_sb = cx.sb("a_w", [128, 8, 2304], BF16)
    for wk, c0, c1 in (("a_wk", 512, 1024), ("a_wv", 1024, 1536), ("a_wp", 1536, 2304), ("a_wq", 0, 512)):
        cx.dma("pool", w_sb[:, :, c0:c1], w_in[:, :, c0:c1], r=[], w=[wk], key=wk)
    mods = cx.sb("a_mod", [128, 48, 2], F32)
    n1gs = cx.sb("a_n1g", [128, 8], F32)
    qkgs = cx.sb("a_qkg", [128, 2], F32)
    cos_s = cx.sb("a_cos", [128, TL], F32)
    sin_s = cx.sb("a_sin", [128, TL], F32)
    cx.dma("sp", mods[:], mod, r=[], w=["a_mod"], key="a_mod")
    cx.dma("sp", n1gs[:], n1g, r=[], w=["a_n1g"], key="a_n1g")
    cx.dma("sp", qkgs[:], qkg, r=[], w=["a_qkg"], key="a_qkg")
    Gs = cx.sb("a_G", [128, 8, 2], F32)
    cx.stt(Gs[:], mods[:, 8:16, :], 1.0, n1gs[:, :, None].to_broadcast([128, 8, 2]), ALU.add, ALU.mult,
           r=["a_mod", "a_n1g"], w=["modc"])
    shs = mods[:, 0:8, :]

    xg_rot = Rot(cx, "a_xg", [128, 8, 512], F32, 2)
    sq = cx.sb("a_sq", [128, 8, 512], BF16)
    u = None
    rs_rot = Rot(cx, "a_rs", [128, 512], F32, 2)
    sq2_rot = Rot(cx, "a_sq2", [128, 512], BF16, 2)
    r2_rot = Rot(cx, "a_r2", [128, 512], F32, 2)
    qn_rot = Rot(cx, "a_qn", [128, 512], F32, 3)
    hi_rot = Rot(cx, "a_hi", [128, 512], BF16, 2)
    lo_rot = Rot(cx, "a_lo", [128, 512], BF16, 2)
    t1_rot = Rot(cx, "a_t1", [128, 512], F32, 2)
    t2_rot = Rot(cx, "a_t2", [128, 512], F32, 2)
    qo_rot = Rot(cx, "a_qo", [128, 512], BF16, 8)
    po_rot = Rot(cx, "a_po", [128, 512], F32, 4)
    th_rot = Rot(cx, "a_th", [128, 512], F32, 4)
    gl_rot = Rot(cx, "a_gl", [128, 512], F32, 4)
    vo_rot = Rot(cx, "a_vo", [128, 512], BF16, 4)

    pending = []

    def advance():
        for gen in list(pending):
            try:
                next(gen)
            except StopIteration:
                pending.remove(gen)

    def load_x(g):
        off, n = GROUPS[g]
        xg, xgk = xg_rot.next()
        cx.dma("sp", xg[:, :, :n], xT[:, :, off:off + n], r=[], w=[xgk], key=xgk)
        return xg, xgk

    def qk_post(g, which, h, b, bk):
        off, n = GROUPS[g]
        latent = off < TL
        sq2, sq2k = sq2_rot.next()
        cx.act(sq2[:, :n], ps[:, b, :n], AF.Square, r=[bk], w=[sq2k])
        yield
        b2, b2k = auxb.next()
        cx.mm(ps[:, b2, :n], ones64, sq2[:, :n], start=True, stop=True, r=[sq2k, "consts"], w=[b2k])
        yield
        r2, r2k = r2_rot.next()
        rstd_from_psum(cx, ps[:, b2, :n], b2k, r2[:, :n], r2k, mhalf[:, :n], n)
        auxb.release(b2)
        qn, qnk = qn_rot.next()
        cx.stt(qn[:, :n], ps[:, b, :n], qkgs[:, which:which + 1], r2[:, :n], ALU.mult, ALU.mult,
               r=[bk, r2k, "a_qkg"], w=[qnk])
        mainb.release(b)
        dst, dres = sinks["q" if which == 0 else "k"](h, off, n)
        dkey = f"{'qT' if which == 0 else 'kT'}_o"
        qo, qok = qo_rot.next()
        if not latent:
            cx.act(qo[:, :n], qn[:, :n], AF.Copy, r=[qnk], w=[qok])
            yield
            cx.dma("sp", dst, qo[:, :n], r=[qok], w=[dres or cx.uid(dkey)], key=qok)
            return
        hi, hik = hi_rot.next()
        lo, lok = lo_rot.next()
        cx.act(hi[:, :n], qn[:, :n], AF.Copy, r=[qnk], w=[hik])
        cx.tt("dve", lo[:, :n], qn[:, :n], hi[:, :n], ALU.subtract, r=[qnk, hik], w=[lok])
        yield
        b3, b3k = auxb.next()
        cx.mm(ps[:, b3, :n], pswap, hi[:, :n], start=True, stop=False, r=[hik, "consts"], w=[b3k])
        cx.mm(ps[:, b3, :n], pswap, lo[:, :n], start=False, stop=True, r=[lok, "consts"], w=[b3k])
        t1, t1k = t1_rot.next()
        cx.tt("dve", t1[:, :n], qn[:, :n], cos_s[:, off:off + n], ALU.mult, r=[qnk, "a_cos"], w=[t1k])
        yield
        t2, t2k = t2_rot.next()
        cx.tt("dve", t2[:, :n], ps[:, b3, :n], sin_s[:, off:off + n], ALU.mult, r=[b3k, "a_sin"], w=[t2k])
        auxb.release(b3)
        cx.tt("dve", qo[:, :n], t1[:, :n], t2[:, :n], ALU.add, r=[t1k, t2k], w=[qok])
        yield
        cx.dma("sp", dst, qo[:, :n], r=[qok], w=[dres or cx.uid(dkey)], key=qok)

    def pool_post(g, ci, b, bk):
        off, n = GROUPS[g]
        po, pok = po_rot.next()
        cx.act(po[:, :n], ps[:, b, :n], AF.Copy, r=[bk], w=[pok])
        mainb.release(b)
        yield
        cx.dma("sp", sinks["up"](ci, off, n), po[:, :n], r=[pok], w=[cx.uid("up_o")], key=pok)
        for side, sl in ((0, slice(0, HALO)), (1, slice(n - HALO, n))):
            e_ap = sinks["edge"](0, ci, side, off)
            if e_ap is not None:
                cx.dma("sp", e_ap, po[:, sl], r=[pok], w=["sendE"], key=f"edge0{ci}{side}")

    def glu_post(g, ci, ba, bak, bb, bbk):
        off, n = GROUPS[g]
        th, thk = th_rot.next()
        cx.act(th[:, :n], ps[:, bb, :n], AF.Exp, r=[bbk], w=[thk], scale=-1.0)
        mainb.release(bb)
        yield
        gl, glk = gl_rot.next()
        cx.recip_act(th[:, :n], th[:, :n], r=[thk], w=[thk], one=cf[:, 4, 0:1])
        cx.tt("dve", gl[:, :n], th[:, :n], ps[:, ba, :n], ALU.mult, r=[thk, bak], w=[glk])
        mainb.release(ba)
        yield
        cx.dma("sp", sinks["glu"](ci, off, n), gl[:, :n], r=[glk], w=[cx.uid("glu_o")], key=glk)
        for side, sl in ((0, slice(0, HALO)), (1, slice(n - HALO, n))):
            e_ap = sinks["edge"](1, ci, side, off)
            if e_ap is not None:
                cx.dma("sp", e_ap, gl[:, sl], r=[glk], w=["sendE"], key=f"edge1{ci}{side}")

    def v_post(g, ti, b, bk):
        off, n = GROUPS[g]
        vo, vok = vo_rot.next()
        cx.copy("dve", vo[:], ps[:, b, :], r=[bk], w=[vok])
        mainb.release(b)
        yield
        vdst, vres = sinks["v"](off // 128 + ti)
        cx.dma("sp", vdst, vo[:].rearrange("p (h d) -> p h d", h=4), r=[vok],
               w=[vres or cx.uid("v_o")], key=vok)

    ng = len(GROUPS)
    hT_all = cx.sb("a_hTall", [128, 8, T], BF16)

    def hT_of(g):
        off, n = GROUPS[g]
        return hT_all[:, :, off:off + n], f"a_hT{g}"

    def run_chunks(g, chunks, hook=None, alloc=None, per_chunk=None):
        alloc = alloc or mainb
        off, n = GROUPS[g]
        hT, hTk = hT_of(g)
        ca_banks = {}
        for ci, (kind, idx, co, wk) in enumerate(chunks):
            b, bk = alloc.next()
            if kind == "v":
                for c in range(8):
                    cx.mm(ps[:, b, :], hT[:, c, idx * 128:(idx + 1) * 128], w_sb[:, c, 1024:1536],
                          start=(c == 0), stop=(c == 7), r=[hTk, wk], w=[bk])
            else:
                for c in range(8):
                    cx.mm(ps[:, b, :n], w_sb[:, c, co:co + 128], hT[:, c, :n],
                          start=(c == 0), stop=(c == 7), r=[hTk, wk], w=[bk])
            advance()
            if kind == "q":
                pending.append(qk_post(g, 0, idx, b, bk))
            elif kind == "k":
                pending.append(qk_post(g, 1, idx, b, bk))
            elif kind == "pool":
                pending.append(pool_post(g, idx, b, bk))
            elif kind == "ca":
                ca_banks[idx] = (b, bk)
            elif kind == "cb":
                ba, bak = ca_banks[idx]
                pending.append(glu_post(g, idx, ba, bak, b, bk))
            elif kind == "v":
                pending.append(v_post(g, idx, b, bk))
            if hook is not None and ci == 3:
                hook()
            if per_chunk is not None:
                per_chunk(ci)

    def drain():
        while pending:
            advance()

    def norm_group(g, xgx):
        xg, xgk = xgx
        hT, hTk = hT_of(g)
        emit_norm_mod(cx, g, xg, xgk, hT, hTk, sq, u, rs_rot, onesD, mhalf, ps, auxb, Gs, shs)

    cc = sinks.get("cc", lambda name: None)
    nxt_x = load_x(0)
    norm_group(0, nxt_x)
    nxt_x = load_x(1)
    cx.dma("sp", cos_s[:], cosT, r=[], w=["a_cos"], key="a_cos")
    cx.dma("sp", sin_s[:], sinT, r=[], w=["a_sin"], key="a_sin")
    for g in range(ng):
        off, n = GROUPS[g]
        chunks = [("k", h, 512 + h * 128, "a_wk") for h in range(4)]
        chunks += [("v", ti, 1024, "a_wv") for ti in range(n // 128)]

        ngen = [None]

        def per_chunk(ci, g=g, ngen=ngen):
            nonlocal nxt_x
            if ci == 0 and g + 1 < ng:
                xg1, xg1k = nxt_x
                hT1, hT1k = hT_of(g + 1)
                ngen[0] = norm_mod_gen(cx, g + 1, xg1, xg1k, hT1, hT1k, sq, u, rs_rot, onesD, mhalf, ps, auxb,
                                       Gs, shs)
                if g + 2 < ng:
                    nxt_x = load_x(g + 2)
            if ngen[0] is not None:
                try:
                    next(ngen[0])
                except StopIteration:
                    ngen[0] = None
            if ci == 3 and g in (2, 4):
                cc("K%d" % (g // 2 - 1))
                cc("V%d" % (g // 2 - 1))

        run_chunks(g, chunks, None, per_chunk=per_chunk)
        while ngen[0] is not None:
            per_chunk(-1)
    for g in range(ng):
        chunks = [("pool", i, 1536 + i * 128, "a_wp") for i in range(2)]
        chunks += [("ca", i, 1792 + i * 128, "a_wp") for i in range(2)]
        chunks += [("cb", i, 2048 + i * 128, "a_wp") for i in range(2)]
        run_chunks(g, chunks, (lambda: cc("E")) if g == 4 else None, alloc=allb)
    for g in range(ng):
        run_chunks(g, [("q", h, h * 128, "a_wq") for h in range(4)])
    drain()


def fm(v, nch):
    return np.ascontiguousarray(np.asarray(v, np.float32).reshape(nch, 128).T)


def fm_w(w):
    k, e = w.shape
    return np.ascontiguousarray(w.reshape(k // 128, 128, e).transpose(1, 0, 2))


def rope_tables(core):
    t = (core % 4) * TL + np.arange(TL)
    row = (t // 64).astype(np.float64)
    col = (t % 64).astype(np.float64)
    inv_freq = 10000.0 ** (-np.arange(0, 32, 2, dtype=np.float64) / 32)
    ang = np.concatenate([row[:, None] * inv_freq, col[:, None] * inv_freq], -1)
    p = np.arange(128)
    pair = (p % 64) // 2
    cosT = np.cos(ang)[:, pair].T
    sgn = np.where(p % 2 == 0, -1.0, 1.0)[:, None]
    sinT = np.sin(ang)[:, pair].T * sgn
    return np.ascontiguousarray(cosT, np.float32), np.ascontiguousarray(sinT, np.float32)


def const_mats():
    m = np.zeros((128, 6, 128), np.float32)
    m[:, 0, :] = 1.0 / 1024
    m[0:64, 1, 0:64] = 1.0 / 64
    m[64:128, 1, 64:128] = 1.0 / 64
    idx = np.arange(128)
    m[idx ^ 1, 2, idx] = 1.0
    m[:, 3, :] = 1.0 / 128
    m[:, 4, :] = 1.0
    m[:, 5, :] = 1.0 / 256
    return m


NVEC = 24
TPL = TL + 2 * HALO
TPC = TC + 2 * HALO
TP = TPL + TPC
KPIECES = 6
KTP = NKT // KPIECES


class SAlloc:
    def __init__(self):
        self.held = set()
        self.i = 0
        self.j = 0

    def pair(self):
        for _ in range(2):
            p = self.i % 2
            self.i += 1
            if (2 * p) not in self.held and (2 * p + 1) not in self.held:
                return 2 * p
        raise RuntimeError("no free score pair")

    def one(self):
        for _ in range(2):
            b = self.j % 2
            self.j += 1
            if b not in self.held:
                self.held.add(b)
                return b, f"ps{b}"
        raise RuntimeError("no free stats bank")

    def release(self, b):
        self.held.discard(b)


def emit_B(cx, consts, ps, xT, mod, lam_ap, qT, pieces, kv_src, up_pad, glu_pad, invcnt, poolw, vecs, convw,
           w_out, w_ffn_in, w_ffn_out, xT_o_fn, last=False, bg=None):
    nc = cx.nc
    S = cx.S
    cf, cb, mhalf = consts
    onesD, ones128n, ones1, = cb[:, 0, :], cb[:, 3, :], cb[:, 4, :]
    ones256f = cf[:, 5, :]
    sa = SAlloc()
    groups_run = [g for g in range(len(GROUPS)) if not (last and GROUPS[g][0] >= TL)]
    piece_of = {}
    for pi, (k0, kc) in enumerate(pieces):
        for kt in range(k0, k0 + kc):
            piece_of[kt] = pi

    mods = cx.sb("b_mod", [128, 48, 2], F32)
    vec = cx.sb("b_vec", [128, NVEC], F32)
    cw = cx.sb("b_cw", [128, 2, 31], F32)
    pwf = cx.sb("b_pwf", [128, 2, 128], F32)
    pwb = cx.sb("b_pwb", [128, 2, 128], BF16)
    cx.dma("sp", vec[:], vecs, r=[], w=["b_vec"], key="b_vec")
    cx.dma("sp", cw[:], convw, r=[], w=["b_cw"], key="b_cw")
    cx.dma("sp", pwf[:], poolw, r=[], w=["b_pwf"], key="b_pwf")
    lamt = cx.sb("b_lamt", [128, 1], F32)
    cx.dma("sp", lamt[:], lam_ap, r=[], w=["b_lamt"], key="b_lamt", slow=True)
    cx.copy("dve", pwb[:], pwf[:], r=["b_pwf"], w=["b_pwb"])
    der = cx.sb("b_der", [128, 8], F32)
    cx.tt("dve", der[:, 0:1], lamt[:, 0:1], vec[:, 10:11], ALU.add, r=["b_vec", "b_lamt"], w=["b_der0"])
    cx.ts("dve", der[:, 0:1], der[:, 0:1], -1.0, ALU.mult, r=["b_der0"], w=["b_der0"])
    cx.tt("dve", der[:, 1:2], vec[:, 8:9], vec[:, 11:12], ALU.mult, r=["b_vec"], w=["b_der1"])
    cx.copy("dve", der[:, 2:6], vec[:, 4:8], r=["b_vec"], w=["b_der2"])
    derk = ["b_der0", "b_der1", "b_der2"]
    G2 = cx.sb("b_G2", [128, 8, 2], F32)
    g2h = cx.sb("b_g2h", [128, 8, 2], F32)
    sh2 = mods[:, 24:32, :]
    g1 = mods[:, 16:24, :]

    mixT = cx.sb("b_mix", [128, 8, T], BF16)

    cx.push()
    sb1 = cx.sb

    NP = 512 + 2 * HALO
    U = sb1("p_U", [128, 2, NP], F32)
    IC = sb1("p_IC", [128, 2, 512], F32)
    Sa = sb1("p_Sa", [128, 2, NP], F32)
    Sb = sb1("p_Sb", [128, 2, NP], F32)
    Sc = sb1("p_Sc", [128, 2, NP], F32)
    Sd = sb1("p_Sd", [128, 2, NP], F32)
    ptmp = sb1("p_tmp", [128, 2, 512], F32)
    pin = sb1("p_in", [128, 2, 512], BF16)
    G = sb1("c_G", [128, 2, NP], F32)
    acc = sb1("c_acc", [128, 2, 512], F32)
    sqa = sb1("c_sqa", [128, 2, 512], F32)
    msb = sb1("c_msb", [128, 512], F32)
    m2 = sb1("c_m2", [128, 512], F32)
    vr = sb1("c_vr", [128, 512], F32)
    dd = sb1("c_dd", [128, 2, 512], F32)
    zh = sb1("c_zh", [128, 2, 512], F32)
    cth = sb1("c_th", [128, 2, 512], F32)

    def b1_gen():
        for g in groups_run:
            off, n = GROUPS[g]
            po = off if off < TL else TPL + (off - TL)
            m = n + 2 * HALO
            cx.dma("sp", U[:, :, :m], up_pad[:, :, po:po + m], r=["halo_up"], w=["p_U"], key="p_U")
            cx.dma("sp", IC[:, :, :n], invcnt[:, :, off:off + n], r=[], w=["p_IC"], key="p_IC")
            yield
            cx.tt("dve", Sa[:, :, 1:m], U[:, :, 0:m - 1], U[:, :, 1:m], ALU.add, r=["p_U"], w=["p_Sa"])
            yield
            cx.tt("dve", Sb[:, :, 2:m - 1], Sa[:, :, 1:m - 2], Sa[:, :, 3:m], ALU.add, r=["p_Sa"], w=["p_Sb"])
            yield
            cx.tt("dve", Sc[:, 1, 4:m - 3], Sb[:, 1, 2:m - 5], Sb[:, 1, 6:m - 1], ALU.add, r=["p_Sb"], w=["p_Sc"])
            yield
            cx.tt("dve", Sd[64:128, 1, 8:m - 7], Sc[64:128, 1, 4:m - 11], Sc[64:128, 1, 12:m - 3], ALU.add,
                  r=["p_Sc"], w=["p_Sd"])
            yield
            srcs = [(Sa, "p_Sa", 0, 0), (Sb, "p_Sb", 64, 0), (Sc, "p_Sc", 0, 1), (Sd, "p_Sd", 64, 1)]
            for sbuf, sk, p0, ci in srcs:
                cx.tt("dve", ptmp[p0:p0 + 64, ci, :n], sbuf[p0:p0 + 64, ci, HALO:HALO + n],
                      IC[p0:p0 + 64, ci, :n], ALU.mult, r=[sk, "p_IC"], w=[f"p_tmp{p0}{ci}"])
                cx.tt("dve", pin[p0:p0 + 64, ci, :n], ptmp[p0:p0 + 64, ci, :n],
                      U[p0:p0 + 64, ci, HALO:HALO + n], ALU.subtract, r=[f"p_tmp{p0}{ci}", "p_U"],
                      w=[f"p_in{p0}{ci}"])
                yield
            pk = [f"p_in{p0}{ci}" for _, _, p0, ci in srcs]
            for ci in range(2):
                b, bk = sa.one()
                cx.mm(ps[:, b, :n], pwb[:, ci, :], pin[:, ci, :n], start=True, stop=True,
                      r=pk + ["b_pwb"], w=[bk])
                cx.ts("dve", mixT[:, 4 + ci, off:off + n], ps[:, b, :n], vec[:, ci:ci + 1], ALU.mult,
                      r=[bk, "b_vec"], w=[f"mix{4 + ci}_{g}"])
                sa.release(b)
                yield
            cx.dma("sp", G[:, :, :m], glu_pad[:, :, po:po + m], r=["halo_glu"], w=["c_G"], key="c_G")
            yield
            for ci in range(2):
                cx.ts("dve", acc[:, ci, :n], G[:, ci, 1:1 + n], cw[:, ci, 0:1], ALU.mult,
                      r=["c_G", "b_cw", "b_vec"], w=[f"c_acc{ci}"], s2=vec[:, 2 + ci:3 + ci], op1=ALU.add)
                yield
                for k in range(1, 31):
                    cx.stt(acc[:, ci, :n], G[:, ci, 1 + k:1 + k + n], cw[:, ci, k:k + 1], acc[:, ci, :n],
                           ALU.mult, ALU.add, r=["c_G", "b_cw", f"c_acc{ci}"], w=[f"c_acc{ci}"])
                    yield
            cx.tt("dve", sqa[:, :, :n], acc[:, :, :n], acc[:, :, :n], ALU.mult, r=["c_acc0", "c_acc1"], w=["c_sqa"])
            yield
            b1_, b1k = sa.one()
            for ci in range(2):
                cx.mm(ps[:, b1_, :n], ones256f, acc[:, ci, :n], start=(ci == 0), stop=(ci == 1),
                      r=["c_acc0", "c_acc1", "cmat_f"], w=[b1k])
            cx.copy("dve", msb[:, :n], ps[:, b1_, :n], r=[b1k], w=["c_msb"])
            sa.release(b1_)
            cx.tt("dve", m2[:, :n], msb[:, :n], msb[:, :n], ALU.mult, r=["c_msb"], w=["c_m2"])
            yield
            b2_, b2k = sa.one()
            for ci in range(2):
                cx.mm(ps[:, b2_, :n], ones256f, sqa[:, ci, :n], start=(ci == 0), stop=(ci == 1),
                      r=["c_sqa", "cmat_f"], w=[b2k])
            cx.stt(vr[:, :n], ps[:, b2_, :n], EPS, m2[:, :n], ALU.add, ALU.subtract, r=[b2k, "c_m2"], w=["c_vr"])
            sa.release(b2_)
            cx.act(vr[:, :n], vr[:, :n], AF.Ln, r=["c_vr"], w=["c_vr"])
            cx.act(vr[:, :n], vr[:, :n], AF.Exp, r=["c_vr"], w=["c_vr"], scale=-0.5)
            yield
            cx.tt("dve", dd[:, :, :n], acc[:, :, :n], msb[:, None, :n].to_broadcast([128, 2, n]), ALU.subtract,
                  r=["c_acc0", "c_acc1", "c_msb"], w=["c_dd"])
            cx.tt("dve", dd[:, :, :n], dd[:, :, :n], vr[:, None, :n].to_broadcast([128, 2, n]), ALU.mult,
                  r=["c_dd", "c_vr"], w=["c_dd"])
            yield
            for ci in range(2):
                cx.ts("dve", zh[:, ci, :n], dd[:, ci, :n], der[:, 2 + ci:3 + ci], ALU.mult,
                      r=["c_dd", "b_der2"], w=[f"c_zh{ci}"], s2=der[:, 4 + ci:5 + ci], op1=ALU.add)
            yield
            cx.act(cth[:, :, :n], zh[:, :, :n], AF.Exp, r=["c_zh0", "c_zh1"], w=["c_th"], scale=-1.0)
            yield
            cx.recip_act(cth[:, :, :n], cth[:, :, :n], r=["c_th"], w=["c_th"], one=cf[:, 4, 0:1])
            yield
            cx.tt("dve", mixT[:, 6:8, off:off + n], cth[:, :, :n], zh[:, :, :n], ALU.mult,
                  r=["c_th", "c_zh0", "c_zh1"], w=[f"mix6_{g}", f"mix7_{g}"])
            yield

    b1 = b1_gen()
    b1_done = [False]

    def bg_bank():
        b, bk = sa.one()
        return b, bk, (lambda: sa.release(b))

    bg_gen = bg(sb1, bg_bank) if bg is not None else None
    bg_done = [bg_gen is None]

    def bg_step():
        if bg_done[0]:
            return
        try:
            next(bg_gen)
        except StopIteration:
            bg_done[0] = True

    def b1_step():
        if b1_done[0]:
            return
        try:
            next(b1)
        except StopIteration:
            b1_done[0] = True

    Kh = sb1("a_K", [128, NKEY], BF16)
    Vh = sb1("a_V", [128, NKT, 128], BF16)
    qh_rot = [sb1(f"a_q{i}", [128, T], BF16) for i in range(2)]
    NPT = 6
    Pt = [sb1(f"a_P{i}", [128, 2, 512], BF16) for i in range(NPT)]
    rD = sb1("a_rD", [128, 2, 512], F32)
    oo = sb1("a_oo", [128, 2, 512], F32)
    pacc_rot = Rot(cx, "a_Pacc", [128, 2, 512], BF16, 2, alloc=sb1)
    QD = 8
    o_rot = Rot(cx, "a_o", [128, 512], F32, 2, alloc=sb1)
    osq_rot = Rot(cx, "a_osq", [128, 512], BF16, 2, alloc=sb1)
    r3_rot = Rot(cx, "a_r3", [128, 512], F32, 2, alloc=sb1)

    def kv_dep(piece):
        if piece == 0:
            return [], []
        hf = (piece - 1) // 4
        return [f"recv_sendK{hf}"], [f"recv_sendV{hf}"]

    def load_kv(h, piece):
        k0, kc = pieces[piece]
        ksrc, vsrc = kv_src(h, piece)
        rk, rv = kv_dep(piece)
        cx.dma("sp", Kh[:, k0 * 128:(k0 + kc) * 128], ksrc, r=rk, w=[f"K{piece}"], key=f"K{piece}")
        cx.dma("sp", Vh[:, k0:k0 + kc, :], vsrc, r=rv, w=[f"V{piece}"], key=f"V{piece}")

    def load_q(h):
        cx.dma("sp", qh_rot[h % 2][:], qT[h], r=[], w=[f"a_q{h % 2}"], key=f"a_q{h % 2}")

    load_q(0)
    for piece in range(len(pieces)):
        load_kv(0, piece)

    gorder = [g for g in [4, 0, 1, 2, 3] if g in groups_run]
    post_pending = []
    it = 0
    for h in range(4):
        qh = qh_rot[h % 2]
        qk = f"a_q{h % 2}"
        if h + 1 < 4:
            load_q(h + 1)
        for gi, g in enumerate(gorder):
            off, n = GROUPS[g]
            kts = list(range(2)) if off >= TL else list(range(NKT))
            last_group = (gi == len(gorder) - 1)
            pvq = []
            quad = []
            dq = []
            dstate = {"first": True, "acc": None}

            def flush_d(final, n=n, dq=dq, dstate=dstate):
                while dq:
                    src, srck = dq.pop(0)
                    lastd = final and not dq
                    for c in range(2):
                        cx.mm(ps[:, 6 + c, :n], ones1, src[:, c, :n], start=dstate["first"], stop=lastd,
                              r=["consts", srck], w=[f"ps{6 + c}"])
                    dstate["first"] = False

            def emit_pv(prev, final, n=n, quad=quad, dq=dq, dstate=dstate):
                pkt, ppt, pptk, pidx = prev
                pp = piece_of[pkt]
                flush_d(False)
                for c in range(2):
                    cx.mm(ps[:, 4 + c, :n], Vh[:, pkt, :], ppt[:, c, :n], start=(pidx == 0), stop=final,
                          r=[f"V{pp}", pptk], w=[f"ps{4 + c}"])
                quad.append((ppt, pptk))
                if len(quad) == 2:
                    dstate["acc"] = pacc_rot.next()
                    acc, acck = dstate["acc"]
                    cx.tt("dve", acc[:, :, :n], quad[0][0][:, :, :n], ppt[:, :, :n], ALU.add,
                          r=[quad[0][1], pptk], w=[acck])
                elif len(quad) > 2:
                    acc, acck = dstate["acc"]
                    cx.tt("dve", acc[:, :, :n], acc[:, :, :n], ppt[:, :, :n], ALU.add, r=[acck, pptk], w=[acck])
                if len(quad) == QD or final:
                    dq.append(dstate["acc"] if len(quad) > 1 else (ppt, pptk))
                    quad.clear()
                if final:
                    flush_d(True)

            for idx, kt in enumerate(kts):
                piece = piece_of[kt]
                b0 = sa.pair()
                for c in range(2):
                    cx.mm(ps[:, b0 + c, :n], Kh[64 * c:64 * c + 64, kt * 128:(kt + 1) * 128],
                          qh[64 * c:64 * c + 64, off:off + n], start=True, stop=True,
                          r=[f"K{piece}", qk], w=[f"ps{b0 + c}"])
                pt = Pt[it % NPT]
                ptk = f"a_P{it % NPT}"
                cx.act(pt[:, :, :n], ps[:, b0:b0 + 2, :n], AF.Exp, r=[f"ps{b0}", f"ps{b0 + 1}"], w=[ptk],
                       scale=0.125)
                it += 1
                pvq.append((kt, pt, ptk, idx))
                if len(pvq) > 2:
                    prev = pvq.pop(0)
                    emit_pv(prev, False)
                    pkt = prev[0]
                    pp = piece_of[pkt]
                    if last_group and h + 1 < 4 and pkt == pieces[pp][0] + pieces[pp][1] - 1 \
                            and pp != len(pieces) - 1:
                        load_kv(h + 1, pp)
                if it % 2 == 0:
                    b1_step()
                if it % 5 == 2:
                    bg_step()
                if idx in (1, 5) and post_pending:
                    for gen in list(post_pending):
                        try:
                            next(gen)
                        except StopIteration:
                            post_pending.remove(gen)
            while pvq:
                prev = pvq.pop(0)
                emit_pv(prev, not pvq)
            if last_group and h + 1 < 4:
                load_kv(h + 1, len(pieces) - 1)
            for gen in list(post_pending):
                for _ in gen:
                    pass
                post_pending.remove(gen)
            cx.copy("dve", oo[:, :, :n], ps[:, 4:6, :n], r=["ps4", "ps5"], w=["a_oo"])
            o, ok = o_rot.next()
            osq, osqk = osq_rot.next()

            def post(h=h, off=off, n=n, o=o, ok=ok, osq=osq, osqk=osqk):
                cx.recip_act(rD[:, :, :n], ps[:, 6:8, :n], r=["ps6", "ps7"], w=["a_rD"])
                cx.tt("dve", oo[:, :, :n], oo[:, :, :n], rD[:, :, :n], ALU.mult, r=["a_oo", "a_rD"], w=["a_oo"])
                cx.stt(o[:, :n], oo[:, 1, :n], der[:, 0:1], oo[:, 0, :n], ALU.mult, ALU.add,
                       r=["a_oo", "b_der0"], w=[ok])
                cx.tt("dve", osq[:, :n], o[:, :n], o[:, :n], ALU.mult, r=[ok], w=[osqk])
                yield
                b, bk = sa.one()
                cx.mm(ps[:, b, :n], ones128n, osq[:, :n], start=True, stop=True, r=[osqk, "consts"], w=[bk])
                r3, r3k = r3_rot.next()
                rstd_from_psum(cx, ps[:, b, :n], bk, r3[:, :n], r3k, mhalf[:, :n], n)
                sa.release(b)
                cx.stt(mixT[:, h, off:off + n], o[:, :n], der[:, 1:2], r3[:, :n], ALU.mult, ALU.mult,
                       r=[ok, r3k, "b_der1"], w=[f"mix{h}_{off}"])
                yield

            post_pending.append(post())
    for gen in list(post_pending):
        for _ in gen:
            pass
    while not b1_done[0]:
        b1_step()
    while not bg_done[0]:
        bg_step()

    cx.pop()
    cx.dma("sp", mods[:], mod, r=[], w=["b_mod"], key="b_mod")
    cx.stt(G2[:], mods[:, 32:40, :], 1.0, vec[:, 12:20, None].to_broadcast([128, 8, 2]), ALU.add, ALU.mult,
           r=["b_mod", "b_vec"], w=["modc"])
    cx.copy("dve", g2h[:], mods[:, 40:48, :], r=["b_mod"], w=["modc2"])
    banks = BankRot([0, 1, 2, 3, 4, 5, 6, 7])
    wo_sb = cx.sb("f_wout", [128, 8, 1024], BF16)
    for hf_ in range(2):
        cx.dma("pool", wo_sb[:, :, hf_ * 512:(hf_ + 1) * 512], w_out[:, :, hf_ * 512:(hf_ + 1) * 512], r=[],
               w=[f"f_wout{hf_}"], key=f"f_wout{hf_}")
    xg_rot = Rot(cx, "f_xg", [128, 8, 512], F32, 2)
    sq = cx.sb("f_sq", [128, 8, 512], BF16)
    u = cx.sb("f_u", [128, 8, 512], F32)
    rs_rot = Rot(cx, "f_rs", [128, 512], F32, 2)
    aT = cx.sb("f_aT", [128, NJ, 512], BF16)
    win_rot = Rot(cx, "f_win", [128, 8, 256], BF16, 4)
    wo2_rot = Rot(cx, "f_wo2", [128, NJ, 128], BF16, 3)
    th_rot = Rot(cx, "f_th", [128, 512], F32, 2)
    s_rot = Rot(cx, "f_s", [128, 512], F32, 2)

    h2_rot = Rot(cx, "f_h2T", [128, 8, 512], BF16, 2)

    def outproj_norm(g):
        off, n = GROUPS[g]
        col = 1 if off >= TL else 0
        xg, xgk = xg_rot.next()
        cx.dma("sp", xg[:, :, :n], xT[:, :, off:off + n], r=[], w=[xgk], key=xgk)
        mixk = [f"mix{m}_{off}" for m in range(4)] + [f"mix{m}_{g}" for m in range(4, 8)]
        for dc in range(8):
            b, bk = banks.next()
            for m in range(8):
                cx.mm(ps[:, b, :n], wo_sb[:, m, dc * 128:(dc + 1) * 128], mixT[:, m, off:off + n],
                      start=(m == 0), stop=(m == 7), r=[f"f_wout{dc // 4}", mixk[m]], w=[bk])
            cx.stt(xg[:, dc, :n], ps[:, b, :n], g1[:, dc, col:col + 1], xg[:, dc, :n], ALU.mult, ALU.add,
                   r=[bk, "b_mod", xgk], w=[xgk])
            banks.release(b)
        h2T, h2k = h2_rot.next()
        gen = norm_mod_gen(cx, g, xg, xgk, h2T, h2k, sq, u, rs_rot, onesD, mhalf, ps, banks, G2, sh2)
        return xg, xgk, h2T, h2k, gen

    def run_gen(gen):
        for _ in gen:
            pass

    nxt = outproj_norm(groups_run[0])
    run_gen(nxt[4])
    for gi, g in enumerate(groups_run):
        off, n = GROUPS[g]
        col = 1 if off >= TL else 0
        xg, xgk, h2T, h2k, _ = nxt
        ngen = None
        if gi + 1 < len(groups_run):
            nxt = outproj_norm(groups_run[gi + 1])
            ngen = nxt[4]
        for j in range(NJ):
            wj, wjk = win_rot.next()
            cx.dma("pool", wj[:], w_ffn_in[j], r=[], w=[wjk], key=wjk)
            bg, bgk = banks.next()
            bu, buk = banks.next()
            for c in range(8):
                cx.mm(ps[:, bg, :n], wj[:, c, 0:128], h2T[:, c, :n], start=(c == 0), stop=(c == 7),
                      r=[wjk, h2k], w=[bgk])
            for c in range(8):
                cx.mm(ps[:, bu, :n], wj[:, c, 128:256], h2T[:, c, :n], start=(c == 0), stop=(c == 7),
                      r=[wjk, h2k], w=[buk])
            th, thk = th_rot.next()
            cx.act(th[:, :n], ps[:, bg, :n], AF.Exp, r=[bgk], w=[thk], scale=-1.0)
            sv, svk = s_rot.next()
            cx.recip_act(th[:, :n], th[:, :n], r=[thk], w=[thk], one=cf[:, 4, 0:1])
            cx.tt("dve", sv[:, :n], th[:, :n], ps[:, bg, :n], ALU.mult, r=[thk, bgk], w=[svk])
            banks.release(bg)
            cx.tt("dve", aT[:, j, :n], sv[:, :n], ps[:, bu, :n], ALU.mult, r=[svk, buk], w=[f"f_aT{j}"])
            banks.release(bu)
            if ngen is not None and j >= 6 and j % 2 == 0:
                try:
                    next(ngen)
                except StopIteration:
                    ngen = None
        if ngen is not None:
            run_gen(ngen)
        for dc in range(8):
            w2, w2k = wo2_rot.next()
            cx.dma("pool", w2[:], w_ffn_out[dc], r=[], w=[w2k], key=w2k)
            b, bk = banks.next()
            for j in range(NJ):
                cx.mm(ps[:, b, :n], w2[:, j, :], aT[:, j, :n], start=(j == 0), stop=(j == NJ - 1),
                      r=[w2k, f"f_aT{j}"], w=[bk])
            cx.stt(xg[:, dc, :n], ps[:, b, :n], g2h[:, dc, col:col + 1], xg[:, dc, :n], ALU.mult, ALU.add,
                   r=[bk, "modc2", xgk], w=[xgk])
            banks.release(b)
        cx.dma("sp", xT_o_fn(off, n), xg[:, :, :n], r=[xgk], w=[cx.uid("xT_o")], key=xgk)


def invcnt_table(core):
    out = np.zeros((128, 2, T), np.float32)
    wins = {(0, 0): 2, (64, 0): 4, (0, 1): 8, (64, 1): 16}
    for (p0, ci), w in wins.items():
        for off, L, base in ((0, SEQ, (core % 4) * TL), (TL, TC, 0)):
            nt = TL if off == 0 else TC
            t = base + np.arange(nt)
            lo = np.clip(t - w // 2, 0, L)
            hi = np.clip(t + w - w // 2, 0, L)
            out[p0:p0 + 64, ci, off:off + nt] = (1.0 / (hi - lo))[None, :]
    return out


CC_GROUPS = [[0, 1, 2, 3], [4, 5, 6, 7]]
PIECES = [(0, 2)] + [(2 + 8 * i, 8) for i in range(8)]


def build_fused(depth=DEPTH):
    cx = Ctx()
    nc = cx.nc
    x0T = cx.din("x0T", [128, 8, T], F32)
    cv = cx.din("cv", [128, 8, 2], F32)
    w_mod = cx.din("w_mod", [DEPTH, D, 6 * D], F32)
    b_mod_fm = cx.din("b_mod_fm", [DEPTH, 128, 48], F32)
    lam_in = cx.din("lam_in", [128, DEPTH, 4, 64], F32)
    cmat = cx.din("cmat", [128, 6, 128], F32)
    n1g = cx.din("n1g", [DEPTH, 128, 8], F32)
    w_in = cx.din("w_in", [DEPTH, 128, 8, 2304], F32)
    qkg = cx.din("qkg", [DEPTH, 128, 2], F32)
    poolw = cx.din("poolw", [DEPTH, 128, 2, 128], F32)
    vecs = cx.din("vecs", [DEPTH, 128, NVEC], F32)
    convw = cx.din("convw", [DEPTH, 128, 2, 31], F32)
    w_out = cx.din("w_out", [DEPTH, 128, 8, 1024], F32)
    w_ffn_in = cx.din("w_ffn_in", [DEPTH, NJ, 128, 8, 256], F32)
    w_ffn_out = cx.din("w_ffn_out", [DEPTH, 8, 128, NJ, 128], F32)
    cosT = cx.din("cosT", [128, TL], F32)
    sinT = cx.din("sinT", [128, TL], F32)
    invcnt = cx.din("invcnt", [128, 2, T], F32)
    selT = cx.din("selT", [128, 8], F32)
    outT = cx.dout("outT", [128, 8, TL], F32)
    modT = cx.dscratch("modT", [128, DEPTH, 48, 2], F32)
    lamd = cx.dscratch("lamd", [128, DEPTH], F32)
    xs = [cx.dscratch(f"xs{i}", [128, 8, T], F32) for i in range(2)]
    qTs = cx.dscratch("qTs", [4, 128, T], BF16)
    kcs = cx.dscratch("kcs", [4, 128, TC], BF16)
    vcs = cx.dscratch("vcs", [4, 128, 2, 128], BF16)
    up_pad = cx.dscratch("up_pad", [128, 2, TP], F32)
    glu_pad = cx.dscratch("glu_pad", [128, 2, TP], F32)
    sendK = [[cx.dscratch(f"sendK{p}{h}", [512, 1024], BF16) for h in range(2)] for p in range(2)]
    sendV = [[cx.dscratch(f"sendV{p}{h}", [512, 1024], BF16) for h in range(2)] for p in range(2)]
    recvK = [[cx.dscratch(f"recvK{p}{h}", [2048, 1024], BF16) for h in range(2)] for p in range(2)]
    recvV = [[cx.dscratch(f"recvV{p}{h}", [2048, 1024], BF16) for h in range(2)] for p in range(2)]
    sendE = [cx.dscratch(f"sendE{p}", [256, 64], F32) for p in range(2)]
    recvE = [cx.dscratch(f"recvE{p}", [1024, 64], F32) for p in range(2)]

    ps = cx.es.enter_context(nc.psum_tensor("ps", [128, 8, 512], F32))
    consts = load_consts(cx, cmat)
    sel = cx.sb("sel_sb", [128, 8], F32)
    cx.dma("sp", sel[:], selT, r=[], w=["selT"], key="selT")
    E = cx.sb("E", [128, 4, 2, 64], F32)
    H = cx.sb("H", [128, 2, 2, 2, HALO], F32)
    zt = cx.sb("zt", [128, 2, HALO], F32)
    cx.memset("dve", zt[:], 0.0, w=["zt"])
    zi = 0
    for pad in (up_pad, glu_pad):
        for o in (0, HALO + TL, TPL, TPL + HALO + TC):
            cx.dma("sp", pad[:, :, o:o + HALO], zt[:], r=["zt"], w=[f"zpad{zi}"], key=f"zpad{zi % 4}")
            zi += 1

    silb = cx.sb("silb", [128, 8, 2], BF16)
    cx.push("M_")
    emit_mod_pre(cx, cv, lam_in, lamd, silb)
    for _ in mod_layer_gen(cx, ps, [(0, 0, 6)], silb, w_mod, b_mod_fm, modT, lambda: (0, "ps0", lambda: None),
                           cx.sb):
        pass
    cx.pop()

    for l in range(depth):
        par = l % 2
        last = (l == depth - 1)
        x_in = x0T if l == 0 else xs[(l - 1) % 2]

        def k_sink(h, off, n, par=par):
            if off >= TL:
                return kcs[h, :, :], None
            return (sendK[par][off // 1024][h * 128:(h + 1) * 128, off % 1024: off % 1024 + n],
                    f"sendK{off // 1024}")

        def v_sink(ti, par=par):
            if ti >= 16:
                return vcs.rearrange("h p t d -> p h t d")[:, :, ti - 16, :], None
            return (sendV[par][ti // 8].rearrange("(h p) c -> p h c", p=128)[:, :, (ti % 8) * 128:(ti % 8 + 1) * 128],
                    f"sendV{ti // 8}")

        def pad_sink(pad):
            def f(ci, off, n):
                o = HALO + off if off < TL else TPL + HALO + (off - TL)
                return pad[:, ci, o:o + n]
            return f

        def edge_sink(tz, ci, side, off, par=par):
            if (side == 0 and off == 0) or (side == 1 and off == TL - 512):
                return sendE[par][ci * 128:(ci + 1) * 128, tz * 32 + side * HALO: tz * 32 + (side + 1) * HALO]
            return None

        def issue_cc(name, par=par):
            tab = {"K0": (sendK[par][0], recvK[par][0], "sendK0", 0), "V0": (sendV[par][0], recvV[par][0], "sendV0", 1),
                   "K1": (sendK[par][1], recvK[par][1], "sendK1", 2), "V1": (sendV[par][1], recvV[par][1], "sendV1", 3),
                   "E": (sendE[par], recvE[par], "sendE", 4)}
            sbuf_, rbuf_, skey, ci_ = tab[name]
            cx.S.op("pool", lambda e, a=sbuf_, b=rbuf_: e.collective_compute(
                "AllGather", ALU.bypass, replica_groups=CC_GROUPS, ins=[a], outs=[b]),
                r=[skey], w=["recv", f"recv_{skey}"], key=f"cc{ci_}", inc1=True)

        sinks = {"q": lambda h, off, n: (qTs[h, :, off:off + n], None), "k": k_sink, "v": v_sink,
                 "up": pad_sink(up_pad), "glu": pad_sink(glu_pad), "edge": edge_sink, "cc": issue_cc}
        cx.push(f"A{l}_")
        emit_A(cx, consts, ps, x_in, modT[:, l], n1g[l], w_in[l], qkg[l], cosT, sinT, sinks)
        cx.pop()
        cx.dma("sp", E[:], recvE[par].rearrange("(j c p) x -> p j c x", j=4, c=2, p=128), r=["recv_sendE"],
               w=["E"], key="E")
        for tz in range(2):
            for side in range(2):
                c0 = tz * 32 + (HALO if side == 0 else 0)
                hv = H[:, tz, side, :, :]
                for j in range(4):
                    sc = sel[:, 4 * side + j: 4 * side + j + 1]
                    if j == 0:
                        cx.ts("dve", hv, E[:, j, :, c0:c0 + HALO], sc, ALU.mult, r=["E", "selT"], w=[f"H{tz}{side}"])
                    else:
                        cx.stt(hv, E[:, j, :, c0:c0 + HALO], sc, hv, ALU.mult, ALU.add,
                               r=["E", "selT", f"H{tz}{side}"], w=[f"H{tz}{side}"])
                pad = up_pad if tz == 0 else glu_pad
                o = 0 if side == 0 else HALO + TL
                cx.dma("sp", pad[:, :, o:o + HALO], hv, r=[f"H{tz}{side}"], w=["halo_up" if tz == 0 else "halo_glu"],
                       key=f"zpad{2 * tz + side}")
        def kv_src(h, piece, par=par):
            if piece == 0:
                return kcs[h], vcs[h]
            hf, j = (piece - 1) // 4, (piece - 1) % 4
            rows = slice(j * 512 + h * 128, j * 512 + (h + 1) * 128)
            return recvK[par][hf][rows, :], recvV[par][hf][rows, :].rearrange("p (t d) -> p t d", d=128)

        if last:
            xo_fn = lambda off, n: outT[:, :, off:off + n]
        else:
            xo_fn = lambda off, n, l=l: xs[l % 2][:, :, off:off + n]
        cx.push(f"B{l}_")
        jobs = ([(0, 6, 16)] if l == 0 else []) + ([(l + 1, 0, 16)] if not last else [])
        bg = None
        if jobs:
            bg = lambda alloc, bank_fn, jobs=jobs: mod_layer_gen(cx, ps, jobs, silb, w_mod, b_mod_fm, modT,
                                                                 bank_fn, alloc)
        emit_B(cx, consts, ps, x_in, modT[:, l], lamd[:, l:l + 1], qTs, PIECES, kv_src, up_pad, glu_pad, invcnt,
               poolw[l], vecs[l], convw[l], w_out[l], w_ffn_in[l], w_ffn_out[l], xo_fn, last=last, bg=bg)
        cx.pop()
    return cx.finish()


_CACHE = {}


def layer_consts(l):
    return 0.8 - 0.6 * math.exp(-0.3 * l)


def prep_weights(inputs):
    f32 = np.float32
    w = {}
    w["n1g"] = np.stack([fm(inputs["norm1_g"][l], 8) for l in range(DEPTH)])
    w["w_in"] = np.stack([fm_w(np.asarray(inputs["w_in"][l], f32)) for l in range(DEPTH)])
    w["qkg"] = np.stack([np.stack([np.tile(inputs["q_norm_g"][l], 2), np.tile(inputs["k_norm_g"][l], 2)], -1)
                         for l in range(DEPTH)]).astype(f32)
    poolw = np.zeros((DEPTH, 128, 2, 128), f32)
    vecs = np.zeros((DEPTH, 128, NVEC), f32)
    for l in range(DEPTH):
        for ci in range(2):
            poolw[l, 0:64, ci, 0:64] = inputs["pool_w"][l][2 * ci]
            poolw[l, 64:128, ci, 64:128] = inputs["pool_w"][l][2 * ci + 1]
        li = layer_consts(l)
        vecs[l, :, 0:2] = fm(inputs["pool_scale"][l], 2)
        vecs[l, :, 2:4] = fm(inputs["conv_dw_b"][l], 2)
        vecs[l, :, 4:6] = fm(inputs["conv_ln_g"][l], 2)
        vecs[l, :, 6:8] = fm(inputs["conv_ln_b"][l], 2)
        vecs[l, :, 8] = inputs["subln_g"][l]
        vecs[l, :, 10] = li
        vecs[l, :, 11] = 1.0 - li
        vecs[l, :, 12:20] = fm(inputs["norm2_g"][l], 8)
    w["poolw"] = poolw
    w["vecs"] = vecs
    w["convw"] = np.stack([np.asarray(inputs["conv_dw_w"][l], f32).T.reshape(2, 128, 31).transpose(1, 0, 2)
                           for l in range(DEPTH)])
    w["w_out"] = np.stack([fm_w(np.asarray(inputs["w_out"][l], f32)) for l in range(DEPTH)])
    wfi = np.asarray(inputs["w_ffn_in"], f32)
    wg = wfi[:, :, :FF].reshape(DEPTH, 8, 128, NJ, 128)
    wu = wfi[:, :, FF:].reshape(DEPTH, 8, 128, NJ, 128)
    w["w_ffn_in"] = np.ascontiguousarray(np.concatenate([wg, wu], -1).transpose(0, 3, 2, 1, 4))
    w["w_ffn_out"] = np.ascontiguousarray(
        np.asarray(inputs["w_ffn_out"], f32).reshape(DEPTH, NJ, 128, 8, 128).transpose(0, 3, 2, 1, 4))
    return {k: np.ascontiguousarray(v, dtype=f32) for k, v in w.items()}


def make_in_maps(inputs):
    f32 = np.float32
    x = np.asarray(inputs["x"], f32)
    c = np.asarray(inputs["c"], f32)
    ctx = np.asarray(inputs["ctx"], f32)
    c_ctx = np.asarray(inputs["c_ctx"], f32)
    w = prep_weights(inputs)
    lam_in = np.stack([inputs["lambda_q1"], inputs["lambda_k1"], inputs["lambda_q2"], inputs["lambda_k2"]], 1)
    shared = dict(w)
    shared["lam_in"] = np.ascontiguousarray(np.broadcast_to(np.asarray(lam_in, f32)[None], (128, DEPTH, 4, 64)))
    shared["w_mod"] = np.ascontiguousarray(np.asarray(inputs["w_mod"], f32))
    shared["b_mod_fm"] = np.ascontiguousarray(np.asarray(inputs["b_mod"], f32).reshape(DEPTH, 48, 128).transpose(0, 2, 1))
    shared["cmat"] = const_mats()
    in_maps = []
    for i in range(NCORE):
        b, r = i // 4, i % 4
        t0 = r * TL
        m = dict(shared)
        xall = np.concatenate([x[b, t0:t0 + TL], ctx[b]], 0)
        m["x0T"] = np.ascontiguousarray(xall.T.reshape(8, 128, T).transpose(1, 0, 2))
        cvv = np.stack([c[b], c_ctx], -1)
        m["cv"] = np.ascontiguousarray(cvv.reshape(8, 128, 2).transpose(1, 0, 2))
        m["cosT"], m["sinT"] = rope_tables(i)
        m["invcnt"] = invcnt_table(i)
        sel = np.zeros((128, 8), f32)
        if r > 0:
            sel[:, r - 1] = 1.0
        if r < 3:
            sel[:, 4 + r + 1] = 1.0
        m["selT"] = sel
        in_maps.append(m)
    return in_maps


def kernel(**inputs):
    in_maps = make_in_maps(inputs)
    if "nc" not in _CACHE:
        _CACHE["nc"] = build_fused()
    res = run_bass_kernel_spmd(_CACHE["nc"], in_maps, core_ids=list(range(NCORE))).results
    out = np.zeros((2, SEQ, D), np.float32)
    for i in range(NCORE):
        b, t0 = i // 4, (i % 4) * TL
        xo = res[i]["outT"].transpose(1, 0, 2).reshape(D, TL)
        out[b, t0:t0 + TL] = xo.T
    return out
```

```python
import math
from contextlib import ExitStack

import numpy as np
import ml_dtypes

import concourse.bass as bass
import concourse.mybir as mybir
from concourse.bass_utils import run_bass_kernel_spmd

F32 = mybir.dt.float32
BF16 = mybir.dt.bfloat16
AF = mybir.ActivationFunctionType
ALU = mybir.AluOpType
AX = mybir.AxisListType
NPBF = ml_dtypes.bfloat16

D = 1024
DEPTH = 4
NCORE = 8
TL = 2048
TC = 256
T = TL + TC
SEQ = 8192
NKEY = SEQ + TC
NKT = NKEY // 128
FF = 2816
NJ = FF // 128
EPS = 1e-6
HALO = 16
GROUPS = [(0, 512), (512, 512), (1024, 512), (1536, 512), (2048, 256)]


class Sched:
    ENGS = ("pe", "act", "dve", "pool", "sp")

    def __init__(self, nc):
        self.nc = nc
        self.q = {e: [] for e in self.ENGS}
        self.cnt = {}
        self.res = {}
        self.waited = {e: {} for e in self.ENGS}
        self.bar = {}

    def barrier(self):
        self.bar = dict(self.cnt)

    def op(self, eng, fn, r=(), w=(), key=None, inc1=False):
        if key is None:
            sem, inc = "S_" + eng, 1
        elif inc1:
            sem, inc = "C_" + key, 1
        else:
            sem, inc = "D_" + key, 16
        deps = dict(self.bar)

        def need(sv):
            s, v = sv
            if deps.get(s, 0) < v:
                deps[s] = v

        for k in r:
            st = self.res.get(k)
            if st and st[0]:
                need(st[0])
            if st and k.startswith("ps"):
                for sv in st[1].items():
                    if sv[0] != sem:
                        need(sv)
        for k in w:
            st = self.res.get(k)
            if st:
                if st[0]:
                    need(st[0])
                for sv in st[1].items():
                    need(sv)
        waits = []
        for s, v in deps.items():
            if eng == "pe" and s == "S_pe":
                continue
            if self.waited[eng].get(s, 0) >= v:
                continue
            self.waited[eng][s] = v
            waits.append((s, v))
        val = self.cnt.get(sem, 0) + inc
        self.cnt[sem] = val
        self.q[eng].append((waits, fn, sem, inc))
        for k in r:
            st = self.res.setdefault(k, [None, {}])
            if st[1].get(sem, 0) < val:
                st[1][sem] = val
        for k in w:
            self.res[k] = [(sem, val), {}]

    def finish(self):
        waits = [(s, v) for s, v in self.cnt.items() if s.startswith("D_") or s.startswith("C_")]
        self.q["sp"].append((waits, None, None, 0))

    def emit(self, es):
        nc = self.nc
        sems = {n: es.enter_context(nc.semaphore(n)) for n in sorted(self.cnt)}
        block = es.enter_context(nc.Block())

        def run(name):
            def f(e):
                for waits, fn, sem, inc in self.q[name]:
                    attach = fn is not None and waits and not sem.startswith("C_")
                    for s, v in (waits[:-1] if attach else waits):
                        e.wait_ge(sems[s], v)
                    if fn is not None:
                        ins = fn(e)
                        if attach:
                            ins._wait_ge(sems[waits[-1][0]], waits[-1][1])
                        ins.then_inc(sems[sem], inc)
            return f

        block.tensor(run("pe"))
        block.scalar(run("act"))
        block.vector(run("dve"))
        block.gpsimd(run("pool"))
        block.sync(run("sp"))


class Ctx:
    def __init__(self):
        self.nc = bass.Bass("TRN2", target_bir_lowering=False)
        self.es = ExitStack()
        self.S = Sched(self.nc)
        self.n = 0
        self.stacks = [self.es]
        self.pfx = ""

    def push(self, pfx=None):
        if pfx is not None:
            self.pfx = pfx
        st = ExitStack()
        self.stacks.append(st)
        return st

    def pop(self):
        self.stacks.pop().close()
        self.S.barrier()

    def sb(self, name, shape, dt):
        return self.stacks[-1].enter_context(self.nc.sbuf_tensor(self.pfx + name, list(shape), dt))

    def din(self, name, shape, dt):
        return self.nc.dram_tensor(name, list(shape), dt, kind="ExternalInput").ap()

    def dout(self, name, shape, dt):
        return self.nc.dram_tensor(name, list(shape), dt, kind="ExternalOutput").ap()

    def dscratch(self, name, shape, dt):
        return self.nc.dram_tensor(name, list(shape), dt, kind="Internal").ap()

    def uid(self, p):
        self.n += 1
        return f"{p}{self.n}"

    def dma(self, q, out, in_, r, w, key, slow=False):
        if slow:
            self.S.op(q, lambda e: e.dma_start(out=out, in_=in_, allow_slow_non_contiguous=True), r=r, w=w, key=key)
        else:
            self.S.op(q, lambda e: e.dma_start(out=out, in_=in_), r=r, w=w, key=key)

    def mm(self, out, lhsT, rhs, start, stop, r, w):
        self.S.op("pe", lambda e: e.matmul(out, lhsT, rhs, start=start, stop=stop), r=r, w=w)

    def act(self, out, in_, func, r, w, bias=None, scale=None):
        kw = {}
        if bias is not None:
            kw["bias"] = bias
        if scale is not None:
            kw["scale"] = scale
        self.S.op("act", lambda e: e.activation(out=out, in_=in_, func=func, **kw), r=r, w=w)

    def tt(self, eng, out, in0, in1, op, r, w):
        self.S.op(eng, lambda e: e.tensor_tensor(out=out, in0=in0, in1=in1, op=op), r=r, w=w)

    def ts(self, eng, out, in0, s1, op0, r, w, s2=None, op1=None):
        if op1 is None:
            self.S.op(eng, lambda e: e.tensor_scalar(out=out, in0=in0, scalar1=s1, scalar2=None, op0=op0),
                      r=r, w=w)
        else:
            self.S.op(eng, lambda e: e.tensor_scalar(out=out, in0=in0, scalar1=s1, scalar2=s2, op0=op0, op1=op1),
                      r=r, w=w)

    def stt(self, out, in0, scalar, in1, op0, op1, r, w):
        self.S.op("dve", lambda e: e.scalar_tensor_tensor(out=out, in0=in0, scalar=scalar, in1=in1,
                                                          op0=op0, op1=op1), r=r, w=w)

    def copy(self, eng, out, in_, r, w):
        self.S.op(eng, lambda e: e.tensor_copy(out=out, in_=in_), r=r, w=w)

    def memset(self, eng, ap, val, w):
        self.S.op(eng, lambda e: e.memset(ap, val), w=w)

    def recip(self, out, in_, r, w):
        self.S.op("dve", lambda e: e.reciprocal(out=out, in_=in_), r=r, w=w)

    def recip_act(self, out, in_, r, w, one=None):
        if one is not None:
            self.act(out, in_, AF.Ln, r=list(r) + ["cmat_f"], w=w, bias=one)
        else:
            self.act(out, in_, AF.Ln, r=r, w=w)
        self.act(out, out, AF.Exp, r=w, w=w, scale=-1.0)

    def finish(self):
        self.S.finish()
        self.S.emit(self.es)
        self.es.close()
        return self.nc


def rstd_from_psum(cx, ps_ap, ps_key, tmp_ap, tmp_key, mhalf_ap, n):
    cx.act(tmp_ap, ps_ap, AF.Ln, r=[ps_key, "consts"], w=[tmp_key], bias=mhalf_ap[:, 0:1])
    cx.act(tmp_ap, tmp_ap, AF.Exp, r=[tmp_key], w=[tmp_key], scale=-0.5)


def emit_mod_pre(cx, cv, lam_in, lam_out, silb):
    cvs = cx.sb("m_cv", [128, 8, 2], F32)
    th = cx.sb("m_th", [128, 8, 2], F32)
    sil = cx.sb("m_sil", [128, 8, 2], F32)
    lamt = cx.sb("m_lamt", [128, 4, 4, 64], F32)
    prod = cx.sb("m_prod", [128, 4, 2, 64], F32)
    lsum = cx.sb("m_lsum", [128, 4, 2], F32)
    lexp = cx.sb("m_lexp", [128, 4, 2], F32)
    lams = cx.sb("m_lams", [128, 4], F32)
    cx.dma("sp", cvs[:], cv, r=[], w=["m_cv"], key="m_cv")
    cx.dma("sp", lamt[:], lam_in, r=[], w=["m_lamt"], key="m_lamt")
    cx.act(th[:], cvs[:], AF.Exp, r=["m_cv"], w=["m_th"], scale=-1.0)
    cx.ts("dve", th[:], th[:], 1.0, ALU.add, r=["m_th"], w=["m_th"])
    cx.recip(th[:], th[:], r=["m_th"], w=["m_th"])
    cx.tt("dve", sil[:], th[:], cvs[:], ALU.mult, r=["m_th", "m_cv"], w=["m_sil"])
    cx.copy("dve", silb[:], sil[:], r=["m_sil"], w=["silb"])
    cx.tt("dve", prod[:, :, 0, :], lamt[:, :, 0, :], lamt[:, :, 1, :], ALU.mult, r=["m_lamt"], w=["m_prod0"])
    cx.tt("dve", prod[:, :, 1, :], lamt[:, :, 2, :], lamt[:, :, 3, :], ALU.mult, r=["m_lamt"], w=["m_prod1"])
    cx.S.op("dve", lambda e: e.tensor_reduce(out=lsum[:], in_=prod[:], axis=AX.X, op=ALU.add),
            r=["m_prod0", "m_prod1"], w=["m_lsum"])
    cx.act(lexp[:], lsum[:], AF.Exp, r=["m_lsum"], w=["m_lexp"])
    cx.tt("dve", lams[:], lexp[:, :, 0], lexp[:, :, 1], ALU.subtract, r=["m_lexp"], w=["m_lams"])
    cx.dma("sp", lam_out, lams[:], r=["m_lams"], w=["lam_out"], key="m_lamo")


def mod_layer_gen(cx, ps, jobs, silb, w_mod, b_mod_fm, modT_out, bank_fn, alloc):
    SW = 384
    slab = [alloc(f"ml_slab{i}", [128, 8, SW], BF16) for i in range(2)]
    modsb = alloc("ml_modsb", [128, 48, 2], F32)
    bfm = alloc("ml_bfm", [128, 48], F32)
    NE = SW // 128
    cnt = 0
    for l, s_lo, s_hi in jobs:
        cx.dma("sp", bfm[:], b_mod_fm[l], r=[], w=["ml_bfm"], key="ml_bfm")
        for sidx in range(s_lo, s_hi):
            sl = slab[cnt % 2]
            sk = f"ml_slab{cnt % 2}"
            cnt += 1
            e0 = sidx * SW
            cx.dma("pool", sl[:], w_mod[l, :, e0:e0 + SW].rearrange("(c p) e -> p c e", p=128), r=[], w=[sk],
                   key=sk)
            yield
            bank, pk, rel = bank_fn()
            for j in range(NE):
                out = ps[:, bank, 2 * j:2 * j + 2]
                for c in range(8):
                    cx.mm(out, sl[:, c, j * 128:(j + 1) * 128], silb[:, c, :], start=(c == 0), stop=(c == 7),
                          r=[sk, "silb"], w=[pk])
            cx.tt("dve", modsb[:, sidx * NE:(sidx + 1) * NE, :],
                  ps[:, bank, 0:2 * NE].rearrange("p (j t) -> p j t", t=2),
                  bfm[:, sidx * NE:(sidx + 1) * NE, None].to_broadcast([128, NE, 2]), ALU.add,
                  r=[pk, "ml_bfm"], w=["ml_modsb"])
            rel()
            yield
        cx.dma("sp", modT_out[:, l, s_lo * NE:s_hi * NE, :], modsb[:, s_lo * NE:s_hi * NE, :], r=["ml_modsb"],
               w=[cx.uid("modT")], key="ml_modo")
        yield


class Rot:
    def __init__(self, cx, name, shape, dt, n, alloc=None):
        alloc = alloc or cx.sb
        self.bufs = [alloc(f"{name}{i}", shape, dt) for i in range(n)]
        self.keys = [f"{name}{i}" for i in range(n)]
        self.i = 0

    def next(self):
        i = self.i % len(self.bufs)
        self.i += 1
        return self.bufs[i], self.keys[i]


class BankRot:
    def __init__(self, banks, held=None):
        self.banks = banks
        self.held = set() if held is None else held
        self.i = 0

    def next(self):
        for _ in range(len(self.banks)):
            b = self.banks[self.i % len(self.banks)]
            self.i += 1
            if b not in self.held:
                self.held.add(b)
                return b, f"ps{b}"
        raise RuntimeError(f"no free PSUM bank among {self.banks}")

    def release(self, b):
        self.held.discard(b)


def load_consts(cx, cmat_d, nm=6):
    cf = cx.sb("cmat_f", [128, nm, 128], F32)
    cb = cx.sb("cmat_b", [128, nm, 128], BF16)
    mh = cx.sb("epsb", [128, 1024], F32)
    cx.dma("sp", cf[:], cmat_d, r=[], w=["cmat_f"], key="cmat_f")
    cx.copy("dve", cb[:], cf[:], r=["cmat_f"], w=["consts"])
    cx.memset("dve", mh[:], EPS, w=["consts"])
    return cf, cb, mh


def emit_norm_mod(*a, **k):
    for _ in norm_mod_gen(*a, **k):
        pass


def norm_mod_gen(cx, g, xg, xgk, hT, hTk, sq, u, rs_rot, onesD, mhalf, ps, auxb, Gs, shs):
    off, n = GROUPS[g]
    col = 1 if off >= TL else 0
    cx.act(sq[:, :, :n], xg[:, :, :n], AF.Square, r=[xgk], w=["sq"])
    yield
    b, bk = auxb.next()
    for c in range(8):
        cx.mm(ps[:, b, :n], onesD, sq[:, c, :n], start=(c == 0), stop=(c == 7), r=["sq", "consts"], w=[bk])
    rs, rsk = rs_rot.next()
    rstd_from_psum(cx, ps[:, b, :n], bk, rs[:, :n], rsk, mhalf[:, :n], n)
    auxb.release(b)
    yield
    uk = "u"
    if u is None:
        u, uk = xg, xgk
    cx.tt("dve", u[:, :, :n], xg[:, :, :n], rs[:, None, :n].to_broadcast([128, 8, n]), ALU.mult,
          r=[xgk, rsk], w=[uk])
    yield
    yield
    for c in range(8):
        if c % 2 == 0:
            cx.act(hT[:, c, :n], u[:, c, :n], AF.Identity, r=[uk, "modc"], w=[hTk],
                   bias=shs[:, c, col:col + 1], scale=Gs[:, c, col:col + 1])
        else:
            cx.ts("dve", hT[:, c, :n], u[:, c, :n], Gs[:, c, col:col + 1], ALU.mult, r=[uk, "modc"], w=[hTk],
                  s2=shs[:, c, col:col + 1], op1=ALU.add)


def emit_A(cx, consts, ps, xT, mod, n1g, w_in, qkg, cosT, sinT, sinks):
    nc = cx.nc
    S = cx.S
    cf, cb, mhalf = consts
    onesD, ones64, pswap = cb[:, 0, :], cb[:, 1, :], cb[:, 2, :]
    mainb = BankRot([0, 1, 2, 3, 4])
    auxb = BankRot([5, 6, 7], held=mainb.held)
    allb = BankRot([0, 1, 2, 3, 4, 5, 6, 7], held=mainb.held)

    w_sb = cx.sb("a_w", [128, 8, 2304], BF16)
    for wk, c0, c1 in (("a_wk", 512, 1024), ("a_wv", 1024, 1536), ("a_wp", 1536, 2304), ("a_wq", 0, 512)):
        cx.dma("pool", w_sb[:, :, c0:c1], w_in[:, :, c0:c1], r=[], w=[wk], key=wk)
    mods = cx.sb("a_mod", [128, 48, 2], F32)
    n1gs = cx.sb("a_n1g", [128, 8], F32)
    qkgs = cx.sb("a_qkg", [128, 2], F32)
    cos_s = cx.sb("a_cos", [128, TL], F32)
    sin_s = cx.sb("a_sin", [128, TL], F32)
    cx.dma("sp", mods[:], mod, r=[], w=["a_mod"], key="a_mod")
    cx.dma("sp", n1gs[:], n1g, r=[], w=["a_n1g"], key="a_n1g")
    cx.dma("sp", qkgs[:], qkg, r=[], w=["a_qkg"], key="a_qkg")
    Gs = cx.sb("a_G", [128, 8, 2], F32)
    cx.stt(Gs[:], mods[:, 8:16, :], 1.0, n1gs[:, :, None].to_broadcast([128, 8, 2]), ALU.add, ALU.mult,
           r=["a_mod", "a_n1g"], w=["modc"])
    shs = mods[:, 0:8, :]

    xg_rot = Rot(cx, "a_xg", [128, 8, 512], F32, 2)
    sq = cx.sb("a_sq", [128, 8, 512], BF16)
    u = None
    rs_rot = Rot(cx, "a_rs", [128, 512], F32, 2)
    sq2_rot = Rot(cx, "a_sq2", [128, 512], BF16, 2)
    r2_rot = Rot(cx, "a_r2", [128, 512], F32, 2)
    qn_rot = Rot(cx, "a_qn", [128, 512], F32, 3)
    hi_rot = Rot(cx, "a_hi", [128, 512], BF16, 2)
    lo_rot = Rot(cx, "a_lo", [128, 512], BF16, 2)
    t1_rot = Rot(cx, "a_t1", [128, 512], F32, 2)
    t2_rot = Rot(cx, "a_t2", [128, 512], F32, 2)
    qo_rot = Rot(cx, "a_qo", [128, 512], BF16, 8)
    po_rot = Rot(cx, "a_po", [128, 512], F32, 4)
    th_rot = Rot(cx, "a_th", [128, 512], F32, 4)
    gl_rot = Rot(cx, "a_gl", [128, 512], F32, 4)
    vo_rot = Rot(cx, "a_vo", [128, 512], BF16, 4)

    pending = []

    def advance():
        for gen in list(pending):
            try:
                next(gen)
            except StopIteration:
                pending.remove(gen)

    def load_x(g):
        off, n = GROUPS[g]
        xg, xgk = xg_rot.next()
        cx.dma("sp", xg[:, :, :n], xT[:, :, off:off + n], r=[], w=[xgk], key=xgk)
        return xg, xgk

    def qk_post(g, which, h, b, bk):
        off, n = GROUPS[g]
        latent = off < TL
        sq2, sq2k = sq2_rot.next()
        cx.act(sq2[:, :n], ps[:, b, :n], AF.Square, r=[bk], w=[sq2k])
        yield
        b2, b2k = auxb.next()
        cx.mm(ps[:, b2, :n], ones64, sq2[:, :n], start=True, stop=True, r=[sq2k, "consts"], w=[b2k])
        yield
        r2, r2k = r2_rot.next()
        rstd_from_psum(cx, ps[:, b2, :n], b2k, r2[:, :n], r2k, mhalf[:, :n], n)
        auxb.release(b2)
        qn, qnk = qn_rot.next()
        cx.stt(qn[:, :n], ps[:, b, :n], qkgs[:, which:which + 1], r2[:, :n], ALU.mult, ALU.mult,
               r=[bk, r2k, "a_qkg"], w=[qnk])
        mainb.release(b)
        dst, dres = sinks["q" if which == 0 else "k"](h, off, n)
        dkey = f"{'qT' if which == 0 else 'kT'}_o"
        qo, qok = qo_rot.next()
        if not latent:
            cx.act(qo[:, :n], qn[:, :n], AF.Copy, r=[qnk], w=[qok])
            yield
            cx.dma("sp", dst, qo[:, :n], r=[qok], w=[dres or cx.uid(dkey)], key=qok)
            return
        hi, hik = hi_rot.next()
        lo, lok = lo_rot.next()
        cx.act(hi[:, :n], qn[:, :n], AF.Copy, r=[qnk], w=[hik])
        cx.tt("dve", lo[:, :n], qn[:, :n], hi[:, :n], ALU.subtract, r=[qnk, hik], w=[lok])
        yield
        b3, b3k = auxb.next()
        cx.mm(ps[:, b3, :n], pswap, hi[:, :n], start=True, stop=False, r=[hik, "consts"], w=[b3k])
        cx.mm(ps[:, b3, :n], pswap, lo[:, :n], start=False, stop=True, r=[lok, "consts"], w=[b3k])
        t1, t1k = t1_rot.next()
        cx.tt("dve", t1[:, :n], qn[:, :n], cos_s[:, off:off + n], ALU.mult, r=[qnk, "a_cos"], w=[t1k])
        yield
        t2, t2k = t2_rot.next()
        cx.tt("dve", t2[:, :n], ps[:, b3, :n], sin_s[:, off:off + n], ALU.mult, r=[b3k, "a_sin"], w=[t2k])
        auxb.release(b3)
        cx.tt("dve", qo[:, :n], t1[:, :n], t2[:, :n], ALU.add, r=[t1k, t2k], w=[qok])
        yield
        cx.dma("sp", dst, qo[:, :n], r=[qok], w=[dres or cx.uid(dkey)], key=qok)

    def pool_post(g, ci, b, bk):
        off, n = GROUPS[g]
        po, pok = po_rot.next()
        cx.act(po[:, :n], ps[:, b, :n], AF.Copy, r=[bk], w=[pok])
        mainb.release(b)
        yield
        cx.dma("sp", sinks["up"](ci, off, n), po[:, :n], r=[pok], w=[cx.uid("up_o")], key=pok)
        for side, sl in ((0, slice(0, HALO)), (1, slice(n - HALO, n))):
            e_ap = sinks["edge"](0, ci, side, off)
            if e_ap is not None:
                cx.dma("sp", e_ap, po[:, sl], r=[pok], w=["sendE"], key=f"edge0{ci}{side}")

    def glu_post(g, ci, ba, bak, bb, bbk):
        off, n = GROUPS[g]
        th, thk = th_rot.next()
        cx.act(th[:, :n], ps[:, bb, :n], AF.Exp, r=[bbk], w=[thk], scale=-1.0)
        mainb.release(bb)
        yield
        gl, glk = gl_rot.next()
        cx.recip_act(th[:, :n], th[:, :n], r=[thk], w=[thk], one=cf[:, 4, 0:1])
        cx.tt("dve", gl[:, :n], th[:, :n], ps[:, ba, :n], ALU.mult, r=[thk, bak], w=[glk])
        mainb.release(ba)
        yield
        cx.dma("sp", sinks["glu"](ci, off, n), gl[:, :n], r=[glk], w=[cx.uid("glu_o")], key=glk)
        for side, sl in ((0, slice(0, HALO)), (1, slice(n - HALO, n))):
            e_ap = sinks["edge"](1, ci, side, off)
            if e_ap is not None:
                cx.dma("sp", e_ap, gl[:, sl], r=[glk], w=["sendE"], key=f"edge1{ci}{side}")

    def v_post(g, ti, b, bk):
        off, n = GROUPS[g]
        vo, vok = vo_rot.next()
        cx.copy("dve", vo[:], ps[:, b, :], r=[bk], w=[vok])
        mainb.release(b)
        yield
        vdst, vres = sinks["v"](off // 128 + ti)
        cx.dma("sp", vdst, vo[:].rearrange("p (h d) -> p h d", h=4), r=[vok],
               w=[vres or cx.uid("v_o")], key=vok)

    ng = len(GROUPS)
    hT_all = cx.sb("a_hTall", [128, 8, T], BF16)

    def hT_of(g):
        off, n = GROUPS[g]
        return hT_all[:, :, off:off + n], f"a_hT{g}"

    def run_chunks(g, chunks, hook=None, alloc=None, per_chunk=None):
        alloc = alloc or mainb
        off, n = GROUPS[g]
        hT, hTk = hT_of(g)
        ca_banks = {}
        for ci, (kind, idx, co, wk) in enumerate(chunks):
            b, bk = alloc.next()
            if kind == "v":
                for c in range(8):
                    cx.mm(ps[:, b, :], hT[:, c, idx * 128:(idx + 1) * 128], w_sb[:, c, 1024:1536],
                          start=(c == 0), stop=(c == 7), r=[hTk, wk], w=[bk])
            else:
                for c in range(8):
                    cx.mm(ps[:, b, :n], w_sb[:, c, co:co + 128], hT[:, c, :n],
                          start=(c == 0), stop=(c == 7), r=[hTk, wk], w=[bk])
            advance()
            if kind == "q":
                pending.append(qk_post(g, 0, idx, b, bk))
            elif kind == "k":
                pending.append(qk_post(g, 1, idx, b, bk))
            elif kind == "pool":
                pending.append(pool_post(g, idx, b, bk))
            elif kind == "ca":
                ca_banks[idx] = (b, bk)
            elif kind == "cb":
                ba, bak = ca_banks[idx]
                pending.append(glu_post(g, idx, ba, bak, b, bk))
            elif kind == "v":
                pending.append(v_post(g, idx, b, bk))
            if hook is not None and ci == 3:
                hook()
            if per_chunk is not None:
                per_chunk(ci)

    def drain():
        while pending:
            advance()

    def norm_group(g, xgx):
        xg, xgk = xgx
        hT, hTk = hT_of(g)
        emit_norm_mod(cx, g, xg, xgk, hT, hTk, sq, u, rs_rot, onesD, mhalf, ps, auxb, Gs, shs)

    cc = sinks.get("cc", lambda name: None)
    nxt_x = load_x(0)
    norm_group(0, nxt_x)
    nxt_x = load_x(1)
    cx.dma("sp", cos_s[:], cosT, r=[], w=["a_cos"], key="a_cos")
    cx.dma("sp", sin_s[:], sinT, r=[], w=["a_sin"], key="a_sin")
    for g in range(ng):
        off, n = GROUPS[g]
        chunks = [("k", h, 512 + h * 128, "a_wk") for h in range(4)]
        chunks += [("v", ti, 1024, "a_wv") for ti in range(n // 128)]

        ngen = [None]

        def per_chunk(ci, g=g, ngen=ngen):
            nonlocal nxt_x
            if ci == 0 and g + 1 < ng:
                xg1, xg1k = nxt_x
                hT1, hT1k = hT_of(g + 1)
                ngen[0] = norm_mod_gen(cx, g + 1, xg1, xg1k, hT1, hT1k, sq, u, rs_rot, onesD, mhalf, ps, auxb,
                                       Gs, shs)
                if g + 2 < ng:
                    nxt_x = load_x(g + 2)
            if ngen[0] is not None:
                try:
                    next(ngen[0])
                except StopIteration:
                    ngen[0] = None
            if ci == 3 and g in (2, 4):
                cc("K%d" % (g // 2 - 1))
                cc("V%d" % (g // 2 - 1))

        run_chunks(g, chunks, None, per_chunk=per_chunk)
        while ngen[0] is not None:
            per_chunk(-1)
    for g in range(ng):
        chunks = [("pool", i, 1536 + i * 128, "a_wp") for i in range(2)]
        chunks += [("ca", i, 1792 + i * 128, "a_wp") for i in range(2)]
        chunks += [("cb", i, 2048 + i * 128, "a_wp") for i in range(2)]
        run_chunks(g, chunks, (lambda: cc("E")) if g == 4 else None, alloc=allb)
    for g in range(ng):
        run_chunks(g, [("q", h, h * 128, "a_wq") for h in range(4)])
    drain()


def fm(v, nch):
    return np.ascontiguousarray(np.asarray(v, np.float32).reshape(nch, 128).T)


def fm_w(w):
    k, e = w.shape
    return np.ascontiguousarray(w.reshape(k // 128, 128, e).transpose(1, 0, 2))


def rope_tables(core):
    t = (core % 4) * TL + np.arange(TL)
    row = (t // 64).astype(np.float64)
    col = (t % 64).astype(np.float64)
    inv_freq = 10000.0 ** (-np.arange(0, 32, 2, dtype=np.float64) / 32)
    ang = np.concatenate([row[:, None] * inv_freq, col[:, None] * inv_freq], -1)
    p = np.arange(128)
    pair = (p % 64) // 2
    cosT = np.cos(ang)[:, pair].T
    sgn = np.where(p % 2 == 0, -1.0, 1.0)[:, None]
    sinT = np.sin(ang)[:, pair].T * sgn
    return np.ascontiguousarray(cosT, np.float32), np.ascontiguousarray(sinT, np.float32)


def const_mats():
    m = np.zeros((128, 6, 128), np.float32)
    m[:, 0, :] = 1.0 / 1024
    m[0:64, 1, 0:64] = 1.0 / 64
    m[64:128, 1, 64:128] = 1.0 / 64
    idx = np.arange(128)
    m[idx ^ 1, 2, idx] = 1.0
    m[:, 3, :] = 1.0 / 128
    m[:, 4, :] = 1.0
    m[:, 5, :] = 1.0 / 256
    return m


NVEC = 24
TPL = TL + 2 * HALO
TPC = TC + 2 * HALO
TP = TPL + TPC
KPIECES = 6
KTP = NKT // KPIECES


class SAlloc:
    def __init__(self):
        self.held = set()
        self.i = 0
        self.j = 0

    def pair(self):
        for _ in range(2):
            p = self.i % 2
            self.i += 1
            if (2 * p) not in self.held and (2 * p + 1) not in self.held:
                return 2 * p
        raise RuntimeError("no free score pair")

    def one(self):
        for _ in range(2):
            b = self.j % 2
            self.j += 1
            if b not in self.held:
                self.held.add(b)
                return b, f"ps{b}"
        raise RuntimeError("no free stats bank")

    def release(self, b):
        self.held.discard(b)


def emit_B(cx, consts, ps, xT, mod, lam_ap, qT, pieces, kv_src, up_pad, glu_pad, invcnt, poolw, vecs, convw,
           w_out, w_ffn_in, w_ffn_out, xT_o_fn, last=False, bg=None):
    nc = cx.nc
    S = cx.S
    cf, cb, mhalf = consts
    onesD, ones128n, ones1, = cb[:, 0, :], cb[:, 3, :], cb[:, 4, :]
    ones256f = cf[:, 5, :]
    sa = SAlloc()
    groups_run = [g for g in range(len(GROUPS)) if not (last and GROUPS[g][0] >= TL)]
    piece_of = {}
    for pi, (k0, kc) in enumerate(pieces):
        for kt in range(k0, k0 + kc):
            piece_of[kt] = pi

    mods = cx.sb("b_mod", [128, 48, 2], F32)
    vec = cx.sb("b_vec", [128, NVEC], F32)
    cw = cx.sb("b_cw", [128, 2, 31], F32)
    pwf = cx.sb("b_pwf", [128, 2, 128], F32)
    pwb = cx.sb("b_pwb", [128, 2, 128], BF16)
    cx.dma("sp", vec[:], vecs, r=[], w=["b_vec"], key="b_vec")
    cx.dma("sp", cw[:], convw, r=[], w=["b_cw"], key="b_cw")
    cx.dma("sp", pwf[:], poolw, r=[], w=["b_pwf"], key="b_pwf")
    lamt = cx.sb("b_lamt", [128, 1], F32)
    cx.dma("sp", lamt[:], lam_ap, r=[], w=["b_lamt"], key="b_lamt", slow=True)
    cx.copy("dve", pwb[:], pwf[:], r=["b_pwf"], w=["b_pwb"])
    der = cx.sb("b_der", [128, 8], F32)
    cx.tt("dve", der[:, 0:1], lamt[:, 0:1], vec[:, 10:11], ALU.add, r=["b_vec", "b_lamt"], w=["b_der0"])
    cx.ts("dve", der[:, 0:1], der[:, 0:1], -1.0, ALU.mult, r=["b_der0"], w=["b_der0"])
    cx.tt("dve", der[:, 1:2], vec[:, 8:9], vec[:, 11:12], ALU.mult, r=["b_vec"], w=["b_der1"])
    cx.copy("dve", der[:, 2:6], vec[:, 4:8], r=["b_vec"], w=["b_der2"])
    derk = ["b_der0", "b_der1", "b_der2"]
    G2 = cx.sb("b_G2", [128, 8, 2], F32)
    g2h = cx.sb("b_g2h", [128, 8, 2], F32)
    sh2 = mods[:, 24:32, :]
    g1 = mods[:, 16:24, :]

    mixT = cx.sb("b_mix", [128, 8, T], BF16)

    cx.push()
    sb1 = cx.sb

    NP = 512 + 2 * HALO
    U = sb1("p_U", [128, 2, NP], F32)
    IC = sb1("p_IC", [128, 2, 512], F32)
    Sa = sb1("p_Sa", [128, 2, NP], F32)
    Sb = sb1("p_Sb", [128, 2, NP], F32)
    Sc = sb1("p_Sc", [128, 2, NP], F32)
    Sd = sb1("p_Sd", [128, 2, NP], F32)
    ptmp = sb1("p_tmp", [128, 2, 512], F32)
    pin = sb1("p_in", [128, 2, 512], BF16)
    G = sb1("c_G", [128, 2, NP], F32)
    acc = sb1("c_acc", [128, 2, 512], F32)
    sqa = sb1("c_sqa", [128, 2, 512], F32)
    msb = sb1("c_msb", [128, 512], F32)
    m2 = sb1("c_m2", [128, 512], F32)
    vr = sb1("c_vr", [128, 512], F32)
    dd = sb1("c_dd", [128, 2, 512], F32)
    zh = sb1("c_zh", [128, 2, 512], F32)
    cth = sb1("c_th", [128, 2, 512], F32)

    def b1_gen():
        for g in groups_run:
            off, n = GROUPS[g]
            po = off if off < TL else TPL + (off - TL)
            m = n + 2 * HALO
            cx.dma("sp", U[:, :, :m], up_pad[:, :, po:po + m], r=["halo_up"], w=["p_U"], key="p_U")
            cx.dma("sp", IC[:, :, :n], invcnt[:, :, off:off + n], r=[], w=["p_IC"], key="p_IC")
            yield
            cx.tt("dve", Sa[:, :, 1:m], U[:, :, 0:m - 1], U[:, :, 1:m], ALU.add, r=["p_U"], w=["p_Sa"])
            yield
            cx.tt("dve", Sb[:, :, 2:m - 1], Sa[:, :, 1:m - 2], Sa[:, :, 3:m], ALU.add, r=["p_Sa"], w=["p_Sb"])
            yield
            cx.tt("dve", Sc[:, 1, 4:m - 3], Sb[:, 1, 2:m - 5], Sb[:, 1, 6:m - 1], ALU.add, r=["p_Sb"], w=["p_Sc"])
            yield
            cx.tt("dve", Sd[64:128, 1, 8:m - 7], Sc[64:128, 1, 4:m - 11], Sc[64:128, 1, 12:m - 3], ALU.add,
                  r=["p_Sc"], w=["p_Sd"])
            yield
            srcs = [(Sa, "p_Sa", 0, 0), (Sb, "p_Sb", 64, 0), (Sc, "p_Sc", 0, 1), (Sd, "p_Sd", 64, 1)]
            for sbuf, sk, p0, ci in srcs:
                cx.tt("dve", ptmp[p0:p0 + 64, ci, :n], sbuf[p0:p0 + 64, ci, HALO:HALO + n],
                      IC[p0:p0 + 64, ci, :n], ALU.mult, r=[sk, "p_IC"], w=[f"p_tmp{p0}{ci}"])
                cx.tt("dve", pin[p0:p0 + 64, ci, :n], ptmp[p0:p0 + 64, ci, :n],
                      U[p0:p0 + 64, ci, HALO:HALO + n], ALU.subtract, r=[f"p_tmp{p0}{ci}", "p_U"],
                      w=[f"p_in{p0}{ci}"])
                yield
            pk = [f"p_in{p0}{ci}" for _, _, p0, ci in srcs]
            for ci in range(2):
                b, bk = sa.one()
                cx.mm(ps[:, b, :n], pwb[:, ci, :], pin[:, ci, :n], start=True, stop=True,
                      r=pk + ["b_pwb"], w=[bk])
                cx.ts("dve", mixT[:, 4 + ci, off:off + n], ps[:, b, :n], vec[:, ci:ci + 1], ALU.mult,
                      r=[bk, "b_vec"], w=[f"mix{4 + ci}_{g}"])
                sa.release(b)
                yield
            cx.dma("sp", G[:, :, :m], glu_pad[:, :, po:po + m], r=["halo_glu"], w=["c_G"], key="c_G")
            yield
            for ci in range(2):
                cx.ts("dve", acc[:, ci, :n], G[:, ci, 1:1 + n], cw[:, ci, 0:1], ALU.mult,
                      r=["c_G", "b_cw", "b_vec"], w=[f"c_acc{ci}"], s2=vec[:, 2 + ci:3 + ci], op1=ALU.add)
                yield
                for k in range(1, 31):
                    cx.stt(acc[:, ci, :n], G[:, ci, 1 + k:1 + k + n], cw[:, ci, k:k + 1], acc[:, ci, :n],
                           ALU.mult, ALU.add, r=["c_G", "b_cw", f"c_acc{ci}"], w=[f"c_acc{ci}"])
                    yield
            cx.tt("dve", sqa[:, :, :n], acc[:, :, :n], acc[:, :, :n], ALU.mult, r=["c_acc0", "c_acc1"], w=["c_sqa"])
            yield
            b1_, b1k = sa.one()
            for ci in range(2):
                cx.mm(ps[:, b1_, :n], ones256f, acc[:, ci, :n], start=(ci == 0), stop=(ci == 1),
                      r=["c_acc0", "c_acc1", "cmat_f"], w=[b1k])
            cx.copy("dve", msb[:, :n], ps[:, b1_, :n], r=[b1k], w=["c_msb"])
            sa.release(b1_)
            cx.tt("dve", m2[:, :n], msb[:, :n], msb[:, :n], ALU.mult, r=["c_msb"], w=["c_m2"])
            yield
            b2_, b2k = sa.one()
            for ci in range(2):
                cx.mm(ps[:, b2_, :n], ones256f, sqa[:, ci, :n], start=(ci == 0), stop=(ci == 1),
                      r=["c_sqa", "cmat_f"], w=[b2k])
            cx.stt(vr[:, :n], ps[:, b2_, :n], EPS, m2[:, :n], ALU.add, ALU.subtract, r=[b2k, "c_m2"], w=["c_vr"])
            sa.release(b2_)
            cx.act(vr[:, :n], vr[:, :n], AF.Ln, r=["c_vr"], w=["c_vr"])
            cx.act(vr[:, :n], vr[:, :n], AF.Exp, r=["c_vr"], w=["c_vr"], scale=-0.5)
            yield
            cx.tt("dve", dd[:, :, :n], acc[:, :, :n], msb[:, None, :n].to_broadcast([128, 2, n]), ALU.subtract,
                  r=["c_acc0", "c_acc1", "c_msb"], w=["c_dd"])
            cx.tt("dve", dd[:, :, :n], dd[:, :, :n], vr[:, None, :n].to_broadcast([128, 2, n]), ALU.mult,
                  r=["c_dd", "c_vr"], w=["c_dd"])
            yield
            for ci in range(2):
                cx.ts("dve", zh[:, ci, :n], dd[:, ci, :n], der[:, 2 + ci:3 + ci], ALU.mult,
                      r=["c_dd", "b_der2"], w=[f"c_zh{ci}"], s2=der[:, 4 + ci:5 + ci], op1=ALU.add)
            yield
            cx.act(cth[:, :, :n], zh[:, :, :n], AF.Exp, r=["c_zh0", "c_zh1"], w=["c_th"], scale=-1.0)
            yield
            cx.recip_act(cth[:, :, :n], cth[:, :, :n], r=["c_th"], w=["c_th"], one=cf[:, 4, 0:1])
            yield
            cx.tt("dve", mixT[:, 6:8, off:off + n], cth[:, :, :n], zh[:, :, :n], ALU.mult,
                  r=["c_th", "c_zh0", "c_zh1"], w=[f"mix6_{g}", f"mix7_{g}"])
            yield

    b1 = b1_gen()
    b1_done = [False]

    def bg_bank():
        b, bk = sa.one()
        return b, bk, (lambda: sa.release(b))

    bg_gen = bg(sb1, bg_bank) if bg is not None else None
    bg_done = [bg_gen is None]

    def bg_step():
        if bg_done[0]:
            return
        try:
            next(bg_gen)
        except StopIteration:
            bg_done[0] = True

    def b1_step():
        if b1_done[0]:
            return
        try:
            next(b1)
        except StopIteration:
            b1_done[0] = True

    Kh = sb1("a_K", [128, NKEY], BF16)
    Vh = sb1("a_V", [128, NKT, 128], BF16)
    qh_rot = [sb1(f"a_q{i}", [128, T], BF16) for i in range(2)]
    NPT = 6
    Pt = [sb1(f"a_P{i}", [128, 2, 512], BF16) for i in range(NPT)]
    rD = sb1("a_rD", [128, 2, 512], F32)
    oo = sb1("a_oo", [128, 2, 512], F32)
    pacc_rot = Rot(cx, "a_Pacc", [128, 2, 512], BF16, 2, alloc=sb1)
    QD = 16
    o_rot = Rot(cx, "a_o", [128, 512], F32, 2, alloc=sb1)
    osq_rot = Rot(cx, "a_osq", [128, 512], BF16, 2, alloc=sb1)
    r3_rot = Rot(cx, "a_r3", [128, 512], F32, 2, alloc=sb1)

    def load_kv(h, piece):
        k0, kc = pieces[piece]
        ksrc, vsrc = kv_src(h, piece)
        cx.dma("sp", Kh[:, k0 * 128:(k0 + kc) * 128], ksrc, r=["recv"], w=[f"K{piece}"], key=f"K{piece}")
        cx.dma("sp", Vh[:, k0:k0 + kc, :], vsrc, r=["recv"], w=[f"V{piece}"], key=f"V{piece}")

    def load_q(h):
        cx.dma("sp", qh_rot[h % 2][:], qT[h], r=[], w=[f"a_q{h % 2}"], key=f"a_q{h % 2}")

    load_q(0)
    for piece in range(len(pieces)):
        load_kv(0, piece)

    gorder = [g for g in [4, 0, 1, 2, 3] if g in groups_run]
    post_pending = []
    it = 0
    for h in range(4):
        qh = qh_rot[h % 2]
        qk = f"a_q{h % 2}"
        if h + 1 < 4:
            load_q(h + 1)
        for gi, g in enumerate(gorder):
            off, n = GROUPS[g]
            kts = list(range(2)) if off >= TL else list(range(NKT))
            last_group = (gi == len(gorder) - 1)
            pvq = []
            quad = []
            dq = []
            dstate = {"first": True, "acc": None}

            def flush_d(final, n=n, dq=dq, dstate=dstate):
                while dq:
                    src, srck = dq.pop(0)
                    lastd = final and not dq
                    for c in range(2):
                        cx.mm(ps[:, 6 + c, :n], ones1, src[:, c, :n], start=dstate["first"], stop=lastd,
                              r=["consts", srck], w=[f"ps{6 + c}"])
                    dstate["first"] = False

            def emit_pv(prev, final, n=n, quad=quad, dq=dq, dstate=dstate):
                pkt, ppt, pptk, pidx = prev
                pp = piece_of[pkt]
                flush_d(False)
                for c in range(2):
                    cx.mm(ps[:, 4 + c, :n], Vh[:, pkt, :], ppt[:, c, :n], start=(pidx == 0), stop=final,
                          r=[f"V{pp}", pptk], w=[f"ps{4 + c}"])
                quad.append((ppt, pptk))
                if len(quad) == 2:
                    dstate["acc"] = pacc_rot.next()
                    acc, acck = dstate["acc"]
                    cx.tt("dve", acc[:, :, :n], quad[0][0][:, :, :n], ppt[:, :, :n], ALU.add,
                          r=[quad[0][1], pptk], w=[acck])
                elif len(quad) > 2:
                    acc, acck = dstate["acc"]
                    cx.tt("dve", acc[:, :, :n], acc[:, :, :n], ppt[:, :, :n], ALU.add, r=[acck, pptk], w=[acck])
                if len(quad) == QD or final:
                    dq.append(dstate["acc"] if len(quad) > 1 else (ppt, pptk))
                    quad.clear()
                if final:
                    flush_d(True)

            for idx, kt in enumerate(kts):
                piece = piece_of[kt]
                b0 = sa.pair()
                for c in range(2):
                    cx.mm(ps[:, b0 + c, :n], Kh[64 * c:64 * c + 64, kt * 128:(kt + 1) * 128],
                          qh[64 * c:64 * c + 64, off:off + n], start=True, stop=True,
                          r=[f"K{piece}", qk], w=[f"ps{b0 + c}"])
                pt = Pt[it % NPT]
                ptk = f"a_P{it % NPT}"
                cx.act(pt[:, :, :n], ps[:, b0:b0 + 2, :n], AF.Exp, r=[f"ps{b0}", f"ps{b0 + 1}"], w=[ptk],
                       scale=0.125)
                it += 1
                pvq.append((kt, pt, ptk, idx))
                if len(pvq) > 2:
                    prev = pvq.pop(0)
                    emit_pv(prev, False)
                    pkt = prev[0]
                    pp = piece_of[pkt]
                    if last_group and h + 1 < 4 and pkt == pieces[pp][0] + pieces[pp][1] - 1 \
                            and pp != len(pieces) - 1:
                        load_kv(h + 1, pp)
                if it % 2 == 0:
                    b1_step()
                if it % 5 == 2:
                    bg_step()
                if idx in (1, 5) and post_pending:
                    for gen in list(post_pending):
                        try:
                            next(gen)
                        except StopIteration:
                            post_pending.remove(gen)
            while pvq:
                prev = pvq.pop(0)
                emit_pv(prev, not pvq)
            if last_group and h + 1 < 4:
                load_kv(h + 1, len(pieces) - 1)
            for gen in list(post_pending):
                for _ in gen:
                    pass
                post_pending.remove(gen)
            cx.copy("dve", oo[:, :, :n], ps[:, 4:6, :n], r=["ps4", "ps5"], w=["a_oo"])
            o, ok = o_rot.next()
            osq, osqk = osq_rot.next()

            def post(h=h, off=off, n=n, o=o, ok=ok, osq=osq, osqk=osqk):
                cx.recip_act(rD[:, :, :n], ps[:, 6:8, :n], r=["ps6", "ps7"], w=["a_rD"])
                cx.tt("dve", oo[:, :, :n], oo[:, :, :n], rD[:, :, :n], ALU.mult, r=["a_oo", "a_rD"], w=["a_oo"])
                cx.stt(o[:, :n], oo[:, 1, :n], der[:, 0:1], oo[:, 0, :n], ALU.mult, ALU.add,
                       r=["a_oo", "b_der0"], w=[ok])
                cx.tt("dve", osq[:, :n], o[:, :n], o[:, :n], ALU.mult, r=[ok], w=[osqk])
                yield
                b, bk = sa.one()
                cx.mm(ps[:, b, :n], ones128n, osq[:, :n], start=True, stop=True, r=[osqk, "consts"], w=[bk])
                r3, r3k = r3_rot.next()
                rstd_from_psum(cx, ps[:, b, :n], bk, r3[:, :n], r3k, mhalf[:, :n], n)
                sa.release(b)
                cx.stt(mixT[:, h, off:off + n], o[:, :n], der[:, 1:2], r3[:, :n], ALU.mult, ALU.mult,
                       r=[ok, r3k, "b_der1"], w=[f"mix{h}_{off}"])
                yield

            post_pending.append(post())
    for gen in list(post_pending):
        for _ in gen:
            pass
    while not b1_done[0]:
        b1_step()
    while not bg_done[0]:
        bg_step()

    cx.pop()
    cx.dma("sp", mods[:], mod, r=[], w=["b_mod"], key="b_mod")
    cx.stt(G2[:], mods[:, 32:40, :], 1.0, vec[:, 12:20, None].to_broadcast([128, 8, 2]), ALU.add, ALU.mult,
           r=["b_mod", "b_vec"], w=["modc"])
    cx.copy("dve", g2h[:], mods[:, 40:48, :], r=["b_mod"], w=["modc2"])
    banks = BankRot([0, 1, 2, 3, 4, 5, 6, 7])
    wo_sb = cx.sb("f_wout", [128, 8, 1024], BF16)
    for hf_ in range(2):
        cx.dma("pool", wo_sb[:, :, hf_ * 512:(hf_ + 1) * 512], w_out[:, :, hf_ * 512:(hf_ + 1) * 512], r=[],
               w=[f"f_wout{hf_}"], key=f"f_wout{hf_}")
    xg_rot = Rot(cx, "f_xg", [128, 8, 512], F32, 2)
    sq = cx.sb("f_sq", [128, 8, 512], BF16)
    u = cx.sb("f_u", [128, 8, 512], F32)
    rs_rot = Rot(cx, "f_rs", [128, 512], F32, 2)
    aT = cx.sb("f_aT", [128, NJ, 512], BF16)
    win_rot = Rot(cx, "f_win", [128, 8, 256], BF16, 4)
    wo2_rot = Rot(cx, "f_wo2", [128, NJ, 128], BF16, 3)
    th_rot = Rot(cx, "f_th", [128, 512], F32, 2)
    s_rot = Rot(cx, "f_s", [128, 512], F32, 2)

    h2_rot = Rot(cx, "f_h2T", [128, 8, 512], BF16, 2)

    def outproj_norm(g):
        off, n = GROUPS[g]
        col = 1 if off >= TL else 0
        xg, xgk = xg_rot.next()
        cx.dma("sp", xg[:, :, :n], xT[:, :, off:off + n], r=[], w=[xgk], key=xgk)
        mixk = [f"mix{m}_{off}" for m in range(4)] + [f"mix{m}_{g}" for m in range(4, 8)]
        for dc in range(8):
            b, bk = banks.next()
            for m in range(8):
                cx.mm(ps[:, b, :n], wo_sb[:, m, dc * 128:(dc + 1) * 128], mixT[:, m, off:off + n],
                      start=(m == 0), stop=(m == 7), r=[f"f_wout{dc // 4}", mixk[m]], w=[bk])
            cx.stt(xg[:, dc, :n], ps[:, b, :n], g1[:, dc, col:col + 1], xg[:, dc, :n], ALU.mult, ALU.add,
                   r=[bk, "b_mod", xgk], w=[xgk])
            banks.release(b)
        h2T, h2k = h2_rot.next()
        gen = norm_mod_gen(cx, g, xg, xgk, h2T, h2k, sq, u, rs_rot, onesD, mhalf, ps, banks, G2, sh2)
        return xg, xgk, h2T, h2k, gen

    def run_gen(gen):
        for _ in gen:
            pass

    nxt = outproj_norm(groups_run[0])
    run_gen(nxt[4])
    for gi, g in enumerate(groups_run):
        off, n = GROUPS[g]
        col = 1 if off >= TL else 0
        xg, xgk, h2T, h2k, _ = nxt
        ngen = None
        if gi + 1 < len(groups_run):
            nxt = outproj_norm(groups_run[gi + 1])
            ngen = nxt[4]
        for j in range(NJ):
            wj, wjk = win_rot.next()
            cx.dma("pool", wj[:], w_ffn_in[j], r=[], w=[wjk], key=wjk)
            bg, bgk = banks.next()
            bu, buk = banks.next()
            for c in range(8):
                cx.mm(ps[:, bg, :n], wj[:, c, 0:128], h2T[:, c, :n], start=(c == 0), stop=(c == 7),
                      r=[wjk, h2k], w=[bgk])
            for c in range(8):
                cx.mm(ps[:, bu, :n], wj[:, c, 128:256], h2T[:, c, :n], start=(c == 0), stop=(c == 7),
                      r=[wjk, h2k], w=[buk])
            th, thk = th_rot.next()
            cx.act(th[:, :n], ps[:, bg, :n], AF.Exp, r=[bgk], w=[thk], scale=-1.0)
            sv, svk = s_rot.next()
            cx.recip_act(th[:, :n], th[:, :n], r=[thk], w=[thk], one=cf[:, 4, 0:1])
            cx.tt("dve", sv[:, :n], th[:, :n], ps[:, bg, :n], ALU.mult, r=[thk, bgk], w=[svk])
            banks.release(bg)
            cx.tt("dve", aT[:, j, :n], sv[:, :n], ps[:, bu, :n], ALU.mult, r=[svk, buk], w=[f"f_aT{j}"])
            banks.release(bu)
            if ngen is not None and j >= 6 and j % 2 == 0:
                try:
                    next(ngen)
                except StopIteration:
                    ngen = None
        if ngen is not None:
            run_gen(ngen)
        for dc in range(8):
            w2, w2k = wo2_rot.next()
            cx.dma("pool", w2[:], w_ffn_out[dc], r=[], w=[w2k], key=w2k)
            b, bk = banks.next()
            for j in range(NJ):
                cx.mm(ps[:, b, :n], w2[:, j, :], aT[:, j, :n], start=(j == 0), stop=(j == NJ - 1),
                      r=[w2k, f"f_aT{j}"], w=[bk])
            cx.stt(xg[:, dc, :n], ps[:, b, :n], g2h[:, dc, col:col + 1], xg[:, dc, :n], ALU.mult, ALU.add,
                   r=[bk, "modc2", xgk], w=[xgk])
            banks.release(b)
        cx.dma("sp", xT_o_fn(off, n), xg[:, :, :n], r=[xgk], w=[cx.uid("xT_o")], key=xgk)


def invcnt_table(core):
    out = np.zeros((128, 2, T), np.float32)
    wins = {(0, 0): 2, (64, 0): 4, (0, 1): 8, (64, 1): 16}
    for (p0, ci), w in wins.items():
        for off, L, base in ((0, SEQ, (core % 4) * TL), (TL, TC, 0)):
            nt = TL if off == 0 else TC
            t = base + np.arange(nt)
            lo = np.clip(t - w // 2, 0, L)
            hi = np.clip(t + w - w // 2, 0, L)
            out[p0:p0 + 64, ci, off:off + nt] = (1.0 / (hi - lo))[None, :]
    return out


CC_GROUPS = [[0, 1, 2, 3], [4, 5, 6, 7]]
PIECES = [(0, 2)] + [(2 + 8 * i, 8) for i in range(8)]


def build_fused(depth=DEPTH):
    cx = Ctx()
    nc = cx.nc
    x0T = cx.din("x0T", [128, 8, T], F32)
    cv = cx.din("cv", [128, 8, 2], F32)
    w_mod = cx.din("w_mod", [DEPTH, D, 6 * D], F32)
    b_mod_fm = cx.din("b_mod_fm", [DEPTH, 128, 48], F32)
    lam_in = cx.din("lam_in", [128, DEPTH, 4, 64], F32)
    cmat = cx.din("cmat", [128, 6, 128], F32)
    n1g = cx.din("n1g", [DEPTH, 128, 8], F32)
    w_in = cx.din("w_in", [DEPTH, 128, 8, 2304], F32)
    qkg = cx.din("qkg", [DEPTH, 128, 2], F32)
    poolw = cx.din("poolw", [DEPTH, 128, 2, 128], F32)
    vecs = cx.din("vecs", [DEPTH, 128, NVEC], F32)
    convw = cx.din("convw", [DEPTH, 128, 2, 31], F32)
    w_out = cx.din("w_out", [DEPTH, 128, 8, 1024], F32)
    w_ffn_in = cx.din("w_ffn_in", [DEPTH, NJ, 128, 8, 256], F32)
    w_ffn_out = cx.din("w_ffn_out", [DEPTH, 8, 128, NJ, 128], F32)
    cosT = cx.din("cosT", [128, TL], F32)
    sinT = cx.din("sinT", [128, TL], F32)
    invcnt = cx.din("invcnt", [128, 2, T], F32)
    selT = cx.din("selT", [128, 8], F32)
    outT = cx.dout("outT", [128, 8, TL], F32)
    modT = cx.dscratch("modT", [128, DEPTH, 48, 2], F32)
    lamd = cx.dscratch("lamd", [128, DEPTH], F32)
    xs = [cx.dscratch(f"xs{i}", [128, 8, T], F32) for i in range(2)]
    qTs = cx.dscratch("qTs", [4, 128, T], BF16)
    kcs = cx.dscratch("kcs", [4, 128, TC], BF16)
    vcs = cx.dscratch("vcs", [4, 128, 2, 128], BF16)
    up_pad = cx.dscratch("up_pad", [128, 2, TP], F32)
    glu_pad = cx.dscratch("glu_pad", [128, 2, TP], F32)
    sendK = [[cx.dscratch(f"sendK{p}{h}", [512, 1024], BF16) for h in range(2)] for p in range(2)]
    sendV = [[cx.dscratch(f"sendV{p}{h}", [512, 1024], BF16) for h in range(2)] for p in range(2)]
    recvK = [[cx.dscratch(f"recvK{p}{h}", [2048, 1024], BF16) for h in range(2)] for p in range(2)]
    recvV = [[cx.dscratch(f"recvV{p}{h}", [2048, 1024], BF16) for h in range(2)] for p in range(2)]
    sendE = [cx.dscratch(f"sendE{p}", [256, 64], F32) for p in range(2)]
    recvE = [cx.dscratch(f"recvE{p}", [1024, 64], F32) for p in range(2)]

    ps = cx.es.enter_context(nc.psum_tensor("ps", [128, 8, 512], F32))
    consts = load_consts(cx, cmat)
    sel = cx.sb("sel_sb", [128, 8], F32)
    cx.dma("sp", sel[:], selT, r=[], w=["selT"], key="selT")
    E = cx.sb("E", [128, 4, 2, 64], F32)
    H = cx.sb("H", [128, 2, 2, 2, HALO], F32)
    zt = cx.sb("zt", [128, 2, HALO], F32)
    cx.memset("dve", zt[:], 0.0, w=["zt"])
    zi = 0
    for pad in (up_pad, glu_pad):
        for o in (0, HALO + TL, TPL, TPL + HALO + TC):
            cx.dma("sp", pad[:, :, o:o + HALO], zt[:], r=["zt"], w=[f"zpad{zi}"], key=f"zpad{zi % 4}")
            zi += 1

    silb = cx.sb("silb", [128, 8, 2], BF16)
    cx.push("M_")
    emit_mod_pre(cx, cv, lam_in, lamd, silb)
    for _ in mod_layer_gen(cx, ps, [(0, 0, 6)], silb, w_mod, b_mod_fm, modT, lambda: (0, "ps0", lambda: None),
                           cx.sb):
        pass
    cx.pop()

    for l in range(depth):
        par = l % 2
        last = (l == depth - 1)
        x_in = x0T if l == 0 else xs[(l - 1) % 2]

        def k_sink(h, off, n, par=par):
            if off >= TL:
                return kcs[h, :, :], None
            return (sendK[par][off // 1024][h * 128:(h + 1) * 128, off % 1024: off % 1024 + n],
                    f"sendK{off // 1024}")

        def v_sink(ti, par=par):
            if ti >= 16:
                return vcs.rearrange("h p t d -> p h t d")[:, :, ti - 16, :], None
            return (sendV[par][ti // 8].rearrange("(h p) c -> p h c", p=128)[:, :, (ti % 8) * 128:(ti % 8 + 1) * 128],
                    f"sendV{ti // 8}")

        def pad_sink(pad):
            def f(ci, off, n):
                o = HALO + off if off < TL else TPL + HALO + (off - TL)
                return pad[:, ci, o:o + n]
            return f

        def edge_sink(tz, ci, side, off, par=par):
            if (side == 0 and off == 0) or (side == 1 and off == TL - 512):
                return sendE[par][ci * 128:(ci + 1) * 128, tz * 32 + side * HALO: tz * 32 + (side + 1) * HALO]
            return None

        def issue_cc(name, par=par):
            tab = {"K0": (sendK[par][0], recvK[par][0], "sendK0", 0), "V0": (sendV[par][0], recvV[par][0], "sendV0", 1),
                   "K1": (sendK[par][1], recvK[par][1], "sendK1", 2), "V1": (sendV[par][1], recvV[par][1], "sendV1", 3),
                   "E": (sendE[par], recvE[par], "sendE", 4)}
            sbuf_, rbuf_, skey, ci_ = tab[name]
            cx.S.op("pool", lambda e, a=sbuf_, b=rbuf_: e.collective_compute(
                "AllGather", ALU.bypass, replica_groups=CC_GROUPS, ins=[a], outs=[b]),
                r=[skey], w=["recv", f"recv_{skey}"], key=f"cc{ci_}", inc1=True)

        sinks = {"q": lambda h, off, n: (qTs[h, :, off:off + n], None), "k": k_sink, "v": v_sink,
                 "up": pad_sink(up_pad), "glu": pad_sink(glu_pad), "edge": edge_sink, "cc": issue_cc}
        cx.push(f"A{l}_")
        emit_A(cx, consts, ps, x_in, modT[:, l], n1g[l], w_in[l], qkg[l], cosT, sinT, sinks)
        cx.pop()
        cx.dma("sp", E[:], recvE[par].rearrange("(j c p) x -> p j c x", j=4, c=2, p=128), r=["recv_sendE"],
               w=["E"], key="E")
        for tz in range(2):
            for side in range(2):
                c0 = tz * 32 + (HALO if side == 0 else 0)
                hv = H[:, tz, side, :, :]
                for j in range(4):
                    sc = sel[:, 4 * side + j: 4 * side + j + 1]
                    if j == 0:
                        cx.ts("dve", hv, E[:, j, :, c0:c0 + HALO], sc, ALU.mult, r=["E", "selT"], w=[f"H{tz}{side}"])
                    else:
                        cx.stt(hv, E[:, j, :, c0:c0 + HALO], sc, hv, ALU.mult, ALU.add,
                               r=["E", "selT", f"H{tz}{side}"], w=[f"H{tz}{side}"])
                pad = up_pad if tz == 0 else glu_pad
                o = 0 if side == 0 else HALO + TL
                cx.dma("sp", pad[:, :, o:o + HALO], hv, r=[f"H{tz}{side}"], w=["halo_up" if tz == 0 else "halo_glu"],
                       key=f"zpad{2 * tz + side}")
        def kv_src(h, piece, par=par):
            if piece == 0:
                return kcs[h], vcs[h]
            j, hf = (piece - 1) // 2, (piece - 1) % 2
            rows = slice(j * 512 + h * 128, j * 512 + (h + 1) * 128)
            return recvK[par][hf][rows, :], recvV[par][hf][rows, :].rearrange("p (t d) -> p t d", d=128)

        if last:
            xo_fn = lambda off, n: outT[:, :, off:off + n]
        else:
            xo_fn = lambda off, n, l=l: xs[l % 2][:, :, off:off + n]
        cx.push(f"B{l}_")
        jobs = ([(0, 6, 16)] if l == 0 else []) + ([(l + 1, 0, 16)] if not last else [])
        bg = None
        if jobs:
            bg = lambda alloc, bank_fn, jobs=jobs: mod_layer_gen(cx, ps, jobs, silb, w_mod, b_mod_fm, modT,
                                                                 bank_fn, alloc)
        emit_B(cx, consts, ps, x_in, modT[:, l], lamd[:, l:l + 1], qTs, PIECES, kv_src, up_pad, glu_pad, invcnt,
               poolw[l], vecs[l], convw[l], w_out[l], w_ffn_in[l], w_ffn_out[l], xo_fn, last=last, bg=bg)
        cx.pop()
    return cx.finish()


_CACHE = {}


def layer_consts(l):
    return 0.8 - 0.6 * math.exp(-0.3 * l)


def prep_weights(inputs):
    f32 = np.float32
    w = {}
    w["n1g"] = np.stack([fm(inputs["norm1_g"][l], 8) for l in range(DEPTH)])
    w["w_in"] = np.stack([fm_w(np.asarray(inputs["w_in"][l], f32)) for l in range(DEPTH)])
    w["qkg"] = np.stack([np.stack([np.tile(inputs["q_norm_g"][l], 2), np.tile(inputs["k_norm_g"][l], 2)], -1)
                         for l in range(DEPTH)]).astype(f32)
    poolw = np.zeros((DEPTH, 128, 2, 128), f32)
    vecs = np.zeros((DEPTH, 128, NVEC), f32)
    for l in range(DEPTH):
        for ci in range(2):
            poolw[l, 0:64, ci, 0:64] = inputs["pool_w"][l][2 * ci]
            poolw[l, 64:128, ci, 64:128] = inputs["pool_w"][l][2 * ci + 1]
        li = layer_consts(l)
        vecs[l, :, 0:2] = fm(inputs["pool_scale"][l], 2)
        vecs[l, :, 2:4] = fm(inputs["conv_dw_b"][l], 2)
        vecs[l, :, 4:6] = fm(inputs["conv_ln_g"][l], 2)
        vecs[l, :, 6:8] = fm(inputs["conv_ln_b"][l], 2)
        vecs[l, :, 8] = inputs["subln_g"][l]
        vecs[l, :, 10] = li
        vecs[l, :, 11] = 1.0 - li
        vecs[l, :, 12:20] = fm(inputs["norm2_g"][l], 8)
    w["poolw"] = poolw
    w["vecs"] = vecs
    w["convw"] = np.stack([np.asarray(inputs["conv_dw_w"][l], f32).T.reshape(2, 128, 31).transpose(1, 0, 2)
                           for l in range(DEPTH)])
    w["w_out"] = np.stack([fm_w(np.asarray(inputs["w_out"][l], f32)) for l in range(DEPTH)])
    wfi = np.asarray(inputs["w_ffn_in"], f32)
    wg = wfi[:, :, :FF].reshape(DEPTH, 8, 128, NJ, 128)
    wu = wfi[:, :, FF:].reshape(DEPTH, 8, 128, NJ, 128)
    w["w_ffn_in"] = np.ascontiguousarray(np.concatenate([wg, wu], -1).transpose(0, 3, 2, 1, 4))
    w["w_ffn_out"] = np.ascontiguousarray(
        np.asarray(inputs["w_ffn_out"], f32).reshape(DEPTH, NJ, 128, 8, 128).transpose(0, 3, 2, 1, 4))
    return {k: np.ascontiguousarray(v, dtype=f32) for k, v in w.items()}


def make_in_maps(inputs):
    f32 = np.float32
    x = np.asarray(inputs["x"], f32)
    c = np.asarray(inputs["c"], f32)
    ctx = np.asarray(inputs["ctx"], f32)
    c_ctx = np.asarray(inputs["c_ctx"], f32)
    w = prep_weights(inputs)
    lam_in = np.stack([inputs["lambda_q1"], inputs["lambda_k1"], inputs["lambda_q2"], inputs["lambda_k2"]], 1)
    shared = dict(w)
    shared["lam_in"] = np.ascontiguousarray(np.broadcast_to(np.asarray(lam_in, f32)[None], (128, DEPTH, 4, 64)))
    shared["w_mod"] = np.ascontiguousarray(np.asarray(inputs["w_mod"], f32))
    shared["b_mod_fm"] = np.ascontiguousarray(np.asarray(inputs["b_mod"], f32).reshape(DEPTH, 48, 128).transpose(0, 2, 1))
    shared["cmat"] = const_mats()
    in_maps = []
    for i in range(NCORE):
        b, r = i // 4, i % 4
        t0 = r * TL
        m = dict(shared)
        xall = np.concatenate([x[b, t0:t0 + TL], ctx[b]], 0)
        m["x0T"] = np.ascontiguousarray(xall.T.reshape(8, 128, T).transpose(1, 0, 2))
        cvv = np.stack([c[b], c_ctx], -1)
        m["cv"] = np.ascontiguousarray(cvv.reshape(8, 128, 2).transpose(1, 0, 2))
        m["cosT"], m["sinT"] = rope_tables(i)
        m["invcnt"] = invcnt_table(i)
        sel = np.zeros((128, 8), f32)
        if r > 0:
            sel[:, r - 1] = 1.0
        if r < 3:
            sel[:, 4 + r + 1] = 1.0
        m["selT"] = sel
        in_maps.append(m)
    return in_maps


def kernel(**inputs):
    in_maps = make_in_maps(inputs)
    if "nc" not in _CACHE:
        _CACHE["nc"] = build_fused()
    res = run_bass_kernel_spmd(_CACHE["nc"], in_maps, core_ids=list(range(NCORE))).results
    out = np.zeros((2, SEQ, D), np.float32)
    for i in range(NCORE):
        b, t0 = i // 4, (i % 4) * TL
        xo = res[i]["outT"].transpose(1, 0, 2).reshape(D, TL)
        out[b, t0:t0 + TL] = xo.T
    return out
```

```python
import math
from contextlib import ExitStack

import numpy as np
import ml_dtypes

import concourse.bass as bass
import concourse.mybir as mybir
from concourse.bass_utils import run_bass_kernel_spmd

F32 = mybir.dt.float32
BF16 = mybir.dt.bfloat16
AF = mybir.ActivationFunctionType
ALU = mybir.AluOpType
AX = mybir.AxisListType
NPBF = ml_dtypes.bfloat16

D = 1024
DEPTH = 4
NCORE = 8
TL = 2048
TC = 256
T = TL + TC
SEQ = 8192
NKEY = SEQ + TC
NKT = NKEY // 128
FF = 2816
NJ = FF // 128
EPS = 1e-6
HALO = 16
GROUPS = [(0, 512), (512, 512), (1024, 512), (1536, 512), (2048, 256)]


class Sched:
    ENGS = ("pe", "act", "dve", "pool", "sp")

    def __init__(self, nc):
        self.nc = nc
        self.q = {e: [] for e in self.ENGS}
        self.cnt = {}
        self.res = {}
        self.waited = {e: {} for e in self.ENGS}
        self.bar = {}

    def barrier(self):
        self.bar = dict(self.cnt)

    def op(self, eng, fn, r=(), w=(), key=None, inc1=False):
        if key is None:
            sem, inc = "S_" + eng, 1
        elif inc1:
            sem, inc = "C_" + key, 1
        else:
            sem, inc = "D_" + key, 16
        deps = dict(self.bar)

        def need(sv):
            s, v = sv
            if deps.get(s, 0) < v:
                deps[s] = v

        for k in r:
            st = self.res.get(k)
            if st and st[0]:
                need(st[0])
            if st and k.startswith("ps"):
                for sv in st[1].items():
                    if sv[0] != sem:
                        need(sv)
        for k in w:
            st = self.res.get(k)
            if st:
                if st[0]:
                    need(st[0])
                for sv in st[1].items():
                    need(sv)
        waits = []
        for s, v in deps.items():
            if eng == "pe" and s == "S_pe":
                continue
            if self.waited[eng].get(s, 0) >= v:
                continue
            self.waited[eng][s] = v
            waits.append((s, v))
        val = self.cnt.get(sem, 0) + inc
        self.cnt[sem] = val
        self.q[eng].append((waits, fn, sem, inc))
        for k in r:
            st = self.res.setdefault(k, [None, {}])
            if st[1].get(sem, 0) < val:
                st[1][sem] = val
        for k in w:
            self.res[k] = [(sem, val), {}]

    def finish(self):
        waits = [(s, v) for s, v in self.cnt.items() if s.startswith("D_") or s.startswith("C_")]
        self.q["sp"].append((waits, None, None, 0))

    def emit(self, es):
        nc = self.nc
        sems = {n: es.enter_context(nc.semaphore(n)) for n in sorted(self.cnt)}
        block = es.enter_context(nc.Block())

        def run(name):
            def f(e):
                for waits, fn, sem, inc in self.q[name]:
                    attach = fn is not None and waits and not sem.startswith("C_")
                    for s, v in (waits[:-1] if attach else waits):
                        e.wait_ge(sems[s], v)
                    if fn is not None:
                        ins = fn(e)
                        if attach:
                            ins._wait_ge(sems[waits[-1][0]], waits[-1][1])
                        ins.then_inc(sems[sem], inc)
            return f

        block.tensor(run("pe"))
        block.scalar(run("act"))
        block.vector(run("dve"))
        block.gpsimd(run("pool"))
        block.sync(run("sp"))


class Ctx:
    def __init__(self):
        self.nc = bass.Bass("TRN2", target_bir_lowering=False)
        self.es = ExitStack()
        self.S = Sched(self.nc)
        self.n = 0
        self.stacks = [self.es]
        self.pfx = ""

    def push(self, pfx=None):
        if pfx is not None:
            self.pfx = pfx
        st = ExitStack()
        self.stacks.append(st)
        return st

    def pop(self):
        self.stacks.pop().close()
        self.S.barrier()

    def sb(self, name, shape, dt):
        return self.stacks[-1].enter_context(self.nc.sbuf_tensor(self.pfx + name, list(shape), dt))

    def din(self, name, shape, dt):
        return self.nc.dram_tensor(name, list(shape), dt, kind="ExternalInput").ap()

    def dout(self, name, shape, dt):
        return self.nc.dram_tensor(name, list(shape), dt, kind="ExternalOutput").ap()

    def dscratch(self, name, shape, dt):
        return self.nc.dram_tensor(name, list(shape), dt, kind="Internal").ap()

    def uid(self, p):
        self.n += 1
        return f"{p}{self.n}"

    def dma(self, q, out, in_, r, w, key, slow=False):
        if slow:
            self.S.op(q, lambda e: e.dma_start(out=out, in_=in_, allow_slow_non_contiguous=True), r=r, w=w, key=key)
        else:
            self.S.op(q, lambda e: e.dma_start(out=out, in_=in_), r=r, w=w, key=key)

    def mm(self, out, lhsT, rhs, start, stop, r, w):
        self.S.op("pe", lambda e: e.matmul(out, lhsT, rhs, start=start, stop=stop), r=r, w=w)

    def act(self, out, in_, func, r, w, bias=None, scale=None):
        kw = {}
        if bias is not None:
            kw["bias"] = bias
        if scale is not None:
            kw["scale"] = scale
        self.S.op("act", lambda e: e.activation(out=out, in_=in_, func=func, **kw), r=r, w=w)

    def tt(self, eng, out, in0, in1, op, r, w):
        self.S.op(eng, lambda e: e.tensor_tensor(out=out, in0=in0, in1=in1, op=op), r=r, w=w)

    def ts(self, eng, out, in0, s1, op0, r, w, s2=None, op1=None):
        if op1 is None:
            self.S.op(eng, lambda e: e.tensor_scalar(out=out, in0=in0, scalar1=s1, scalar2=None, op0=op0),
                      r=r, w=w)
        else:
            self.S.op(eng, lambda e: e.tensor_scalar(out=out, in0=in0, scalar1=s1, scalar2=s2, op0=op0, op1=op1),
                      r=r, w=w)

    def stt(self, out, in0, scalar, in1, op0, op1, r, w):
        self.S.op("dve", lambda e: e.scalar_tensor_tensor(out=out, in0=in0, scalar=scalar, in1=in1,
                                                          op0=op0, op1=op1), r=r, w=w)

    def copy(self, eng, out, in_, r, w):
        self.S.op(eng, lambda e: e.tensor_copy(out=out, in_=in_), r=r, w=w)

    def memset(self, eng, ap, val, w):
        self.S.op(eng, lambda e: e.memset(ap, val), w=w)

    def recip(self, out, in_, r, w):
        self.S.op("dve", lambda e: e.reciprocal(out=out, in_=in_), r=r, w=w)

    def recip_act(self, out, in_, r, w, one=None):
        if one is not None:
            self.act(out, in_, AF.Ln, r=list(r) + ["cmat_f"], w=w, bias=one)
        else:
            self.act(out, in_, AF.Ln, r=r, w=w)
        self.act(out, out, AF.Exp, r=w, w=w, scale=-1.0)

    def finish(self):
        self.S.finish()
        self.S.emit(self.es)
        self.es.close()
        return self.nc


def rstd_from_psum(cx, ps_ap, ps_key, tmp_ap, tmp_key, mhalf_ap, n):
    cx.act(tmp_ap, ps_ap, AF.Ln, r=[ps_key, "consts"], w=[tmp_key], bias=mhalf_ap[:, 0:1])
    cx.act(tmp_ap, tmp_ap, AF.Exp, r=[tmp_key], w=[tmp_key], scale=-0.5)


def emit_mod_pre(cx, cv, lam_in, lam_out, silb):
    cvs = cx.sb("m_cv", [128, 8, 2], F32)
    th = cx.sb("m_th", [128, 8, 2], F32)
    sil = cx.sb("m_sil", [128, 8, 2], F32)
    lamt = cx.sb("m_lamt", [128, 4, 4, 64], F32)
    prod = cx.sb("m_prod", [128, 4, 2, 64], F32)
    lsum = cx.sb("m_lsum", [128, 4, 2], F32)
    lexp = cx.sb("m_lexp", [128, 4, 2], F32)
    lams = cx.sb("m_lams", [128, 4], F32)
    cx.dma("sp", cvs[:], cv, r=[], w=["m_cv"], key="m_cv")
    cx.dma("sp", lamt[:], lam_in, r=[], w=["m_lamt"], key="m_lamt")
    cx.act(th[:], cvs[:], AF.Exp, r=["m_cv"], w=["m_th"], scale=-1.0)
    cx.ts("dve", th[:], th[:], 1.0, ALU.add, r=["m_th"], w=["m_th"])
    cx.recip(th[:], th[:], r=["m_th"], w=["m_th"])
    cx.tt("dve", sil[:], th[:], cvs[:], ALU.mult, r=["m_th", "m_cv"], w=["m_sil"])
    cx.copy("dve", silb[:], sil[:], r=["m_sil"], w=["silb"])
    cx.tt("dve", prod[:, :, 0, :], lamt[:, :, 0, :], lamt[:, :, 1, :], ALU.mult, r=["m_lamt"], w=["m_prod0"])
    cx.tt("dve", prod[:, :, 1, :], lamt[:, :, 2, :], lamt[:, :, 3, :], ALU.mult, r=["m_lamt"], w=["m_prod1"])
    cx.S.op("dve", lambda e: e.tensor_reduce(out=lsum[:], in_=prod[:], axis=AX.X, op=ALU.add),
            r=["m_prod0", "m_prod1"], w=["m_lsum"])
    cx.act(lexp[:], lsum[:], AF.Exp, r=["m_lsum"], w=["m_lexp"])
    cx.tt("dve", lams[:], lexp[:, :, 0], lexp[:, :, 1], ALU.subtract, r=["m_lexp"], w=["m_lams"])
    cx.dma("sp", lam_out, lams[:], r=["m_lams"], w=["lam_out"], key="m_lamo")


def mod_layer_gen(cx, ps, jobs, silb, w_mod, b_mod_fm, modT_out, bank_fn, alloc):
    SW = 384
    slab = [alloc(f"ml_slab{i}", [128, 8, SW], BF16) for i in range(2)]
    modsb = alloc("ml_modsb", [128, 48, 2], F32)
    bfm = alloc("ml_bfm", [128, 48], F32)
    NE = SW // 128
    cnt = 0
    for l, s_lo, s_hi in jobs:
        cx.dma("sp", bfm[:], b_mod_fm[l], r=[], w=["ml_bfm"], key="ml_bfm")
        for sidx in range(s_lo, s_hi):
            sl = slab[cnt % 2]
            sk = f"ml_slab{cnt % 2}"
            cnt += 1
            e0 = sidx * SW
            cx.dma("pool", sl[:], w_mod[l, :, e0:e0 + SW].rearrange("(c p) e -> p c e", p=128), r=[], w=[sk],
                   key=sk)
            yield
            bank, pk, rel = bank_fn()
            for j in range(NE):
                out = ps[:, bank, 2 * j:2 * j + 2]
                for c in range(8):
                    cx.mm(out, sl[:, c, j * 128:(j + 1) * 128], silb[:, c, :], start=(c == 0), stop=(c == 7),
                          r=[sk, "silb"], w=[pk])
            cx.tt("dve", modsb[:, sidx * NE:(sidx + 1) * NE, :],
                  ps[:, bank, 0:2 * NE].rearrange("p (j t) -> p j t", t=2),
                  bfm[:, sidx * NE:(sidx + 1) * NE, None].to_broadcast([128, NE, 2]), ALU.add,
                  r=[pk, "ml_bfm"], w=["ml_modsb"])
            rel()
            yield
        cx.dma("sp", modT_out[:, l, s_lo * NE:s_hi * NE, :], modsb[:, s_lo * NE:s_hi * NE, :], r=["ml_modsb"],
               w=[cx.uid("modT")], key="ml_modo")
        yield


class Rot:
    def __init__(self, cx, name, shape, dt, n, alloc=None):
        alloc = alloc or cx.sb
        self.bufs = [alloc(f"{name}{i}", shape, dt) for i in range(n)]
        self.keys = [f"{name}{i}" for i in range(n)]
        self.i = 0

    def next(self):
        i = self.i % len(self.bufs)
        self.i += 1
        return self.bufs[i], self.keys[i]


class BankRot:
    def __init__(self, banks, held=None):
        self.banks = banks
        self.held = set() if held is None else held
        self.i = 0

    def next(self):
        for _ in range(len(self.banks)):
            b = self.banks[self.i % len(self.banks)]
            self.i += 1
            if b not in self.held:
                self.held.add(b)
                return b, f"ps{b}"
        raise RuntimeError(f"no free PSUM bank among {self.banks}")

    def release(self, b):
        self.held.discard(b)


def load_consts(cx, cmat_d, nm=6):
    cf = cx.sb("cmat_f", [128, nm, 128], F32)
    cb = cx.sb("cmat_b", [128, nm, 128], BF16)
    mh = cx.sb("epsb", [128, 1024], F32)
    cx.dma("sp", cf[:], cmat_d, r=[], w=["cmat_f"], key="cmat_f")
    cx.copy("dve", cb[:], cf[:], r=["cmat_f"], w=["consts"])
    cx.memset("dve", mh[:], EPS, w=["consts"])
    return cf, cb, mh


def emit_norm_mod(*a, **k):
    for _ in norm_mod_gen(*a, **k):
        pass


def norm_mod_gen(cx, g, xg, xgk, hT, hTk, sq, u, rs_rot, onesD, mhalf, ps, auxb, Gs, shs):
    off, n = GROUPS[g]
    col = 1 if off >= TL else 0
    cx.act(sq[:, :, :n], xg[:, :, :n], AF.Square, r=[xgk], w=["sq"])
    yield
    b, bk = auxb.next()
    for c in range(8):
        cx.mm(ps[:, b, :n], onesD, sq[:, c, :n], start=(c == 0), stop=(c == 7), r=["sq", "consts"], w=[bk])
    rs, rsk = rs_rot.next()
    rstd_from_psum(cx, ps[:, b, :n], bk, rs[:, :n], rsk, mhalf[:, :n], n)
    auxb.release(b)
    yield
    uk = "u"
    if u is None:
        u, uk = xg, xgk
    cx.tt("dve", u[:, :, :n], xg[:, :, :n], rs[:, None, :n].to_broadcast([128, 8, n]), ALU.mult,
          r=[xgk, rsk], w=[uk])
    yield
    yield
    for c in range(8):
        if c % 2 == 0:
            cx.act(hT[:, c, :n], u[:, c, :n], AF.Identity, r=[uk, "modc"], w=[hTk],
                   bias=shs[:, c, col:col + 1], scale=Gs[:, c, col:col + 1])
        else:
            cx.ts("dve", hT[:, c, :n], u[:, c, :n], Gs[:, c, col:col + 1], ALU.mult, r=[uk, "modc"], w=[hTk],
                  s2=shs[:, c, col:col + 1], op1=ALU.add)


def emit_A(cx, consts, ps, xT, mod, n1g, w_in, qkg, cosT, sinT, sinks):
    nc = cx.nc
    S = cx.S
    cf, cb, mhalf = consts
    onesD, ones64, pswap = cb[:, 0, :], cb[:, 1, :], cb[:, 2, :]
    mainb = BankRot([0, 1, 2, 3, 4])
    auxb = BankRot([5, 6, 7], held=mainb.held)
    allb = BankRot([0, 1, 2, 3, 4, 5, 6, 7], held=mainb.held)

    w_sb = cx.sb("a_w", [128, 8, 2304], BF16)
    for wk, c0, c1 in (("a_wk", 512, 1024), ("a_wv", 1024, 1536), ("a_wp", 1536, 2304), ("a_wq", 0, 512)):
        cx.dma("pool", w_sb[:, :, c0:c1], w_in[:, :, c0:c1], r=[], w=[wk], key=wk)
    mods = cx.sb("a_mod", [128, 48, 2], F32)
    n1gs = cx.sb("a_n1g", [128, 8], F32)
    qkgs = cx.sb("a_qkg", [128, 2], F32)
    cos_s = cx.sb("a_cos", [128, TL], F32)
    sin_s = cx.sb("a_sin", [128, TL], F32)
    cx.dma("sp", mods[:], mod, r=[], w=["a_mod"], key="a_mod")
    cx.dma("sp", n1gs[:], n1g, r=[], w=["a_n1g"], key="a_n1g")
    cx.dma("sp", qkgs[:], qkg, r=[], w=["a_qkg"], key="a_qkg")
    Gs = cx.sb("a_G", [128, 8, 2], F32)
    cx.stt(Gs[:], mods[:, 8:16, :], 1.0, n1gs[:, :, None].to_broadcast([128, 8, 2]), ALU.add, ALU.mult,
           r=["a_mod", "a_n1g"], w=["modc"])
    shs = mods[:, 0:8, :]

    xg_rot = Rot(cx, "a_xg", [128, 8, 512], F32, 2)
    sq = cx.sb("a_sq", [128, 8, 512], BF16)
    u = None
    rs_rot = Rot(cx, "a_rs", [128, 512], F32, 2)
    sq2_rot = Rot(cx, "a_sq2", [128, 512], BF16, 2)
    r2_rot = Rot(cx, "a_r2", [128, 512], F32, 2)
    qn_rot = Rot(cx, "a_qn", [128, 512], F32, 3)
    hi_rot = Rot(cx, "a_hi", [128, 512], BF16, 2)
    lo_rot = Rot(cx, "a_lo", [128, 512], BF16, 2)
    t1_rot = Rot(cx, "a_t1", [128, 512], F32, 2)
    t2_rot = Rot(cx, "a_t2", [128, 512], F32, 2)
    qo_rot = Rot(cx, "a_qo", [128, 512], BF16, 8)
    po_rot = Rot(cx, "a_po", [128, 512], F32, 4)
    th_rot = Rot(cx, "a_th", [128, 512], F32, 4)
    gl_rot = Rot(cx, "a_gl", [128, 512], F32, 4)
    vo_rot = Rot(cx, "a_vo", [128, 512], BF16, 4)

    pending = []

    def advance():
        for gen in list(pending):
            try:
                next(gen)
            except StopIteration:
                pending.remove(gen)

    def load_x(g):
        off, n = GROUPS[g]
        xg, xgk = xg_rot.next()
        cx.dma("sp", xg[:, :, :n], xT[:, :, off:off + n], r=[], w=[xgk], key=xgk)
        return xg, xgk

    def qk_post(g, which, h, b, bk):
        off, n = GROUPS[g]
        latent = off < TL
        sq2, sq2k = sq2_rot.next()
        cx.act(sq2[:, :n], ps[:, b, :n], AF.Square, r=[bk], w=[sq2k])
        yield
        b2, b2k = auxb.next()
        cx.mm(ps[:, b2, :n], ones64, sq2[:, :n], start=True, stop=True, r=[sq2k, "consts"], w=[b2k])
        yield
        r2, r2k = r2_rot.next()
        rstd_from_psum(cx, ps[:, b2, :n], b2k, r2[:, :n], r2k, mhalf[:, :n], n)
        auxb.release(b2)
        qn, qnk = qn_rot.next()
        cx.stt(qn[:, :n], ps[:, b, :n], qkgs[:, which:which + 1], r2[:, :n], ALU.mult, ALU.mult,
               r=[bk, r2k, "a_qkg"], w=[qnk])
        mainb.release(b)
        dst, dres = sinks["q" if which == 0 else "k"](h, off, n)
        dkey = f"{'qT' if which == 0 else 'kT'}_o"
        qo, qok = qo_rot.next()
        if not latent:
            cx.act(qo[:, :n], qn[:, :n], AF.Copy, r=[qnk], w=[qok])
            yield
            cx.dma("sp", dst, qo[:, :n], r=[qok], w=[dres or cx.uid(dkey)], key=qok)
            return
        hi, hik = hi_rot.next()
        lo, lok = lo_rot.next()
        cx.act(hi[:, :n], qn[:, :n], AF.Copy, r=[qnk], w=[hik])
        cx.tt("dve", lo[:, :n], qn[:, :n], hi[:, :n], ALU.subtract, r=[qnk, hik], w=[lok])
        yield
        b3, b3k = auxb.next()
        cx.mm(ps[:, b3, :n], pswap, hi[:, :n], start=True, stop=False, r=[hik, "consts"], w=[b3k])
        cx.mm(ps[:, b3, :n], pswap, lo[:, :n], start=False, stop=True, r=[lok, "consts"], w=[b3k])
        t1, t1k = t1_rot.next()
        cx.tt("dve", t1[:, :n], qn[:, :n], cos_s[:, off:off + n], ALU.mult, r=[qnk, "a_cos"], w=[t1k])
        yield
        t2, t2k = t2_rot.next()
        cx.tt("dve", t2[:, :n], ps[:, b3, :n], sin_s[:, off:off + n], ALU.mult, r=[b3k, "a_sin"], w=[t2k])
        auxb.release(b3)
        cx.tt("dve", qo[:, :n], t1[:, :n], t2[:, :n], ALU.add, r=[t1k, t2k], w=[qok])
        yield
        cx.dma("sp", dst, qo[:, :n], r=[qok], w=[dres or cx.uid(dkey)], key=qok)

    def pool_post(g, ci, b, bk):
        off, n = GROUPS[g]
        po, pok = po_rot.next()
        cx.act(po[:, :n], ps[:, b, :n], AF.Copy, r=[bk], w=[pok])
        mainb.release(b)
        yield
        cx.dma("sp", sinks["up"](ci, off, n), po[:, :n], r=[pok], w=[cx.uid("up_o")], key=pok)
        for side, sl in ((0, slice(0, HALO)), (1, slice(n - HALO, n))):
            e_ap = sinks["edge"](0, ci, side, off)
            if e_ap is not None:
                cx.dma("sp", e_ap, po[:, sl], r=[pok], w=["sendE"], key=f"edge0{ci}{side}")

    def glu_post(g, ci, ba, bak, bb, bbk):
        off, n = GROUPS[g]
        th, thk = th_rot.next()
        cx.act(th[:, :n], ps[:, bb, :n], AF.Exp, r=[bbk], w=[thk], scale=-1.0)
        mainb.release(bb)
        yield
        gl, glk = gl_rot.next()
        cx.recip_act(th[:, :n], th[:, :n], r=[thk], w=[thk], one=cf[:, 4, 0:1])
        cx.tt("dve", gl[:, :n], th[:, :n], ps[:, ba, :n], ALU.mult, r=[thk, bak], w=[glk])
        mainb.release(ba)
        yield
        cx.dma("sp", sinks["glu"](ci, off, n), gl[:, :n], r=[glk], w=[cx.uid("glu_o")], key=glk)
        for side, sl in ((0, slice(0, HALO)), (1, slice(n - HALO, n))):
            e_ap = sinks["edge"](1, ci, side, off)
            if e_ap is not None:
                cx.dma("sp", e_ap, gl[:, sl], r=[glk], w=["sendE"], key=f"edge1{ci}{side}")

    def v_post(g, ti, b, bk):
        off, n = GROUPS[g]
        vo, vok = vo_rot.next()
        cx.copy("dve", vo[:], ps[:, b, :], r=[bk], w=[vok])
        mainb.release(b)
        yield
        vdst, vres = sinks["v"](off // 128 + ti)
        cx.dma("sp", vdst, vo[:].rearrange("p (h d) -> p h d", h=4), r=[vok],
               w=[vres or cx.uid("v_o")], key=vok)

    ng = len(GROUPS)
    hT_all = cx.sb("a_hTall", [128, 8, T], BF16)

    def hT_of(g):
        off, n = GROUPS[g]
        return hT_all[:, :, off:off + n], f"a_hT{g}"

    def run_chunks(g, chunks, hook=None, alloc=None, per_chunk=None):
        alloc = alloc or mainb
        off, n = GROUPS[g]
        hT, hTk = hT_of(g)
        ca_banks = {}
        for ci, (kind, idx, co, wk) in enumerate(chunks):
            b, bk = alloc.next()
            if kind == "v":
                for c in range(8):
                    cx.mm(ps[:, b, :], hT[:, c, idx * 128:(idx + 1) * 128], w_sb[:, c, 1024:1536],
                          start=(c == 0), stop=(c == 7), r=[hTk, wk], w=[bk])
            else:
                for c in range(8):
                    cx.mm(ps[:, b, :n], w_sb[:, c, co:co + 128], hT[:, c, :n],
                          start=(c == 0), stop=(c == 7), r=[hTk, wk], w=[bk])
            advance()
            if kind == "q":
                pending.append(qk_post(g, 0, idx, b, bk))
            elif kind == "k":
                pending.append(qk_post(g, 1, idx, b, bk))
            elif kind == "pool":
                pending.append(pool_post(g, idx, b, bk))
            elif kind == "ca":
                ca_banks[idx] = (b, bk)
            elif kind == "cb":
                ba, bak = ca_banks[idx]
                pending.append(glu_post(g, idx, ba, bak, b, bk))
            elif kind == "v":
                pending.append(v_post(g, idx, b, bk))
            if hook is not None and ci == 3:
                hook()
            if per_chunk is not None:
                per_chunk(ci)

    def drain():
        while pending:
            advance()

    def norm_group(g, xgx):
        xg, xgk = xgx
        hT, hTk = hT_of(g)
        emit_norm_mod(cx, g, xg, xgk, hT, hTk, sq, u, rs_rot, onesD, mhalf, ps, auxb, Gs, shs)

    cc = sinks.get("cc", lambda name: None)
    nxt_x = load_x(0)
    norm_group(0, nxt_x)
    nxt_x = load_x(1)
    cx.dma("sp", cos_s[:], cosT, r=[], w=["a_cos"], key="a_cos")
    cx.dma("sp", sin_s[:], sinT, r=[], w=["a_sin"], key="a_sin")
    for g in range(ng):
        off, n = GROUPS[g]
        chunks = [("k", h, 512 + h * 128, "a_wk") for h in range(4)]
        chunks += [("v", ti, 1024, "a_wv") for ti in range(n // 128)]

        ngen = [None]

        def per_chunk(ci, g=g, ngen=ngen):
            nonlocal nxt_x
            if ci == 0 and g + 1 < ng:
                xg1, xg1k = nxt_x
                hT1, hT1k = hT_of(g + 1)
                ngen[0] = norm_mod_gen(cx, g + 1, xg1, xg1k, hT1, hT1k, sq, u, rs_rot, onesD, mhalf, ps, auxb,
                                       Gs, shs)
                if g + 2 < ng:
                    nxt_x = load_x(g + 2)
            if ngen[0] is not None:
                try:
                    next(ngen[0])
                except StopIteration:
                    ngen[0] = None
            if ci == 3 and g in (2, 4):
                cc("K%d" % (g // 2 - 1))
                cc("V%d" % (g // 2 - 1))

        run_chunks(g, chunks, None, per_chunk=per_chunk)
        while ngen[0] is not None:
            per_chunk(-1)
    for g in range(ng):
        chunks = [("pool", i, 1536 + i * 128, "a_wp") for i in range(2)]
        chunks += [("ca", i, 1792 + i * 128, "a_wp") for i in range(2)]
        chunks += [("cb", i, 2048 + i * 128, "a_wp") for i in range(2)]
        run_chunks(g, chunks, (lambda: cc("E")) if g == 4 else None, alloc=allb)
    for g in range(ng):
        run_chunks(g, [("q", h, h * 128, "a_wq") for h in range(4)])
    drain()


def fm(v, nch):
    return np.ascontiguousarray(np.asarray(v, np.float32).reshape(nch, 128).T)


def fm_w(w):
    k, e = w.shape
    return np.ascontiguousarray(w.reshape(k // 128, 128, e).transpose(1, 0, 2))


def rope_tables(core):
    t = (core % 4) * TL + np.arange(TL)
    row = (t // 64).astype(np.float64)
    col = (t % 64).astype(np.float64)
    inv_freq = 10000.0 ** (-np.arange(0, 32, 2, dtype=np.float64) / 32)
    ang = np.concatenate([row[:, None] * inv_freq, col[:, None] * inv_freq], -1)
    p = np.arange(128)
    pair = (p % 64) // 2
    cosT = np.cos(ang)[:, pair].T
    sgn = np.where(p % 2 == 0, -1.0, 1.0)[:, None]
    sinT = np.sin(ang)[:, pair].T * sgn
    return np.ascontiguousarray(cosT, np.float32), np.ascontiguousarray(sinT, np.float32)


def const_mats():
    m = np.zeros((128, 6, 128), np.float32)
    m[:, 0, :] = 1.0 / 1024
    m[0:64, 1, 0:64] = 1.0 / 64
    m[64:128, 1, 64:128] = 1.0 / 64
    idx = np.arange(128)
    m[idx ^ 1, 2, idx] = 1.0
    m[:, 3, :] = 1.0 / 128
    m[:, 4, :] = 1.0
    m[:, 5, :] = 1.0 / 256
    return m


NVEC = 24
TPL = TL + 2 * HALO
TPC = TC + 2 * HALO
TP = TPL + TPC
KPIECES = 6
KTP = NKT // KPIECES


class SAlloc:
    def __init__(self):
        self.held = set()
        self.i = 0
        self.j = 0

    def pair(self):
        for _ in range(2):
            p = self.i % 2
            self.i += 1
            if (2 * p) not in self.held and (2 * p + 1) not in self.held:
                return 2 * p
        raise RuntimeError("no free score pair")

    def one(self):
        for _ in range(2):
            b = self.j % 2
            self.j += 1
            if b not in self.held:
                self.held.add(b)
                return b, f"ps{b}"
        raise RuntimeError("no free stats bank")

    def release(self, b):
        self.held.discard(b)


def emit_B(cx, consts, ps, xT, mod, lam_ap, qT, pieces, kv_src, up_pad, glu_pad, invcnt, poolw, vecs, convw,
           w_out, w_ffn_in, w_ffn_out, xT_o_fn, last=False, bg=None):
    nc = cx.nc
    S = cx.S
    cf, cb, mhalf = consts
    onesD, ones128n, ones1, = cb[:, 0, :], cb[:, 3, :], cb[:, 4, :]
    ones256f = cf[:, 5, :]
    sa = SAlloc()
    groups_run = [g for g in range(len(GROUPS)) if not (last and GROUPS[g][0] >= TL)]
    piece_of = {}
    for pi, (k0, kc) in enumerate(pieces):
        for kt in range(k0, k0 + kc):
            piece_of[kt] = pi

    mods = cx.sb("b_mod", [128, 48, 2], F32)
    vec = cx.sb("b_vec", [128, NVEC], F32)
    cw = cx.sb("b_cw", [128, 2, 31], F32)
    pwf = cx.sb("b_pwf", [128, 2, 128], F32)
    pwb = cx.sb("b_pwb", [128, 2, 128], BF16)
    cx.dma("sp", vec[:], vecs, r=[], w=["b_vec"], key="b_vec")
    cx.dma("sp", cw[:], convw, r=[], w=["b_cw"], key="b_cw")
    cx.dma("sp", pwf[:], poolw, r=[], w=["b_pwf"], key="b_pwf")
    lamt = cx.sb("b_lamt", [128, 1], F32)
    cx.dma("sp", lamt[:], lam_ap, r=[], w=["b_lamt"], key="b_lamt", slow=True)
    cx.copy("dve", pwb[:], pwf[:], r=["b_pwf"], w=["b_pwb"])
    der = cx.sb("b_der", [128, 8], F32)
    cx.tt("dve", der[:, 0:1], lamt[:, 0:1], vec[:, 10:11], ALU.add, r=["b_vec", "b_lamt"], w=["b_der0"])
    cx.ts("dve", der[:, 0:1], der[:, 0:1], -1.0, ALU.mult, r=["b_der0"], w=["b_der0"])
    cx.tt("dve", der[:, 1:2], vec[:, 8:9], vec[:, 11:12], ALU.mult, r=["b_vec"], w=["b_der1"])
    cx.copy("dve", der[:, 2:6], vec[:, 4:8], r=["b_vec"], w=["b_der2"])
    derk = ["b_der0", "b_der1", "b_der2"]
    G2 = cx.sb("b_G2", [128, 8, 2], F32)
    g2h = cx.sb("b_g2h", [128, 8, 2], F32)
    sh2 = mods[:, 24:32, :]
    g1 = mods[:, 16:24, :]

    mixT = cx.sb("b_mix", [128, 8, T], BF16)

    cx.push()
    sb1 = cx.sb

    NP = 512 + 2 * HALO
    U = sb1("p_U", [128, 2, NP], F32)
    IC = sb1("p_IC", [128, 2, 512], F32)
    Sa = sb1("p_Sa", [128, 2, NP], F32)
    Sb = sb1("p_Sb", [128, 2, NP], F32)
    Sc = sb1("p_Sc", [128, 2, NP], F32)
    Sd = sb1("p_Sd", [128, 2, NP], F32)
    ptmp = sb1("p_tmp", [128, 2, 512], F32)
    pin = sb1("p_in", [128, 2, 512], BF16)
    G = sb1("c_G", [128, 2, NP], F32)
    acc = sb1("c_acc", [128, 2, 512], F32)
    sqa = sb1("c_sqa", [128, 2, 512], F32)
    msb = sb1("c_msb", [128, 512], F32)
    m2 = sb1("c_m2", [128, 512], F32)
    vr = sb1("c_vr", [128, 512], F32)
    dd = sb1("c_dd", [128, 2, 512], F32)
    zh = sb1("c_zh", [128, 2, 512], F32)
    cth = sb1("c_th", [128, 2, 512], F32)

    def b1_gen():
        for g in groups_run:
            off, n = GROUPS[g]
            po = off if off < TL else TPL + (off - TL)
            m = n + 2 * HALO
            cx.dma("sp", U[:, :, :m], up_pad[:, :, po:po + m], r=["halo_up"], w=["p_U"], key="p_U")
            cx.dma("sp", IC[:, :, :n], invcnt[:, :, off:off + n], r=[], w=["p_IC"], key="p_IC")
            yield
            cx.tt("dve", Sa[:, :, 1:m], U[:, :, 0:m - 1], U[:, :, 1:m], ALU.add, r=["p_U"], w=["p_Sa"])
            yield
            cx.tt("dve", Sb[:, :, 2:m - 1], Sa[:, :, 1:m - 2], Sa[:, :, 3:m], ALU.add, r=["p_Sa"], w=["p_Sb"])
            yield
            cx.tt("dve", Sc[:, 1, 4:m - 3], Sb[:, 1, 2:m - 5], Sb[:, 1, 6:m - 1], ALU.add, r=["p_Sb"], w=["p_Sc"])
            yield
            cx.tt("dve", Sd[64:128, 1, 8:m - 7], Sc[64:128, 1, 4:m - 11], Sc[64:128, 1, 12:m - 3], ALU.add,
                  r=["p_Sc"], w=["p_Sd"])
            yield
            srcs = [(Sa, "p_Sa", 0, 0), (Sb, "p_Sb", 64, 0), (Sc, "p_Sc", 0, 1), (Sd, "p_Sd", 64, 1)]
            for sbuf, sk, p0, ci in srcs:
                cx.tt("dve", ptmp[p0:p0 + 64, ci, :n], sbuf[p0:p0 + 64, ci, HALO:HALO + n],
                      IC[p0:p0 + 64, ci, :n], ALU.mult, r=[sk, "p_IC"], w=[f"p_tmp{p0}{ci}"])
                cx.tt("dve", pin[p0:p0 + 64, ci, :n], ptmp[p0:p0 + 64, ci, :n],
                      U[p0:p0 + 64, ci, HALO:HALO + n], ALU.subtract, r=[f"p_tmp{p0}{ci}", "p_U"],
                      w=[f"p_in{p0}{ci}"])
                yield
            pk = [f"p_in{p0}{ci}" for _, _, p0, ci in srcs]
            for ci in range(2):
                b, bk = sa.one()
                cx.mm(ps[:, b, :n], pwb[:, ci, :], pin[:, ci, :n], start=True, stop=True,
                      r=pk + ["b_pwb"], w=[bk])
                cx.ts("dve", mixT[:, 4 + ci, off:off + n], ps[:, b, :n], vec[:, ci:ci + 1], ALU.mult,
                      r=[bk, "b_vec"], w=[f"mix{4 + ci}_{g}"])
                sa.release(b)
                yield
            cx.dma("sp", G[:, :, :m], glu_pad[:, :, po:po + m], r=["halo_glu"], w=["c_G"], key="c_G")
            yield
            for ci in range(2):
                cx.ts("dve", acc[:, ci, :n], G[:, ci, 1:1 + n], cw[:, ci, 0:1], ALU.mult,
                      r=["c_G", "b_cw", "b_vec"], w=[f"c_acc{ci}"], s2=vec[:, 2 + ci:3 + ci], op1=ALU.add)
                yield
                for k in range(1, 31):
                    cx.stt(acc[:, ci, :n], G[:, ci, 1 + k:1 + k + n], cw[:, ci, k:k + 1], acc[:, ci, :n],
                           ALU.mult, ALU.add, r=["c_G", "b_cw", f"c_acc{ci}"], w=[f"c_acc{ci}"])
                    yield
            cx.tt("dve", sqa[:, :, :n], acc[:, :, :n], acc[:, :, :n], ALU.mult, r=["c_acc0", "c_acc1"], w=["c_sqa"])
            yield
            b1_, b1k = sa.one()
            for ci in range(2):
                cx.mm(ps[:, b1_, :n], ones256f, acc[:, ci, :n], start=(ci == 0), stop=(ci == 1),
                      r=["c_acc0", "c_acc1", "cmat_f"], w=[b1k])
            cx.copy("dve", msb[:, :n], ps[:, b1_, :n], r=[b1k], w=["c_msb"])
            sa.release(b1_)
            cx.tt("dve", m2[:, :n], msb[:, :n], msb[:, :n], ALU.mult, r=["c_msb"], w=["c_m2"])
            yield
            b2_, b2k = sa.one()
            for ci in range(2):
                cx.mm(ps[:, b2_, :n], ones256f, sqa[:, ci, :n], start=(ci == 0), stop=(ci == 1),
                      r=["c_sqa", "cmat_f"], w=[b2k])
            cx.stt(vr[:, :n], ps[:, b2_, :n], EPS, m2[:, :n], ALU.add, ALU.subtract, r=[b2k, "c_m2"], w=["c_vr"])
            sa.release(b2_)
            cx.act(vr[:, :n], vr[:, :n], AF.Ln, r=["c_vr"], w=["c_vr"])
            cx.act(vr[:, :n], vr[:, :n], AF.Exp, r=["c_vr"], w=["c_vr"], scale=-0.5)
            yield
            cx.tt("dve", dd[:, :, :n], acc[:, :, :n], msb[:, None, :n].to_broadcast([128, 2, n]), ALU.subtract,
                  r=["c_acc0", "c_acc1", "c_msb"], w=["c_dd"])
            cx.tt("dve", dd[:, :, :n], dd[:, :, :n], vr[:, None, :n].to_broadcast([128, 2, n]), ALU.mult,
                  r=["c_dd", "c_vr"], w=["c_dd"])
            yield
            for ci in range(2):
                cx.ts("dve", zh[:, ci, :n], dd[:, ci, :n], der[:, 2 + ci:3 + ci], ALU.mult,
                      r=["c_dd", "b_der2"], w=[f"c_zh{ci}"], s2=der[:, 4 + ci:5 + ci], op1=ALU.add)
            yield
            cx.act(cth[:, :, :n], zh[:, :, :n], AF.Exp, r=["c_zh0", "c_zh1"], w=["c_th"], scale=-1.0)
            yield
            cx.recip_act(cth[:, :, :n], cth[:, :, :n], r=["c_th"], w=["c_th"], one=cf[:, 4, 0:1])
            yield
            cx.tt("dve", mixT[:, 6:8, off:off + n], cth[:, :, :n], zh[:, :, :n], ALU.mult,
                  r=["c_th", "c_zh0", "c_zh1"], w=[f"mix6_{g}", f"mix7_{g}"])
            yield

    b1 = b1_gen()
    b1_done = [False]

    def bg_bank():
        b, bk = sa.one()
        return b, bk, (lambda: sa.release(b))

    bg_gen = bg(sb1, bg_bank) if bg is not None else None
    bg_done = [bg_gen is None]

    def bg_step():
        if bg_done[0]:
            return
        try:
            next(bg_gen)
        except StopIteration:
            bg_done[0] = True

    def b1_step():
        if b1_done[0]:
            return
        try:
            next(b1)
        except StopIteration:
            b1_done[0] = True

    Kh = sb1("a_K", [128, NKEY], BF16)
    Vh = sb1("a_V", [128, NKT, 128], BF16)
    qh_rot = [sb1(f"a_q{i}", [128, T], BF16) for i in range(2)]
    NPT = 6
    Pt = [sb1(f"a_P{i}", [128, 2, 512], BF16) for i in range(NPT)]
    rD = sb1("a_rD", [128, 2, 512], F32)
    oo = sb1("a_oo", [128, 2, 512], F32)
    pacc_rot = Rot(cx, "a_Pacc", [128, 2, 512], BF16, 2, alloc=sb1)
    QD = 33
    o_rot = Rot(cx, "a_o", [128, 512], F32, 2, alloc=sb1)
    osq_rot = Rot(cx, "a_osq", [128, 512], BF16, 2, alloc=sb1)
    r3_rot = Rot(cx, "a_r3", [128, 512], F32, 2, alloc=sb1)

    def load_kv(h, piece):
        k0, kc = pieces[piece]
        ksrc, vsrc = kv_src(h, piece)
        cx.dma("sp", Kh[:, k0 * 128:(k0 + kc) * 128], ksrc, r=["recv"], w=[f"K{piece}"], key=f"K{piece}")
        cx.dma("sp", Vh[:, k0:k0 + kc, :], vsrc, r=["recv"], w=[f"V{piece}"], key=f"V{piece}")

    def load_q(h):
        cx.dma("sp", qh_rot[h % 2][:], qT[h], r=[], w=[f"a_q{h % 2}"], key=f"a_q{h % 2}")

    load_q(0)
    for piece in range(len(pieces)):
        load_kv(0, piece)

    gorder = [g for g in [4, 0, 1, 2, 3] if g in groups_run]
    post_pending = []
    it = 0
    for h in range(4):
        qh = qh_rot[h % 2]
        qk = f"a_q{h % 2}"
        if h + 1 < 4:
            load_q(h + 1)
        for gi, g in enumerate(gorder):
            off, n = GROUPS[g]
            kts = list(range(2)) if off >= TL else list(range(NKT))
            last_group = (gi == len(gorder) - 1)
            pvq = []
            quad = []
            dq = []
            dstate = {"first": True, "acc": None}

            def flush_d(final, n=n, dq=dq, dstate=dstate):
                while dq:
                    src, srck = dq.pop(0)
                    lastd = final and not dq
                    for c in range(2):
                        cx.mm(ps[:, 6 + c, :n], ones1, src[:, c, :n], start=dstate["first"], stop=lastd,
                              r=["consts", srck], w=[f"ps{6 + c}"])
                    dstate["first"] = False

            def emit_pv(prev, final, n=n, quad=quad, dq=dq, dstate=dstate):
                pkt, ppt, pptk, pidx = prev
                pp = piece_of[pkt]
                flush_d(False)
                for c in range(2):
                    cx.mm(ps[:, 4 + c, :n], Vh[:, pkt, :], ppt[:, c, :n], start=(pidx == 0), stop=final,
                          r=[f"V{pp}", pptk], w=[f"ps{4 + c}"])
                quad.append((ppt, pptk))
                if len(quad) == 2:
                    dstate["acc"] = pacc_rot.next()
                    acc, acck = dstate["acc"]
                    cx.tt("dve", acc[:, :, :n], quad[0][0][:, :, :n], ppt[:, :, :n], ALU.add,
                          r=[quad[0][1], pptk], w=[acck])
                elif len(quad) > 2:
                    acc, acck = dstate["acc"]
                    cx.tt("dve", acc[:, :, :n], acc[:, :, :n], ppt[:, :, :n], ALU.add, r=[acck, pptk], w=[acck])
                if len(quad) == QD or final:
                    dq.append(dstate["acc"] if len(quad) > 1 else (ppt, pptk))
                    quad.clear()
                if final:
                    flush_d(True)

            for idx, kt in enumerate(kts):
                piece = piece_of[kt]
                b0 = sa.pair()
                for c in range(2):
                    cx.mm(ps[:, b0 + c, :n], Kh[64 * c:64 * c + 64, kt * 128:(kt + 1) * 128],
                          qh[64 * c:64 * c + 64, off:off + n], start=True, stop=True,
                          r=[f"K{piece}", qk], w=[f"ps{b0 + c}"])
                pt = Pt[it % NPT]
                ptk = f"a_P{it % NPT}"
                cx.act(pt[:, :, :n], ps[:, b0:b0 + 2, :n], AF.Exp, r=[f"ps{b0}", f"ps{b0 + 1}"], w=[ptk],
                       scale=0.125)
                it += 1
                pvq.append((kt, pt, ptk, idx))
                if len(pvq) > 2:
                    prev = pvq.pop(0)
                    emit_pv(prev, False)
                    pkt = prev[0]
                    pp = piece_of[pkt]
                    if last_group and h + 1 < 4 and pkt == pieces[pp][0] + pieces[pp][1] - 1 \
                            and pp != len(pieces) - 1:
                        load_kv(h + 1, pp)
                if it % 2 == 0:
                    b1_step()
                if it % 5 == 2:
                    bg_step()
                if idx in (1, 5) and post_pending:
                    for gen in list(post_pending):
                        try:
                            next(gen)
                        except StopIteration:
                            post_pending.remove(gen)
            while pvq:
                prev = pvq.pop(0)
                emit_pv(prev, not pvq)
            if last_group and h + 1 < 4:
                load_kv(h + 1, len(pieces) - 1)
            for gen in list(post_pending):
                for _ in gen:
                    pass
                post_pending.remove(gen)
            cx.copy("dve", oo[:, :, :n], ps[:, 4:6, :n], r=["ps4", "ps5"], w=["a_oo"])
            o, ok = o_rot.next()
            osq, osqk = osq_rot.next()

            def post(h=h, off=off, n=n, o=o, ok=ok, osq=osq, osqk=osqk):
                cx.recip_act(rD[:, :, :n], ps[:, 6:8, :n], r=["ps6", "ps7"], w=["a_rD"])
                cx.tt("dve", oo[:, :, :n], oo[:, :, :n], rD[:, :, :n], ALU.mult, r=["a_oo", "a_rD"], w=["a_oo"])
                cx.stt(o[:, :n], oo[:, 1, :n], der[:, 0:1], oo[:, 0, :n], ALU.mult, ALU.add,
                       r=["a_oo", "b_der0"], w=[ok])
                cx.tt("dve", osq[:, :n], o[:, :n], o[:, :n], ALU.mult, r=[ok], w=[osqk])
                yield
                b, bk = sa.one()
                cx.mm(ps[:, b, :n], ones128n, osq[:, :n], start=True, stop=True, r=[osqk, "consts"], w=[bk])
                r3, r3k = r3_rot.next()
                rstd_from_psum(cx, ps[:, b, :n], bk, r3[:, :n], r3k, mhalf[:, :n], n)
                sa.release(b)
                cx.stt(mixT[:, h, off:off + n], o[:, :n], der[:, 1:2], r3[:, :n], ALU.mult, ALU.mult,
                       r=[ok, r3k, "b_der1"], w=[f"mix{h}_{off}"])
                yield

            post_pending.append(post())
    for gen in list(post_pending):
        for _ in gen:
            pass
    while not b1_done[0]:
        b1_step()
    while not bg_done[0]:
        bg_step()

    cx.pop()
    cx.dma("sp", mods[:], mod, r=[], w=["b_mod"], key="b_mod")
    cx.stt(G2[:], mods[:, 32:40, :], 1.0, vec[:, 12:20, None].to_broadcast([128, 8, 2]), ALU.add, ALU.mult,
           r=["b_mod", "b_vec"], w=["modc"])
    cx.copy("dve", g2h[:], mods[:, 40:48, :], r=["b_mod"], w=["modc2"])
    banks = BankRot([0, 1, 2, 3, 4, 5, 6, 7])
    wo_sb = cx.sb("f_wout", [128, 8, 1024], BF16)
    for hf_ in range(2):
        cx.dma("pool", wo_sb[:, :, hf_ * 512:(hf_ + 1) * 512], w_out[:, :, hf_ * 512:(hf_ + 1) * 512], r=[],
               w=[f"f_wout{hf_}"], key=f"f_wout{hf_}")
    xg_rot = Rot(cx, "f_xg", [128, 8, 512], F32, 2)
    sq = cx.sb("f_sq", [128, 8, 512], BF16)
    u = cx.sb("f_u", [128, 8, 512], F32)
    rs_rot = Rot(cx, "f_rs", [128, 512], F32, 2)
    aT = cx.sb("f_aT", [128, NJ, 512], BF16)
    win_rot = Rot(cx, "f_win", [128, 8, 256], BF16, 4)
    wo2_rot = Rot(cx, "f_wo2", [128, NJ, 128], BF16, 3)
    th_rot = Rot(cx, "f_th", [128, 512], F32, 2)
    s_rot = Rot(cx, "f_s", [128, 512], F32, 2)

    h2_rot = Rot(cx, "f_h2T", [128, 8, 512], BF16, 2)

    def outproj_norm(g):
        off, n = GROUPS[g]
        col = 1 if off >= TL else 0
        xg, xgk = xg_rot.next()
        cx.dma("sp", xg[:, :, :n], xT[:, :, off:off + n], r=[], w=[xgk], key=xgk)
        mixk = [f"mix{m}_{off}" for m in range(4)] + [f"mix{m}_{g}" for m in range(4, 8)]
        for dc in range(8):
            b, bk = banks.next()
            for m in range(8):
                cx.mm(ps[:, b, :n], wo_sb[:, m, dc * 128:(dc + 1) * 128], mixT[:, m, off:off + n],
                      start=(m == 0), stop=(m == 7), r=[f"f_wout{dc // 4}", mixk[m]], w=[bk])
            cx.stt(xg[:, dc, :n], ps[:, b, :n], g1[:, dc, col:col + 1], xg[:, dc, :n], ALU.mult, ALU.add,
                   r=[bk, "b_mod", xgk], w=[xgk])
            banks.release(b)
        h2T, h2k = h2_rot.next()
        gen = norm_mod_gen(cx, g, xg, xgk, h2T, h2k, sq, u, rs_rot, onesD, mhalf, ps, banks, G2, sh2)
        return xg, xgk, h2T, h2k, gen

    def run_gen(gen):
        for _ in gen:
            pass

    nxt = outproj_norm(groups_run[0])
    run_gen(nxt[4])
    for gi, g in enumerate(groups_run):
        off, n = GROUPS[g]
        col = 1 if off >= TL else 0
        xg, xgk, h2T, h2k, _ = nxt
        ngen = None
        if gi + 1 < len(groups_run):
            nxt = outproj_norm(groups_run[gi + 1])
            ngen = nxt[4]
        for j in range(NJ):
            wj, wjk = win_rot.next()
            cx.dma("pool", wj[:], w_ffn_in[j], r=[], w=[wjk], key=wjk)
            bg, bgk = banks.next()
            bu, buk = banks.next()
            for c in range(8):
                cx.mm(ps[:, bg, :n], wj[:, c, 0:128], h2T[:, c, :n], start=(c == 0), stop=(c == 7),
                      r=[wjk, h2k], w=[bgk])
            for c in range(8):
                cx.mm(ps[:, bu, :n], wj[:, c, 128:256], h2T[:, c, :n], start=(c == 0), stop=(c == 7),
                      r=[wjk, h2k], w=[buk])
            th, thk = th_rot.next()
            cx.act(th[:, :n], ps[:, bg, :n], AF.Exp, r=[bgk], w=[thk], scale=-1.0)
            sv, svk = s_rot.next()
            cx.recip_act(th[:, :n], th[:, :n], r=[thk], w=[thk], one=cf[:, 4, 0:1])
            cx.tt("dve", sv[:, :n], th[:, :n], ps[:, bg, :n], ALU.mult, r=[thk, bgk], w=[svk])
            banks.release(bg)
            cx.tt("dve", aT[:, j, :n], sv[:, :n], ps[:, bu, :n], ALU.mult, r=[svk, buk], w=[f"f_aT{j}"])
            banks.release(bu)
            if ngen is not None and j >= 6 and j % 2 == 0:
                try:
                    next(ngen)
                except StopIteration:
                    ngen = None
        if ngen is not None:
            run_gen(ngen)
        for dc in range(8):
            w2, w2k = wo2_rot.next()
            cx.dma("pool", w2[:], w_ffn_out[dc], r=[], w=[w2k], key=w2k)
            b, bk = banks.next()
            for j in range(NJ):
                cx.mm(ps[:, b, :n], w2[:, j, :], aT[:, j, :n], start=(j == 0), stop=(j == NJ - 1),
                      r=[w2k, f"f_aT{j}"], w=[bk])
            cx.stt(xg[:, dc, :n], ps[:, b, :n], g2h[:, dc, col:col + 1], xg[:, dc, :n], ALU.mult, ALU.add,
                   r=[bk, "modc2", xgk], w=[xgk])
            banks.release(b)
        cx.dma("sp", xT_o_fn(off, n), xg[:, :, :n], r=[xgk], w=[cx.uid("xT_o")], key=xgk)


def invcnt_table(core):
    out = np.zeros((128, 2, T), np.float32)
    wins = {(0, 0): 2, (64, 0): 4, (0, 1): 8, (64, 1): 16}
    for (p0, ci), w in wins.items():
        for off, L, base in ((0, SEQ, (core % 4) * TL), (TL, TC, 0)):
            nt = TL if off == 0 else TC
            t = base + np.arange(nt)
            lo = np.clip(t - w // 2, 0, L)
            hi = np.clip(t + w - w // 2, 0, L)
            out[p0:p0 + 64, ci, off:off + nt] = (1.0 / (hi - lo))[None, :]
    return out


CC_GROUPS = [[0, 1, 2, 3], [4, 5, 6, 7]]
PIECES = [(0, 2)] + [(2 + 8 * i, 8) for i in range(8)]


def build_fused(depth=DEPTH):
    cx = Ctx()
    nc = cx.nc
    x0T = cx.din("x0T", [128, 8, T], F32)
    cv = cx.din("cv", [128, 8, 2], F32)
    w_mod = cx.din("w_mod", [DEPTH, D, 6 * D], F32)
    b_mod_fm = cx.din("b_mod_fm", [DEPTH, 128, 48], F32)
    lam_in = cx.din("lam_in", [128, DEPTH, 4, 64], F32)
    cmat = cx.din("cmat", [128, 6, 128], F32)
    n1g = cx.din("n1g", [DEPTH, 128, 8], F32)
    w_in = cx.din("w_in", [DEPTH, 128, 8, 2304], F32)
    qkg = cx.din("qkg", [DEPTH, 128, 2], F32)
    poolw = cx.din("poolw", [DEPTH, 128, 2, 128], F32)
    vecs = cx.din("vecs", [DEPTH, 128, NVEC], F32)
    convw = cx.din("convw", [DEPTH, 128, 2, 31], F32)
    w_out = cx.din("w_out", [DEPTH, 128, 8, 1024], F32)
    w_ffn_in = cx.din("w_ffn_in", [DEPTH, NJ, 128, 8, 256], F32)
    w_ffn_out = cx.din("w_ffn_out", [DEPTH, 8, 128, NJ, 128], F32)
    cosT = cx.din("cosT", [128, TL], F32)
    sinT = cx.din("sinT", [128, TL], F32)
    invcnt = cx.din("invcnt", [128, 2, T], F32)
    selT = cx.din("selT", [128, 8], F32)
    outT = cx.dout("outT", [128, 8, TL], F32)
    modT = cx.dscratch("modT", [128, DEPTH, 48, 2], F32)
    lamd = cx.dscratch("lamd", [128, DEPTH], F32)
    xs = [cx.dscratch(f"xs{i}", [128, 8, T], F32) for i in range(2)]
    qTs = cx.dscratch("qTs", [4, 128, T], BF16)
    kcs = cx.dscratch("kcs", [4, 128, TC], BF16)
    vcs = cx.dscratch("vcs", [4, 128, 2, 128], BF16)
    up_pad = cx.dscratch("up_pad", [128, 2, TP], F32)
    glu_pad = cx.dscratch("glu_pad", [128, 2, TP], F32)
    sendK = [[cx.dscratch(f"sendK{p}{h}", [512, 1024], BF16) for h in range(2)] for p in range(2)]
    sendV = [[cx.dscratch(f"sendV{p}{h}", [512, 1024], BF16) for h in range(2)] for p in range(2)]
    recvK = [[cx.dscratch(f"recvK{p}{h}", [2048, 1024], BF16) for h in range(2)] for p in range(2)]
    recvV = [[cx.dscratch(f"recvV{p}{h}", [2048, 1024], BF16) for h in range(2)] for p in range(2)]
    sendE = [cx.dscratch(f"sendE{p}", [256, 64], F32) for p in range(2)]
    recvE = [cx.dscratch(f"recvE{p}", [1024, 64], F32) for p in range(2)]

    ps = cx.es.enter_context(nc.psum_tensor("ps", [128, 8, 512], F32))
    consts = load_consts(cx, cmat)
    sel = cx.sb("sel_sb", [128, 8], F32)
    cx.dma("sp", sel[:], selT, r=[], w=["selT"], key="selT")
    E = cx.sb("E", [128, 4, 2, 64], F32)
    H = cx.sb("H", [128, 2, 2, 2, HALO], F32)
    zt = cx.sb("zt", [128, 2, HALO], F32)
    cx.memset("dve", zt[:], 0.0, w=["zt"])
    zi = 0
    for pad in (up_pad, glu_pad):
        for o in (0, HALO + TL, TPL, TPL + HALO + TC):
            cx.dma("sp", pad[:, :, o:o + HALO], zt[:], r=["zt"], w=[f"zpad{zi}"], key=f"zpad{zi % 4}")
            zi += 1

    silb = cx.sb("silb", [128, 8, 2], BF16)
    cx.push("M_")
    emit_mod_pre(cx, cv, lam_in, lamd, silb)
    for _ in mod_layer_gen(cx, ps, [(0, 0, 6)], silb, w_mod, b_mod_fm, modT, lambda: (0, "ps0", lambda: None),
                           cx.sb):
        pass
    cx.pop()

    for l in range(depth):
        par = l % 2
        last = (l == depth - 1)
        x_in = x0T if l == 0 else xs[(l - 1) % 2]

        def k_sink(h, off, n, par=par):
            if off >= TL:
                return kcs[h, :, :], None
            return (sendK[par][off // 1024][h * 128:(h + 1) * 128, off % 1024: off % 1024 + n],
                    f"sendK{off // 1024}")

        def v_sink(ti, par=par):
            if ti >= 16:
                return vcs.rearrange("h p t d -> p h t d")[:, :, ti - 16, :], None
            return (sendV[par][ti // 8].rearrange("(h p) c -> p h c", p=128)[:, :, (ti % 8) * 128:(ti % 8 + 1) * 128],
                    f"sendV{ti // 8}")

        def pad_sink(pad):
            def f(ci, off, n):
                o = HALO + off if off < TL else TPL + HALO + (off - TL)
                return pad[:, ci, o:o + n]
            return f

        def edge_sink(tz, ci, side, off, par=par):
            if (side == 0 and off == 0) or (side == 1 and off == TL - 512):
                return sendE[par][ci * 128:(ci + 1) * 128, tz * 32 + side * HALO: tz * 32 + (side + 1) * HALO]
            return None

        def issue_cc(name, par=par):
            tab = {"K0": (sendK[par][0], recvK[par][0], "sendK0", 0), "V0": (sendV[par][0], recvV[par][0], "sendV0", 1),
                   "K1": (sendK[par][1], recvK[par][1], "sendK1", 2), "V1": (sendV[par][1], recvV[par][1], "sendV1", 3),
                   "E": (sendE[par], recvE[par], "sendE", 4)}
            sbuf_, rbuf_, skey, ci_ = tab[name]
            cx.S.op("pool", lambda e, a=sbuf_, b=rbuf_: e.collective_compute(
                "AllGather", ALU.bypass, replica_groups=CC_GROUPS, ins=[a], outs=[b]),
                r=[skey], w=["recv", f"recv_{skey}"], key=f"cc{ci_}", inc1=True)

        sinks = {"q": lambda h, off, n: (qTs[h, :, off:off + n], None), "k": k_sink, "v": v_sink,
                 "up": pad_sink(up_pad), "glu": pad_sink(glu_pad), "edge": edge_sink, "cc": issue_cc}
        cx.push(f"A{l}_")
        emit_A(cx, consts, ps, x_in, modT[:, l], n1g[l], w_in[l], qkg[l], cosT, sinT, sinks)
        cx.pop()
        cx.dma("sp", E[:], recvE[par].rearrange("(j c p) x -> p j c x", j=4, c=2, p=128), r=["recv_sendE"],
               w=["E"], key="E")
        for tz in range(2):
            for side in range(2):
                c0 = tz * 32 + (HALO if side == 0 else 0)
                hv = H[:, tz, side, :, :]
                for j in range(4):
                    sc = sel[:, 4 * side + j: 4 * side + j + 1]
                    if j == 0:
                        cx.ts("dve", hv, E[:, j, :, c0:c0 + HALO], sc, ALU.mult, r=["E", "selT"], w=[f"H{tz}{side}"])
                    else:
                        cx.stt(hv, E[:, j, :, c0:c0 + HALO], sc, hv, ALU.mult, ALU.add,
                               r=["E", "selT", f"H{tz}{side}"], w=[f"H{tz}{side}"])
                pad = up_pad if tz == 0 else glu_pad
                o = 0 if side == 0 else HALO + TL
                cx.dma("sp", pad[:, :, o:o + HALO], hv, r=[f"H{tz}{side}"], w=["halo_up" if tz == 0 else "halo_glu"],
                       key=f"zpad{2 * tz + side}")
        def kv_src(h, piece, par=par):
            if piece == 0:
                return kcs[h], vcs[h]
            j, hf = (piece - 1) // 2, (piece - 1) % 2
            rows = slice(j * 512 + h * 128, j * 512 + (h + 1) * 128)
            return recvK[par][hf][rows, :], recvV[par][hf][rows, :].rearrange("p (t d) -> p t d", d=128)

        if last:
            xo_fn = lambda off, n: outT[:, :, off:off + n]
        else:
            xo_fn = lambda off, n, l=l: xs[l % 2][:, :, off:off + n]
        cx.push(f"B{l}_")
        jobs = ([(0, 6, 16)] if l == 0 else []) + ([(l + 1, 0, 16)] if not last else [])
        bg = None
        if jobs:
            bg = lambda alloc, bank_fn, jobs=jobs: mod_layer_gen(cx, ps, jobs, silb, w_mod, b_mod_fm, modT,
                                                                 bank_fn, alloc)
        emit_B(cx, consts, ps, x_in, modT[:, l], lamd[:, l:l + 1], qTs, PIECES, kv_src, up_pad, glu_pad, invcnt,
               poolw[l], vecs[l], convw[l], w_out[l], w_ffn_in[l], w_ffn_out[l], xo_fn, last=last, bg=bg)
        cx.pop()
    return cx.finish()


_CACHE = {}


def layer_consts(l):
    return 0.8 - 0.6 * math.exp(-0.3 * l)


def prep_weights(inputs):
    f32 = np.float32
    w = {}
    w["n1g"] = np.stack([fm(inputs["norm1_g"][l], 8) for l in range(DEPTH)])
    w["w_in"] = np.stack([fm_w(np.asarray(inputs["w_in"][l], f32)) for l in range(DEPTH)])
    w["qkg"] = np.stack([np.stack([np.tile(inputs["q_norm_g"][l], 2), np.tile(inputs["k_norm_g"][l], 2)], -1)
                         for l in range(DEPTH)]).astype(f32)
    poolw = np.zeros((DEPTH, 128, 2, 128), f32)
    vecs = np.zeros((DEPTH, 128, NVEC), f32)
    for l in range(DEPTH):
        for ci in range(2):
            poolw[l, 0:64, ci, 0:64] = inputs["pool_w"][l][2 * ci]
            poolw[l, 64:128, ci, 64:128] = inputs["pool_w"][l][2 * ci + 1]
        li = layer_consts(l)
        vecs[l, :, 0:2] = fm(inputs["pool_scale"][l], 2)
        vecs[l, :, 2:4] = fm(inputs["conv_dw_b"][l], 2)
        vecs[l, :, 4:6] = fm(inputs["conv_ln_g"][l], 2)
        vecs[l, :, 6:8] = fm(inputs["conv_ln_b"][l], 2)
        vecs[l, :, 8] = inputs["subln_g"][l]
        vecs[l, :, 10] = li
        vecs[l, :, 11] = 1.0 - li
        vecs[l, :, 12:20] = fm(inputs["norm2_g"][l], 8)
    w["poolw"] = poolw
    w["vecs"] = vecs
    w["convw"] = np.stack([np.asarray(inputs["conv_dw_w"][l], f32).T.reshape(2, 128, 31).transpose(1, 0, 2)
                           for l in range(DEPTH)])
    w["w_out"] = np.stack([fm_w(np.asarray(inputs["w_out"][l], f32)) for l in range(DEPTH)])
    wfi = np.asarray(inputs["w_ffn_in"], f32)
    wg = wfi[:, :, :FF].reshape(DEPTH, 8, 128, NJ, 128)
    wu = wfi[:, :, FF:].reshape(DEPTH, 8, 128, NJ, 128)
    w["w_ffn_in"] = np.ascontiguousarray(np.concatenate([wg, wu], -1).transpose(0, 3, 2, 1, 4))
    w["w_ffn_out"] = np.ascontiguousarray(
        np.asarray(inputs["w_ffn_out"], f32).reshape(DEPTH, NJ, 128, 8, 128).transpose(0, 3, 2, 1, 4))
    return {k: np.ascontiguousarray(v, dtype=f32) for k, v in w.items()}


def make_in_maps(inputs):
    f32 = np.float32
    x = np.asarray(inputs["x"], f32)
    c = np.asarray(inputs["c"], f32)
    ctx = np.asarray(inputs["ctx"], f32)
    c_ctx = np.asarray(inputs["c_ctx"], f32)
    w = prep_weights(inputs)
    lam_in = np.stack([inputs["lambda_q1"], inputs["lambda_k1"], inputs["lambda_q2"], inputs["lambda_k2"]], 1)
    shared = dict(w)
    shared["lam_in"] = np.ascontiguousarray(np.broadcast_to(np.asarray(lam_in, f32)[None], (128, DEPTH, 4, 64)))
    shared["w_mod"] = np.ascontiguousarray(np.asarray(inputs["w_mod"], f32))
    shared["b_mod_fm"] = np.ascontiguousarray(np.asarray(inputs["b_mod"], f32).reshape(DEPTH, 48, 128).transpose(0, 2, 1))
    shared["cmat"] = const_mats()
    in_maps = []
    for i in range(NCORE):
        b, r = i // 4, i % 4
        t0 = r * TL
        m = dict(shared)
        xall = np.concatenate([x[b, t0:t0 + TL], ctx[b]], 0)
        m["x0T"] = np.ascontiguousarray(xall.T.reshape(8, 128, T).transpose(1, 0, 2))
        cvv = np.stack([c[b], c_ctx], -1)
        m["cv"] = np.ascontiguousarray(cvv.reshape(8, 128, 2).transpose(1, 0, 2))
        m["cosT"], m["sinT"] = rope_tables(i)
        m["invcnt"] = invcnt_table(i)
        sel = np.zeros((128, 8), f32)
        if r > 0:
            sel[:, r - 1] = 1.0
        if r < 3:
            sel[:, 4 + r + 1] = 1.0
        m["selT"] = sel
        in_maps.append(m)
    return in_maps


def kernel(**inputs):
    in_maps = make_in_maps(inputs)
    if "nc" not in _CACHE:
        _CACHE["nc"] = build_fused()
    res = run_bass_kernel_spmd(_CACHE["nc"], in_maps, core_ids=list(range(NCORE))).results
    out = np.zeros((2, SEQ, D), np.float32)
    for i in range(NCORE):
        b, t0 = i // 4, (i % 4) * TL
        xo = res[i]["outT"].transpose(1, 0, 2).reshape(D, TL)
        out[b, t0:t0 + TL] = xo.T
    return out
```

```python
import math
from contextlib import ExitStack

import numpy as np
import ml_dtypes

import concourse.bass as bass
import concourse.mybir as mybir
from concourse.bass_utils import run_bass_kernel_spmd

F32 = mybir.dt.float32
BF16 = mybir.dt.bfloat16
AF = mybir.ActivationFunctionType
ALU = mybir.AluOpType
AX = mybir.AxisListType
NPBF = ml_dtypes.bfloat16

D = 1024
DEPTH = 4
NCORE = 8
TL = 2048
TC = 256
T = TL + TC
SEQ = 8192
NKEY = SEQ + TC
NKT = NKEY // 128
FF = 2816
NJ = FF // 128
EPS = 1e-6
HALO = 16
GROUPS = [(0, 512), (512, 512), (1024, 512), (1536, 512), (2048, 256)]


class Sched:
    ENGS = ("pe", "act", "dve", "pool", "sp")

    def __init__(self, nc):
        self.nc = nc
        self.q = {e: [] for e in self.ENGS}
        self.cnt = {}
        self.res = {}
        self.waited = {e: {} for e in self.ENGS}
        self.bar = {}

    def barrier(self):
        self.bar = dict(self.cnt)

    def op(self, eng, fn, r=(), w=(), key=None, inc1=False):
        if key is None:
            sem, inc = "S_" + eng, 1
        elif inc1:
            sem, inc = "C_" + key, 1
        else:
            sem, inc = "D_" + key, 16
        deps = dict(self.bar)

        def need(sv):
            s, v = sv
            if deps.get(s, 0) < v:
                deps[s] = v

        for k in r:
            st = self.res.get(k)
            if st and st[0]:
                need(st[0])
            if st and k.startswith("ps"):
                for sv in st[1].items():
                    if sv[0] != sem:
                        need(sv)
        for k in w:
            st = self.res.get(k)
            if st:
                if st[0]:
                    need(st[0])
                for sv in st[1].items():
                    need(sv)
        waits = []
        for s, v in deps.items():
            if eng == "pe" and s == "S_pe":
                continue
            if self.waited[eng].get(s, 0) >= v:
                continue
            self.waited[eng][s] = v
            waits.append((s, v))
        val = self.cnt.get(sem, 0) + inc
        self.cnt[sem] = val
        self.q[eng].append((waits, fn, sem, inc))
        for k in r:
            st = self.res.setdefault(k, [None, {}])
            if st[1].get(sem, 0) < val:
                st[1][sem] = val
        for k in w:
            self.res[k] = [(sem, val), {}]

    def finish(self):
        waits = [(s, v) for s, v in self.cnt.items() if s.startswith("D_") or s.startswith("C_")]
        self.q["sp"].append((waits, None, None, 0))

    def emit(self, es):
        nc = self.nc
        sems = {n: es.enter_context(nc.semaphore(n)) for n in sorted(self.cnt)}
        block = es.enter_context(nc.Block())

        def run(name):
            def f(e):
                for waits, fn, sem, inc in self.q[name]:
                    attach = fn is not None and waits and not sem.startswith("C_")
                    for s, v in (waits[:-1] if attach else waits):
                        e.wait_ge(sems[s], v)
                    if fn is not None:
                        ins = fn(e)
                        if attach:
                            ins._wait_ge(sems[waits[-1][0]], waits[-1][1])
                        ins.then_inc(sems[sem], inc)
            return f

        block.tensor(run("pe"))
        block.scalar(run("act"))
        block.vector(run("dve"))
        block.gpsimd(run("pool"))
        block.sync(run("sp"))


class Ctx:
    def __init__(self):
        self.nc = bass.Bass("TRN2", target_bir_lowering=False)
        self.es = ExitStack()
        self.S = Sched(self.nc)
        self.n = 0
        self.stacks = [self.es]
        self.pfx = ""

    def push(self, pfx=None):
        if pfx is not None:
            self.pfx = pfx
        st = ExitStack()
        self.stacks.append(st)
        return st

    def pop(self):
        self.stacks.pop().close()
        self.S.barrier()

    def sb(self, name, shape, dt):
        return self.stacks[-1].enter_context(self.nc.sbuf_tensor(self.pfx + name, list(shape), dt))

    def din(self, name, shape, dt):
        return self.nc.dram_tensor(name, list(shape), dt, kind="ExternalInput").ap()

    def dout(self, name, shape, dt):
        return self.nc.dram_tensor(name, list(shape), dt, kind="ExternalOutput").ap()

    def dscratch(self, name, shape, dt):
        return self.nc.dram_tensor(name, list(shape), dt, kind="Internal").ap()

    def uid(self, p):
        self.n += 1
        return f"{p}{self.n}"

    def dma(self, q, out, in_, r, w, key, slow=False):
        if slow:
            self.S.op(q, lambda e: e.dma_start(out=out, in_=in_, allow_slow_non_contiguous=True), r=r, w=w, key=key)
        else:
            self.S.op(q, lambda e: e.dma_start(out=out, in_=in_), r=r, w=w, key=key)

    def mm(self, out, lhsT, rhs, start, stop, r, w):
        self.S.op("pe", lambda e: e.matmul(out, lhsT, rhs, start=start, stop=stop), r=r, w=w)

    def act(self, out, in_, func, r, w, bias=None, scale=None):
        kw = {}
        if bias is not None:
            kw["bias"] = bias
        if scale is not None:
            kw["scale"] = scale
        self.S.op("act", lambda e: e.activation(out=out, in_=in_, func=func, **kw), r=r, w=w)

    def tt(self, eng, out, in0, in1, op, r, w):
        self.S.op(eng, lambda e: e.tensor_tensor(out=out, in0=in0, in1=in1, op=op), r=r, w=w)

    def ts(self, eng, out, in0, s1, op0, r, w, s2=None, op1=None):
        if op1 is None:
            self.S.op(eng, lambda e: e.tensor_scalar(out=out, in0=in0, scalar1=s1, scalar2=None, op0=op0),
                      r=r, w=w)
        else:
            self.S.op(eng, lambda e: e.tensor_scalar(out=out, in0=in0, scalar1=s1, scalar2=s2, op0=op0, op1=op1),
                      r=r, w=w)

    def stt(self, out, in0, scalar, in1, op0, op1, r, w):
        self.S.op("dve", lambda e: e.scalar_tensor_tensor(out=out, in0=in0, scalar=scalar, in1=in1,
                                                          op0=op0, op1=op1), r=r, w=w)

    def copy(self, eng, out, in_, r, w):
        self.S.op(eng, lambda e: e.tensor_copy(out=out, in_=in_), r=r, w=w)

    def memset(self, eng, ap, val, w):
        self.S.op(eng, lambda e: e.memset(ap, val), w=w)

    def recip(self, out, in_, r, w):
        self.S.op("dve", lambda e: e.reciprocal(out=out, in_=in_), r=r, w=w)

    def recip_act(self, out, in_, r, w, one=None):
        if one is not None:
            self.act(out, in_, AF.Ln, r=list(r) + ["cmat_f"], w=w, bias=one)
        else:
            self.act(out, in_, AF.Ln, r=r, w=w)
        self.act(out, out, AF.Exp, r=w, w=w, scale=-1.0)

    def finish(self):
        self.S.finish()
        self.S.emit(self.es)
        self.es.close()
        return self.nc


def rstd_from_psum(cx, ps_ap, ps_key, tmp_ap, tmp_key, mhalf_ap, n):
    cx.act(tmp_ap, ps_ap, AF.Ln, r=[ps_key, "consts"], w=[tmp_key], bias=mhalf_ap[:, 0:1])
    cx.act(tmp_ap, tmp_ap, AF.Exp, r=[tmp_key], w=[tmp_key], scale=-0.5)


def emit_mod_pre(cx, cv, lam_in, lam_out, silb):
    cvs = cx.sb("m_cv", [128, 8, 2], F32)
    th = cx.sb("m_th", [128, 8, 2], F32)
    sil = cx.sb("m_sil", [128, 8, 2], F32)
    lamt = cx.sb("m_lamt", [128, 4, 4, 64], F32)
    prod = cx.sb("m_prod", [128, 4, 2, 64], F32)
    lsum = cx.sb("m_lsum", [128, 4, 2], F32)
    lexp = cx.sb("m_lexp", [128, 4, 2], F32)
    lams = cx.sb("m_lams", [128, 4], F32)
    cx.dma("sp", cvs[:], cv, r=[], w=["m_cv"], key="m_cv")
    cx.dma("sp", lamt[:], lam_in, r=[], w=["m_lamt"], key="m_lamt")
    cx.act(th[:], cvs[:], AF.Exp, r=["m_cv"], w=["m_th"], scale=-1.0)
    cx.ts("dve", th[:], th[:], 1.0, ALU.add, r=["m_th"], w=["m_th"])
    cx.recip(th[:], th[:], r=["m_th"], w=["m_th"])
    cx.tt("dve", sil[:], th[:], cvs[:], ALU.mult, r=["m_th", "m_cv"], w=["m_sil"])
    cx.copy("dve", silb[:], sil[:], r=["m_sil"], w=["silb"])
    cx.tt("dve", prod[:, :, 0, :], lamt[:, :, 0, :], lamt[:, :, 1, :], ALU.mult, r=["m_lamt"], w=["m_prod0"])
    cx.tt("dve", prod[:, :, 1, :], lamt[:, :, 2, :], lamt[:, :, 3, :], ALU.mult, r=["m_lamt"], w=["m_prod1"])
    cx.S.op("dve", lambda e: e.tensor_reduce(out=lsum[:], in_=prod[:], axis=AX.X, op=ALU.add),
            r=["m_prod0", "m_prod1"], w=["m_lsum"])
    cx.act(lexp[:], lsum[:], AF.Exp, r=["m_lsum"], w=["m_lexp"])
    cx.tt("dve", lams[:], lexp[:, :, 0], lexp[:, :, 1], ALU.subtract, r=["m_lexp"], w=["m_lams"])
    cx.dma("sp", lam_out, lams[:], r=["m_lams"], w=["lam_out"], key="m_lamo")


def mod_layer_gen(cx, ps, jobs, silb, w_mod, b_mod_fm, modT_out, bank_fn, alloc):
    SW = 384
    slab = [alloc(f"ml_slab{i}", [128, 8, SW], BF16) for i in range(2)]
    modsb = alloc("ml_modsb", [128, 48, 2], F32)
    bfm = alloc("ml_bfm", [128, 48], F32)
    NE = SW // 128
    cnt = 0
    for l, s_lo, s_hi in jobs:
        cx.dma("sp", bfm[:], b_mod_fm[l], r=[], w=["ml_bfm"], key="ml_bfm")
        for sidx in range(s_lo, s_hi):
            sl = slab[cnt % 2]
            sk = f"ml_slab{cnt % 2}"
            cnt += 1
            e0 = sidx * SW
            cx.dma("pool", sl[:], w_mod[l, :, e0:e0 + SW].rearrange("(c p) e -> p c e", p=128), r=[], w=[sk],
                   key=sk)
            yield
            bank, pk, rel = bank_fn()
            for j in range(NE):
                out = ps[:, bank, 2 * j:2 * j + 2]
                for c in range(8):
                    cx.mm(out, sl[:, c, j * 128:(j + 1) * 128], silb[:, c, :], start=(c == 0), stop=(c == 7),
                          r=[sk, "silb"], w=[pk])
            cx.tt("dve", modsb[:, sidx * NE:(sidx + 1) * NE, :],
                  ps[:, bank, 0:2 * NE].rearrange("p (j t) -> p j t", t=2),
                  bfm[:, sidx * NE:(sidx + 1) * NE, None].to_broadcast([128, NE, 2]), ALU.add,
                  r=[pk, "ml_bfm"], w=["ml_modsb"])
            rel()
            yield
        cx.dma("sp", modT_out[:, l, s_lo * NE:s_hi * NE, :], modsb[:, s_lo * NE:s_hi * NE, :], r=["ml_modsb"],
               w=[cx.uid("modT")], key="ml_modo")
        yield


class Rot:
    def __init__(self, cx, name, shape, dt, n, alloc=None):
        alloc = alloc or cx.sb
        self.bufs = [alloc(f"{name}{i}", shape, dt) for i in range(n)]
        self.keys = [f"{name}{i}" for i in range(n)]
        self.i = 0

    def next(self):
        i = self.i % len(self.bufs)
        self.i += 1
        return self.bufs[i], self.keys[i]


class BankRot:
    def __init__(self, banks, held=None):
        self.banks = banks
        self.held = set() if held is None else held
        self.i = 0

    def next(self):
        for _ in range(len(self.banks)):
            b = self.banks[self.i % len(self.banks)]
            self.i += 1
            if b not in self.held:
                self.held.add(b)
                return b, f"ps{b}"
        raise RuntimeError(f"no free PSUM bank among {self.banks}")

    def release(self, b):
        self.held.discard(b)


def load_consts(cx, cmat_d, nm=6):
    cf = cx.sb("cmat_f", [128, nm, 128], F32)
    cb = cx.sb("cmat_b", [128, nm, 128], BF16)
    mh = cx.sb("epsb", [128, 1024], F32)
    cx.dma("sp", cf[:], cmat_d, r=[], w=["cmat_f"], key="cmat_f")
    cx.copy("dve", cb[:], cf[:], r=["cmat_f"], w=["consts"])
    cx.memset("dve", mh[:], EPS, w=["consts"])
    return cf, cb, mh


def emit_norm_mod(*a, **k):
    for _ in norm_mod_gen(*a, **k):
        pass


def norm_mod_gen(cx, g, xg, xgk, hT, hTk, sq, u, rs_rot, onesD, mhalf, ps, auxb, Gs, shs):
    off, n = GROUPS[g]
    col = 1 if off >= TL else 0
    cx.act(sq[:, :, :n], xg[:, :, :n], AF.Square, r=[xgk], w=["sq"])
    yield
    b, bk = auxb.next()
    for c in range(8):
        cx.mm(ps[:, b, :n], onesD, sq[:, c, :n], start=(c == 0), stop=(c == 7), r=["sq", "consts"], w=[bk])
    rs, rsk = rs_rot.next()
    rstd_from_psum(cx, ps[:, b, :n], bk, rs[:, :n], rsk, mhalf[:, :n], n)
    auxb.release(b)
    yield
    uk = "u"
    if u is None:
        u, uk = xg, xgk
    cx.tt("dve", u[:, :, :n], xg[:, :, :n], rs[:, None, :n].to_broadcast([128, 8, n]), ALU.mult,
          r=[xgk, rsk], w=[uk])
    yield
    yield
    for c in range(8):
        if c % 2 == 0:
            cx.act(hT[:, c, :n], u[:, c, :n], AF.Identity, r=[uk, "modc"], w=[hTk],
                   bias=shs[:, c, col:col + 1], scale=Gs[:, c, col:col + 1])
        else:
            cx.ts("dve", hT[:, c, :n], u[:, c, :n], Gs[:, c, col:col + 1], ALU.mult, r=[uk, "modc"], w=[hTk],
                  s2=shs[:, c, col:col + 1], op1=ALU.add)


def emit_A(cx, consts, ps, xT, mod, n1g, w_in, qkg, cosT, sinT, sinks):
    nc = cx.nc
    S = cx.S
    cf, cb, mhalf = consts
    onesD, ones64, pswap = cb[:, 0, :], cb[:, 1, :], cb[:, 2, :]
    mainb = BankRot([0, 1, 2, 3, 4])
    auxb = BankRot([5, 6, 7], held=mainb.held)
    allb = BankRot([0, 1, 2, 3, 4, 5, 6, 7], held=mainb.held)

    w_sb = cx.sb("a_w", [128, 8, 2304], BF16)
    for wk, c0, c1 in (("a_wk", 512, 1024), ("a_wv", 1024, 1536), ("a_wp", 1536, 2304), ("a_wq", 0, 512)):
        cx.dma("pool", w_sb[:, :, c0:c1], w_in[:, :, c0:c1], r=[], w=[wk], key=wk)
    mods = cx.sb("a_mod", [128, 48, 2], F32)
    n1gs = cx.sb("a_n1g", [128, 8], F32)
    qkgs = cx.sb("a_qkg", [128, 2], F32)
    cos_s = cx.sb("a_cos", [128, TL], F32)
    sin_s = cx.sb("a_sin", [128, TL], F32)
    cx.dma("sp", mods[:], mod, r=[], w=["a_mod"], key="a_mod")
    cx.dma("sp", n1gs[:], n1g, r=[], w=["a_n1g"], key="a_n1g")
    cx.dma("sp", qkgs[:], qkg, r=[], w=["a_qkg"], key="a_qkg")
    Gs = cx.sb("a_G", [128, 8, 2], F32)
    cx.stt(Gs[:], mods[:, 8:16, :], 1.0, n1gs[:, :, None].to_broadcast([128, 8, 2]), ALU.add, ALU.mult,
           r=["a_mod", "a_n1g"], w=["modc"])
    shs = mods[:, 0:8, :]

    xg_rot = Rot(cx, "a_xg", [128, 8, 512], F32, 2)
    sq = cx.sb("a_sq", [128, 8, 512], BF16)
    u = None
    rs_rot = Rot(cx, "a_rs", [128, 512], F32, 2)
    sq2_rot = Rot(cx, "a_sq2", [128, 512], BF16, 2)
    r2_rot = Rot(cx, "a_r2", [128, 512], F32, 2)
    qn_rot = Rot(cx, "a_qn", [128, 512], F32, 3)
    hi_rot = Rot(cx, "a_hi", [128, 512], BF16, 2)
    lo_rot = Rot(cx, "a_lo", [128, 512], BF16, 2)
    t1_rot = Rot(cx, "a_t1", [128, 512], F32, 2)
    t2_rot = Rot(cx, "a_t2", [128, 512], F32, 2)
    qo_rot = Rot(cx, "a_qo", [128, 512], BF16, 8)
    po_rot = Rot(cx, "a_po", [128, 512], F32, 4)
    th_rot = Rot(cx, "a_th", [128, 512], F32, 4)
    gl_rot = Rot(cx, "a_gl", [128, 512], F32, 4)
    vo_rot = Rot(cx, "a_vo", [128, 512], BF16, 4)

    pending = []

    def advance():
        for gen in list(pending):
            try:
                next(gen)
            except StopIteration:
                pending.remove(gen)

    def load_x(g):
        off, n = GROUPS[g]
        xg, xgk = xg_rot.next()
        cx.dma("sp", xg[:, :, :n], xT[:, :, off:off + n], r=[], w=[xgk], key=xgk)
        return xg, xgk

    def qk_post(g, which, h, b, bk):
        off, n = GROUPS[g]
        latent = off < TL
        sq2, sq2k = sq2_rot.next()
        cx.act(sq2[:, :n], ps[:, b, :n], AF.Square, r=[bk], w=[sq2k])
        yield
        b2, b2k = auxb.next()
        cx.mm(ps[:, b2, :n], ones64, sq2[:, :n], start=True, stop=True, r=[sq2k, "consts"], w=[b2k])
        yield
        r2, r2k = r2_rot.next()
        rstd_from_psum(cx, ps[:, b2, :n], b2k, r2[:, :n], r2k, mhalf[:, :n], n)
        auxb.release(b2)
        qn, qnk = qn_rot.next()
        cx.stt(qn[:, :n], ps[:, b, :n], qkgs[:, which:which + 1], r2[:, :n], ALU.mult, ALU.mult,
               r=[bk, r2k, "a_qkg"], w=[qnk])
        mainb.release(b)
        dst, dres = sinks["q" if which == 0 else "k"](h, off, n)
        dkey = f"{'qT' if which == 0 else 'kT'}_o"
        qo, qok = qo_rot.next()
        if not latent:
            cx.act(qo[:, :n], qn[:, :n], AF.Copy, r=[qnk], w=[qok])
            yield
            cx.dma("sp", dst, qo[:, :n], r=[qok], w=[dres or cx.uid(dkey)], key=qok)
            return
        hi, hik = hi_rot.next()
        lo, lok = lo_rot.next()
        cx.act(hi[:, :n], qn[:, :n], AF.Copy, r=[qnk], w=[hik])
        cx.tt("dve", lo[:, :n], qn[:, :n], hi[:, :n], ALU.subtract, r=[qnk, hik], w=[lok])
        yield
        b3, b3k = auxb.next()
        cx.mm(ps[:, b3, :n], pswap, hi[:, :n], start=True, stop=False, r=[hik, "consts"], w=[b3k])
        cx.mm(ps[:, b3, :n], pswap, lo[:, :n], start=False, stop=True, r=[lok, "consts"], w=[b3k])
        t1, t1k = t1_rot.next()
        cx.tt("dve", t1[:, :n], qn[:, :n], cos_s[:, off:off + n], ALU.mult, r=[qnk, "a_cos"], w=[t1k])
        yield
        t2, t2k = t2_rot.next()
        cx.tt("dve", t2[:, :n], ps[:, b3, :n], sin_s[:, off:off + n], ALU.mult, r=[b3k, "a_sin"], w=[t2k])
        auxb.release(b3)
        cx.tt("dve", qo[:, :n], t1[:, :n], t2[:, :n], ALU.add, r=[t1k, t2k], w=[qok])
        yield
        cx.dma("sp", dst, qo[:, :n], r=[qok], w=[dres or cx.uid(dkey)], key=qok)

    def pool_post(g, ci, b, bk):
        off, n = GROUPS[g]
        po, pok = po_rot.next()
        cx.act(po[:, :n], ps[:, b, :n], AF.Copy, r=[bk], w=[pok])
        mainb.release(b)
        yield
        cx.dma("sp", sinks["up"](ci, off, n), po[:, :n], r=[pok], w=[cx.uid("up_o")], key=pok)
        for side, sl in ((0, slice(0, HALO)), (1, slice(n - HALO, n))):
            e_ap = sinks["edge"](0, ci, side, off)
            if e_ap is not None:
                cx.dma("sp", e_ap, po[:, sl], r=[pok], w=["sendE"], key=f"edge0{ci}{side}")

    def glu_post(g, ci, ba, bak, bb, bbk):
        off, n = GROUPS[g]
        th, thk = th_rot.next()
        cx.act(th[:, :n], ps[:, bb, :n], AF.Exp, r=[bbk], w=[thk], scale=-1.0)
        mainb.release(bb)
        yield
        gl, glk = gl_rot.next()
        cx.recip_act(th[:, :n], th[:, :n], r=[thk], w=[thk], one=cf[:, 4, 0:1])
        cx.tt("dve", gl[:, :n], th[:, :n], ps[:, ba, :n], ALU.mult, r=[thk, bak], w=[glk])
        mainb.release(ba)
        yield
        cx.dma("sp", sinks["glu"](ci, off, n), gl[:, :n], r=[glk], w=[cx.uid("glu_o")], key=glk)
        for side, sl in ((0, slice(0, HALO)), (1, slice(n - HALO, n))):
            e_ap = sinks["edge"](1, ci, side, off)
            if e_ap is not None:
                cx.dma("sp", e_ap, gl[:, sl], r=[glk], w=["sendE"], key=f"edge1{ci}{side}")

    def v_post(g, ti, b, bk):
        off, n = GROUPS[g]
        vo, vok = vo_rot.next()
        cx.copy("dve", vo[:], ps[:, b, :], r=[bk], w=[vok])
        mainb.release(b)
        yield
        vdst, vres = sinks["v"](off // 128 + ti)
        cx.dma("sp", vdst, vo[:].rearrange("p (h d) -> p h d", h=4), r=[vok],
               w=[vres or cx.uid("v_o")], key=vok)

    ng = len(GROUPS)
    hT_all = cx.sb("a_hTall", [128, 8, T], BF16)

    def hT_of(g):
        off, n = GROUPS[g]
        return hT_all[:, :, off:off + n], f"a_hT{g}"

    def run_chunks(g, chunks, hook=None, alloc=None, per_chunk=None):
        alloc = alloc or mainb
        off, n = GROUPS[g]
        hT, hTk = hT_of(g)
        ca_banks = {}
        for ci, (kind, idx, co, wk) in enumerate(chunks):
            b, bk = alloc.next()
            if kind == "v":
                for c in range(8):
                    cx.mm(ps[:, b, :], hT[:, c, idx * 128:(idx + 1) * 128], w_sb[:, c, 1024:1536],
                          start=(c == 0), stop=(c == 7), r=[hTk, wk], w=[bk])
            else:
                for c in range(8):
                    cx.mm(ps[:, b, :n], w_sb[:, c, co:co + 128], hT[:, c, :n],
                          start=(c == 0), stop=(c == 7), r=[hTk, wk], w=[bk])
            advance()
            if kind == "q":
                pending.append(qk_post(g, 0, idx, b, bk))
            elif kind == "k":
                pending.append(qk_post(g, 1, idx, b, bk))
            elif kind == "pool":
                pending.append(pool_post(g, idx, b, bk))
            elif kind == "ca":
                ca_banks[idx] = (b, bk)
            elif kind == "cb":
                ba, bak = ca_banks[idx]
                pending.append(glu_post(g, idx, ba, bak, b, bk))
            elif kind == "v":
                pending.append(v_post(g, idx, b, bk))
            if hook is not None and ci == 3:
                hook()
            if per_chunk is not None:
                per_chunk(ci)

    def drain():
        while pending:
            advance()

    def norm_group(g, xgx):
        xg, xgk = xgx
        hT, hTk = hT_of(g)
        emit_norm_mod(cx, g, xg, xgk, hT, hTk, sq, u, rs_rot, onesD, mhalf, ps, auxb, Gs, shs)

    cc = sinks.get("cc", lambda name: None)
    nxt_x = load_x(0)
    norm_group(0, nxt_x)
    nxt_x = load_x(1)
    cx.dma("sp", cos_s[:], cosT, r=[], w=["a_cos"], key="a_cos")
    cx.dma("sp", sin_s[:], sinT, r=[], w=["a_sin"], key="a_sin")
    for g in range(ng):
        off, n = GROUPS[g]
        chunks = [("k", h, 512 + h * 128, "a_wk") for h in range(4)]
        chunks += [("v", ti, 1024, "a_wv") for ti in range(n // 128)]

        ngen = [None]

        def per_chunk(ci, g=g, ngen=ngen):
            nonlocal nxt_x
            if ci == 0 and g + 1 < ng:
                xg1, xg1k = nxt_x
                hT1, hT1k = hT_of(g + 1)
                ngen[0] = norm_mod_gen(cx, g + 1, xg1, xg1k, hT1, hT1k, sq, u, rs_rot, onesD, mhalf, ps, auxb,
                                       Gs, shs)
                if g + 2 < ng:
                    nxt_x = load_x(g + 2)
            if ngen[0] is not None:
                try:
                    next(ngen[0])
                except StopIteration:
                    ngen[0] = None
            if ci == 3 and g in (2, 4):
                cc("K%d" % (g // 2 - 1))
                cc("V%d" % (g // 2 - 1))

        run_chunks(g, chunks, None, per_chunk=per_chunk)
        while ngen[0] is not None:
            per_chunk(-1)
    for g in range(ng):
        chunks = [("pool", i, 1536 + i * 128, "a_wp") for i in range(2)]
        chunks += [("ca", i, 1792 + i * 128, "a_wp") for i in range(2)]
        chunks += [("cb", i, 2048 + i * 128, "a_wp") for i in range(2)]
        run_chunks(g, chunks, (lambda: cc("E")) if g == 4 else None, alloc=allb)
    for g in range(ng):
        run_chunks(g, [("q", h, h * 128, "a_wq") for h in range(4)])
    drain()


def fm(v, nch):
    return np.ascontiguousarray(np.asarray(v, np.float32).reshape(nch, 128).T)


def fm_w(w):
    k, e = w.shape
    return np.ascontiguousarray(w.reshape(k // 128, 128, e).transpose(1, 0, 2))


def rope_tables(core):
    t = (core % 4) * TL + np.arange(TL)
    row = (t // 64).astype(np.float64)
    col = (t % 64).astype(np.float64)
    inv_freq = 10000.0 ** (-np.arange(0, 32, 2, dtype=np.float64) / 32)
    ang = np.concatenate([row[:, None] * inv_freq, col[:, None] * inv_freq], -1)
    p = np.arange(128)
    pair = (p % 64) // 2
    cosT = np.cos(ang)[:, pair].T
    sgn = np.where(p % 2 == 0, -1.0, 1.0)[:, None]
    sinT = np.sin(ang)[:, pair].T * sgn
    return np.ascontiguousarray(cosT, np.float32), np.ascontiguousarray(sinT, np.float32)


def const_mats():
    m = np.zeros((128, 6, 128), np.float32)
    m[:, 0, :] = 1.0 / 1024
    m[0:64, 1, 0:64] = 1.0 / 64
    m[64:128, 1, 64:128] = 1.0 / 64
    idx = np.arange(128)
    m[idx ^ 1, 2, idx] = 1.0
    m[:, 3, :] = 1.0 / 128
    m[:, 4, :] = 1.0
    m[:, 5, :] = 1.0 / 256
    return m


NVEC = 24
TPL = TL + 2 * HALO
TPC = TC + 2 * HALO
TP = TPL + TPC
KPIECES = 6
KTP = NKT // KPIECES


class SAlloc:
    def __init__(self):
        self.held = set()
        self.i = 0
        self.j = 0

    def pair(self):
        for _ in range(2):
            p = self.i % 2
            self.i += 1
            if (2 * p) not in self.held and (2 * p + 1) not in self.held:
                return 2 * p
        raise RuntimeError("no free score pair")

    def one(self):
        for _ in range(2):
            b = self.j % 2
            self.j += 1
            if b not in self.held:
                self.held.add(b)
                return b, f"ps{b}"
        raise RuntimeError("no free stats bank")

    def release(self, b):
        self.held.discard(b)


def emit_B(cx, consts, ps, xT, mod, lam_ap, qT, pieces, kv_src, up_pad, glu_pad, invcnt, poolw, vecs, convw,
           w_out, w_ffn_in, w_ffn_out, xT_o_fn, last=False, bg=None):
    nc = cx.nc
    S = cx.S
    cf, cb, mhalf = consts
    onesD, ones128n, ones1, = cb[:, 0, :], cb[:, 3, :], cb[:, 4, :]
    ones256f = cf[:, 5, :]
    sa = SAlloc()
    groups_run = [g for g in range(len(GROUPS)) if not (last and GROUPS[g][0] >= TL)]
    piece_of = {}
    for pi, (k0, kc) in enumerate(pieces):
        for kt in range(k0, k0 + kc):
            piece_of[kt] = pi

    mods = cx.sb("b_mod", [128, 48, 2], F32)
    vec = cx.sb("b_vec", [128, NVEC], F32)
    cw = cx.sb("b_cw", [128, 2, 31], F32)
    pwf = cx.sb("b_pwf", [128, 2, 128], F32)
    pwb = cx.sb("b_pwb", [128, 2, 128], BF16)
    cx.dma("sp", vec[:], vecs, r=[], w=["b_vec"], key="b_vec")
    cx.dma("sp", cw[:], convw, r=[], w=["b_cw"], key="b_cw")
    cx.dma("sp", pwf[:], poolw, r=[], w=["b_pwf"], key="b_pwf")
    lamt = cx.sb("b_lamt", [128, 1], F32)
    cx.dma("sp", lamt[:], lam_ap, r=[], w=["b_lamt"], key="b_lamt", slow=True)
    cx.copy("dve", pwb[:], pwf[:], r=["b_pwf"], w=["b_pwb"])
    der = cx.sb("b_der", [128, 8], F32)
    cx.tt("dve", der[:, 0:1], lamt[:, 0:1], vec[:, 10:11], ALU.add, r=["b_vec", "b_lamt"], w=["b_der0"])
    cx.ts("dve", der[:, 0:1], der[:, 0:1], -1.0, ALU.mult, r=["b_der0"], w=["b_der0"])
    cx.tt("dve", der[:, 1:2], vec[:, 8:9], vec[:, 11:12], ALU.mult, r=["b_vec"], w=["b_der1"])
    cx.copy("dve", der[:, 2:6], vec[:, 4:8], r=["b_vec"], w=["b_der2"])
    derk = ["b_der0", "b_der1", "b_der2"]
    G2 = cx.sb("b_G2", [128, 8, 2], F32)
    g2h = cx.sb("b_g2h", [128, 8, 2], F32)
    sh2 = mods[:, 24:32, :]
    g1 = mods[:, 16:24, :]

    mixT = cx.sb("b_mix", [128, 8, T], BF16)

    cx.push()
    sb1 = cx.sb

    NP = 512 + 2 * HALO
    U = sb1("p_U", [128, 2, NP], F32)
    IC = sb1("p_IC", [128, 2, 512], F32)
    Sa = sb1("p_Sa", [128, 2, NP], F32)
    Sb = sb1("p_Sb", [128, 2, NP], F32)
    Sc = sb1("p_Sc", [128, 2, NP], F32)
    Sd = sb1("p_Sd", [128, 2, NP], F32)
    ptmp = sb1("p_tmp", [128, 2, 512], F32)
    pin = sb1("p_in", [128, 2, 512], BF16)
    G = sb1("c_G", [128, 2, NP], F32)
    acc = sb1("c_acc", [128, 2, 512], F32)
    sqa = sb1("c_sqa", [128, 2, 512], F32)
    msb = sb1("c_msb", [128, 512], F32)
    m2 = sb1("c_m2", [128, 512], F32)
    vr = sb1("c_vr", [128, 512], F32)
    dd = sb1("c_dd", [128, 2, 512], F32)
    zh = sb1("c_zh", [128, 2, 512], F32)
    cth = sb1("c_th", [128, 2, 512], F32)

    def b1_gen():
        for g in groups_run:
            off, n = GROUPS[g]
            po = off if off < TL else TPL + (off - TL)
            m = n + 2 * HALO
            cx.dma("sp", U[:, :, :m], up_pad[:, :, po:po + m], r=["halo_up"], w=["p_U"], key="p_U")
            cx.dma("sp", IC[:, :, :n], invcnt[:, :, off:off + n], r=[], w=["p_IC"], key="p_IC")
            yield
            cx.tt("dve", Sa[:, :, 1:m], U[:, :, 0:m - 1], U[:, :, 1:m], ALU.add, r=["p_U"], w=["p_Sa"])
            yield
            cx.tt("dve", Sb[:, :, 2:m - 1], Sa[:, :, 1:m - 2], Sa[:, :, 3:m], ALU.add, r=["p_Sa"], w=["p_Sb"])
            yield
            cx.tt("dve", Sc[:, 1, 4:m - 3], Sb[:, 1, 2:m - 5], Sb[:, 1, 6:m - 1], ALU.add, r=["p_Sb"], w=["p_Sc"])
            yield
            cx.tt("dve", Sd[64:128, 1, 8:m - 7], Sc[64:128, 1, 4:m - 11], Sc[64:128, 1, 12:m - 3], ALU.add,
                  r=["p_Sc"], w=["p_Sd"])
            yield
            srcs = [(Sa, "p_Sa", 0, 0), (Sb, "p_Sb", 64, 0), (Sc, "p_Sc", 0, 1), (Sd, "p_Sd", 64, 1)]
            for sbuf, sk, p0, ci in srcs:
                cx.tt("dve", ptmp[p0:p0 + 64, ci, :n], sbuf[p0:p0 + 64, ci, HALO:HALO + n],
                      IC[p0:p0 + 64, ci, :n], ALU.mult, r=[sk, "p_IC"], w=[f"p_tmp{p0}{ci}"])
                cx.tt("dve", pin[p0:p0 + 64, ci, :n], ptmp[p0:p0 + 64, ci, :n],
                      U[p0:p0 + 64, ci, HALO:HALO + n], ALU.subtract, r=[f"p_tmp{p0}{ci}", "p_U"],
                      w=[f"p_in{p0}{ci}"])
                yield
            pk = [f"p_in{p0}{ci}" for _, _, p0, ci in srcs]
            for ci in range(2):
                b, bk = sa.one()
                cx.mm(ps[:, b, :n], pwb[:, ci, :], pin[:, ci, :n], start=True, stop=True,
                      r=pk + ["b_pwb"], w=[bk])
                cx.ts("dve", mixT[:, 4 + ci, off:off + n], ps[:, b, :n], vec[:, ci:ci + 1], ALU.mult,
                      r=[bk, "b_vec"], w=[f"mix{4 + ci}_{g}"])
                sa.release(b)
                yield
            cx.dma("sp", G[:, :, :m], glu_pad[:, :, po:po + m], r=["halo_glu"], w=["c_G"], key="c_G")
            yield
            for ci in range(2):
                cx.ts("dve", acc[:, ci, :n], G[:, ci, 1:1 + n], cw[:, ci, 0:1], ALU.mult,
                      r=["c_G", "b_cw", "b_vec"], w=[f"c_acc{ci}"], s2=vec[:, 2 + ci:3 + ci], op1=ALU.add)
                yield
                for k in range(1, 31):
                    cx.stt(acc[:, ci, :n], G[:, ci, 1 + k:1 + k + n], cw[:, ci, k:k + 1], acc[:, ci, :n],
                           ALU.mult, ALU.add, r=["c_G", "b_cw", f"c_acc{ci}"], w=[f"c_acc{ci}"])
                    yield
            cx.tt("dve", sqa[:, :, :n], acc[:, :, :n], acc[:, :, :n], ALU.mult, r=["c_acc0", "c_acc1"], w=["c_sqa"])
            yield
            b1_, b1k = sa.one()
            for ci in range(2):
                cx.mm(ps[:, b1_, :n], ones256f, acc[:, ci, :n], start=(ci == 0), stop=(ci == 1),
                      r=["c_acc0", "c_acc1", "cmat_f"], w=[b1k])
            cx.copy("dve", msb[:, :n], ps[:, b1_, :n], r=[b1k], w=["c_msb"])
            sa.release(b1_)
            cx.tt("dve", m2[:, :n], msb[:, :n], msb[:, :n], ALU.mult, r=["c_msb"], w=["c_m2"])
            yield
            b2_, b2k = sa.one()
            for ci in range(2):
                cx.mm(ps[:, b2_, :n], ones256f, sqa[:, ci, :n], start=(ci == 0), stop=(ci == 1),
                      r=["c_sqa", "cmat_f"], w=[b2k])
            cx.stt(vr[:, :n], ps[:, b2_, :n], EPS, m2[:, :n], ALU.add, ALU.subtract, r=[b2k, "c_m2"], w=["c_vr"])
            sa.release(b2_)
            cx.act(vr[:, :n], vr[:, :n], AF.Ln, r=["c_vr"], w=["c_vr"])
            cx.act(vr[:, :n], vr[:, :n], AF.Exp, r=["c_vr"], w=["c_vr"], scale=-0.5)
            yield
            cx.tt("dve", dd[:, :, :n], acc[:, :, :n], msb[:, None, :n].to_broadcast([128, 2, n]), ALU.subtract,
                  r=["c_acc0", "c_acc1", "c_msb"], w=["c_dd"])
            cx.tt("dve", dd[:, :, :n], dd[:, :, :n], vr[:, None, :n].to_broadcast([128, 2, n]), ALU.mult,
                  r=["c_dd", "c_vr"], w=["c_dd"])
            yield
            for ci in range(2):
                cx.ts("dve", zh[:, ci, :n], dd[:, ci, :n], der[:, 2 + ci:3 + ci], ALU.mult,
                      r=["c_dd", "b_der2"], w=[f"c_zh{ci}"], s2=der[:, 4 + ci:5 + ci], op1=ALU.add)
            yield
            cx.act(cth[:, :, :n], zh[:, :, :n], AF.Exp, r=["c_zh0", "c_zh1"], w=["c_th"], scale=-1.0)
            yield
            cx.recip_act(cth[:, :, :n], cth[:, :, :n], r=["c_th"], w=["c_th"], one=cf[:, 4, 0:1])
            yield
            cx.tt("dve", mixT[:, 6:8, off:off + n], cth[:, :, :n], zh[:, :, :n], ALU.mult,
                  r=["c_th", "c_zh0", "c_zh1"], w=[f"mix6_{g}", f"mix7_{g}"])
            yield

    b1 = b1_gen()
    b1_done = [False]

    def bg_bank():
        b, bk = sa.one()
        return b, bk, (lambda: sa.release(b))

    bg_gen = bg(sb1, bg_bank) if bg is not None else None
    bg_done = [bg_gen is None]

    def bg_step():
        if bg_done[0]:
            return
        try:
            next(bg_gen)
        except StopIteration:
            bg_done[0] = True

    def b1_step():
        if b1_done[0]:
            return
        try:
            next(b1)
        except StopIteration:
            b1_done[0] = True

    Kh = sb1("a_K", [128, NKEY], BF16)
    Vh = sb1("a_V", [128, NKT, 128], BF16)
    qh_rot = [sb1(f"a_q{i}", [128, T], BF16) for i in range(2)]
    NPT = 6
    Pt = [sb1(f"a_P{i}", [128, 2, 512], BF16) for i in range(NPT)]
    rD = sb1("a_rD", [128, 2, 512], F32)
    oo = sb1("a_oo", [128, 2, 512], F32)
    pacc_rot = Rot(cx, "a_Pacc", [128, 2, 512], BF16, 2, alloc=sb1)
    QD = 22
    o_rot = Rot(cx, "a_o", [128, 512], F32, 2, alloc=sb1)
    osq_rot = Rot(cx, "a_osq", [128, 512], BF16, 2, alloc=sb1)
    r3_rot = Rot(cx, "a_r3", [128, 512], F32, 2, alloc=sb1)

    def load_kv(h, piece):
        k0, kc = pieces[piece]
        ksrc, vsrc = kv_src(h, piece)
        cx.dma("sp", Kh[:, k0 * 128:(k0 + kc) * 128], ksrc, r=["recv"], w=[f"K{piece}"], key=f"K{piece}")
        cx.dma("sp", Vh[:, k0:k0 + kc, :], vsrc, r=["recv"], w=[f"V{piece}"], key=f"V{piece}")

    def load_q(h):
        cx.dma("sp", qh_rot[h % 2][:], qT[h], r=[], w=[f"a_q{h % 2}"], key=f"a_q{h % 2}")

    load_q(0)
    for piece in range(len(pieces)):
        load_kv(0, piece)

    gorder = [g for g in [4, 0, 1, 2, 3] if g in groups_run]
    post_pending = []
    it = 0
    for h in range(4):
        qh = qh_rot[h % 2]
        qk = f"a_q{h % 2}"
        if h + 1 < 4:
            load_q(h + 1)
        for gi, g in enumerate(gorder):
            off, n = GROUPS[g]
            kts = list(range(2)) if off >= TL else list(range(NKT))
            last_group = (gi == len(gorder) - 1)
            pvq = []
            quad = []
            dq = []
            dstate = {"first": True, "acc": None}

            def flush_d(final, n=n, dq=dq, dstate=dstate):
                while dq:
                    src, srck = dq.pop(0)
                    lastd = final and not dq
                    for c in range(2):
                        cx.mm(ps[:, 6 + c, :n], ones1, src[:, c, :n], start=dstate["first"], stop=lastd,
                              r=["consts", srck], w=[f"ps{6 + c}"])
                    dstate["first"] = False

            def emit_pv(prev, final, n=n, quad=quad, dq=dq, dstate=dstate):
                pkt, ppt, pptk, pidx = prev
                pp = piece_of[pkt]
                flush_d(False)
                for c in range(2):
                    cx.mm(ps[:, 4 + c, :n], Vh[:, pkt, :], ppt[:, c, :n], start=(pidx == 0), stop=final,
                          r=[f"V{pp}", pptk], w=[f"ps{4 + c}"])
                quad.append((ppt, pptk))
                if len(quad) == 2:
                    dstate["acc"] = pacc_rot.next()
                    acc, acck = dstate["acc"]
                    cx.tt("dve", acc[:, :, :n], quad[0][0][:, :, :n], ppt[:, :, :n], ALU.add,
                          r=[quad[0][1], pptk], w=[acck])
                elif len(quad) > 2:
                    acc, acck = dstate["acc"]
                    cx.tt("dve", acc[:, :, :n], acc[:, :, :n], ppt[:, :, :n], ALU.add, r=[acck, pptk], w=[acck])
                if len(quad) == QD or final:
                    dq.append(dstate["acc"] if len(quad) > 1 else (ppt, pptk))
                    quad.clear()
                if final:
                    flush_d(True)

            for idx, kt in enumerate(kts):
                piece = piece_of[kt]
                b0 = sa.pair()
                for c in range(2):
                    cx.mm(ps[:, b0 + c, :n], Kh[64 * c:64 * c + 64, kt * 128:(kt + 1) * 128],
                          qh[64 * c:64 * c + 64, off:off + n], start=True, stop=True,
                          r=[f"K{piece}", qk], w=[f"ps{b0 + c}"])
                pt = Pt[it % NPT]
                ptk = f"a_P{it % NPT}"
                cx.act(pt[:, :, :n], ps[:, b0:b0 + 2, :n], AF.Exp, r=[f"ps{b0}", f"ps{b0 + 1}"], w=[ptk],
                       scale=0.125)
                it += 1
                pvq.append((kt, pt, ptk, idx))
                if len(pvq) > 2:
                    prev = pvq.pop(0)
                    emit_pv(prev, False)
                    pkt = prev[0]
                    pp = piece_of[pkt]
                    if last_group and h + 1 < 4 and pkt == pieces[pp][0] + pieces[pp][1] - 1 \
                            and pp != len(pieces) - 1:
                        load_kv(h + 1, pp)
                if it % 2 == 0:
                    b1_step()
                if it % 5 == 2:
                    bg_step()
                if idx in (1, 5) and post_pending:
                    for gen in list(post_pending):
                        try:
                            next(gen)
                        except StopIteration:
                            post_pending.remove(gen)
            while pvq:
                prev = pvq.pop(0)
                emit_pv(prev, not pvq)
            if last_group and h + 1 < 4:
                load_kv(h + 1, len(pieces) - 1)
            for gen in list(post_pending):
                for _ in gen:
                    pass
                post_pending.remove(gen)
            cx.copy("dve", oo[:, :, :n], ps[:, 4:6, :n], r=["ps4", "ps5"], w=["a_oo"])
            o, ok = o_rot.next()
            osq, osqk = osq_rot.next()

            def post(h=h, off=off, n=n, o=o, ok=ok, osq=osq, osqk=osqk):
                cx.recip_act(rD[:, :, :n], ps[:, 6:8, :n], r=["ps6", "ps7"], w=["a_rD"])
                cx.tt("dve", oo[:, :, :n], oo[:, :, :n], rD[:, :, :n], ALU.mult, r=["a_oo", "a_rD"], w=["a_oo"])
                cx.stt(o[:, :n], oo[:, 1, :n], der[:, 0:1], oo[:, 0, :n], ALU.mult, ALU.add,
                       r=["a_oo", "b_der0"], w=[ok])
                cx.tt("dve", osq[:, :n], o[:, :n], o[:, :n], ALU.mult, r=[ok], w=[osqk])
                yield
                b, bk = sa.one()
                cx.mm(ps[:, b, :n], ones128n, osq[:, :n], start=True, stop=True, r=[osqk, "consts"], w=[bk])
                r3, r3k = r3_rot.next()
                rstd_from_psum(cx, ps[:, b, :n], bk, r3[:, :n], r3k, mhalf[:, :n], n)
                sa.release(b)
                cx.stt(mixT[:, h, off:off + n], o[:, :n], der[:, 1:2], r3[:, :n], ALU.mult, ALU.mult,
                       r=[ok, r3k, "b_der1"], w=[f"mix{h}_{off}"])
                yield

            post_pending.append(post())
    for gen in list(post_pending):
        for _ in gen:
            pass
    while not b1_done[0]:
        b1_step()
    while not bg_done[0]:
        bg_step()

    cx.pop()
    cx.dma("sp", mods[:], mod, r=[], w=["b_mod"], key="b_mod")
    cx.stt(G2[:], mods[:, 32:40, :], 1.0, vec[:, 12:20, None].to_broadcast([128, 8, 2]), ALU.add, ALU.mult,
           r=["b_mod", "b_vec"], w=["modc"])
    cx.copy("dve", g2h[:], mods[:, 40:48, :], r=["b_mod"], w=["modc2"])
    banks = BankRot([0, 1, 2, 3, 4, 5, 6, 7])
    wo_sb = cx.sb("f_wout", [128, 8, 1024], BF16)
    for hf_ in range(2):
        cx.dma("pool", wo_sb[:, :, hf_ * 512:(hf_ + 1) * 512], w_out[:, :, hf_ * 512:(hf_ + 1) * 512], r=[],
               w=[f"f_wout{hf_}"], key=f"f_wout{hf_}")
    xg_rot = Rot(cx, "f_xg", [128, 8, 512], F32, 2)
    sq = cx.sb("f_sq", [128, 8, 512], BF16)
    u = cx.sb("f_u", [128, 8, 512], F32)
    rs_rot = Rot(cx, "f_rs", [128, 512], F32, 2)
    aT = cx.sb("f_aT", [128, NJ, 512], BF16)
    win_rot = Rot(cx, "f_win", [128, 8, 256], BF16, 4)
    wo2_rot = Rot(cx, "f_wo2", [128, NJ, 128], BF16, 3)
    th_rot = Rot(cx, "f_th", [128, 512], F32, 2)
    s_rot = Rot(cx, "f_s", [128, 512], F32, 2)

    h2_rot = Rot(cx, "f_h2T", [128, 8, 512], BF16, 2)

    def outproj_norm(g):
        off, n = GROUPS[g]
        col = 1 if off >= TL else 0
        xg, xgk = xg_rot.next()
        cx.dma("sp", xg[:, :, :n], xT[:, :, off:off + n], r=[], w=[xgk], key=xgk)
        mixk = [f"mix{m}_{off}" for m in range(4)] + [f"mix{m}_{g}" for m in range(4, 8)]
        for dc in range(8):
            b, bk = banks.next()
            for m in range(8):
                cx.mm(ps[:, b, :n], wo_sb[:, m, dc * 128:(dc + 1) * 128], mixT[:, m, off:off + n],
                      start=(m == 0), stop=(m == 7), r=[f"f_wout{dc // 4}", mixk[m]], w=[bk])
            cx.stt(xg[:, dc, :n], ps[:, b, :n], g1[:, dc, col:col + 1], xg[:, dc, :n], ALU.mult, ALU.add,
                   r=[bk, "b_mod", xgk], w=[xgk])
            banks.release(b)
        h2T, h2k = h2_rot.next()
        gen = norm_mod_gen(cx, g, xg, xgk, h2T, h2k, sq, u, rs_rot, onesD, mhalf, ps, banks, G2, sh2)
        return xg, xgk, h2T, h2k, gen

    def run_gen(gen):
        for _ in gen:
            pass

    nxt = outproj_norm(groups_run[0])
    run_gen(nxt[4])
    for gi, g in enumerate(groups_run):
        off, n = GROUPS[g]
        col = 1 if off >= TL else 0
        xg, xgk, h2T, h2k, _ = nxt
        ngen = None
        if gi + 1 < len(groups_run):
            nxt = outproj_norm(groups_run[gi + 1])
            ngen = nxt[4]
        for j in range(NJ):
            wj, wjk = win_rot.next()
            cx.dma("pool", wj[:], w_ffn_in[j], r=[], w=[wjk], key=wjk)
            bg, bgk = banks.next()
            bu, buk = banks.next()
            for c in range(8):
                cx.mm(ps[:, bg, :n], wj[:, c, 0:128], h2T[:, c, :n], start=(c == 0), stop=(c == 7),
                      r=[wjk, h2k], w=[bgk])
            for c in range(8):
                cx.mm(ps[:, bu, :n], wj[:, c, 128:256], h2T[:, c, :n], start=(c == 0), stop=(c == 7),
                      r=[wjk, h2k], w=[buk])
            th, thk = th_rot.next()
            cx.act(th[:, :n], ps[:, bg, :n], AF.Exp, r=[bgk], w=[thk], scale=-1.0)
            sv, svk = s_rot.next()
            cx.recip_act(th[:, :n], th[:, :n], r=[thk], w=[thk], one=cf[:, 4, 0:1])
            cx.tt("dve", sv[:, :n], th[:, :n], ps[:, bg, :n], ALU.mult, r=[thk, bgk], w=[svk])
            banks.release(bg)
            cx.tt("dve", aT[:, j, :n], sv[:, :n], ps[:, bu, :n], ALU.mult, r=[svk, buk], w=[f"f_aT{j}"])
            banks.release(bu)
            if ngen is not None and j >= 6 and j % 2 == 0:
                try:
                    next(ngen)
                except StopIteration:
                    ngen = None
        if ngen is not None:
            run_gen(ngen)
        for dc in range(8):
            w2, w2k = wo2_rot.next()
            cx.dma("pool", w2[:], w_ffn_out[dc], r=[], w=[w2k], key=w2k)
            b, bk = banks.next()
            for j in range(NJ):
                cx.mm(ps[:, b, :n], w2[:, j, :], aT[:, j, :n], start=(j == 0), stop=(j == NJ - 1),
                      r=[w2k, f"f_aT{j}"], w=[bk])
            cx.stt(xg[:, dc, :n], ps[:, b, :n], g2h[:, dc, col:col + 1], xg[:, dc, :n], ALU.mult, ALU.add,
                   r=[bk, "modc2", xgk], w=[xgk])
            banks.release(b)
        cx.dma("sp", xT_o_fn(off, n), xg[:, :, :n], r=[xgk], w=[cx.uid("xT_o")], key=xgk)


def invcnt_table(core):
    out = np.zeros((128, 2, T), np.float32)
    wins = {(0, 0): 2, (64, 0): 4, (0, 1): 8, (64, 1): 16}
    for (p0, ci), w in wins.items():
        for off, L, base in ((0, SEQ, (core % 4) * TL), (TL, TC, 0)):
            nt = TL if off == 0 else TC
            t = base + np.arange(nt)
            lo = np.clip(t - w // 2, 0, L)
            hi = np.clip(t + w - w // 2, 0, L)
            out[p0:p0 + 64, ci, off:off + nt] = (1.0 / (hi - lo))[None, :]
    return out


CC_GROUPS = [[0, 1, 2, 3], [4, 5, 6, 7]]
PIECES = [(0, 2)] + [(2 + 8 * i, 8) for i in range(8)]


def build_fused(depth=DEPTH):
    cx = Ctx()
    nc = cx.nc
    x0T = cx.din("x0T", [128, 8, T], F32)
    cv = cx.din("cv", [128, 8, 2], F32)
    w_mod = cx.din("w_mod", [DEPTH, D, 6 * D], F32)
    b_mod_fm = cx.din("b_mod_fm", [DEPTH, 128, 48], F32)
    lam_in = cx.din("lam_in", [128, DEPTH, 4, 64], F32)
    cmat = cx.din("cmat", [128, 6, 128], F32)
    n1g = cx.din("n1g", [DEPTH, 128, 8], F32)
    w_in = cx.din("w_in", [DEPTH, 128, 8, 2304], F32)
    qkg = cx.din("qkg", [DEPTH, 128, 2], F32)
    poolw = cx.din("poolw", [DEPTH, 128, 2, 128], F32)
    vecs = cx.din("vecs", [DEPTH, 128, NVEC], F32)
    convw = cx.din("convw", [DEPTH, 128, 2, 31], F32)
    w_out = cx.din("w_out", [DEPTH, 128, 8, 1024], F32)
    w_ffn_in = cx.din("w_ffn_in", [DEPTH, NJ, 128, 8, 256], F32)
    w_ffn_out = cx.din("w_ffn_out", [DEPTH, 8, 128, NJ, 128], F32)
    cosT = cx.din("cosT", [128, TL], F32)
    sinT = cx.din("sinT", [128, TL], F32)
    invcnt = cx.din("invcnt", [128, 2, T], F32)
    selT = cx.din("selT", [128, 8], F32)
    outT = cx.dout("outT", [128, 8, TL], F32)
    modT = cx.dscratch("modT", [128, DEPTH, 48, 2], F32)
    lamd = cx.dscratch("lamd", [128, DEPTH], F32)
    xs = [cx.dscratch(f"xs{i}", [128, 8, T], F32) for i in range(2)]
    qTs = cx.dscratch("qTs", [4, 128, T], BF16)
    kcs = cx.dscratch("kcs", [4, 128, TC], BF16)
    vcs = cx.dscratch("vcs", [4, 128, 2, 128], BF16)
    up_pad = cx.dscratch("up_pad", [128, 2, TP], F32)
    glu_pad = cx.dscratch("glu_pad", [128, 2, TP], F32)
    sendK = [[cx.dscratch(f"sendK{p}{h}", [512, 1024], BF16) for h in range(2)] for p in range(2)]
    sendV = [[cx.dscratch(f"sendV{p}{h}", [512, 1024], BF16) for h in range(2)] for p in range(2)]
    recvK = [[cx.dscratch(f"recvK{p}{h}", [2048, 1024], BF16) for h in range(2)] for p in range(2)]
    recvV = [[cx.dscratch(f"recvV{p}{h}", [2048, 1024], BF16) for h in range(2)] for p in range(2)]
    sendE = [cx.dscratch(f"sendE{p}", [256, 64], F32) for p in range(2)]
    recvE = [cx.dscratch(f"recvE{p}", [1024, 64], F32) for p in range(2)]

    ps = cx.es.enter_context(nc.psum_tensor("ps", [128, 8, 512], F32))
    consts = load_consts(cx, cmat)
    sel = cx.sb("sel_sb", [128, 8], F32)
    cx.dma("sp", sel[:], selT, r=[], w=["selT"], key="selT")
    E = cx.sb("E", [128, 4, 2, 64], F32)
    H = cx.sb("H", [128, 2, 2, 2, HALO], F32)
    zt = cx.sb("zt", [128, 2, HALO], F32)
    cx.memset("dve", zt[:], 0.0, w=["zt"])
    zi = 0
    for pad in (up_pad, glu_pad):
        for o in (0, HALO + TL, TPL, TPL + HALO + TC):
            cx.dma("sp", pad[:, :, o:o + HALO], zt[:], r=["zt"], w=[f"zpad{zi}"], key=f"zpad{zi % 4}")
            zi += 1

    silb = cx.sb("silb", [128, 8, 2], BF16)
    cx.push("M_")
    emit_mod_pre(cx, cv, lam_in, lamd, silb)
    for _ in mod_layer_gen(cx, ps, [(0, 0, 6)], silb, w_mod, b_mod_fm, modT, lambda: (0, "ps0", lambda: None),
                           cx.sb):
        pass
    cx.pop()

    for l in range(depth):
        par = l % 2
        last = (l == depth - 1)
        x_in = x0T if l == 0 else xs[(l - 1) % 2]

        def k_sink(h, off, n, par=par):
            if off >= TL:
                return kcs[h, :, :], None
            return (sendK[par][off // 1024][h * 128:(h + 1) * 128, off % 1024: off % 1024 + n],
                    f"sendK{off // 1024}")

        def v_sink(ti, par=par):
            if ti >= 16:
                return vcs.rearrange("h p t d -> p h t d")[:, :, ti - 16, :], None
            return (sendV[par][ti // 8].rearrange("(h p) c -> p h c", p=128)[:, :, (ti % 8) * 128:(ti % 8 + 1) * 128],
                    f"sendV{ti // 8}")

        def pad_sink(pad):
            def f(ci, off, n):
                o = HALO + off if off < TL else TPL + HALO + (off - TL)
                return pad[:, ci, o:o + n]
            return f

        def edge_sink(tz, ci, side, off, par=par):
            if (side == 0 and off == 0) or (side == 1 and off == TL - 512):
                return sendE[par][ci * 128:(ci + 1) * 128, tz * 32 + side * HALO: tz * 32 + (side + 1) * HALO]
            return None

        def issue_cc(name, par=par):
            tab = {"K0": (sendK[par][0], recvK[par][0], "sendK0", 0), "V0": (sendV[par][0], recvV[par][0], "sendV0", 1),
                   "K1": (sendK[par][1], recvK[par][1], "sendK1", 2), "V1": (sendV[par][1], recvV[par][1], "sendV1", 3),
                   "E": (sendE[par], recvE[par], "sendE", 4)}
            sbuf_, rbuf_, skey, ci_ = tab[name]
            cx.S.op("pool", lambda e, a=sbuf_, b=rbuf_: e.collective_compute(
                "AllGather", ALU.bypass, replica_groups=CC_GROUPS, ins=[a], outs=[b]),
                r=[skey], w=["recv", f"recv_{skey}"], key=f"cc{ci_}", inc1=True)

        sinks = {"q": lambda h, off, n: (qTs[h, :, off:off + n], None), "k": k_sink, "v": v_sink,
                 "up": pad_sink(up_pad), "glu": pad_sink(glu_pad), "edge": edge_sink, "cc": issue_cc}
        cx.push(f"A{l}_")
        emit_A(cx, consts, ps, x_in, modT[:, l], n1g[l], w_in[l], qkg[l], cosT, sinT, sinks)
        cx.pop()
        cx.dma("sp", E[:], recvE[par].rearrange("(j c p) x -> p j c x", j=4, c=2, p=128), r=["recv_sendE"],
               w=["E"], key="E")
        for tz in range(2):
            for side in range(2):
                c0 = tz * 32 + (HALO if side == 0 else 0)
                hv = H[:, tz, side, :, :]
                for j in range(4):
                    sc = sel[:, 4 * side + j: 4 * side + j + 1]
                    if j == 0:
                        cx.ts("dve", hv, E[:, j, :, c0:c0 + HALO], sc, ALU.mult, r=["E", "selT"], w=[f"H{tz}{side}"])
                    else:
                        cx.stt(hv, E[:, j, :, c0:c0 + HALO], sc, hv, ALU.mult, ALU.add,
                               r=["E", "selT", f"H{tz}{side}"], w=[f"H{tz}{side}"])
                pad = up_pad if tz == 0 else glu_pad
                o = 0 if side == 0 else HALO + TL
                cx.dma("sp", pad[:, :, o:o + HALO], hv, r=[f"H{tz}{side}"], w=["halo_up" if tz == 0 else "halo_glu"],
                       key=f"zpad{2 * tz + side}")
        def kv_src(h, piece, par=par):
            if piece == 0:
                return kcs[h], vcs[h]
            j, hf = (piece - 1) // 2, (piece - 1) % 2
            rows = slice(j * 512 + h * 128, j * 512 + (h + 1) * 128)
            return recvK[par][hf][rows, :], recvV[par][hf][rows, :].rearrange("p (t d) -> p t d", d=128)

        if last:
            xo_fn = lambda off, n: outT[:, :, off:off + n]
        else:
            xo_fn = lambda off, n, l=l: xs[l % 2][:, :, off:off + n]
        cx.push(f"B{l}_")
        jobs = ([(0, 6, 16)] if l == 0 else []) + ([(l + 1, 0, 16)] if not last else [])
        bg = None
        if jobs:
            bg = lambda alloc, bank_fn, jobs=jobs: mod_layer_gen(cx, ps, jobs, silb, w_mod, b_mod_fm, modT,
                                                                 bank_fn, alloc)
        emit_B(cx, consts, ps, x_in, modT[:, l], lamd[:, l:l + 1], qTs, PIECES, kv_src, up_pad, glu_pad, invcnt,
               poolw[l], vecs[l], convw[l], w_out[l], w_ffn_in[l], w_ffn_out[l], xo_fn, last=last, bg=bg)
        cx.pop()
    return cx.finish()


_CACHE = {}


def layer_consts(l):
    return 0.8 - 0.6 * math.exp(-0.3 * l)


def prep_weights(inputs):
    f32 = np.float32
    w = {}
    w["n1g"] = np.stack([fm(inputs["norm1_g"][l], 8) for l in range(DEPTH)])
    w["w_in"] = np.stack([fm_w(np.asarray(inputs["w_in"][l], f32)) for l in range(DEPTH)])
    w["qkg"] = np.stack([np.stack([np.tile(inputs["q_norm_g"][l], 2), np.tile(inputs["k_norm_g"][l], 2)], -1)
                         for l in range(DEPTH)]).astype(f32)
    poolw = np.zeros((DEPTH, 128, 2, 128), f32)
    vecs = np.zeros((DEPTH, 128, NVEC), f32)
    for l in range(DEPTH):
        for ci in range(2):
            poolw[l, 0:64, ci, 0:64] = inputs["pool_w"][l][2 * ci]
            poolw[l, 64:128, ci, 64:128] = inputs["pool_w"][l][2 * ci + 1]
        li = layer_consts(l)
        vecs[l, :, 0:2] = fm(inputs["pool_scale"][l], 2)
        vecs[l, :, 2:4] = fm(inputs["conv_dw_b"][l], 2)
        vecs[l, :, 4:6] = fm(inputs["conv_ln_g"][l], 2)
        vecs[l, :, 6:8] = fm(inputs["conv_ln_b"][l], 2)
        vecs[l, :, 8] = inputs["subln_g"][l]
        vecs[l, :, 10] = li
        vecs[l, :, 11] = 1.0 - li
        vecs[l, :, 12:20] = fm(inputs["norm2_g"][l], 8)
    w["poolw"] = poolw
    w["vecs"] = vecs
    w["convw"] = np.stack([np.asarray(inputs["conv_dw_w"][l], f32).T.reshape(2, 128, 31).transpose(1, 0, 2)
                           for l in range(DEPTH)])
    w["w_out"] = np.stack([fm_w(np.asarray(inputs["w_out"][l], f32)) for l in range(DEPTH)])
    wfi = np.asarray(inputs["w_ffn_in"], f32)
    wg = wfi[:, :, :FF].reshape(DEPTH, 8, 128, NJ, 128)
    wu = wfi[:, :, FF:].reshape(DEPTH, 8, 128, NJ, 128)
    w["w_ffn_in"] = np.ascontiguousarray(np.concatenate([wg, wu], -1).transpose(0, 3, 2, 1, 4))
    w["w_ffn_out"] = np.ascontiguousarray(
        np.asarray(inputs["w_ffn_out"], f32).reshape(DEPTH, NJ, 128, 8, 128).transpose(0, 3, 2, 1, 4))
    return {k: np.ascontiguousarray(v, dtype=f32) for k, v in w.items()}


def make_in_maps(inputs):
    f32 = np.float32
    x = np.asarray(inputs["x"], f32)
    c = np.asarray(inputs["c"], f32)
    ctx = np.asarray(inputs["ctx"], f32)
    c_ctx = np.asarray(inputs["c_ctx"], f32)
    w = prep_weights(inputs)
    lam_in = np.stack([inputs["lambda_q1"], inputs["lambda_k1"], inputs["lambda_q2"], inputs["lambda_k2"]], 1)
    shared = dict(w)
    shared["lam_in"] = np.ascontiguousarray(np.broadcast_to(np.asarray(lam_in, f32)[None], (128, DEPTH, 4, 64)))
    shared["w_mod"] = np.ascontiguousarray(np.asarray(inputs["w_mod"], f32))
    shared["b_mod_fm"] = np.ascontiguousarray(np.asarray(inputs["b_mod"], f32).reshape(DEPTH, 48, 128).transpose(0, 2, 1))
    shared["cmat"] = const_mats()
    in_maps = []
    for i in range(NCORE):
        b, r = i // 4, i % 4
        t0 = r * TL
        m = dict(shared)
        xall = np.concatenate([x[b, t0:t0 + TL], ctx[b]], 0)
        m["x0T"] = np.ascontiguousarray(xall.T.reshape(8, 128, T).transpose(1, 0, 2))
        cvv = np.stack([c[b], c_ctx], -1)
        m["cv"] = np.ascontiguousarray(cvv.reshape(8, 128, 2).transpose(1, 0, 2))
        m["cosT"], m["sinT"] = rope_tables(i)
        m["invcnt"] = invcnt_table(i)
        sel = np.zeros((128, 8), f32)
        if r > 0:
            sel[:, r - 1] = 1.0
        if r < 3:
            sel[:, 4 + r + 1] = 1.0
        m["selT"] = sel
        in_maps.append(m)
    return in_maps


def kernel(**inputs):
    in_maps = make_in_maps(inputs)
    if "nc" not in _CACHE:
        _CACHE["nc"] = build_fused()
    res = run_bass_kernel_spmd(_CACHE["nc"], in_maps, core_ids=list(range(NCORE))).results
    out = np.zeros((2, SEQ, D), np.float32)
    for i in range(NCORE):
        b, t0 = i // 4, (i % 4) * TL
        xo = res[i]["outT"].transpose(1, 0, 2).reshape(D, TL)
        out[b, t0:t0 + TL] = xo.T
    return out
```
